# Optimizing a Trainium2 kernel written in Bass

```python
import jax, jax.numpy as jnp
from jax import lax
import numpy as np

D_MODEL = 2048
BATCH = 8
SEQ = 4096
DEPTH = 4
DEC_BATCH = 4
DEC_SEQ = 4096
PAST_LEN = 128

GRID_W = 64
A_HEAD_DIM = 64
A_HEADS = (D_MODEL // 2) // A_HEAD_DIM
A_W = A_HEADS * A_HEAD_DIM
A_DECAY_LORA = 64
A_ICLR_LORA = 64
A_GATE_LORA = 128
A_COLS = 3 * A_W + A_DECAY_LORA + A_ICLR_LORA + A_GATE_LORA
A_GN_EPS = 64e-5
B_HEAD_DIM = 64
B_HEADS = (D_MODEL // 2) // B_HEAD_DIM
B_W = B_HEADS * B_HEAD_DIM
NA_ROWS = 8
NA_COLS = 16
NA_QBLOCK = 16
NA_SPAN = 2 * NA_COLS
AB_IN = A_COLS + 3 * B_W
AB_OUT_IN = A_W + B_W
C_HEADS = 8
C_QK_DIM = D_MODEL // C_HEADS
C_V_DIM = 2 * C_QK_DIM
C_QK = C_HEADS * C_QK_DIM
C_V = C_HEADS * C_V_DIM
C_IN = 2 * C_QK + 2 * C_V
RET_CHUNK = 128
ROPE_BASE = 10000.0
D_FF = ((8 * D_MODEL // 3 + 127) // 128) * 128
CONV_W = 3
DN_ALPHA = (2.0 * DEPTH) ** 0.25
DN_BETA = (8.0 * DEPTH) ** -0.25
LN_EPS = 1e-5
N_EVEN = (DEPTH + 1) // 2
N_ODD = DEPTH // 2
NEG_INF = -1e30

kernel_name = 'bidir_rwkv7_natten_retnet_encoder'


def layer_norm(x, g, b):
    xf = x.astype(jnp.float32)
    mu = xf.mean(-1, keepdims=True)
    var = jnp.square(xf - mu).mean(-1, keepdims=True)
    return ((xf - mu) * lax.rsqrt(var + LN_EPS) * g + b).astype(x.dtype)


def head_group_norm(y, g, b, eps):
    bn, s, h, d = y.shape
    mu = y.mean(-1, keepdims=True)
    var = jnp.square(y - mu).mean(-1, keepdims=True)
    return ((y - mu) * lax.rsqrt(var + eps)).reshape(bn, s, h * d) * g + b


def centred_dwconv3(h, w):
    prev = jnp.pad(h[:, :-1], ((0, 0), (1, 0), (0, 0)))
    nxt = jnp.pad(h[:, 1:], ((0, 0), (0, 1), (0, 0)))
    return w[0] * prev + w[1] * h + w[2] * nxt


def rotary(x, pos):
    half = x.shape[-1] // 2
    inv = ROPE_BASE ** (-jnp.arange(half, dtype=jnp.float32) / half)
    ang = pos[:, None] * inv[None, :]
    cos = jnp.cos(ang)[:, None, :]
    sin = jnp.sin(ang)[:, None, :]
    x1, x2 = x[..., :half], x[..., half:]
    return jnp.concatenate([x1 * cos - x2 * sin, x1 * sin + x2 * cos], axis=-1)


def rwkv7_scan(r, decay, k, v, kk, a, reverse):
    bn, s, h, n = r.shape

    def step(state, inp):
        r_t, w_t, k_t, v_t, kk_t, a_t = inp
        sa = jnp.einsum('bhij,bhj->bhi', state, -kk_t)
        new = (state * w_t[:, :, None, :]
               + sa[..., None] * (kk_t * a_t)[:, :, None, :]
               + v_t[..., None] * k_t[:, :, None, :])
        read = state if reverse else new
        return new, jnp.einsum('bhij,bhj->bhi', read, r_t)

    xs = tuple(jnp.moveaxis(t, 1, 0) for t in (r, decay, k, v, kk, a))
    s0 = jnp.zeros((bn, h, n, n), jnp.float32)
    _, ys = lax.scan(step, s0, xs, reverse=reverse)
    return jnp.moveaxis(ys, 0, 1)


def rwkv7_mix(ha, w0, w_up, a0, a_up, g_up, k_k, k_a, r_k, gn_g, gn_b):
    bn, s, _ = ha.shape
    haf = ha.astype(jnp.float32)
    r = haf[..., :A_W]
    k = haf[..., A_W:2 * A_W]
    v = haf[..., 2 * A_W:3 * A_W]
    o = 3 * A_W
    w_lo = haf[..., o:o + A_DECAY_LORA]
    o += A_DECAY_LORA
    a_lo = haf[..., o:o + A_ICLR_LORA]
    o += A_ICLR_LORA
    g_lo = haf[..., o:o + A_GATE_LORA]

    def heads(t):
        return t.reshape(bn, s, A_HEADS, A_HEAD_DIM)

    gate = jax.nn.sigmoid(g_lo) @ g_up
    kk = heads(k * k_k)
    kk = kk / jnp.maximum(jnp.linalg.norm(kk, axis=-1, keepdims=True), 1e-12)
    w_act = jnp.tanh(w_lo)

    def direction(d, reverse):
        w_log = -jax.nn.softplus(-(w0[d] + w_act @ w_up[d])) - 0.5
        decay = jnp.exp(-jnp.exp(w_log))
        a = jax.nn.sigmoid(a0[d] + a_lo @ a_up[d])
        k_d = k * (1.0 + (a - 1.0) * k_a)
        y_d = rwkv7_scan(heads(r), heads(decay), heads(k_d), heads(v), kk, heads(a), reverse)
        return y_d, k_d

    y_f, k_f = direction(0, False)
    y_b, _ = direction(1, True)
    y = head_group_norm(y_f + y_b, gn_g, gn_b, A_GN_EPS)
    bonus = (jnp.sum(heads(r) * heads(k_f) * r_k, axis=-1, keepdims=True) * heads(v)).reshape(bn, s, A_W)
    return ((y + bonus) * gate).astype(ha.dtype)


def na_static(wr):
    nb = GRID_W // NA_QBLOCK
    qcol = np.arange(nb)[:, None] * NA_QBLOCK + np.arange(NA_QBLOCK)[None, :]
    span0 = np.clip(np.arange(nb) * NA_QBLOCK - NA_COLS // 2, 0, GRID_W - NA_SPAN)
    kcol = span0[:, None] + np.arange(NA_SPAN)[None, :]
    cs = np.clip(qcol - NA_COLS // 2, 0, GRID_W - NA_COLS)
    valid = (kcol[:, None, :] >= cs[:, :, None]) & (kcol[:, None, :] < cs[:, :, None] + NA_COLS)
    col_off = np.clip(kcol[:, None, :] - qcol[:, :, None], -(NA_COLS - 1), NA_COLS - 1) + NA_COLS - 1
    mask = np.broadcast_to(valid[:, :, None, :], (nb, NA_QBLOCK, wr, NA_SPAN)).reshape(nb, NA_QBLOCK, wr * NA_SPAN)
    return kcol, col_off, mask


def neighbourhood_attention(hb, rpb):
    bn, s, _ = hb.shape
    rows = s // GRID_W
    wr = min(NA_ROWS, rows)
    nb = GRID_W // NA_QBLOCK
    kcol, col_off, mask = na_static(wr)

    def grid(t):
        return t.reshape(bn, rows, GRID_W, B_HEADS, B_HEAD_DIM)

    q = grid(hb[..., :B_W])
    k = grid(hb[..., B_W:2 * B_W])
    v = grid(hb[..., 2 * B_W:])
    scale = B_HEAD_DIM ** -0.5

    def gather_keys(t, rs):
        t = lax.dynamic_slice_in_dim(t, rs, wr, axis=1)[:, :, kcol]
        return t.transpose(0, 2, 1, 3, 4, 5).reshape(bn, nb, wr * NA_SPAN, B_HEADS, B_HEAD_DIM)

    def row_block(r):
        rs = jnp.clip(r - wr // 2, 0, rows - wr)
        qr = lax.dynamic_index_in_dim(q, r, axis=1, keepdims=False).reshape(bn, nb, NA_QBLOCK, B_HEADS, B_HEAD_DIM)
        kb = gather_keys(k, rs)
        vb = gather_keys(v, rs)
        row_off = rs - r + jnp.arange(wr) + NA_ROWS - 1
        bias = rpb[:, row_off][:, :, col_off]
        bias = bias.transpose(0, 2, 3, 1, 4).reshape(B_HEADS, nb, NA_QBLOCK, wr * NA_SPAN)
        sc = jnp.einsum('bnqhd,bnkhd->bhnqk', qr, kb).astype(jnp.float32) * scale + bias.astype(jnp.float32)
        sc = jnp.where(mask, sc, NEG_INF)
        p = jax.nn.softmax(sc, axis=-1).astype(vb.dtype)
        return jnp.einsum('bhnqk,bnkhd->bnqhd', p, vb).reshape(bn, GRID_W, B_W)

    out = lax.map(row_block, jnp.arange(rows))
    return out.transpose(1, 0, 2, 3).reshape(bn, s, B_W)


def mixer_ab(x, w_in, shift, w0, w_up, a0, a_up, g_up, k_k, k_a, r_k, gn_g, gn_b, rpb, w_out):
    h = x @ w_in
    ya = rwkv7_mix(centred_dwconv3(h[..., :A_COLS], shift), w0, w_up, a0, a_up, g_up, k_k, k_a, r_k, gn_g, gn_b)
    yb = neighbourhood_attention(h[..., A_COLS:], rpb)
    return jnp.concatenate([ya, yb], axis=-1) @ w_out


def retention_direction(q, k, v, log_gamma, inclusive):
    bn, s, h, dk = q.shape
    dv = v.shape[-1]
    c = RET_CHUNK
    n = s // c

    def chunks(t):
        return t.reshape(bn, n, c, h, t.shape[-1]).transpose(1, 0, 3, 2, 4)

    pos = jnp.arange(c, dtype=jnp.float32)
    diff = pos[:, None] - pos[None, :]
    keep = (diff >= 0) if inclusive else (diff > 0)
    intra = jnp.where(keep[None], jnp.exp(log_gamma[:, None, None] * jnp.maximum(diff, 0.0)[None]), 0.0)
    q_dec = jnp.exp(log_gamma[:, None] * (pos + 1.0)[None])[..., None]
    k_dec = jnp.exp(log_gamma[:, None] * (c - 1.0 - pos)[None])[..., None]
    c_dec = jnp.exp(log_gamma * c)[:, None, None]

    def step(state, inp):
        qc, kc, vc = inp
        sc = jnp.einsum('bhqd,bhkd->bhqk', qc, kc) * intra
        o = jnp.einsum('bhqk,bhkv->bhqv', sc, vc) + jnp.einsum('bhqd,bhdv->bhqv', qc * q_dec, state)
        state = state * c_dec + jnp.einsum('bhkd,bhkv->bhdv', kc * k_dec, vc)
        return state, o

    s0 = jnp.zeros((bn, h, dk, dv), jnp.float32)
    _, out = lax.scan(step, s0, (chunks(q), chunks(k), chunks(v)))
    return out.transpose(1, 0, 3, 2, 4).reshape(bn, s, h, dv)


def mixer_c(x, w_in, gn_g, gn_b, w_out):
    bn, s, _ = x.shape
    h = (x @ w_in).astype(jnp.float32)
    q = h[..., :C_QK].reshape(bn, s, C_HEADS, C_QK_DIM)
    k = h[..., C_QK:2 * C_QK].reshape(bn, s, C_HEADS, C_QK_DIM)
    v = h[..., 2 * C_QK:2 * C_QK + C_V].reshape(bn, s, C_HEADS, C_V_DIM)
    g = h[..., 2 * C_QK + C_V:]
    pos = jnp.arange(s, dtype=jnp.float32)
    q = rotary(q, pos)
    k = rotary(k, pos) * (C_QK_DIM ** -0.5)
    log_gamma = jnp.log(1.0 - jnp.power(2.0, -5.0 - jnp.arange(C_HEADS, dtype=jnp.float32)))
    y_f = retention_direction(q, k, v, log_gamma, True)
    y_b = jnp.flip(retention_direction(jnp.flip(q, 1), jnp.flip(k, 1), jnp.flip(v, 1), log_gamma, False), 1)
    y = head_group_norm(y_f + y_b, gn_g, gn_b, LN_EPS)
    return (jax.nn.silu(g) * y).astype(x.dtype) @ w_out


def conv_ffn(x, w_up, conv_w, conv_b, w_down):
    h = x @ w_up
    gate = centred_dwconv3(h[..., :D_FF], conv_w) + conv_b
    return (jax.nn.gelu(gate, approximate=False) * h[..., D_FF:]) @ w_down


def run_trunk(x, ab_w_in, ab_shift, ab_w0, ab_w_up, ab_a0, ab_a_up, ab_g_up, ab_k_k, ab_k_a, ab_r_k,
              ab_gn_g, ab_gn_b, ab_rpb, ab_w_out, c_w_in, c_gn_g, c_gn_b, c_w_out,
              ln1_g, ln1_b, ffn_w_up, ffn_conv, ffn_conv_b, ffn_w_down, ln2_g, ln2_b):
    for l in range(DEPTH):
        e = l // 2
        if l % 2 == 0:
            mix = mixer_ab(x, ab_w_in[e], ab_shift[e], ab_w0[e], ab_w_up[e], ab_a0[e], ab_a_up[e], ab_g_up[e],
                           ab_k_k[e], ab_k_a[e], ab_r_k[e], ab_gn_g[e], ab_gn_b[e], ab_rpb[e], ab_w_out[e])
        else:
            mix = mixer_c(x, c_w_in[e], c_gn_g[e], c_gn_b[e], c_w_out[e])
        x = layer_norm(DN_ALPHA * x + mix, ln1_g[l], ln1_b[l])
        x = layer_norm(DN_ALPHA * x + conv_ffn(x, ffn_w_up[l], ffn_conv[l], ffn_conv_b[l], ffn_w_down[l]),
                       ln2_g[l], ln2_b[l])
    return x


def setup_inputs(seed: int = 0) -> dict:
    key = jax.random.key(seed)
    ks = iter(jax.random.split(key, 40))

    def nrm(shape, scale):
        return jax.random.normal(next(ks), shape, jnp.float32) * scale

    ne, no = N_EVEN, N_ODD
    ramp = (jnp.arange(A_W, dtype=jnp.float32) / (A_W - 1)) ** 0.9
    shift_base = jnp.array([0.25, 0.5, 0.25], jnp.float32)[:, None]
    conv_base = jnp.array([0.2, 0.6, 0.2], jnp.float32)[:, None]
    return {
        'x_prompt': nrm((BATCH, SEQ, D_MODEL), 1.0),
        'x_sample': nrm((DEC_BATCH, DEC_SEQ, D_MODEL), 1.0),
        'ab_w_in': nrm((ne, D_MODEL, AB_IN), D_MODEL ** -0.5),
        'ab_shift': shift_base + nrm((ne, CONV_W, A_COLS), 0.05),
        'ab_w0': -6.0 + 5.0 * ramp + nrm((ne, 2, A_W), 0.1),
        'ab_w_up': nrm((ne, 2, A_DECAY_LORA, A_W), 0.1 * A_DECAY_LORA ** -0.5),
        'ab_a0': nrm((ne, 2, A_W), 0.1),
        'ab_a_up': nrm((ne, 2, A_ICLR_LORA, A_W), A_ICLR_LORA ** -0.5),
        'ab_g_up': nrm((ne, A_GATE_LORA, A_W), A_GATE_LORA ** -0.5),
        'ab_k_k': 0.85 + nrm((ne, A_W), 0.02),
        'ab_k_a': 1.0 + nrm((ne, A_W), 0.02),
        'ab_r_k': nrm((ne, A_HEADS, A_HEAD_DIM), 0.1),
        'ab_gn_g': 1.0 + nrm((ne, A_W), 0.02),
        'ab_gn_b': nrm((ne, A_W), 0.02),
        'ab_rpb': nrm((ne, B_HEADS, 2 * NA_ROWS - 1, 2 * NA_COLS - 1), 0.02),
        'ab_w_out': nrm((ne, AB_OUT_IN, D_MODEL), DN_BETA * AB_OUT_IN ** -0.5),
        'c_w_in': nrm((no, D_MODEL, C_IN), D_MODEL ** -0.5),
        'c_gn_g': 1.0 + nrm((no, C_V), 0.02),
        'c_gn_b': nrm((no, C_V), 0.02),
        'c_w_out': nrm((no, C_V, D_MODEL), DN_BETA * C_V ** -0.5),
        'ln1_g': 1.0 + nrm((DEPTH, D_MODEL), 0.02),
        'ln1_b': nrm((DEPTH, D_MODEL), 0.02),
        'ffn_w_up': nrm((DEPTH, D_MODEL, 2 * D_FF), D_MODEL ** -0.5),
        'ffn_conv': conv_base + nrm((DEPTH, CONV_W, D_FF), 0.05),
        'ffn_conv_b': nrm((DEPTH, D_FF), 0.02),
        'ffn_w_down': nrm((DEPTH, D_FF, D_MODEL), DN_BETA * D_FF ** -0.5),
        'ln2_g': 1.0 + nrm((DEPTH, D_MODEL), 0.02),
        'ln2_b': nrm((DEPTH, D_MODEL), 0.02),
    }


def reference(x_prompt, x_sample, ab_w_in, ab_shift, ab_w0, ab_w_up, ab_a0, ab_a_up, ab_g_up, ab_k_k, ab_k_a,
              ab_r_k, ab_gn_g, ab_gn_b, ab_rpb, ab_w_out, c_w_in, c_gn_g, c_gn_b, c_w_out,
              ln1_g, ln1_b, ffn_w_up, ffn_conv, ffn_conv_b, ffn_w_down, ln2_g, ln2_b):
    weights = (ab_w_in, ab_shift, ab_w0, ab_w_up, ab_a0, ab_a_up, ab_g_up, ab_k_k, ab_k_a, ab_r_k,
               ab_gn_g, ab_gn_b, ab_rpb, ab_w_out, c_w_in, c_gn_g, c_gn_b, c_w_out,
               ln1_g, ln1_b, ffn_w_up, ffn_conv, ffn_conv_b, ffn_w_down, ln2_g, ln2_b)
    y_prompt = run_trunk(x_prompt, *weights)
    y_sample = run_trunk(x_sample, *weights)
    return (y_prompt, y_sample)
```

```python
import numpy as np
import concourse.bass as bass
import concourse.mybir as mybir
from concourse.bass_utils import run_bass_kernel_spmd

F32 = mybir.dt.float32
BF16 = mybir.dt.bfloat16
I32 = mybir.dt.int32
AF = mybir.ActivationFunctionType
ALU = mybir.AluOpType
AX = mybir.AxisListType

ENGS = ('pe', 'act', 'dve', 'pool', 'sp')
DMAQ = {'sp': 16, 'act': 8, 'pool': 8}


class Buf:
    __slots__ = ('name', 'lw', 'rd')

    def __init__(self, name=''):
        self.name = name
        self.lw = None
        self.rd = []


class Op:
    __slots__ = ('eng', 'fn', 'dma', 'deps', 'sig', 'slot', 'slotval', 'idx', 'bar')

    def __init__(self, eng, fn, dma):
        self.eng = eng
        self.fn = fn
        self.dma = dma
        self.deps = ()
        self.sig = False
        self.slot = None
        self.slotval = 0
        self.idx = 0
        self.bar = False


class Prog:
    def __init__(self, nc):
        self.nc = nc
        self.ops = []
        self.last = {e: None for e in ENGS}
        self.sb_off = 16640
        self.sb_base = 16640
        self.nalloc = 0

    def sb(self, shape, dtype, name='t'):
        nbytes = int(np.prod(shape[1:])) * (4 if dtype in (F32, I32) else 2)
        off = (self.sb_off + 63) // 64 * 64
        assert off + nbytes <= 229000, (off, nbytes, name)
        self.sb_off = off + nbytes
        self.nalloc += 1
        return self.nc.alloc_sbuf_tensor_at(f"{name}{self.nalloc}", list(shape), dtype, offset=off)

    def sb_mark(self):
        return self.sb_off

    def sb_reset(self, mark):
        self.sb_off = mark
        self.barrier()

    def add(self, eng, fn, reads=(), writes=(), dma=False):
        op = Op(eng, fn, dma)
        deps = set()
        for b in reads:
            if b.lw is not None:
                deps.add(b.lw)
        for b in writes:
            if b.lw is not None:
                deps.add(b.lw)
            deps.update(b.rd)
        dl = []
        for d in deps:
            if d is op:
                continue
            if (not dma) and (not d.dma) and d.eng == 'pe' and eng == 'pe':
                continue
            d.sig = True
            dl.append(d)
        op.deps = dl
        for b in reads:
            b.rd.append(op)
        for b in writes:
            b.lw = op
            b.rd = []
        self.ops.append(op)
        if not dma:
            self.last[eng] = op
        return op

    def dma(self, q, out, in_, reads=(), writes=(), **kw):
        return self.add(q, lambda e: e.dma_start(out=out, in_=in_, **kw), reads, writes, dma=True)

    def barrier(self):
        op = Op('all', None, False)
        op.bar = True
        for e in ENGS:
            if self.last[e] is not None:
                self.last[e].sig = True
        self.ops.append(op)

    def emit(self):
        nc = self.nc
        self.barrier()
        engsem = {e: nc.alloc_semaphore(f"s_{e}") for e in ENGS}
        slotsem = {q: [nc.alloc_semaphore(f"d_{q}{i}") for i in range(n)] for q, n in DMAQ.items()}
        slotuse = {q: [0] * n for q, n in DMAQ.items()}
        rr = {q: 0 for q in DMAQ}
        cnt = {e: 0 for e in ENGS}
        waited = {e: {} for e in ENGS}
        streams = {e: [] for e in ENGS}

        def want(e, sem, val):
            if val <= 0:
                return
            w = waited[e]
            k = id(sem)
            if w.get(k, 0) >= val:
                return
            w[k] = val
            streams[e].append(('w', sem, val))

        for op in self.ops:
            if op.bar:
                for e in ENGS:
                    for x in ENGS:
                        if x != e:
                            want(e, engsem[x], cnt[x])
                    for q in DMAQ:
                        for i, s in enumerate(slotsem[q]):
                            want(e, s, 16 * slotuse[q][i])
                continue
            e = op.eng
            for d in op.deps:
                if d.dma:
                    want(e, slotsem[d.eng][d.slot], d.slotval)
                else:
                    want(e, engsem[d.eng], d.idx)
            if op.dma:
                k = rr[e]
                rr[e] = (k + 1) % DMAQ[e]
                want(e, slotsem[e][k], 16 * slotuse[e][k])
                slotuse[e][k] += 1
                op.slot = k
                op.slotval = 16 * slotuse[e][k]
                streams[e].append(('d', op, slotsem[e][k]))
            else:
                if op.sig:
                    cnt[e] += 1
                    op.idx = cnt[e]
                streams[e].append(('o', op, engsem[e]))

        def run_stream(eng_handle, items):
            for it in items:
                if it[0] == 'w':
                    eng_handle.wait_ge(it[1], it[2])
                elif it[0] == 'd':
                    it[1].fn(eng_handle).then_inc(it[2], 16)
                else:
                    ins = it[1].fn(eng_handle)
                    if it[1].sig:
                        ins.then_inc(it[2], 1)

        with nc.Block() as block:
            @block.sync
            def _(e):
                run_stream(e, streams['sp'])

            @block.tensor
            def _(e):
                run_stream(e, streams['pe'])

            @block.scalar
            def _(e):
                run_stream(e, streams['act'])

            @block.vector
            def _(e):
                run_stream(e, streams['dve'])

            @block.gpsimd
            def _(e):
                run_stream(e, streams['pool'])
        self.nitems = {e: len(streams[e]) for e in ENGS}


D = 2048
DFF = 5504
ALPHA = 8.0 ** 0.25
LN_EPS = 1e-5


class Ctx:
    pass


def setup_consts(P, C):
    nc = P.nc
    idn = nc.inline_tensor(np.eye(128, dtype=np.float32), "ident_d").ap()
    C.ident = P.sb([128, 128], F32, 'ident')
    C.identb = P.sb([128, 128], BF16, 'identb')
    C.b_ident = Buf('ident')
    P.dma('sp', C.ident[:], idn[:, :], writes=[C.b_ident])
    P.add('dve', lambda e: e.tensor_copy(out=C.identb[:], in_=C.ident[:]), [C.b_ident], [C.b_ident])
    C.ps = [nc.alloc_psum_tensor(f"psb{i}", [128, 512], F32) for i in range(8)]
    C.psb = [Buf(f'ps{i}') for i in range(8)]
    C.psi = 0


def next_ps(C, n=8, base=0):
    i = base + (C.psi % n)
    C.psi += 1
    return C.ps[i], C.psb[i]


def cast_weight(P, C, src, dst, K, N):
    mark = P.sb_mark()
    CB = 2048
    NBUF = 3
    st = [(P.sb([128, CB], F32, 'cw'), P.sb([128, CB], BF16, 'cwb'), Buf(), Buf()) for _ in range(NBUF)]
    engs = ['dve', 'act', 'pool']
    i = 0
    for r0 in range(0, K, 128):
        for c0 in range(0, N, CB):
            cn = min(CB, N - c0)
            a, b, ba, bb = st[i % NBUF]
            eng = engs[i % 3]
            P.dma('sp', a[:, 0:cn], src[r0:r0 + 128, c0:c0 + cn], writes=[ba])
            if eng == 'act':
                P.add('act', lambda e, a=a, b=b, cn=cn: e.copy(out=b[:, 0:cn], in_=a[:, 0:cn]), [ba], [bb])
            else:
                P.add(eng, lambda e, a=a, b=b, cn=cn: e.tensor_copy(out=b[:, 0:cn], in_=a[:, 0:cn]), [ba], [bb])
            P.dma('pool', dst[r0:r0 + 128, c0:c0 + cn], b[:, 0:cn], reads=[bb])
            i += 1
    P.sb_reset(mark)


def xprep(P, C, x, xT, T):
    mark = P.sb_mark()
    KC = D // 128
    NB = 2
    st = [(P.sb([128, D], F32, 'xp'), P.sb([128, KC, 128], BF16, 'xpt'), Buf(), Buf()) for _ in range(NB)]
    xTv = xT.rearrange("(kc p) t -> p kc t", p=128)
    for ti in range(T // 128):
        a, b, ba, bb = st[ti % NB]
        P.dma('sp', a[:], x[ti * 128:(ti + 1) * 128, :], writes=[ba])
        transpose_to(P, C, a, ba, b, bb, KC)
        P.dma('pool', xTv[:, :, ti * 128:(ti + 1) * 128], b[:], reads=[bb])
    P.sb_reset(mark)


def transpose_to(P, C, a, ba, b, bb, KC, evi=[0]):
    for g in range(0, KC, 4):
        ps, psb = next_ps(C)
        ng = min(4, KC - g)
        for j in range(ng):
            kc = g + j
            P.add('pe', lambda e, ps=ps, j=j, kc=kc: e.transpose(ps[:, j * 128:(j + 1) * 128], a[:, kc * 128:(kc + 1) * 128], C.ident[:]),
                  [ba, C.b_ident], [psb])
        eng = 'act' if evi[0] % 2 else 'dve'
        evi[0] += 1
        src = lambda ps=ps, ng=ng: ps[:, 0:ng * 128].rearrange("p (j t) -> p j t", t=128)
        if eng == 'act':
            P.add('act', lambda e, g=g, ng=ng, src=src: e.copy(out=b[:, g:g + ng, :], in_=src()), [psb], [bb])
        else:
            P.add('dve', lambda e, g=g, ng=ng, src=src: e.tensor_copy(out=b[:, g:g + ng, :], in_=src()), [psb], [bb])


def gemm(P, C, xT, w, K, T, N, mode, epi, TT=512, n_off=0):
    mark = P.sb_mark()
    KC = K // 128
    NBW = 512
    xt = [(P.sb([128, KC, TT], BF16, 'gx'), Buf()) for _ in range(2)]
    wt = [(P.sb([128, KC, NBW], BF16, 'gw'), Buf()) for _ in range(2)]
    xTv = xT.rearrange("(kc p) t -> p kc t", p=128)
    wv = w.rearrange("(kc p) n -> p kc n", p=128)
    wi = 0
    for tti, t0 in enumerate(range(0, T, TT)):
        tt = min(TT, T - t0)
        xa, xb = xt[tti % 2]
        h = KC // 2
        P.dma('sp', xa[:, 0:h, 0:tt], xTv[:, 0:h, t0:t0 + tt], writes=[xb])
        P.dma('sp', xa[:, h:KC, 0:tt], xTv[:, h:KC, t0:t0 + tt], writes=[xb])
        for n0 in range(0, N, NBW):
            nn = min(NBW, N - n0)
            wa, wb = wt[wi % 2]
            wi += 1
            P.dma('sp', wa[:, 0:h, 0:nn], wv[:, 0:h, n_off + n0:n_off + n0 + nn], writes=[wb])
            P.dma('sp', wa[:, h:KC, 0:nn], wv[:, h:KC, n_off + n0:n_off + n0 + nn], writes=[wb])
            if mode == 'tok':
                for s0 in range(0, tt, 128):
                    ps, psb = next_ps(C)
                    for kc in range(KC):
                        P.add('pe', lambda e, ps=ps, kc=kc, s0=s0, xa=xa, wa=wa, nn=nn: e.matmul(
                            ps[:, 0:nn], xa[:, kc, s0:s0 + 128], wa[:, kc, 0:nn], start=(kc == 0), stop=(kc == KC - 1)),
                            [xb, wb], [psb])
                    epi(ps, psb, t0 + s0, 128, n0, nn)
            else:
                for m0 in range(0, nn, 128):
                    ps, psb = next_ps(C)
                    for kc in range(KC):
                        P.add('pe', lambda e, ps=ps, kc=kc, m0=m0, xa=xa, wa=wa, tt=tt: e.matmul(
                            ps[:, 0:tt], wa[:, kc, m0:m0 + 128], xa[:, kc, 0:tt], start=(kc == 0), stop=(kc == KC - 1)),
                            [xb, wb], [psb])
                    epi(ps, psb, t0, tt, n0 + m0, 128)
    P.sb_reset(mark)


class StoreEpi:
    def __init__(self, P, C, out, mode, dtype=F32):
        self.P, self.C, self.out, self.mode = P, C, out, mode
        self.st = [(P.sb([128, 512], dtype, 'se'), Buf()) for _ in range(4)]
        self.i = 0

    def __call__(self, ps, psb, t0, nt, n0, nn):
        P = self.P
        a, ab = self.st[self.i % 4]
        eng = 'act' if self.i % 2 else 'dve'
        self.i += 1
        if self.mode == 'tok':
            w = nn
            dst = self.out[t0:t0 + nt, n0:n0 + nn]
        else:
            w = nt
            dst = self.out[n0:n0 + nn, t0:t0 + nt]
        if eng == 'act':
            P.add('act', lambda e: e.copy(out=a[:, 0:w], in_=ps[:, 0:w]), [psb], [ab])
        else:
            P.add('dve', lambda e: e.tensor_copy(out=a[:, 0:w], in_=ps[:, 0:w]), [psb], [ab])
        P.dma('pool', dst, a[:, 0:w], reads=[ab])


def bcast_rows(P, C, src_row, n, name):
    t = P.sb([128, n], F32, name)
    b = Buf(name)
    P.dma('sp', t[:], src_row.partition_broadcast(128) if hasattr(src_row, 'partition_broadcast') else src_row, writes=[b])
    return t, b


def ln_phase(P, C, xin, mix, g_row, b_row, xout, xTout, T, final_out=None):
    mark = P.sb_mark()
    nc = P.nc
    KC = D // 128
    gt = P.sb([128, D], F32, 'lng'); gb = Buf()
    bt = P.sb([128, D], F32, 'lnb'); bb_ = Buf()
    P.dma('sp', gt[:], bass.AP(g_row.tensor, g_row.offset, [[0, 128], [1, D]]), writes=[gb])
    P.dma('sp', bt[:], bass.AP(b_row.tensor, b_row.offset, [[0, 128], [1, D]]), writes=[bb_])
    NB = 2
    st = []
    for _ in range(NB):
        st.append(dict(x=P.sb([128, D], F32, 'lx'), m=P.sb([128, D], F32, 'lm'), xb=Buf(), mb=Buf(),
                       stats=P.sb([128, 4, 6], F32, 'ls'), mv=P.sb([128, 2], F32, 'lmv'), sb=Buf(),
                       rstd=P.sb([128, 1], F32, 'lr'), nmr=P.sb([128, 1], F32, 'ln'),
                       xt=P.sb([128, KC, 128], BF16, 'lxt'), xtb=Buf()))
    xTv = xTout.rearrange("(kc p) t -> p kc t", p=128) if xTout is not None else None
    for ti in range(T // 128):
        s = st[ti % NB]
        x, m = s['x'], s['m']
        r = slice(ti * 128, (ti + 1) * 128)
        P.dma('sp', x[:], xin[r, :], writes=[s['xb']])
        P.dma('sp', m[:], mix[r, :], writes=[s['mb']])
        P.add('dve', lambda e, x=x, m=m: e.scalar_tensor_tensor(out=m[:], in0=x[:], scalar=ALPHA, in1=m[:], op0=ALU.mult, op1=ALU.add),
              [s['xb'], s['mb']], [s['mb']])
        for c in range(4):
            P.add('dve', lambda e, s=s, m=m, c=c: e.bn_stats(out=s['stats'][:, c, :], in_=m[:, c * 512:(c + 1) * 512]), [s['mb']], [s['sb']])
        P.add('dve', lambda e, s=s: e.bn_aggr(out=s['mv'][:], in_=s['stats'][:].rearrange("p a b -> p (a b)")), [s['sb']], [s['sb']])
        P.add('dve', lambda e, s=s: e.tensor_scalar(out=s['rstd'][:], in0=s['mv'][:, 1:2], scalar1=LN_EPS, scalar2=None, op0=ALU.add),
              [s['sb']], [s['sb']])
        P.add('act', lambda e, s=s: e.activation(out=s['rstd'][:], in_=s['rstd'][:], func=AF.Sqrt), [s['sb']], [s['sb']])
        P.add('dve', lambda e, s=s: e.reciprocal(out=s['rstd'][:], in_=s['rstd'][:]), [s['sb']], [s['sb']])
        P.add('dve', lambda e, s=s: e.scalar_tensor_tensor(out=s['nmr'][:], in0=s['mv'][:, 0:1], scalar=-1.0, in1=s['rstd'][:], op0=ALU.mult, op1=ALU.mult),
              [s['sb']], [s['sb']])
        P.add('act', lambda e, s=s, x=x, m=m: e.activation(out=x[:], in_=m[:], func=AF.Identity, bias=s['nmr'][:], scale=s['rstd'][:]),
              [s['mb'], s['sb']], [s['xb']])
        P.add('pool', lambda e, x=x: e.tensor_tensor(out=x[:], in0=x[:], in1=gt[:], op=ALU.mult), [s['xb'], gb], [s['xb']])
        P.add('dve', lambda e, x=x: e.tensor_tensor(out=x[:], in0=x[:], in1=bt[:], op=ALU.add), [s['xb'], bb_], [s['xb']])
        P.dma('pool', xout[r, :], x[:], reads=[s['xb']])
        if xTv is not None:
            transpose_to(P, C, x, s['xb'], s['xt'], s['xtb'], KC)
            P.dma('pool', xTv[:, :, r], s['xt'][:], reads=[s['xtb']])
    P.sb_reset(mark)


def ffn_mid(P, C, hT, conv_w, conv_b, gT, T, S):
    mark = P.sb_mark()
    NCH = DFF // 128
    cw = P.sb([128, NCH, 3], F32, 'fcw'); cwb = Buf()
    cb = P.sb([128, NCH], F32, 'fcb'); cbb = Buf()
    for k in range(3):
        P.dma('sp', cw[:, :, k], conv_w[k, :].rearrange("(c p) -> p c", p=128), writes=[cwb], allow_slow_non_contiguous=True)
    P.dma('sp', cb[:], conv_b.rearrange("(c p) -> p c", p=128), writes=[cbb], allow_slow_non_contiguous=True)
    NB = 2
    st = [dict(g=P.sb([128, S + 2], F32, 'fg'), u=P.sb([128, S], F32, 'fu'), a=P.sb([128, S], F32, 'fa'),
               o=P.sb([128, S], BF16, 'fo'), gb=Buf(), ub=Buf(), ab=Buf(), ob=Buf()) for _ in range(NB)]
    for s in st:
        P.add('pool', lambda e, s=s: e.memset(s['g'][:, 0:1], 0.0), [], [s['gb']])
        P.add('pool', lambda e, s=s: e.memset(s['g'][:, S + 1:S + 2], 0.0), [], [s['gb']])
    i = 0
    for q in range(T // S):
        for ch in range(NCH):
            s = st[i % NB]
            i += 1
            g, u, a, o = s['g'], s['u'], s['a'], s['o']
            tr = slice(q * S, (q + 1) * S)
            P.dma('sp', g[:, 1:S + 1], hT[ch * 128:(ch + 1) * 128, tr], writes=[s['gb']])
            P.dma('sp', u[:], hT[DFF + ch * 128:DFF + (ch + 1) * 128, tr], writes=[s['ub']])
            P.add('act', lambda e, g=g, a=a, ch=ch: e.activation(out=a[:], in_=g[:, 1:S + 1], func=AF.Identity, bias=cb[:, ch:ch + 1], scale=cw[:, ch, 1:2]),
                  [s['gb'], cwb, cbb], [s['ab']])
            P.add('dve', lambda e, g=g, a=a, ch=ch: e.scalar_tensor_tensor(out=a[:], in0=g[:, 0:S], scalar=cw[:, ch, 0:1], in1=a[:], op0=ALU.mult, op1=ALU.add),
                  [s['gb'], s['ab'], cwb], [s['ab']])
            P.add('dve', lambda e, g=g, a=a, ch=ch: e.scalar_tensor_tensor(out=a[:], in0=g[:, 2:S + 2], scalar=cw[:, ch, 2:3], in1=a[:], op0=ALU.mult, op1=ALU.add),
                  [s['gb'], s['ab'], cwb], [s['ab']])
            P.add('act', lambda e, a=a: e.activation(out=a[:], in_=a[:], func=AF.Gelu), [s['ab']], [s['ab']])
            P.add('pool', lambda e, a=a, u=u, o=o: e.tensor_tensor(out=o[:], in0=a[:], in1=u[:], op=ALU.mult), [s['ab'], s['ub']], [s['ob']])
            P.dma('pool', gT[ch * 128:(ch + 1) * 128, tr], o[:], reads=[s['ob']])
    P.sb_reset(mark)


def xprep_n(P, C, x, xT, T, DD):
    mark = P.sb_mark()
    KC = DD // 128
    NB = 2
    st = [(P.sb([128, DD], F32, 'xp'), P.sb([128, KC, 128], BF16, 'xpt'), Buf(), Buf()) for _ in range(NB)]
    xTv = xT.rearrange("(kc p) t -> p kc t", p=128)
    for ti in range(T // 128):
        a, b, ba, bb = st[ti % NB]
        P.dma('sp', a[:], x[ti * 128:(ti + 1) * 128, :], writes=[ba])
        transpose_to(P, C, a, ba, b, bb, KC)
        P.dma('pool', xTv[:, :, ti * 128:(ti + 1) * 128], b[:], reads=[bb])
    P.sb_reset(mark)


def rot_tables(S):
    half = 128
    inv = (10000.0 ** (-np.arange(half, dtype=np.float32) / half)).astype(np.float32)
    pos = np.arange(S, dtype=np.float32)
    ang = (pos[:, None] * inv[None, :]).astype(np.float32)
    return np.cos(ang).T.astype(np.float32).copy(), np.sin(ang).T.astype(np.float32).copy()


def ret_rotary(P, C, qkT, qkr, T, S):
    nc = P.nc
    mark = P.sb_mark()
    if not hasattr(C, 'rot'):
        ct, sn = rot_tables(S)
        C.rot = (nc.inline_tensor(ct, "rot_cos").ap(), nc.inline_tensor(sn, "rot_sin").ap())
    cd, sd = C.rot
    cos = P.sb([128, S], F32, 'cos'); sin = P.sb([128, S], F32, 'sin')
    cosk = P.sb([128, S], F32, 'cosk'); sink = P.sb([128, S], F32, 'sink')
    tb = Buf()
    P.dma('sp', cos[:], cd[:, :], writes=[tb])
    P.dma('sp', sin[:], sd[:, :], writes=[tb])
    P.add('act', lambda e: e.mul(out=cosk[:], in_=cos[:], mul=1.0 / 16), [tb], [tb])
    P.add('act', lambda e: e.mul(out=sink[:], in_=sin[:], mul=1.0 / 16), [tb], [tb])
    NB = 2 if S <= 2048 else 1
    st = [dict(x1=P.sb([128, S], F32, 'r1'), x2=P.sb([128, S], F32, 'r2'), a=P.sb([128, S], F32, 'ra'), b=P.sb([128, S], F32, 'rb'),
               o1=P.sb([128, S], BF16, 'ro1'), o2=P.sb([128, S], BF16, 'ro2'), xb=Buf(), ab=Buf(), ob=Buf()) for _ in range(NB)]
    i = 0
    for q in range(T // S):
        tr = slice(q * S, (q + 1) * S)
        for hh in range(16):
            s = st[i % NB]; i += 1
            c_, s_ = (cos, sin) if hh < 8 else (cosk, sink)
            r0 = hh * 256
            x1, x2, a, b, o1, o2 = s['x1'], s['x2'], s['a'], s['b'], s['o1'], s['o2']
            P.dma('sp', x1[:], qkT[r0:r0 + 128, tr], writes=[s['xb']])
            P.dma('sp', x2[:], qkT[r0 + 128:r0 + 256, tr], writes=[s['xb']])
            P.add('dve', lambda e, x1=x1, a=a, c_=c_: e.tensor_tensor(out=a[:], in0=x1[:], in1=c_[:], op=ALU.mult), [s['xb'], tb], [s['ab']])
            P.add('pool', lambda e, x2=x2, b=b, s_=s_: e.tensor_tensor(out=b[:], in0=x2[:], in1=s_[:], op=ALU.mult), [s['xb'], tb], [s['ab']])
            P.add('dve', lambda e, a=a, b=b, o1=o1: e.tensor_tensor(out=o1[:], in0=a[:], in1=b[:], op=ALU.subtract), [s['ab']], [s['ob']])
            P.add('pool', lambda e, x1=x1, a=a, s_=s_: e.tensor_tensor(out=a[:], in0=x1[:], in1=s_[:], op=ALU.mult), [s['xb'], tb, s['ab']], [s['ab']])
            P.add('dve', lambda e, x2=x2, b=b, c_=c_: e.tensor_tensor(out=b[:], in0=x2[:], in1=c_[:], op=ALU.mult), [s['xb'], tb, s['ab']], [s['ab']])
            P.add('pool', lambda e, a=a, b=b, o2=o2: e.tensor_tensor(out=o2[:], in0=a[:], in1=b[:], op=ALU.add), [s['ab']], [s['ob']])
            P.dma('pool', qkr[r0:r0 + 128, tr], o1[:], reads=[s['ob']])
            P.dma('pool', qkr[r0 + 128:r0 + 256, tr], o2[:], reads=[s['ob']])
    P.sb_reset(mark)


def ret_attn(P, C, qkr, vg, gn_g, gn_b, ytok, T, S):
    nc = P.nc
    mark = P.sb_mark()
    NQB = S // 512
    NKC = S // 128
    lg = [float(np.log(np.float32(1.0) - np.float32(2.0) ** np.float32(-5.0 - h))) for h in range(8)]
    d0 = P.sb([128, 512], F32, 'd0'); d0b = Buf()
    P.add('pool', lambda e: e.iota(d0[:], [[1, 512]], base=0, channel_multiplier=-1, allow_small_or_imprecise_dtypes=True), [], [d0b])
    deltas = sorted({qb * 512 - kc * 128 for qb in range(NQB) for kc in range(NKC)})
    gam = {dl: P.sb([128, 512], BF16, 'gam') for dl in deltas}
    gamb = Buf()
    gtmp = [(P.sb([128, 512], F32, 'gt'), Buf()) for _ in range(2)]
    gg = P.sb([128, 4096], F32, 'gng'); gbt = P.sb([128, 4096], F32, 'gnb'); ggb = Buf()
    P.dma('sp', gg[:], bass.AP(gn_g.tensor, gn_g.offset, [[0, 128], [1, 4096]]), writes=[ggb])
    P.dma('sp', gbt[:], bass.AP(gn_b.tensor, gn_b.offset, [[0, 128], [1, 4096]]), writes=[ggb])
    qt = P.sb([128, 2, S], BF16, 'qt'); kt = P.sb([128, 2, S], BF16, 'kt'); qkb = Buf()
    vt = P.sb([128, NKC, 512], BF16, 'vt'); vb = Buf()
    vst = [(P.sb([128, 512], F32, 'vs'), Buf()) for _ in range(2)]
    pt = [(P.sb([128, 512], BF16, 'pt'), Buf()) for _ in range(2)]
    ep = [dict(o=P.sb([128, 512], F32, 'eo'), g=P.sb([128, 512], F32, 'eg'), st=P.sb([128, 6], F32, 'es'), mv=P.sb([128, 2], F32, 'em'),
               r=P.sb([128, 1], F32, 'er'), n=P.sb([128, 1], F32, 'en'), ob=Buf(), gb=Buf(), sb=Buf()) for _ in range(2)]
    obank = [(C.ps[i], C.psb[i]) for i in range(4)]
    sbank = [(C.ps[4 + i], C.psb[4 + i]) for i in range(2)]
    cnt = 0
    ec = 0
    for h in range(8):
        for gi, dl in enumerate(deltas):
            t, tbuf = gtmp[gi % 2]
            P.add('act', lambda e, t=t, dl=dl: e.activation(out=t[:], in_=d0[:], func=AF.Abs, bias=float(dl), scale=1.0), [d0b], [tbuf])
            P.add('act', lambda e, t=t, dl=dl, h=h: e.activation(out=gam[dl][:], in_=t[:], func=AF.Exp, scale=lg[h]), [tbuf], [gamb])
        for q in range(T // S):
            tb0 = q * S
            P.dma('sp', qt[:], qkr[h * 256:(h + 1) * 256, tb0:tb0 + S].rearrange("(c p) t -> p c t", p=128), writes=[qkb])
            P.dma('sp', kt[:], qkr[2048 + h * 256:2048 + (h + 1) * 256, tb0:tb0 + S].rearrange("(c p) t -> p c t", p=128), writes=[qkb])
            for kc in range(NKC):
                a, ab = vst[kc % 2]
                P.dma('sp', a[:], vg[tb0 + kc * 128:tb0 + (kc + 1) * 128, h * 512:(h + 1) * 512], writes=[ab])
                P.add('pool', lambda e, a=a, kc=kc: e.tensor_copy(out=vt[:, kc, :], in_=a[:]), [ab], [vb])
            for qb in range(NQB):
                for kc in range(NKC):
                    sp_, spb = sbank[cnt % 2]
                    p_, pb = pt[cnt % 2]
                    cnt += 1
                    for c in range(2):
                        P.add('pe', lambda e, sp_=sp_, c=c, kc=kc, qb=qb: e.matmul(sp_[:, :], kt[:, c, kc * 128:(kc + 1) * 128], qt[:, c, qb * 512:(qb + 1) * 512],
                                                                                  start=(c == 0), stop=(c == 1)), [qkb], [spb])
                    dl = qb * 512 - kc * 128
                    P.add('dve', lambda e, sp_=sp_, p_=p_, dl=dl: e.tensor_tensor(out=p_[:], in0=sp_[:, :], in1=gam[dl][:], op=ALU.mult), [spb, gamb], [pb])
                    for sub in range(4):
                        ob_, obb = obank[sub]
                        P.add('pe', lambda e, ob_=ob_, p_=p_, sub=sub, kc=kc: e.matmul(ob_[:, :], p_[:, sub * 128:(sub + 1) * 128], vt[:, kc, :],
                                                                                     start=(kc == 0), stop=(kc == NKC - 1)), [pb, vb], [obb])
                for sub in range(4):
                    ob_, obb = obank[sub]
                    s = ep[ec % 2]; ec += 1
                    r = slice(tb0 + qb * 512 + sub * 128, tb0 + qb * 512 + (sub + 1) * 128)
                    o, g = s['o'], s['g']
                    P.dma('sp', g[:], vg[r, 4096 + h * 512:4096 + (h + 1) * 512], writes=[s['gb']])
                    P.add('act', lambda e, g=g: e.activation(out=g[:], in_=g[:], func=AF.Silu), [s['gb']], [s['gb']])
                    P.add('dve', lambda e, s=s, ob_=ob_: e.bn_stats(out=s['st'][:], in_=ob_[:, :]), [obb], [s['sb']])
                    P.add('dve', lambda e, s=s: e.bn_aggr(out=s['mv'][:], in_=s['st'][:]), [s['sb']], [s['sb']])
                    P.add('dve', lambda e, s=s: e.tensor_scalar(out=s['r'][:], in0=s['mv'][:, 1:2], scalar1=LN_EPS, scalar2=None, op0=ALU.add), [s['sb']], [s['sb']])
                    P.add('act', lambda e, s=s: e.activation(out=s['r'][:], in_=s['r'][:], func=AF.Sqrt), [s['sb']], [s['sb']])
                    P.add('dve', lambda e, s=s: e.reciprocal(out=s['r'][:], in_=s['r'][:]), [s['sb']], [s['sb']])
                    P.add('dve', lambda e, s=s: e.scalar_tensor_tensor(out=s['n'][:], in0=s['mv'][:, 0:1], scalar=-1.0, in1=s['r'][:], op0=ALU.mult, op1=ALU.mult), [s['sb']], [s['sb']])
                    P.add('act', lambda e, s=s, o=o, ob_=ob_: e.activation(out=o[:], in_=ob_[:, :], func=AF.Identity, bias=s['n'][:], scale=s['r'][:]), [obb, s['sb']], [s['ob']])
                    P.add('pool', lambda e, o=o, h=h: e.tensor_tensor(out=o[:], in0=o[:], in1=gg[:, h * 512:(h + 1) * 512], op=ALU.mult), [s['ob'], ggb], [s['ob']])
                    P.add('pool', lambda e, o=o, h=h: e.tensor_tensor(out=o[:], in0=o[:], in1=gbt[:, h * 512:(h + 1) * 512], op=ALU.add), [s['ob'], ggb], [s['ob']])
                    P.add('dve', lambda e, o=o, g=g: e.tensor_tensor(out=o[:], in0=o[:], in1=g[:], op=ALU.mult), [s['ob'], s['gb']], [s['ob']])
                    P.dma('pool', ytok[r, h * 512:(h + 1) * 512], o[:], reads=[s['ob']])
    P.sb_reset(mark)


NEG = -30000.0


def na_bias_build(P, C, nc, rpb, scr_fn, e):
    mark = P.sb_mark()
    biasS = scr_fn(f"na_bias{e}", [240, 4096], F32)
    G = P.sb([31, 240], F32, 'nG'); Gb = Buf()
    for h_ in range(16):
        P.dma('sp', G[:, h_ * 15:(h_ + 1) * 15], rpb[h_].rearrange("r d -> d r"), writes=[Gb], allow_slow_non_contiguous=True)
    M = P.sb([31, 4096], F32, 'nM'); Mb = Buf()
    P.add('pool', lambda e_: e_.iota(M[:].rearrange("p (a b) -> p a b", b=64), [[1, 64], [-1, 64]], base=15, channel_multiplier=-1,
                                     allow_small_or_imprecise_dtypes=True), [], [Mb])
    P.add('dve', lambda e_: e_.tensor_single_scalar(out=M[:], in_=M[:], scalar=0.0, op=ALU.is_equal), [Mb], [Mb])
    A = P.sb([120, 4096], F32, 'nA'); Q = P.sb([120, 4096], F32, 'nQ'); nb = Buf()
    P.add('pool', lambda e_: e_.iota(A[:].rearrange("p (a b) -> p a b", b=64), [[1, 64], [0, 64]], base=0, channel_multiplier=0,
                                     allow_small_or_imprecise_dtypes=True), [], [nb])
    P.add('pool', lambda e_: e_.iota(Q[:].rearrange("p (a b) -> p a b", b=64), [[0, 64], [1, 64]], base=0, channel_multiplier=0,
                                     allow_small_or_imprecise_dtypes=True), [nb], [nb])
    for (s1, op) in ((-8.0, ALU.add), (0.0, ALU.max), (48.0, ALU.min)):
        P.add('dve', lambda e_, s1=s1, op=op: e_.tensor_single_scalar(out=Q[:], in_=Q[:], scalar=s1, op=op), [nb], [nb])
    P.add('dve', lambda e_: e_.tensor_tensor(out=A[:], in0=A[:], in1=Q[:], op=ALU.subtract), [nb], [nb])
    P.add('dve', lambda e_: e_.tensor_single_scalar(out=Q[:], in_=A[:], scalar=0.0, op=ALU.is_ge), [nb], [nb])
    P.add('dve', lambda e_: e_.tensor_single_scalar(out=A[:], in_=A[:], scalar=15.0, op=ALU.is_le), [nb], [nb])
    P.add('dve', lambda e_: e_.tensor_tensor(out=A[:], in0=A[:], in1=Q[:], op=ALU.mult), [nb], [nb])
    P.add('dve', lambda e_: e_.tensor_single_scalar(out=A[:], in_=A[:], scalar=-1.0, op=ALU.add), [nb], [nb])
    P.add('dve', lambda e_: e_.tensor_single_scalar(out=A[:], in_=A[:], scalar=-NEG, op=ALU.mult), [nb], [nb])
    st = [(P.sb([120, 512], F32, 'nbs'), Buf()) for _ in range(2)]
    i = 0
    for half in range(2):
        for cb in range(8):
            ps, psb = next_ps(C)
            P.add('pe', lambda e_, ps=ps, half=half, cb=cb: e_.matmul(ps[0:120, :], G[:, half * 120:(half + 1) * 120], M[:, cb * 512:(cb + 1) * 512],
                                                                     start=True, stop=True), [Gb, Mb], [psb])
            a, ab = st[i % 2]; i += 1
            P.add('dve', lambda e_, ps=ps, a=a, cb=cb: e_.tensor_tensor(out=a[:], in0=ps[0:120, :], in1=A[:, cb * 512:(cb + 1) * 512], op=ALU.add), [psb, nb], [ab])
            P.dma('pool', biasS[half * 120:(half + 1) * 120, cb * 512:(cb + 1) * 512], a[:], reads=[ab])
    P.sb_reset(mark)
    return biasS


def na_attn(P, C, nc, biasS, hqk, hv, yab, T, S, stage=3):
    mark = P.sb_mark()
    rows = S // 64
    NCH = S // 128
    assert rows >= 8
    B2 = P.sb([128, 16, 14, 64], F32, 'nB2'); B2b = Buf()
    bv = biasS.rearrange("(h r) (k q) -> k h r q", r=15, q=64)
    for i2 in range(2):
        for h in range(16):
            P.dma('sp', B2[i2 * 64:(i2 + 1) * 64, h, :, :], bv[:, h, i2:i2 + 14, :], writes=[B2b])
    stg = P.sb([64, 2, S], F32, 'nstg'); stgb = Buf()
    qTb = P.sb([64, 2, S], BF16, 'nq'); kTb = P.sb([64, 2, S], BF16, 'nk'); qkb = Buf()
    vstg = P.sb([128, NCH, 128], F32, 'nvs'); vsb = Buf()
    Vt = P.sb([128, NCH, 2, 80], BF16, 'nV'); Vt2 = P.sb([128, NCH, 2, 80], BF16, 'nV2'); Vb = Buf()
    P.add('pool', lambda e_: e_.memset(Vt[:], 1.0), [], [Vb])
    P.add('pool', lambda e_: e_.memset(Vt2[:], 1.0), [], [Vb])
    sc = [(P.sb([128, 512], F32, 'nsc'), Buf()) for _ in range(2)]
    pT = [(P.sb([128, 512], BF16, 'npT'), Buf()) for _ in range(2)]
    yr = [(P.sb([64, 128], F32, 'nyr'), P.sb([64, 2], F32, 'nrc'), Buf()) for _ in range(2)]
    sbank = [(C.ps[0], C.psb[0]), (C.ps[1], C.psb[1])]
    obank = [(C.ps[2], C.psb[2]), (C.ps[3], C.psb[3])]
    it = 0
    for q in range(T // S):
        t0 = q * S
        for hp in range(8):
            P.dma('sp', stg[:], hqk[hp * 128:(hp + 1) * 128, t0:t0 + S].rearrange("(a d) t -> d a t", d=64), writes=[stgb])
            P.add('act', lambda e_: e_.mul(out=qTb[:], in_=stg[:], mul=0.125), [stgb], [qkb])
            P.dma('sp', stg[:], hqk[1024 + hp * 128:1024 + (hp + 1) * 128, t0:t0 + S].rearrange("(a d) t -> d a t", d=64), writes=[stgb])
            P.add('dve', lambda e_: e_.tensor_copy(out=kTb[:], in_=stg[:]), [stgb], [qkb])
            P.dma('sp', vstg[:], hv[t0:t0 + S, hp * 128:(hp + 1) * 128].rearrange("(c p) f -> p c f", p=128), writes=[vsb])
            P.add('pool', lambda e_: e_.tensor_copy(out=Vt[:, :, :, 0:64], in_=vstg[:].rearrange("p c (a d) -> p c a d", d=64)), [vsb], [Vb])
            P.dma('sp', vstg[:, 0:NCH - 1, :], hv[t0 + 64:t0 + S - 64, hp * 128:(hp + 1) * 128].rearrange("(c p) f -> p c f", p=128), writes=[vsb])
            P.add('pool', lambda e_: e_.tensor_copy(out=Vt2[:, 0:NCH - 1, :, 0:64], in_=vstg[:, 0:NCH - 1, :].rearrange("p c (a d) -> p c a d", d=64)), [vsb], [Vb])
            for r in range(rows if stage >= 2 else 0):
                rs = min(max(r - 4, 0), rows - 8)
                ro0 = rs - r + 7
                sp_, spb = sbank[it % 2]
                ob_, obb = obank[it % 2]
                s_, sb_ = sc[it % 2]
                p_, pb = pT[it % 2]
                y_, rc_, yb_ = yr[it % 2]
                it += 1
                for hh in range(2):
                    for c in range(4):
                        k0 = rs * 64 + c * 128
                        P.add('pe', lambda e_, sp_=sp_, hh=hh, c=c, k0=k0, r=r: e_.matmul(
                            sp_[:, (hh * 4 + c) * 64:(hh * 4 + c + 1) * 64], kTb[:, hh, k0:k0 + 128], qTb[:, hh, r * 64:(r + 1) * 64],
                            start=True, stop=True), [qkb], [spb])
                for hh in range(2):
                    h = 2 * hp + hh
                    P.add('dve', lambda e_, sp_=sp_, s_=s_, hh=hh, h=h, ro0=ro0: e_.tensor_tensor(
                        out=s_[:, hh * 256:(hh + 1) * 256].rearrange("p (c q) -> p c q", q=64),
                        in0=sp_[:, hh * 256:(hh + 1) * 256].rearrange("p (c q) -> p c q", q=64),
                        in1=B2[:, h, ro0:ro0 + 7:2, :], op=ALU.add), [spb, B2b], [sb_])
                P.add('act', lambda e_, s_=s_, p_=p_: e_.activation(out=p_[:], in_=s_[:], func=AF.Exp), [sb_], [pb])
                if stage < 3:
                    continue
                for hh in range(2):
                    for c in range(4):
                        if rs % 2 == 0:
                            vap = Vt[:, rs // 2 + c, hh, 0:65]
                        else:
                            vap = Vt2[:, (rs - 1) // 2 + c, hh, 0:65]
                        P.add('pe', lambda e_, ob_=ob_, p_=p_, hh=hh, c=c, vap=vap: e_.matmul(
                            ob_[0:64, hh * 128:hh * 128 + 65], p_[:, (hh * 4 + c) * 64:(hh * 4 + c + 1) * 64], vap, start=(c == 0), stop=(c == 3)), [pb, Vb], [obb])
                for hh in range(2):
                    P.add('dve', lambda e_, ob_=ob_, rc_=rc_, hh=hh: e_.reciprocal(out=rc_[:, hh:hh + 1], in_=ob_[0:64, hh * 128 + 64:hh * 128 + 65]), [obb], [yb_])
                    P.add('dve', lambda e_, ob_=ob_, rc_=rc_, y_=y_, hh=hh: e_.tensor_scalar(
                        out=y_[:, hh * 64:(hh + 1) * 64], in0=ob_[0:64, hh * 128:hh * 128 + 64], scalar1=rc_[:, hh:hh + 1], scalar2=None, op0=ALU.mult), [obb, yb_], [yb_])
                P.dma('pool', yab[t0 + r * 64:t0 + (r + 1) * 64, 1024 + hp * 128:1024 + (hp + 1) * 128], y_[:], reads=[yb_])
    P.sb_reset(mark)


A_GN_EPS = 64e-5
EHALF = float(np.exp(-0.5))


def bc_ap(t, ncols_src, n1, n2):
    return bass.AP(t, 0, [[ncols_src, t.shape[0]], [1, n1], [0, n2]])


def load_bc(P, row_ap, n, name):
    t = P.sb([128, n], F32, name); b = Buf(name)
    P.dma('sp', t[:], bass.AP(row_ap.tensor, row_ap.offset, [[0, 128], [1, n]]), writes=[b])
    return t, b


def rwkv_prep(P, C, nc, W, e, hA, PR, T, S):
    mark = P.sb_mark()
    sh = [load_bc(P, W['ab_shift'][e, k], 3328, f'sh{k}') for k in range(3)]
    w0 = [load_bc(P, W['ab_w0'][e, d], 1024, f'w0{d}') for d in range(2)]
    a0 = [load_bc(P, W['ab_a0'][e, d], 1024, f'a0{d}') for d in range(2)]
    kk_ = load_bc(P, W['ab_k_k'][e], 1024, 'kk')
    ka_ = load_bc(P, W['ab_k_a'][e], 1024, 'ka')
    rk_ = load_bc(P, W['ab_r_k'][e].rearrange("h d -> (h d)"), 1024, 'rk')
    wst = P.sb([128, 2048], F32, 'wst'); wsb = Buf()
    wup = P.sb([64, 2, 1024], BF16, 'wup'); aup = P.sb([64, 2, 1024], BF16, 'aup'); gup = P.sb([128, 1024], BF16, 'gup'); lwb = Buf()
    P.dma('sp', wst[0:64, :].rearrange("p (d n) -> p d n", d=2), W['ab_w_up'][e].rearrange("d k n -> k d n"), writes=[wsb])
    P.add('dve', lambda e_: e_.tensor_copy(out=wup[:].rearrange("p d n -> p (d n)"), in_=wst[0:64, :]), [wsb], [lwb])
    P.dma('sp', wst[0:64, :].rearrange("p (d n) -> p d n", d=2), W['ab_a_up'][e].rearrange("d k n -> k d n"), writes=[wsb])
    P.add('dve', lambda e_: e_.tensor_copy(out=aup[:].rearrange("p d n -> p (d n)"), in_=wst[0:64, :]), [wsb], [lwb])
    P.dma('sp', wst[:, 0:1024], W['ab_g_up'][e], writes=[wsb])
    P.add('dve', lambda e_: e_.tensor_copy(out=gup[:], in_=wst[:, 0:1024]), [wsb], [lwb])
    hp = P.sb([128, 3328], F32, 'hp'); hc = P.sb([128, 3328], F32, 'hc'); hn = P.sb([128, 3328], F32, 'hn')
    hpb, hcb, hnb = Buf(), Buf(), Buf()
    L = P.sb([128, 256], F32, 'L'); Lb = Buf()
    LT = P.sb([128, 3, 128], BF16, 'LT'); LTb = Buf()
    t1 = P.sb([128, 1024], F32, 't1'); t1b = Buf()
    t2 = P.sb([128, 1024], F32, 't2'); t2b = Buf()
    kap = P.sb([128, 1024], F32, 'kap'); kapb = Buf()
    ad = P.sb([128, 1024], F32, 'ad'); adb = Buf()
    o1 = [(P.sb([128, 1024], F32, 'o1'), Buf()) for _ in range(3)]
    sm = P.sb([128, 16], F32, 'sm'); smb = Buf()
    bon = P.sb([128, 16], F32, 'bon'); bonb = Buf()
    oi = [0]

    def outbuf():
        o = o1[oi[0] % 3]; oi[0] += 1
        return o
    for ti in range(T // 128):
        t0 = ti * 128
        r = slice(t0, t0 + 128)
        first = (t0 % S == 0)
        lastt = ((t0 + 128) % S == 0)
        P.dma('sp', hc[:], hA[r, :], writes=[hcb])
        if first:
            P.add('pool', lambda e_: e_.memset(hp[0:1, :], 0.0), [], [hpb])
            P.dma('sp', hp[1:128, :], hA[t0:t0 + 127, :], writes=[hpb])
        else:
            P.dma('sp', hp[:], hA[t0 - 1:t0 + 127, :], writes=[hpb])
        if lastt:
            P.add('pool', lambda e_: e_.memset(hn[:], 0.0), [], [hnb])
            P.dma('sp', hn[0:127, :], hA[t0 + 1:t0 + 128, :], writes=[hnb])
        else:
            P.dma('sp', hn[:], hA[t0 + 1:t0 + 129, :], writes=[hnb])
        P.add('dve', lambda e_: e_.tensor_tensor(out=hc[:], in0=hc[:], in1=sh[1][0][:], op=ALU.mult), [hcb, sh[1][1]], [hcb])
        P.add('pool', lambda e_: e_.tensor_tensor(out=hp[:], in0=hp[:], in1=sh[0][0][:], op=ALU.mult), [hpb, sh[0][1]], [hpb])
        P.add('pool', lambda e_: e_.tensor_tensor(out=hn[:], in0=hn[:], in1=sh[2][0][:], op=ALU.mult), [hnb, sh[2][1]], [hnb])
        P.add('dve', lambda e_: e_.tensor_tensor(out=hc[:], in0=hc[:], in1=hp[:], op=ALU.add), [hcb, hpb], [hcb])
        P.add('dve', lambda e_: e_.tensor_tensor(out=hc[:], in0=hc[:], in1=hn[:], op=ALU.add), [hcb, hnb], [hcb])
        R_ = hc[:, 0:1024]; K_ = hc[:, 1024:2048]; V_ = hc[:, 2048:3072]
        P.dma('pool', PR['R'][r, :], R_, reads=[hcb])
        P.dma('pool', PR['V'][r, :], V_, reads=[hcb])
        P.add('act', lambda e_: e_.activation(out=L[:, 0:64], in_=hc[:, 3072:3136], func=AF.Tanh), [hcb], [Lb])
        P.add('act', lambda e_: e_.activation(out=L[:, 64:128], in_=hc[:, 3136:3200], func=AF.Identity), [hcb], [Lb])
        P.add('act', lambda e_: e_.activation(out=L[:, 128:256], in_=hc[:, 3200:3328], func=AF.Sigmoid), [hcb], [Lb])
        ps, psb = next_ps(C)
        P.add('pe', lambda e_, ps=ps: e_.transpose(ps[0:64, 0:128], L[:, 0:64], C.ident[:]), [Lb, C.b_ident], [psb])
        P.add('pe', lambda e_, ps=ps: e_.transpose(ps[0:64, 128:256], L[:, 64:128], C.ident[:]), [Lb, C.b_ident], [psb])
        P.add('pe', lambda e_, ps=ps: e_.transpose(ps[:, 256:384], L[:, 128:256], C.ident[:]), [Lb, C.b_ident], [psb])
        P.add('dve', lambda e_, ps=ps: e_.tensor_copy(out=LT[0:64, 0:2, :], in_=ps[0:64, 0:256].rearrange("p (a t) -> p a t", t=128)), [psb], [LTb])
        P.add('dve', lambda e_, ps=ps: e_.tensor_copy(out=LT[:, 2, :], in_=ps[:, 256:384]), [psb], [LTb])
        P.add('pool', lambda e_: e_.tensor_tensor(out=kap[:], in0=K_, in1=kk_[0][:], op=ALU.mult), [hcb, kk_[1]], [kapb])
        P.add('dve', lambda e_: e_.tensor_tensor(out=t1[:], in0=kap[:], in1=kap[:], op=ALU.mult), [kapb], [t1b])
        P.add('dve', lambda e_: e_.tensor_reduce(out=sm[:], in_=t1[:].rearrange("p (h d) -> p h d", d=64), axis=AX.X, op=ALU.add), [t1b], [smb])
        P.add('act', lambda e_: e_.activation(out=sm[:], in_=sm[:], func=AF.Sqrt), [smb], [smb])
        P.add('dve', lambda e_: e_.tensor_single_scalar(out=sm[:], in_=sm[:], scalar=1e-12, op=ALU.max), [smb], [smb])
        P.add('dve', lambda e_: e_.reciprocal(out=sm[:], in_=sm[:]), [smb], [smb])
        P.add('dve', lambda e_: e_.tensor_tensor(out=kap[:].rearrange("p (h d) -> p h d", d=64), in0=kap[:].rearrange("p (h d) -> p h d", d=64),
                                                 in1=bc_ap(sm, 16, 16, 64), op=ALU.mult), [kapb, smb], [kapb])
        P.dma('pool', PR['KAP'][r, :], kap[:], reads=[kapb])
        o, ob = outbuf()
        for hf in range(2):
            ps, psb = next_ps(C)
            P.add('pe', lambda e_, ps=ps, hf=hf: e_.matmul(ps[:, :], LT[:, 2, :], gup[:, hf * 512:(hf + 1) * 512], start=True, stop=True), [LTb, lwb], [psb])
            P.add('act', lambda e_, ps=ps, hf=hf, o=o: e_.activation(out=o[:, hf * 512:(hf + 1) * 512], in_=ps[:, :], func=AF.Identity), [psb], [ob])
        P.dma('pool', PR['GATE'][r, :], o[:], reads=[ob])
        for d in range(2):
            o, ob = outbuf()
            for hf in range(2):
                ps, psb = next_ps(C)
                P.add('pe', lambda e_, ps=ps, hf=hf, d=d: e_.matmul(ps[:, :], LT[0:64, 0, :], wup[:, d, hf * 512:(hf + 1) * 512], start=True, stop=True), [LTb, lwb], [psb])
                P.add('dve', lambda e_, ps=ps, hf=hf, d=d: e_.tensor_tensor(out=t1[:, hf * 512:(hf + 1) * 512], in0=ps[:, :], in1=w0[d][0][:, hf * 512:(hf + 1) * 512], op=ALU.add), [psb, w0[d][1]], [t1b])
            P.add('act', lambda e_: e_.activation(out=t1[:], in_=t1[:], func=AF.Sigmoid), [t1b], [t1b])
            P.add('act', lambda e_, o=o: e_.mul(out=o[:], in_=t1[:], mul=-EHALF), [t1b], [ob])
            P.dma('pool', PR[f'LW{d}'][r, :], o[:], reads=[ob])
            for hf in range(2):
                ps, psb = next_ps(C)
                P.add('pe', lambda e_, ps=ps, hf=hf, d=d: e_.matmul(ps[:, :], LT[0:64, 1, :], aup[:, d, hf * 512:(hf + 1) * 512], start=True, stop=True), [LTb, lwb], [psb])
                P.add('dve', lambda e_, ps=ps, hf=hf, d=d: e_.tensor_tensor(out=ad[:, hf * 512:(hf + 1) * 512], in0=ps[:, :], in1=a0[d][0][:, hf * 512:(hf + 1) * 512], op=ALU.add), [psb, a0[d][1]], [adb])
            P.add('act', lambda e_: e_.activation(out=ad[:], in_=ad[:], func=AF.Sigmoid), [adb], [adb])
            o, ob = outbuf()
            P.add('pool', lambda e_, o=o: e_.tensor_tensor(out=o[:], in0=kap[:], in1=ad[:], op=ALU.mult), [kapb, adb], [ob])
            P.dma('pool', PR[f'B{d}'][r, :], o[:], reads=[ob])
            P.add('dve', lambda e_: e_.scalar_tensor_tensor(out=t2[:], in0=ad[:], scalar=-1.0, in1=ka_[0][:], op0=ALU.add, op1=ALU.mult), [adb, ka_[1]], [t2b])
            o, ob = outbuf()
            P.add('dve', lambda e_, o=o: e_.scalar_tensor_tensor(out=o[:], in0=t2[:], scalar=1.0, in1=K_, op0=ALU.add, op1=ALU.mult), [t2b, hcb], [ob])
            P.dma('pool', PR[f'KD{d}'][r, :], o[:], reads=[ob])
            if d == 0:
                P.add('pool', lambda e_, o=o: e_.tensor_tensor(out=t2[:], in0=o[:], in1=R_, op=ALU.mult), [ob, hcb, t2b], [t2b])
                P.add('pool', lambda e_: e_.tensor_tensor(out=t2[:], in0=t2[:], in1=rk_[0][:], op=ALU.mult), [t2b, rk_[1]], [t2b])
                P.add('dve', lambda e_: e_.tensor_reduce(out=bon[:], in_=t2[:].rearrange("p (h d) -> p h d", d=64), axis=AX.X, op=ALU.add), [t2b], [bonb])
                P.dma('pool', PR['BON'][r, :], bon[:], reads=[bonb])
    P.sb_reset(mark)


def rwkv_consts(P, C, d):
    K = {}
    D = P.sb([128, 128], F32, 'cD'); Db = Buf()
    P.add('pool', lambda e_: e_.iota(D[:], [[1, 128]], base=0, channel_multiplier=-1, allow_small_or_imprecise_dtypes=True), [], [Db])
    PI = P.sb([128, 128], F32, 'cPI'); PIb = Buf()
    P.add('pool', lambda e_: e_.iota(PI[:], [[0, 128]], base=0, channel_multiplier=1, allow_small_or_imprecise_dtypes=True), [], [PIb])
    cb = Buf()

    def mk(name, src, srcb, scalar, op):
        t = P.sb([128, 128], F32, name)
        P.add('dve', lambda e_: e_.tensor_single_scalar(out=t[:], in_=src[:], scalar=scalar, op=op), [srcb], [cb])
        return t
    TRI = mk('cTRI', D, Db, 0.0, ALU.is_ge if d == 0 else ALU.is_le)
    TRIS = mk('cTRIS', D, Db, 0.0, ALU.is_gt if d == 0 else ALU.is_lt)
    HALF = mk('cHALF', PI, PIb, 63.5, ALU.is_lt if d == 0 else ALU.is_gt)
    A1 = P.sb([128, 128], F32, 'cA1'); A2 = P.sb([128, 128], F32, 'cA2'); A5 = P.sb([128, 128], F32, 'cA5')
    P.add('dve', lambda e_: e_.tensor_tensor(out=A1[:], in0=TRIS[:], in1=HALF[:], op=ALU.subtract), [cb], [cb])
    P.add('dve', lambda e_: e_.tensor_tensor(out=A2[:], in0=TRI[:], in1=HALF[:], op=ALU.subtract), [cb], [cb])
    P.add('dve', lambda e_: e_.tensor_scalar(out=A5[:], in0=TRI[:], scalar1=-1.0, scalar2=None, op0=ALU.mult), [cb], [cb])
    P.add('dve', lambda e_: e_.tensor_single_scalar(out=A5[:], in_=A5[:], scalar=1.0, op=ALU.add), [cb], [cb])
    K['A1'], K['A2'], K['A3'], K['A4'], K['A5'] = A1, A2, TRIS, TRI, A5
    ones = P.sb([128, 1], F32, 'cone')
    P.add('dve', lambda e_: e_.memset(ones[:], 1.0), [], [cb])
    K['ones'] = ones
    D4 = P.sb([128, 4, 128], F32, 'cD4'); D4b = Buf()
    P.add('pool', lambda e_: e_.iota(D4[:], [[0, 4], [1, 128]], base=0, channel_multiplier=-1, allow_small_or_imprecise_dtypes=True), [], [D4b])

    def mk4(name, op, neg=False):
        t = P.sb([128, 4, 128], F32, name)
        P.add('dve', lambda e_: e_.tensor_single_scalar(out=t[:], in_=D4[:], scalar=0.0, op=op), [D4b], [cb])
        if neg:
            P.add('dve', lambda e_: e_.tensor_single_scalar(out=t[:], in_=t[:], scalar=-1.0, op=ALU.mult), [cb], [cb])
        return t
    K['mT_strict'] = mk4('mTs', ALU.is_gt if d == 0 else ALU.is_lt)
    K['mT_M'] = mk4('mTm', ALU.is_ge, False) if d == 0 else K['mT_strict']
    K['mT_negstrict'] = mk4('mTn', ALU.is_gt if d == 0 else ALU.is_lt, True)
    K['mN_negstrict'] = mk4('mNn', ALU.is_lt if d == 0 else ALU.is_gt, True)
    K['b'] = cb
    return K


def rwkv_scan(P, C, nc, W, e, PR, d, YF, yab, T, S):
    mark = P.sb_mark()
    K = rwkv_consts(P, C, d)
    cb = K['b']
    NCK = S // 128
    f32 = lambda n, nm: P.sb([128, n], F32, nm)
    inp = {k: (f32(1024, 'i' + k), Buf()) for k in ('R', 'V', 'KAP', 'LW', 'B', 'KD')}
    src = {'R': PR['R'], 'V': PR['V'], 'KAP': PR['KAP'], 'LW': PR[f'LW{d}'], 'B': PR[f'B{d}'], 'KD': PR[f'KD{d}']}
    E = [(f32(1024, 'E'), Buf()) for _ in range(3)]
    prod = {k: (f32(1024, 'p' + k), Buf()) for k in ('Kr', 'Br', 'Kdr', 'Rr', 'Ra')}
    tm = {k: (P.sb([128, 1024], BF16, 'b' + k), Buf()) for k in ('Bend', 'Kend', 'V')}
    tr = {k: (P.sb([64, 16, 128], BF16, 't' + k), Buf()) for k in ('Kr', 'Br', 'Kdr', 'Rr', 'Ra')}
    mat = {k: (P.sb([128, 16, 128], BF16, 'm' + k), Buf()) for k in ('P0', 'P0T', 'LkT', 'MrbT', 'MrkT', 'Pa', 'PaT', 'Pb', 'PbT')}
    X32 = P.sb([128, 16, 128], F32, 'X32'); X32b = Buf()
    Xbf = P.sb([128, 16, 128], BF16, 'Xbf'); Xbfb = Buf()
    WT = P.sb([64, 16, 128], BF16, 'WT'); WTb = Buf()
    Ubf = P.sb([128, 16, 64], BF16, 'Ubf'); Ubb = Buf()
    Z32 = P.sb([64, 16, 64], F32, 'Z32'); Zbf = P.sb([64, 16, 64], BF16, 'Zbf'); Zb = Buf()
    ztmp = P.sb([64, 16, 64], F32, 'ztmp'); ztb = Buf()
    cC = P.sb([64, 16], F32, 'cC'); cCb = Buf()
    Y = f32(1024, 'Y'); Yb = Buf()
    if d == 1:
        YFt = f32(1024, 'YFt'); YFb = Buf()
        gat = f32(1024, 'gat'); gatb = Buf()
        bon = P.sb([128, 16], F32, 'bon'); bonb = Buf()
        gng = load_bc(P, W['ab_gn_g'][e], 1024, 'gng'); gnb = load_bc(P, W['ab_gn_b'][e], 1024, 'gnb')
        s1 = P.sb([128, 16], F32, 's1'); s2 = P.sb([128, 16], F32, 's2'); stb = Buf()
        sq = f32(1024, 'sq'); sqb = Buf()
    evi = [0]

    def evac(out_ap, in_ap, reads, writes):
        if evi[0] % 2:
            P.add('act', lambda e_: e_.activation(out=out_ap, in_=in_ap, func=AF.Identity), reads, writes)
        else:
            P.add('dve', lambda e_: e_.tensor_copy(out=out_ap, in_=in_ap), reads, writes)
        evi[0] += 1

    h3 = lambda t: t[:].rearrange("p (h d) -> p h d", d=64)
    for q in range(T // S):
        P.add('pool', lambda e_: e_.memset(Z32[:], 0.0), [], [Zb])
        P.add('pool', lambda e_: e_.memset(Zbf[:], 0.0), [], [Zb])
        order = range(NCK) if d == 0 else range(NCK - 1, -1, -1)
        for ck in order:
            t0 = q * S + ck * 128
            r = slice(t0, t0 + 128)
            for k in inp:
                P.dma('sp', inp[k][0][:], src[k][r, :], writes=[inp[k][1]])
            LW, LWb = inp['LW']
            ei = [0]

            def expo(Akey, scale):
                t, tb = E[ei[0] % 3]; ei[0] += 1
                for hf in range(2):
                    ps, psb = next_ps(C)
                    P.add('pe', lambda e_, ps=ps, hf=hf: e_.matmul(ps[:, :], K[Akey][:], LW[:, hf * 512:(hf + 1) * 512], start=True, stop=True), [cb, LWb], [psb])
                    P.add('act', lambda e_, ps=ps, hf=hf, t=t: e_.activation(out=t[:, hf * 512:(hf + 1) * 512], in_=ps[:, :], func=AF.Exp, scale=scale), [psb], [tb])
                return t, tb

            def mul(out, outb, a, ab, b, bb, eng):
                P.add(eng, lambda e_: e_.tensor_tensor(out=out[:], in0=a[:], in1=b[:], op=ALU.mult), [ab, bb], [outb])
            E1, E1b = expo('A1', 1.0)
            mul(prod['Kr'][0], prod['Kr'][1], inp['KAP'][0], inp['KAP'][1], E1, E1b, 'dve')
            if d == 1:
                mul(prod['Rr'][0], prod['Rr'][1], inp['R'][0], inp['R'][1], E1, E1b, 'pool')
            E2, E2b = expo('A2', -1.0)
            mul(prod['Br'][0], prod['Br'][1], inp['B'][0], inp['B'][1], E2, E2b, 'dve')
            mul(prod['Kdr'][0], prod['Kdr'][1], inp['KD'][0], inp['KD'][1], E2, E2b, 'pool')
            if d == 0:
                E2p, E2pb = expo('A2', 1.0)
                mul(prod['Rr'][0], prod['Rr'][1], inp['R'][0], inp['R'][1], E2p, E2pb, 'dve')
            E3, E3b = expo('A3', 1.0)
            P.add('dve', lambda e_, E3=E3: e_.tensor_tensor(out=X32[:, :, 64:128], in0=h3(inp['KAP'][0]), in1=h3(E3), op=ALU.mult), [inp['KAP'][1], E3b], [X32b])
            P.add('pool', lambda e_: e_.tensor_copy(out=Xbf[:, :, 64:128], in_=X32[:, :, 64:128]), [X32b], [Xbfb])
            if d == 1:
                mul(prod['Ra'][0], prod['Ra'][1], inp['R'][0], inp['R'][1], E3, E3b, 'pool')
            else:
                E4, E4b = expo('A4', 1.0)
                mul(prod['Ra'][0], prod['Ra'][1], inp['R'][0], inp['R'][1], E4, E4b, 'pool')
            E5, E5b = expo('A5', 1.0)
            mul(tm['Bend'][0], tm['Bend'][1], inp['B'][0], inp['B'][1], E5, E5b, 'dve')
            mul(tm['Kend'][0], tm['Kend'][1], inp['KD'][0], inp['KD'][1], E5, E5b, 'pool')
            P.add('pool', lambda e_: e_.tensor_copy(out=tm['V'][0][:], in_=inp['V'][0][:]), [inp['V'][1]], [tm['V'][1]])
            ps, psb = next_ps(C)
            for h in range(16):
                P.add('pe', lambda e_, ps=ps, h=h: e_.matmul(ps[0:64, h:h + 1], LW[:, h * 64:(h + 1) * 64], K['ones'][:], start=True, stop=True), [LWb, cb], [psb])
            P.add('act', lambda e_, ps=ps: e_.activation(out=cC[:], in_=ps[0:64, 0:16], func=AF.Exp), [psb], [cCb])
            for k in ('Kr', 'Br', 'Kdr', 'Rr', 'Ra'):
                pa, pab = prod[k]
                ta, tab = tr[k]
                for g in range(4):
                    ps, psb = next_ps(C)
                    for hi in range(4):
                        h = 4 * g + hi
                        P.add('pe', lambda e_, ps=ps, hi=hi, h=h, pa=pa: e_.transpose(ps[0:64, hi * 128:(hi + 1) * 128], pa[:, h * 64:(h + 1) * 64], C.ident[:]),
                              [pab, C.b_ident], [psb])
                    evac(ta[:, 4 * g:4 * g + 4, :], ps[0:64, :].rearrange("p (a t) -> p a t", t=128), [psb], [tab])
            def score(dst, lk, rk_, mask):
                da, dab = mat[dst]
                for g in range(4):
                    ps, psb = next_ps(C)
                    for hi in range(4):
                        h = 4 * g + hi
                        P.add('pe', lambda e_, ps=ps, hi=hi, h=h: e_.matmul(ps[:, hi * 128:(hi + 1) * 128], tr[lk][0][:, h, :], tr[rk_][0][:, h, :], start=True, stop=True),
                              [tr[lk][1], tr[rk_][1]], [psb])
                    P.add('dve' if g % 2 else 'pool' if False else 'dve', lambda e_, ps=ps, g=g, da=da: e_.tensor_tensor(
                        out=da[:, 4 * g:4 * g + 4, :], in0=ps[:, :].rearrange("p (a t) -> p a t", t=128), in1=K[mask][:], op=ALU.mult), [psb, cb], [dab])
            score('P0', 'Kr', 'Br', 'mN_negstrict')
            score('P0T', 'Br', 'Kr', 'mT_negstrict')
            score('LkT', 'Kdr', 'Kr', 'mT_strict')
            score('MrbT', 'Br', 'Rr', 'mT_M')
            score('MrkT', 'Kdr', 'Rr', 'mT_M')
            for g in range(2):
                ps, psb = next_ps(C)
                for hi in range(8):
                    h = 8 * g + hi
                    P.add('pe', lambda e_, ps=ps, hi=hi, h=h: e_.matmul(ps[:, hi * 64:(hi + 1) * 64], mat['LkT'][0][:, h, :], tm['V'][0][:, h * 64:(h + 1) * 64], start=True, stop=True),
                          [mat['LkT'][1], tm['V'][1]], [psb])
                P.add('dve', lambda e_, ps=ps, g=g: e_.tensor_copy(out=X32[:, 8 * g:8 * g + 8, 0:64], in_=ps[:, :].rearrange("p (a i) -> p a i", i=64)), [psb], [X32b])
                P.add('act', lambda e_, ps=ps, g=g: e_.activation(out=Xbf[:, 8 * g:8 * g + 8, 0:64], in_=ps[:, :].rearrange("p (a i) -> p a i", i=64), func=AF.Identity), [psb], [Xbfb])
            cur, curT = 'P0', 'P0T'
            nxt = [('Pa', 'PaT'), ('Pb', 'PbT')]
            for lev in range(7):
                for g in range(4):
                    ps, psb = next_ps(C)
                    for hi in range(4):
                        h = 4 * g + hi
                        P.add('pe', lambda e_, ps=ps, hi=hi, h=h, curT=curT: e_.matmul(ps[:, hi * 128:(hi + 1) * 128], mat[curT][0][:, h, :], Xbf[:, h, :], start=True, stop=True),
                              [mat[curT][1], Xbfb], [psb])
                    P.add('dve', lambda e_, ps=ps, g=g: e_.tensor_tensor(out=X32[:, 4 * g:4 * g + 4, :], in0=X32[:, 4 * g:4 * g + 4, :],
                                                                        in1=ps[:, :].rearrange("p (a t) -> p a t", t=128), op=ALU.add), [psb, X32b], [X32b])
                for g in range(4):
                    P.add('act', lambda e_, g=g: e_.activation(out=Xbf[:, 4 * g:4 * g + 4, :], in_=X32[:, 4 * g:4 * g + 4, :], func=AF.Identity), [X32b], [Xbfb])
                if lev < 6:
                    n, nT = nxt[lev % 2]
                    for g in range(4):
                        psA, psAb = next_ps(C)
                        psB, psBb = next_ps(C)
                        for hi in range(4):
                            h = 4 * g + hi
                            P.add('pe', lambda e_, psA=psA, hi=hi, h=h, cur=cur, curT=curT: e_.matmul(psA[:, hi * 128:(hi + 1) * 128], mat[curT][0][:, h, :], mat[cur][0][:, h, :], start=True, stop=True),
                                  [mat[cur][1], mat[curT][1]], [psAb])
                            P.add('pe', lambda e_, psB=psB, hi=hi, h=h, cur=cur, curT=curT: e_.matmul(psB[:, hi * 128:(hi + 1) * 128], mat[cur][0][:, h, :], mat[curT][0][:, h, :], start=True, stop=True),
                                  [mat[cur][1], mat[curT][1]], [psBb])
                        evac(mat[n][0][:, 4 * g:4 * g + 4, :], psA[:, :].rearrange("p (a t) -> p a t", t=128), [psAb], [mat[n][1]])
                        evac(mat[nT][0][:, 4 * g:4 * g + 4, :], psB[:, :].rearrange("p (a t) -> p a t", t=128), [psBb], [mat[nT][1]])
                    cur, curT = n, nT
            for g in range(4):
                ps, psb = next_ps(C)
                for hi in range(4):
                    h = 4 * g + hi
                    P.add('pe', lambda e_, ps=ps, hi=hi, h=h: e_.transpose(ps[0:64, hi * 128:(hi + 1) * 128], X32[:, h, 64:128], C.ident[:]), [X32b, C.b_ident], [psb])
                evac(WT[:, 4 * g:4 * g + 4, :], ps[0:64, :].rearrange("p (a t) -> p a t", t=128), [psb], [WTb])
            for g in range(2):
                ps, psb = next_ps(C)
                for hi in range(8):
                    h = 8 * g + hi
                    P.add('pe', lambda e_, ps=ps, hi=hi, h=h: e_.matmul(ps[:, hi * 64:(hi + 1) * 64], WT[:, h, :], Zbf[:, h, :], start=True, stop=True), [WTb, Zb], [psb])
                P.add('dve', lambda e_, ps=ps, g=g: e_.scalar_tensor_tensor(out=Ubf[:, 8 * g:8 * g + 8, :], in0=ps[:, :].rearrange("p (a i) -> p a i", i=64), scalar=-1.0,
                                                                          in1=X32[:, 8 * g:8 * g + 8, 0:64], op0=ALU.mult, op1=ALU.subtract), [psb, X32b], [Ubb])
            for g in range(2):
                ps, psb = next_ps(C)
                for hi in range(8):
                    h = 8 * g + hi
                    o = ps[:, hi * 64:(hi + 1) * 64]
                    P.add('pe', lambda e_, o=o, h=h: e_.matmul(o, tr['Ra'][0][:, h, :], Zbf[:, h, :], start=True, stop=False), [tr['Ra'][1], Zb], [psb])
                    P.add('pe', lambda e_, o=o, h=h: e_.matmul(o, mat['MrbT'][0][:, h, :], Ubf[:, h, :], start=False, stop=False), [mat['MrbT'][1], Ubb], [psb])
                    P.add('pe', lambda e_, o=o, h=h: e_.matmul(o, mat['MrkT'][0][:, h, :], tm['V'][0][:, h * 64:(h + 1) * 64], start=False, stop=True), [mat['MrkT'][1], tm['V'][1]], [psb])
                evac(Y[:, g * 512:(g + 1) * 512], ps[:, :], [psb], [Yb])
            for g in range(2):
                ps, psb = next_ps(C)
                for hi in range(8):
                    h = 8 * g + hi
                    o = ps[0:64, hi * 64:(hi + 1) * 64]
                    P.add('pe', lambda e_, o=o, h=h: e_.matmul(o, tm['Bend'][0][:, h * 64:(h + 1) * 64], Ubf[:, h, :], start=True, stop=False), [tm['Bend'][1], Ubb], [psb])
                    P.add('pe', lambda e_, o=o, h=h: e_.matmul(o, tm['Kend'][0][:, h * 64:(h + 1) * 64], tm['V'][0][:, h * 64:(h + 1) * 64], start=False, stop=True), [tm['Kend'][1], tm['V'][1]], [psb])
                P.add('dve', lambda e_, g=g: e_.tensor_tensor(out=ztmp[:, 8 * g:8 * g + 8, :], in0=Z32[:, 8 * g:8 * g + 8, :],
                                                             in1=bass.AP(cC, 8 * g, [[16, 64], [1, 8], [0, 64]]), op=ALU.mult), [Zb, cCb], [ztb])
                P.add('dve', lambda e_, ps=ps, g=g: e_.tensor_tensor(out=Z32[:, 8 * g:8 * g + 8, :], in0=ztmp[:, 8 * g:8 * g + 8, :],
                                                                    in1=ps[0:64, :].rearrange("p (a i) -> p a i", i=64), op=ALU.add), [ztb, psb], [Zb])
            P.add('act', lambda e_: e_.activation(out=Zbf[:], in_=Z32[:], func=AF.Identity), [Zb], [Zb])
            if d == 0:
                P.dma('pool', YF[r, :], Y[:], reads=[Yb])
            else:
                P.dma('sp', YFt[:], YF[r, :], writes=[YFb])
                P.dma('sp', gat[:], PR['GATE'][r, :], writes=[gatb])
                P.dma('sp', bon[:], PR['BON'][r, :], writes=[bonb])
                P.add('dve', lambda e_: e_.tensor_tensor(out=Y[:], in0=Y[:], in1=YFt[:], op=ALU.add), [Yb, YFb], [Yb])
                P.add('dve', lambda e_: e_.tensor_reduce(out=s1[:], in_=h3(Y), axis=AX.X, op=ALU.add), [Yb], [stb])
                P.add('pool', lambda e_: e_.tensor_tensor(out=sq[:], in0=Y[:], in1=Y[:], op=ALU.mult), [Yb], [sqb])
                P.add('dve', lambda e_: e_.tensor_reduce(out=s2[:], in_=h3(sq), axis=AX.X, op=ALU.add), [sqb], [stb])
                P.add('dve', lambda e_: e_.tensor_single_scalar(out=s1[:], in_=s1[:], scalar=1.0 / 64, op=ALU.mult), [stb], [stb])
                P.add('dve', lambda e_: e_.tensor_single_scalar(out=s2[:], in_=s2[:], scalar=1.0 / 64, op=ALU.mult), [stb], [stb])
                P.add('dve', lambda e_: e_.tensor_tensor(out=sq[:, 0:16], in0=s1[:], in1=s1[:], op=ALU.mult), [stb, sqb], [sqb])
                P.add('dve', lambda e_: e_.tensor_tensor(out=s2[:], in0=s2[:], in1=sq[:, 0:16], op=ALU.subtract), [stb, sqb], [stb])
                P.add('dve', lambda e_: e_.tensor_single_scalar(out=s2[:], in_=s2[:], scalar=A_GN_EPS, op=ALU.add), [stb], [stb])
                P.add('act', lambda e_: e_.activation(out=s2[:], in_=s2[:], func=AF.Sqrt), [stb], [stb])
                P.add('dve', lambda e_: e_.reciprocal(out=s2[:], in_=s2[:]), [stb], [stb])
                P.add('dve', lambda e_: e_.tensor_tensor(out=h3(Y), in0=h3(Y), in1=bc_ap(s1, 16, 16, 64), op=ALU.subtract), [Yb, stb], [Yb])
                P.add('dve', lambda e_: e_.tensor_tensor(out=h3(Y), in0=h3(Y), in1=bc_ap(s2, 16, 16, 64), op=ALU.mult), [Yb, stb], [Yb])
                P.add('pool', lambda e_: e_.tensor_tensor(out=Y[:], in0=Y[:], in1=gng[0][:], op=ALU.mult), [Yb, gng[1]], [Yb])
                P.add('pool', lambda e_: e_.tensor_tensor(out=Y[:], in0=Y[:], in1=gnb[0][:], op=ALU.add), [Yb, gnb[1]], [Yb])
                P.add('dve', lambda e_: e_.tensor_tensor(out=h3(sq), in0=h3(inp['V'][0]), in1=bc_ap(bon, 16, 16, 64), op=ALU.mult), [inp['V'][1], bonb, sqb], [sqb])
                P.add('dve', lambda e_: e_.tensor_tensor(out=Y[:], in0=Y[:], in1=sq[:], op=ALU.add), [Yb, sqb], [Yb])
                P.add('dve', lambda e_: e_.tensor_tensor(out=Y[:], in0=Y[:], in1=gat[:], op=ALU.mult), [Yb, gatb], [Yb])
                P.dma('pool', yab[r, 0:1024], Y[:], reads=[Yb])
    P.sb_reset(mark)


WSPEC = [
    ('ab_w_in', (2, 2048, 6400)), ('ab_shift', (2, 3, 3328)), ('ab_w0', (2, 2, 1024)), ('ab_w_up', (2, 2, 64, 1024)),
    ('ab_a0', (2, 2, 1024)), ('ab_a_up', (2, 2, 64, 1024)), ('ab_g_up', (2, 128, 1024)), ('ab_k_k', (2, 1024)),
    ('ab_k_a', (2, 1024)), ('ab_r_k', (2, 16, 64)), ('ab_gn_g', (2, 1024)), ('ab_gn_b', (2, 1024)),
    ('ab_rpb', (2, 16, 15, 31)), ('ab_w_out', (2, 2048, 2048)), ('c_w_in', (2, 2048, 12288)), ('c_gn_g', (2, 4096)),
    ('c_gn_b', (2, 4096)), ('c_w_out', (2, 4096, 2048)), ('ln1_g', (4, 2048)), ('ln1_b', (4, 2048)),
    ('ffn_w_up', (4, 2048, 11008)), ('ffn_conv', (4, 3, 5504)), ('ffn_conv_b', (4, 5504)), ('ffn_w_down', (4, 5504, 2048)),
    ('ln2_g', (4, 2048)), ('ln2_b', (4, 2048)),
]
BIGW = ['ab_w_in', 'ab_w_out', 'c_w_in', 'c_w_out', 'ffn_w_up', 'ffn_w_down']


def build(NSEQ, S, layers, debug_outs=()):
    T = NSEQ * S
    nc = bass.Bass("TRN2", target_bir_lowering=False)
    W = {}
    for name, shp in WSPEC:
        W[name] = nc.dram_tensor(name, list(shp), F32, kind="ExternalInput").ap()
    x_in = nc.dram_tensor("x", [T, D], F32, kind="ExternalInput").ap()
    y_out = nc.dram_tensor("y", [T, D], F32, kind="ExternalOutput").ap()

    def scr(name, shape, dt):
        kind = "ExternalOutput" if name in debug_outs else "Internal"
        return nc.dram_tensor(name, shape, dt, kind=kind).ap()
    WB = {}
    for name in BIGW:
        shp = dict(WSPEC)[name]
        WB[name] = scr(name + "_b", list(shp), BF16)
    xs = [scr("xs0", [T, D], F32), scr("xs1", [T, D], F32)]
    xTs = [scr("xT0", [D, T], BF16), scr("xT1", [D, T], BF16)]
    hTs = [scr(f"hT{q}", [11008, S], F32) for q in range(NSEQ)]
    vgs = [scr(f"vg{q}", [S, 8192], F32) for q in range(NSEQ)]
    gT = scr("gT", [5504, T], BF16)
    qkr = scr("qkr", [4096, T], BF16)
    ytok = scr("ytok", [T, 4096], F32)
    yT = scr("yT", [4096, T], BF16)
    mix = scr("mix", [T, D], F32)

    P = Prog(nc); C = Ctx()
    setup_consts(P, C)
    TT = min(512, S)
    for l in layers:
        e = l // 2
        if l % 2 == 0:
            cast_weight(P, C, W['ab_w_in'][e], WB['ab_w_in'][e], 2048, 6400)
            cast_weight(P, C, W['ab_w_out'][e], WB['ab_w_out'][e], 2048, 2048)
        else:
            cast_weight(P, C, W['c_w_in'][e], WB['c_w_in'][e], 2048, 12288)
            cast_weight(P, C, W['c_w_out'][e], WB['c_w_out'][e], 4096, 2048)
        cast_weight(P, C, W['ffn_w_up'][l], WB['ffn_w_up'][l], 2048, 11008)
        cast_weight(P, C, W['ffn_w_down'][l], WB['ffn_w_down'][l], 5504, 2048)
    xprep(P, C, x_in, xTs[0], T)
    xcur, xTcur, pp = x_in, xTs[0], 0

    def G(xT_, w_, K, N, mode, out, n_off=0, TG=T):
        m = P.sb_mark()
        gemm(P, C, xT_, w_, K, TG, N, mode, StoreEpi(P, C, out, mode), TT=TT, n_off=n_off)
        P.sb_reset(m)

    for li, l in enumerate(layers):
        e = l // 2
        if l % 2 == 0:
            mixer_ab(P, C, nc, W, WB, e, xTcur, mix, T, S, NSEQ, scr_fn=scr, G=G)
        else:
            for q in range(NSEQ):
                sq = slice(q * S, (q + 1) * S)
                G(xTcur[:, sq], WB['c_w_in'][e], 2048, 4096, 'feat', hTs[q][0:4096, :], TG=S)
                G(xTcur[:, sq], WB['c_w_in'][e], 2048, 8192, 'tok', vgs[q], n_off=4096, TG=S)
                ret_rotary(P, C, hTs[q][0:4096, :], qkr[:, sq], S, S)
                ret_attn(P, C, qkr[:, sq], vgs[q], W['c_gn_g'][e], W['c_gn_b'][e], ytok[sq, :], S, S)
            xprep_n(P, C, ytok, yT, T, 4096)
            G(yT, WB['c_w_out'][e], 4096, 2048, 'tok', mix)
        x1, x1T = xs[pp], xTs[1 - pp]
        ln_phase(P, C, xcur, mix, W['ln1_g'][l], W['ln1_b'][l], x1, x1T, T)
        for q in range(NSEQ):
            sq = slice(q * S, (q + 1) * S)
            G(x1T[:, sq], WB['ffn_w_up'][l], 2048, 11008, 'feat', hTs[q], TG=S)
            ffn_mid(P, C, hTs[q], W['ffn_conv'][l], W['ffn_conv_b'][l], gT[:, sq], S, S)
        G(gT, WB['ffn_w_down'][l], 5504, 2048, 'tok', mix)
        last = (li == len(layers) - 1)
        x2 = y_out if last else xs[1 - pp]
        x2T = xTs[pp]
        ln_phase(P, C, x1, mix, W['ln2_g'][l], W['ln2_b'][l], x2, None if last else x2T, T)
        xcur, xTcur = x2, x2T
        pp = 1 - pp
    P.emit()
    return nc, P


def mixer_ab(P, C, nc, W, WB, e, xTcur, mix, T, S, NSEQ, scr_fn, G):
    if not hasattr(C, 'ab_scr'):
        d = {}
        d['hA'] = scr_fn("hA", [T, 3328], F32)
        d['hqk'] = scr_fn("hqk", [2048, T], F32)
        d['hv'] = scr_fn("hv", [T, 1024], F32)
        d['yab'] = scr_fn("yab", [T, 2048], F32)
        d['yabT'] = scr_fn("yabT", [2048, T], BF16)
        d['YF'] = scr_fn("YF", [T, 1024], F32)
        d['PR'] = {k: scr_fn("pr_" + k, [T, 1024], F32) for k in ('R', 'V', 'KAP', 'GATE', 'LW0', 'LW1', 'B0', 'B1', 'KD0', 'KD1')}
        d['PR']['BON'] = scr_fn("pr_BON", [T, 16], F32)
        C.ab_scr = d
    d = C.ab_scr
    G(xTcur, WB['ab_w_in'][e], 2048, 3328, 'tok', d['hA'])
    G(xTcur, WB['ab_w_in'][e], 2048, 2048, 'feat', d['hqk'], n_off=3328)
    G(xTcur, WB['ab_w_in'][e], 2048, 1024, 'tok', d['hv'], n_off=3328 + 2048)
    rwkv_prep(P, C, nc, W, e, d['hA'], d['PR'], T, S)
    rwkv_scan(P, C, nc, W, e, d['PR'], 0, d['YF'], d['yab'], T, S)
    rwkv_scan(P, C, nc, W, e, d['PR'], 1, d['YF'], d['yab'], T, S)
    bS = na_bias_build(P, C, nc, W['ab_rpb'][e], scr_fn, e)
    na_attn(P, C, nc, bS, d['hqk'], d['hv'], d['yab'], T, S)
    xprep_n(P, C, d['yab'], d['yabT'], T, 2048)
    G(d['yabT'], WB['ab_w_out'][e], 2048, 2048, 'tok', mix)


_CACHE = {}
NSEQ_CORE = 2
SEQ = 4096


def _assign():
    return [(('p', c), ('s', c) if c < 4 else ('p', c)) for c in range(8)]


def kernel(**inputs):
    if 'nc' not in _CACHE:
        _CACHE['nc'] = build(NSEQ_CORE, SEQ, [0, 1, 2, 3])[0]
    nc = _CACHE['nc']
    xp = np.asarray(inputs['x_prompt'], dtype=np.float32)
    xs_ = np.asarray(inputs['x_sample'], dtype=np.float32)
    wd = {n: np.ascontiguousarray(np.asarray(inputs[n], dtype=np.float32)) for n, _ in WSPEC}
    in_maps = []
    asg = _assign()
    for c in range(8):
        rows = []
        for kind, i in asg[c]:
            rows.append(xp[i] if kind == 'p' else xs_[i])
        m = dict(wd)
        m['x'] = np.ascontiguousarray(np.concatenate(rows, axis=0))
        in_maps.append(m)
    res = run_bass_kernel_spmd(nc, in_maps, core_ids=list(range(8)))
    yp = np.empty_like(xp)
    ys = np.empty_like(xs_)
    for c in range(8):
        y = np.asarray(res.results[c]['y'])
        for slot, (kind, i) in enumerate(asg[c]):
            blk = y[slot * SEQ:(slot + 1) * SEQ]
            if slot == 1 and c >= 4:
                continue
            if kind == 'p':
                yp[i] = blk
            else:
                ys[i] = blk
    return (yp, ys)
```

```python
import numpy as np
import concourse.bass as bass
import concourse.mybir as mybir
from concourse.bass_utils import run_bass_kernel_spmd

F32 = mybir.dt.float32
BF16 = mybir.dt.bfloat16
I32 = mybir.dt.int32
AF = mybir.ActivationFunctionType
ALU = mybir.AluOpType
AX = mybir.AxisListType

ENGS = ('pe', 'act', 'dve', 'pool', 'sp')
DMAQ = {'sp': 16, 'act': 8, 'pool': 8}


class Buf:
    __slots__ = ('name', 'lw', 'rd')

    def __init__(self, name=''):
        self.name = name
        self.lw = None
        self.rd = []


class Op:
    __slots__ = ('eng', 'fn', 'dma', 'deps', 'sig', 'slot', 'slotval', 'idx', 'bar')

    def __init__(self, eng, fn, dma):
        self.eng = eng
        self.fn = fn
        self.dma = dma
        self.deps = ()
        self.sig = False
        self.slot = None
        self.slotval = 0
        self.idx = 0
        self.bar = False


class Prog:
    def __init__(self, nc):
        self.nc = nc
        self.ops = []
        self.last = {e: None for e in ENGS}
        self.sb_off = 16640
        self.sb_base = 16640
        self.nalloc = 0

    def sb(self, shape, dtype, name='t'):
        nbytes = int(np.prod(shape[1:])) * (4 if dtype in (F32, I32) else 2)
        off = (self.sb_off + 63) // 64 * 64
        assert off + nbytes <= 229000, (off, nbytes, name)
        self.sb_off = off + nbytes
        self.nalloc += 1
        return self.nc.alloc_sbuf_tensor_at(f"{name}{self.nalloc}", list(shape), dtype, offset=off)

    def sb_mark(self):
        return self.sb_off

    def sb_reset(self, mark):
        self.sb_off = mark
        self.barrier()

    def add(self, eng, fn, reads=(), writes=(), dma=False):
        op = Op(eng, fn, dma)
        deps = set()
        for b in reads:
            if b.lw is not None:
                deps.add(b.lw)
        for b in writes:
            if b.lw is not None:
                deps.add(b.lw)
            deps.update(b.rd)
        dl = []
        for d in deps:
            if d is op:
                continue
            if (not dma) and (not d.dma) and d.eng == 'pe' and eng == 'pe':
                continue
            d.sig = True
            dl.append(d)
        op.deps = dl
        for b in reads:
            b.rd.append(op)
        for b in writes:
            b.lw = op
            b.rd = []
        self.ops.append(op)
        if not dma:
            self.last[eng] = op
        return op

    def dma(self, q, out, in_, reads=(), writes=(), **kw):
        return self.add(q, lambda e: e.dma_start(out=out, in_=in_, **kw), reads, writes, dma=True)

    def mark(self, name):
        op = Op('mark', None, False)
        op.bar = name
        self.ops.append(op)

    def barrier(self):
        op = Op('all', None, False)
        op.bar = True
        for e in ENGS:
            if self.last[e] is not None:
                self.last[e].sig = True
        self.ops.append(op)

    def emit(self):
        nc = self.nc
        self.barrier()
        engsem = {e: nc.alloc_semaphore(f"s_{e}") for e in ENGS}
        slotsem = {q: [nc.alloc_semaphore(f"d_{q}{i}") for i in range(n)] for q, n in DMAQ.items()}
        slotuse = {q: [0] * n for q, n in DMAQ.items()}
        rr = {q: 0 for q in DMAQ}
        cnt = {e: 0 for e in ENGS}
        waited = {e: {} for e in ENGS}
        streams = {e: [] for e in ENGS}

        def want(e, sem, val):
            if val <= 0:
                return
            w = waited[e]
            k = id(sem)
            if w.get(k, 0) >= val:
                return
            w[k] = val
            streams[e].append(('w', sem, val))

        self.marks = []
        for op in self.ops:
            if op.eng == 'mark':
                self.marks.append((op.bar, dict(cnt)))
                continue
            if op.bar:
                for e in ENGS:
                    for x in ENGS:
                        if x != e:
                            want(e, engsem[x], cnt[x])
                    for q in DMAQ:
                        for i, s in enumerate(slotsem[q]):
                            want(e, s, 16 * slotuse[q][i])
                continue
            e = op.eng
            for d in op.deps:
                if d.dma:
                    want(e, slotsem[d.eng][d.slot], d.slotval)
                else:
                    want(e, engsem[d.eng], d.idx)
            if op.dma:
                k = rr[e]
                rr[e] = (k + 1) % DMAQ[e]
                want(e, slotsem[e][k], 16 * slotuse[e][k])
                slotuse[e][k] += 1
                op.slot = k
                op.slotval = 16 * slotuse[e][k]
                streams[e].append(('d', op, slotsem[e][k]))
            else:
                if op.sig:
                    cnt[e] += 1
                    op.idx = cnt[e]
                streams[e].append(('o', op, engsem[e]))

        def run_stream(eng_handle, items):
            for it in items:
                if it[0] == 'w':
                    eng_handle.wait_ge(it[1], it[2])
                elif it[0] == 'd':
                    it[1].fn(eng_handle).then_inc(it[2], 16)
                else:
                    ins = it[1].fn(eng_handle)
                    if it[1].sig:
                        ins.then_inc(it[2], 1)

        with nc.Block() as block:
            @block.sync
            def _(e):
                run_stream(e, streams['sp'])

            @block.tensor
            def _(e):
                run_stream(e, streams['pe'])

            @block.scalar
            def _(e):
                run_stream(e, streams['act'])

            @block.vector
            def _(e):
                run_stream(e, streams['dve'])

            @block.gpsimd
            def _(e):
                run_stream(e, streams['pool'])
        self.nitems = {e: len(streams[e]) for e in ENGS}


D = 2048
DFF = 5504
ALPHA = 8.0 ** 0.25
LN_EPS = 1e-5


class Ctx:
    pass


def setup_consts(P, C):
    nc = P.nc
    idn = nc.inline_tensor(np.eye(128, dtype=np.float32), "ident_d").ap()
    C.ident = P.sb([128, 128], F32, 'ident')
    C.identb = P.sb([128, 128], BF16, 'identb')
    C.b_ident = Buf('ident')
    P.dma('sp', C.ident[:], idn[:, :], writes=[C.b_ident])
    P.add('dve', lambda e: e.tensor_copy(out=C.identb[:], in_=C.ident[:]), [C.b_ident], [C.b_ident])
    C.ps = [nc.alloc_psum_tensor(f"psb{i}", [128, 512], F32) for i in range(8)]
    C.psb = [Buf(f'ps{i}') for i in range(8)]
    C.psi = 0


def next_ps(C, n=8, base=0):
    i = base + (C.psi % n)
    C.psi += 1
    return C.ps[i], C.psb[i]


def cast_weight(P, C, src, dst, K, N):
    P.mark('cast_weight')
    mark = P.sb_mark()
    CB = 2048
    NBUF = 3
    st = [(P.sb([128, CB], F32, 'cw'), P.sb([128, CB], BF16, 'cwb'), Buf(), Buf()) for _ in range(NBUF)]
    engs = ['dve', 'act', 'pool']
    i = 0
    for r0 in range(0, K, 128):
        for c0 in range(0, N, CB):
            cn = min(CB, N - c0)
            a, b, ba, bb = st[i % NBUF]
            eng = engs[i % 3]
            P.dma('sp', a[:, 0:cn], src[r0:r0 + 128, c0:c0 + cn], writes=[ba])
            if eng == 'act':
                P.add('act', lambda e, a=a, b=b, cn=cn: e.copy(out=b[:, 0:cn], in_=a[:, 0:cn]), [ba], [bb])
            else:
                P.add(eng, lambda e, a=a, b=b, cn=cn: e.tensor_copy(out=b[:, 0:cn], in_=a[:, 0:cn]), [ba], [bb])
            P.dma('pool', dst[r0:r0 + 128, c0:c0 + cn], b[:, 0:cn], reads=[bb])
            i += 1
    P.sb_reset(mark)


def xprep(P, C, x, xT, T):
    P.mark('xprep')
    mark = P.sb_mark()
    KC = D // 128
    NB = 2
    st = [(P.sb([128, D], F32, 'xp'), P.sb([128, KC, 128], BF16, 'xpt'), Buf(), Buf()) for _ in range(NB)]
    xTv = xT.rearrange("(kc p) t -> p kc t", p=128)
    for ti in range(T // 128):
        a, b, ba, bb = st[ti % NB]
        P.dma('sp', a[:], x[ti * 128:(ti + 1) * 128, :], writes=[ba])
        transpose_to(P, C, a, ba, b, bb, KC)
        P.dma('pool', xTv[:, :, ti * 128:(ti + 1) * 128], b[:], reads=[bb])
    P.sb_reset(mark)


def transpose_to(P, C, a, ba, b, bb, KC, evi=[0]):
    for g in range(0, KC, 4):
        ps, psb = next_ps(C)
        ng = min(4, KC - g)
        for j in range(ng):
            kc = g + j
            P.add('pe', lambda e, ps=ps, j=j, kc=kc: e.transpose(ps[:, j * 128:(j + 1) * 128], a[:, kc * 128:(kc + 1) * 128], C.ident[:]),
                  [ba, C.b_ident], [psb])
        eng = 'act' if evi[0] % 2 else 'dve'
        evi[0] += 1
        src = lambda ps=ps, ng=ng: ps[:, 0:ng * 128].rearrange("p (j t) -> p j t", t=128)
        if eng == 'act':
            P.add('act', lambda e, g=g, ng=ng, src=src: e.copy(out=b[:, g:g + ng, :], in_=src()), [psb], [bb])
        else:
            P.add('dve', lambda e, g=g, ng=ng, src=src: e.tensor_copy(out=b[:, g:g + ng, :], in_=src()), [psb], [bb])


def gemm(P, C, xT, w, K, T, N, mode, epi, TT=512, n_off=0):
    P.mark('gemm')
    mark = P.sb_mark()
    KC = K // 128
    NBW = 512
    xt = [(P.sb([128, KC, TT], BF16, 'gx'), Buf()) for _ in range(2)]
    wt = [(P.sb([128, KC, NBW], BF16, 'gw'), Buf()) for _ in range(2)]
    xTv = xT.rearrange("(kc p) t -> p kc t", p=128)
    wv = w.rearrange("(kc p) n -> p kc n", p=128)
    wi = 0
    for tti, t0 in enumerate(range(0, T, TT)):
        tt = min(TT, T - t0)
        xa, xb = xt[tti % 2]
        h = KC // 2
        P.dma('sp', xa[:, 0:h, 0:tt], xTv[:, 0:h, t0:t0 + tt], writes=[xb])
        P.dma('sp', xa[:, h:KC, 0:tt], xTv[:, h:KC, t0:t0 + tt], writes=[xb])
        for n0 in range(0, N, NBW):
            nn = min(NBW, N - n0)
            wa, wb = wt[wi % 2]
            wi += 1
            P.dma('sp', wa[:, 0:h, 0:nn], wv[:, 0:h, n_off + n0:n_off + n0 + nn], writes=[wb])
            P.dma('sp', wa[:, h:KC, 0:nn], wv[:, h:KC, n_off + n0:n_off + n0 + nn], writes=[wb])
            if mode == 'tok':
                for s0 in range(0, tt, 128):
                    ps, psb = next_ps(C)
                    for kc in range(KC):
                        P.add('pe', lambda e, ps=ps, kc=kc, s0=s0, xa=xa, wa=wa, nn=nn: e.matmul(
                            ps[:, 0:nn], xa[:, kc, s0:s0 + 128], wa[:, kc, 0:nn], start=(kc == 0), stop=(kc == KC - 1)),
                            [xb, wb], [psb])
                    epi(ps, psb, t0 + s0, 128, n0, nn)
            else:
                for m0 in range(0, nn, 128):
                    ps, psb = next_ps(C)
                    for kc in range(KC):
                        P.add('pe', lambda e, ps=ps, kc=kc, m0=m0, xa=xa, wa=wa, tt=tt: e.matmul(
                            ps[:, 0:tt], wa[:, kc, m0:m0 + 128], xa[:, kc, 0:tt], start=(kc == 0), stop=(kc == KC - 1)),
                            [xb, wb], [psb])
                    epi(ps, psb, t0, tt, n0 + m0, 128)
    P.sb_reset(mark)


class StoreEpi:
    def __init__(self, P, C, out, mode, dtype=F32):
        self.P, self.C, self.out, self.mode = P, C, out, mode
        self.st = [(P.sb([128, 512], dtype, 'se'), Buf()) for _ in range(4)]
        self.i = 0

    def __call__(self, ps, psb, t0, nt, n0, nn):
        P = self.P
        a, ab = self.st[self.i % 4]
        eng = 'act' if self.i % 2 else 'dve'
        self.i += 1
        if self.mode == 'tok':
            w = nn
            dst = self.out[t0:t0 + nt, n0:n0 + nn]
        else:
            w = nt
            dst = self.out[n0:n0 + nn, t0:t0 + nt]
        if eng == 'act':
            P.add('act', lambda e: e.copy(out=a[:, 0:w], in_=ps[:, 0:w]), [psb], [ab])
        else:
            P.add('dve', lambda e: e.tensor_copy(out=a[:, 0:w], in_=ps[:, 0:w]), [psb], [ab])
        P.dma('pool', dst, a[:, 0:w], reads=[ab])


def bcast_rows(P, C, src_row, n, name):
    t = P.sb([128, n], F32, name)
    b = Buf(name)
    P.dma('sp', t[:], src_row.partition_broadcast(128) if hasattr(src_row, 'partition_broadcast') else src_row, writes=[b])
    return t, b


def ln_phase(P, C, xin, mix, g_row, b_row, xout, xTout, T, final_out=None):
    P.mark('ln_phase')
    mark = P.sb_mark()
    nc = P.nc
    KC = D // 128
    gt = P.sb([128, D], F32, 'lng'); gb = Buf()
    bt = P.sb([128, D], F32, 'lnb'); bb_ = Buf()
    P.dma('sp', gt[:], bass.AP(g_row.tensor, g_row.offset, [[0, 128], [1, D]]), writes=[gb])
    P.dma('sp', bt[:], bass.AP(b_row.tensor, b_row.offset, [[0, 128], [1, D]]), writes=[bb_])
    NB = 2
    st = []
    for _ in range(NB):
        st.append(dict(x=P.sb([128, D], F32, 'lx'), m=P.sb([128, D], F32, 'lm'), xb=Buf(), mb=Buf(),
                       stats=P.sb([128, 4, 6], F32, 'ls'), mv=P.sb([128, 2], F32, 'lmv'), sb=Buf(),
                       rstd=P.sb([128, 1], F32, 'lr'), nmr=P.sb([128, 1], F32, 'ln'),
                       xt=P.sb([128, KC, 128], BF16, 'lxt'), xtb=Buf()))
    xTv = xTout.rearrange("(kc p) t -> p kc t", p=128) if xTout is not None else None
    for ti in range(T // 128):
        s = st[ti % NB]
        x, m = s['x'], s['m']
        r = slice(ti * 128, (ti + 1) * 128)
        P.dma('sp', x[:], xin[r, :], writes=[s['xb']])
        P.dma('sp', m[:], mix[r, :], writes=[s['mb']])
        P.add('dve', lambda e, x=x, m=m: e.scalar_tensor_tensor(out=m[:], in0=x[:], scalar=ALPHA, in1=m[:], op0=ALU.mult, op1=ALU.add),
              [s['xb'], s['mb']], [s['mb']])
        for c in range(4):
            P.add('dve', lambda e, s=s, m=m, c=c: e.bn_stats(out=s['stats'][:, c, :], in_=m[:, c * 512:(c + 1) * 512]), [s['mb']], [s['sb']])
        P.add('dve', lambda e, s=s: e.bn_aggr(out=s['mv'][:], in_=s['stats'][:].rearrange("p a b -> p (a b)")), [s['sb']], [s['sb']])
        P.add('dve', lambda e, s=s: e.tensor_scalar(out=s['rstd'][:], in0=s['mv'][:, 1:2], scalar1=LN_EPS, scalar2=None, op0=ALU.add),
              [s['sb']], [s['sb']])
        P.add('act', lambda e, s=s: e.activation(out=s['rstd'][:], in_=s['rstd'][:], func=AF.Sqrt), [s['sb']], [s['sb']])
        P.add('dve', lambda e, s=s: e.reciprocal(out=s['rstd'][:], in_=s['rstd'][:]), [s['sb']], [s['sb']])
        P.add('dve', lambda e, s=s: e.scalar_tensor_tensor(out=s['nmr'][:], in0=s['mv'][:, 0:1], scalar=-1.0, in1=s['rstd'][:], op0=ALU.mult, op1=ALU.mult),
              [s['sb']], [s['sb']])
        P.add('act', lambda e, s=s, x=x, m=m: e.activation(out=x[:], in_=m[:], func=AF.Identity, bias=s['nmr'][:], scale=s['rstd'][:]),
              [s['mb'], s['sb']], [s['xb']])
        P.add('pool', lambda e, x=x: e.tensor_tensor(out=x[:], in0=x[:], in1=gt[:], op=ALU.mult), [s['xb'], gb], [s['xb']])
        P.add('dve', lambda e, x=x: e.tensor_tensor(out=x[:], in0=x[:], in1=bt[:], op=ALU.add), [s['xb'], bb_], [s['xb']])
        P.dma('pool', xout[r, :], x[:], reads=[s['xb']])
        if xTv is not None:
            transpose_to(P, C, x, s['xb'], s['xt'], s['xtb'], KC)
            P.dma('pool', xTv[:, :, r], s['xt'][:], reads=[s['xtb']])
    P.sb_reset(mark)


def ffn_mid(P, C, hT, conv_w, conv_b, gT, T, S):
    P.mark('ffn_mid')
    mark = P.sb_mark()
    NCH = DFF // 128
    cw = P.sb([128, NCH, 3], F32, 'fcw'); cwb = Buf()
    cb = P.sb([128, NCH], F32, 'fcb'); cbb = Buf()
    for k in range(3):
        P.dma('sp', cw[:, :, k], conv_w[k, :].rearrange("(c p) -> p c", p=128), writes=[cwb], allow_slow_non_contiguous=True)
    P.dma('sp', cb[:], conv_b.rearrange("(c p) -> p c", p=128), writes=[cbb], allow_slow_non_contiguous=True)
    NB = 2
    st = [dict(g=P.sb([128, S + 2], F32, 'fg'), u=P.sb([128, S], F32, 'fu'), a=P.sb([128, S], F32, 'fa'),
               o=P.sb([128, S], BF16, 'fo'), gb=Buf(), ub=Buf(), ab=Buf(), ob=Buf()) for _ in range(NB)]
    for s in st:
        P.add('pool', lambda e, s=s: e.memset(s['g'][:, 0:1], 0.0), [], [s['gb']])
        P.add('pool', lambda e, s=s: e.memset(s['g'][:, S + 1:S + 2], 0.0), [], [s['gb']])
    i = 0
    for q in range(T // S):
        for ch in range(NCH):
            s = st[i % NB]
            i += 1
            g, u, a, o = s['g'], s['u'], s['a'], s['o']
            tr = slice(q * S, (q + 1) * S)
            P.dma('sp', g[:, 1:S + 1], hT[ch * 128:(ch + 1) * 128, tr], writes=[s['gb']])
            P.dma('sp', u[:], hT[DFF + ch * 128:DFF + (ch + 1) * 128, tr], writes=[s['ub']])
            P.add('act', lambda e, g=g, a=a, ch=ch: e.activation(out=a[:], in_=g[:, 1:S + 1], func=AF.Identity, bias=cb[:, ch:ch + 1], scale=cw[:, ch, 1:2]),
                  [s['gb'], cwb, cbb], [s['ab']])
            P.add('dve', lambda e, g=g, a=a, ch=ch: e.scalar_tensor_tensor(out=a[:], in0=g[:, 0:S], scalar=cw[:, ch, 0:1], in1=a[:], op0=ALU.mult, op1=ALU.add),
                  [s['gb'], s['ab'], cwb], [s['ab']])
            P.add('dve', lambda e, g=g, a=a, ch=ch: e.scalar_tensor_tensor(out=a[:], in0=g[:, 2:S + 2], scalar=cw[:, ch, 2:3], in1=a[:], op0=ALU.mult, op1=ALU.add),
                  [s['gb'], s['ab'], cwb], [s['ab']])
            P.add('act', lambda e, a=a: e.activation(out=a[:], in_=a[:], func=AF.Gelu), [s['ab']], [s['ab']])
            P.add('pool', lambda e, a=a, u=u, o=o: e.tensor_tensor(out=o[:], in0=a[:], in1=u[:], op=ALU.mult), [s['ab'], s['ub']], [s['ob']])
            P.dma('pool', gT[ch * 128:(ch + 1) * 128, tr], o[:], reads=[s['ob']])
    P.sb_reset(mark)


def xprep_n(P, C, x, xT, T, DD):
    P.mark('xprep_n')
    mark = P.sb_mark()
    KC = DD // 128
    NB = 2
    st = [(P.sb([128, DD], F32, 'xp'), P.sb([128, KC, 128], BF16, 'xpt'), Buf(), Buf()) for _ in range(NB)]
    xTv = xT.rearrange("(kc p) t -> p kc t", p=128)
    for ti in range(T // 128):
        a, b, ba, bb = st[ti % NB]
        P.dma('sp', a[:], x[ti * 128:(ti + 1) * 128, :], writes=[ba])
        transpose_to(P, C, a, ba, b, bb, KC)
        P.dma('pool', xTv[:, :, ti * 128:(ti + 1) * 128], b[:], reads=[bb])
    P.sb_reset(mark)


def rot_tables(S):
    half = 128
    inv = (10000.0 ** (-np.arange(half, dtype=np.float32) / half)).astype(np.float32)
    pos = np.arange(S, dtype=np.float32)
    ang = (pos[:, None] * inv[None, :]).astype(np.float32)
    return np.cos(ang).T.astype(np.float32).copy(), np.sin(ang).T.astype(np.float32).copy()


def ret_rotary(P, C, qkT, qkr, T, S):
    P.mark('ret_rotary')
    nc = P.nc
    mark = P.sb_mark()
    if not hasattr(C, 'rot'):
        ct, sn = rot_tables(S)
        C.rot = (nc.inline_tensor(ct, "rot_cos").ap(), nc.inline_tensor(sn, "rot_sin").ap())
    cd, sd = C.rot
    cos = P.sb([128, S], F32, 'cos'); sin = P.sb([128, S], F32, 'sin')
    cosk = P.sb([128, S], F32, 'cosk'); sink = P.sb([128, S], F32, 'sink')
    tb = Buf()
    P.dma('sp', cos[:], cd[:, :], writes=[tb])
    P.dma('sp', sin[:], sd[:, :], writes=[tb])
    P.add('act', lambda e: e.mul(out=cosk[:], in_=cos[:], mul=1.0 / 16), [tb], [tb])
    P.add('act', lambda e: e.mul(out=sink[:], in_=sin[:], mul=1.0 / 16), [tb], [tb])
    NB = 2 if S <= 2048 else 1
    st = [dict(x1=P.sb([128, S], F32, 'r1'), x2=P.sb([128, S], F32, 'r2'), a=P.sb([128, S], F32, 'ra'), b=P.sb([128, S], F32, 'rb'),
               o1=P.sb([128, S], BF16, 'ro1'), o2=P.sb([128, S], BF16, 'ro2'), xb=Buf(), ab=Buf(), ob=Buf()) for _ in range(NB)]
    i = 0
    for q in range(T // S):
        tr = slice(q * S, (q + 1) * S)
        for hh in range(16):
            s = st[i % NB]; i += 1
            c_, s_ = (cos, sin) if hh < 8 else (cosk, sink)
            r0 = hh * 256
            x1, x2, a, b, o1, o2 = s['x1'], s['x2'], s['a'], s['b'], s['o1'], s['o2']
            P.dma('sp', x1[:], qkT[r0:r0 + 128, tr], writes=[s['xb']])
            P.dma('sp', x2[:], qkT[r0 + 128:r0 + 256, tr], writes=[s['xb']])
            P.add('dve', lambda e, x1=x1, a=a, c_=c_: e.tensor_tensor(out=a[:], in0=x1[:], in1=c_[:], op=ALU.mult), [s['xb'], tb], [s['ab']])
            P.add('pool', lambda e, x2=x2, b=b, s_=s_: e.tensor_tensor(out=b[:], in0=x2[:], in1=s_[:], op=ALU.mult), [s['xb'], tb], [s['ab']])
            P.add('dve', lambda e, a=a, b=b, o1=o1: e.tensor_tensor(out=o1[:], in0=a[:], in1=b[:], op=ALU.subtract), [s['ab']], [s['ob']])
            P.add('pool', lambda e, x1=x1, a=a, s_=s_: e.tensor_tensor(out=a[:], in0=x1[:], in1=s_[:], op=ALU.mult), [s['xb'], tb, s['ab']], [s['ab']])
            P.add('dve', lambda e, x2=x2, b=b, c_=c_: e.tensor_tensor(out=b[:], in0=x2[:], in1=c_[:], op=ALU.mult), [s['xb'], tb, s['ab']], [s['ab']])
            P.add('pool', lambda e, a=a, b=b, o2=o2: e.tensor_tensor(out=o2[:], in0=a[:], in1=b[:], op=ALU.add), [s['ab']], [s['ob']])
            P.dma('pool', qkr[r0:r0 + 128, tr], o1[:], reads=[s['ob']])
            P.dma('pool', qkr[r0 + 128:r0 + 256, tr], o2[:], reads=[s['ob']])
    P.sb_reset(mark)


def ret_attn(P, C, qkr, vg, gn_g, gn_b, ytok, T, S):
    P.mark('ret_attn')
    nc = P.nc
    mark = P.sb_mark()
    NQB = S // 512
    NKC = S // 128
    lg = [float(np.log(np.float32(1.0) - np.float32(2.0) ** np.float32(-5.0 - h))) for h in range(8)]
    d0 = P.sb([128, 512], F32, 'd0'); d0b = Buf()
    P.add('pool', lambda e: e.iota(d0[:], [[1, 512]], base=0, channel_multiplier=-1, allow_small_or_imprecise_dtypes=True), [], [d0b])
    deltas = sorted({qb * 512 - kc * 128 for qb in range(NQB) for kc in range(NKC)})
    gam = {dl: P.sb([128, 512], BF16, 'gam') for dl in deltas}
    gamb = Buf()
    gtmp = [(P.sb([128, 512], F32, 'gt'), Buf()) for _ in range(2)]
    gg = P.sb([128, 4096], F32, 'gng'); gbt = P.sb([128, 4096], F32, 'gnb'); ggb = Buf()
    P.dma('sp', gg[:], bass.AP(gn_g.tensor, gn_g.offset, [[0, 128], [1, 4096]]), writes=[ggb])
    P.dma('sp', gbt[:], bass.AP(gn_b.tensor, gn_b.offset, [[0, 128], [1, 4096]]), writes=[ggb])
    qt = P.sb([128, 2, S], BF16, 'qt'); kt = P.sb([128, 2, S], BF16, 'kt'); qkb = Buf()
    vt = P.sb([128, NKC, 512], BF16, 'vt'); vb = Buf()
    vst = [(P.sb([128, 512], F32, 'vs'), Buf()) for _ in range(2)]
    pt = [(P.sb([128, 512], BF16, 'pt'), Buf()) for _ in range(3)]
    ep = [dict(o=P.sb([128, 512], F32, 'eo'), g=P.sb([128, 512], F32, 'eg'), st=P.sb([128, 6], F32, 'es'), mv=P.sb([128, 2], F32, 'em'),
               r=P.sb([128, 1], F32, 'er'), n=P.sb([128, 1], F32, 'en'), ob=Buf(), gb=Buf(), sb=Buf()) for _ in range(2)]
    obank = [(C.ps[i], C.psb[i]) for i in range(4)]
    sbank = [(C.ps[4 + i], C.psb[4 + i]) for i in range(3)]
    cnt = 0
    ec = 0
    for h in range(8):
        for gi, dl in enumerate(deltas):
            t, tbuf = gtmp[gi % 2]
            P.add('act', lambda e, t=t, dl=dl: e.activation(out=t[:], in_=d0[:], func=AF.Abs, bias=float(dl), scale=1.0), [d0b], [tbuf])
            P.add('act', lambda e, t=t, dl=dl, h=h: e.activation(out=gam[dl][:], in_=t[:], func=AF.Exp, scale=lg[h]), [tbuf], [gamb])
        for q in range(T // S):
            tb0 = q * S
            P.dma('sp', qt[:], qkr[h * 256:(h + 1) * 256, tb0:tb0 + S].rearrange("(c p) t -> p c t", p=128), writes=[qkb])
            P.dma('sp', kt[:], qkr[2048 + h * 256:2048 + (h + 1) * 256, tb0:tb0 + S].rearrange("(c p) t -> p c t", p=128), writes=[qkb])
            for kc in range(NKC):
                a, ab = vst[kc % 2]
                P.dma('sp', a[:], vg[tb0 + kc * 128:tb0 + (kc + 1) * 128, h * 512:(h + 1) * 512], writes=[ab])
                P.add('pool', lambda e, a=a, kc=kc: e.tensor_copy(out=vt[:, kc, :], in_=a[:]), [ab], [vb])
            for qb in range(NQB):
                def emit_s(kc, cnt):
                    sp_, spb = sbank[cnt % 3]
                    p_, pb = pt[cnt % 3]
                    for c in range(2):
                        P.add('pe', lambda e, sp_=sp_, c=c, kc=kc, qb=qb: e.matmul(sp_[:, :], kt[:, c, kc * 128:(kc + 1) * 128], qt[:, c, qb * 512:(qb + 1) * 512],
                                                                                  start=(c == 0), stop=(c == 1)), [qkb], [spb])
                    dl = qb * 512 - kc * 128
                    P.add('dve', lambda e, sp_=sp_, p_=p_, dl=dl: e.tensor_tensor(out=p_[:], in0=sp_[:, :], in1=gam[dl][:], op=ALU.mult), [spb, gamb], [pb])

                def emit_pv(kc, cnt):
                    p_, pb = pt[cnt % 3]
                    for sub in range(4):
                        ob_, obb = obank[sub]
                        P.add('pe', lambda e, ob_=ob_, p_=p_, sub=sub, kc=kc: e.matmul(ob_[:, :], p_[:, sub * 128:(sub + 1) * 128], vt[:, kc, :],
                                                                                     start=(kc == 0), stop=(kc == NKC - 1)), [pb, vb], [obb])
                emit_s(0, cnt)
                emit_s(1, cnt + 1)
                for kc in range(NKC):
                    if kc + 2 < NKC:
                        emit_s(kc + 2, cnt + 2)
                    emit_pv(kc, cnt)
                    cnt += 1
                for sub in range(4):
                    ob_, obb = obank[sub]
                    s = ep[ec % 2]; ec += 1
                    r = slice(tb0 + qb * 512 + sub * 128, tb0 + qb * 512 + (sub + 1) * 128)
                    o, g = s['o'], s['g']
                    P.dma('sp', g[:], vg[r, 4096 + h * 512:4096 + (h + 1) * 512], writes=[s['gb']])
                    P.add('act', lambda e, g=g: e.activation(out=g[:], in_=g[:], func=AF.Silu), [s['gb']], [s['gb']])
                    P.add('dve', lambda e, s=s, ob_=ob_: e.bn_stats(out=s['st'][:], in_=ob_[:, :]), [obb], [s['sb']])
                    P.add('dve', lambda e, s=s: e.bn_aggr(out=s['mv'][:], in_=s['st'][:]), [s['sb']], [s['sb']])
                    P.add('dve', lambda e, s=s: e.tensor_scalar(out=s['r'][:], in0=s['mv'][:, 1:2], scalar1=LN_EPS, scalar2=None, op0=ALU.add), [s['sb']], [s['sb']])
                    P.add('act', lambda e, s=s: e.activation(out=s['r'][:], in_=s['r'][:], func=AF.Sqrt), [s['sb']], [s['sb']])
                    P.add('dve', lambda e, s=s: e.reciprocal(out=s['r'][:], in_=s['r'][:]), [s['sb']], [s['sb']])
                    P.add('dve', lambda e, s=s: e.scalar_tensor_tensor(out=s['n'][:], in0=s['mv'][:, 0:1], scalar=-1.0, in1=s['r'][:], op0=ALU.mult, op1=ALU.mult), [s['sb']], [s['sb']])
                    P.add('act', lambda e, s=s, o=o, ob_=ob_: e.activation(out=o[:], in_=ob_[:, :], func=AF.Identity, bias=s['n'][:], scale=s['r'][:]), [obb, s['sb']], [s['ob']])
                    P.add('pool', lambda e, o=o, h=h: e.tensor_tensor(out=o[:], in0=o[:], in1=gg[:, h * 512:(h + 1) * 512], op=ALU.mult), [s['ob'], ggb], [s['ob']])
                    P.add('pool', lambda e, o=o, h=h: e.tensor_tensor(out=o[:], in0=o[:], in1=gbt[:, h * 512:(h + 1) * 512], op=ALU.add), [s['ob'], ggb], [s['ob']])
                    P.add('dve', lambda e, o=o, g=g: e.tensor_tensor(out=o[:], in0=o[:], in1=g[:], op=ALU.mult), [s['ob'], s['gb']], [s['ob']])
                    P.dma('pool', ytok[r, h * 512:(h + 1) * 512], o[:], reads=[s['ob']])
    P.sb_reset(mark)


NEG = -30000.0


def na_bias_build(P, C, nc, rpb, scr_fn, e):
    P.mark('na_bias_build')
    mark = P.sb_mark()
    biasS = scr_fn(f"na_bias{e}", [240, 4096], F32)
    G = P.sb([31, 240], F32, 'nG'); Gb = Buf()
    for h_ in range(16):
        P.dma('sp', G[:, h_ * 15:(h_ + 1) * 15], rpb[h_].rearrange("r d -> d r"), writes=[Gb], allow_slow_non_contiguous=True)
    M = P.sb([31, 4096], F32, 'nM'); Mb = Buf()
    P.add('pool', lambda e_: e_.iota(M[:].rearrange("p (a b) -> p a b", b=64), [[1, 64], [-1, 64]], base=15, channel_multiplier=-1,
                                     allow_small_or_imprecise_dtypes=True), [], [Mb])
    P.add('dve', lambda e_: e_.tensor_single_scalar(out=M[:], in_=M[:], scalar=0.0, op=ALU.is_equal), [Mb], [Mb])
    A = P.sb([120, 4096], F32, 'nA'); Q = P.sb([120, 4096], F32, 'nQ'); nb = Buf()
    P.add('pool', lambda e_: e_.iota(A[:].rearrange("p (a b) -> p a b", b=64), [[1, 64], [0, 64]], base=0, channel_multiplier=0,
                                     allow_small_or_imprecise_dtypes=True), [], [nb])
    P.add('pool', lambda e_: e_.iota(Q[:].rearrange("p (a b) -> p a b", b=64), [[0, 64], [1, 64]], base=0, channel_multiplier=0,
                                     allow_small_or_imprecise_dtypes=True), [nb], [nb])
    for (s1, op) in ((-8.0, ALU.add), (0.0, ALU.max), (48.0, ALU.min)):
        P.add('dve', lambda e_, s1=s1, op=op: e_.tensor_single_scalar(out=Q[:], in_=Q[:], scalar=s1, op=op), [nb], [nb])
    P.add('dve', lambda e_: e_.tensor_tensor(out=A[:], in0=A[:], in1=Q[:], op=ALU.subtract), [nb], [nb])
    P.add('dve', lambda e_: e_.tensor_single_scalar(out=Q[:], in_=A[:], scalar=0.0, op=ALU.is_ge), [nb], [nb])
    P.add('dve', lambda e_: e_.tensor_single_scalar(out=A[:], in_=A[:], scalar=15.0, op=ALU.is_le), [nb], [nb])
    P.add('dve', lambda e_: e_.tensor_tensor(out=A[:], in0=A[:], in1=Q[:], op=ALU.mult), [nb], [nb])
    P.add('dve', lambda e_: e_.tensor_single_scalar(out=A[:], in_=A[:], scalar=-1.0, op=ALU.add), [nb], [nb])
    P.add('dve', lambda e_: e_.tensor_single_scalar(out=A[:], in_=A[:], scalar=-NEG, op=ALU.mult), [nb], [nb])
    st = [(P.sb([120, 512], F32, 'nbs'), Buf()) for _ in range(2)]
    i = 0
    for half in range(2):
        for cb in range(8):
            ps, psb = next_ps(C)
            P.add('pe', lambda e_, ps=ps, half=half, cb=cb: e_.matmul(ps[0:120, :], G[:, half * 120:(half + 1) * 120], M[:, cb * 512:(cb + 1) * 512],
                                                                     start=True, stop=True), [Gb, Mb], [psb])
            a, ab = st[i % 2]; i += 1
            P.add('dve', lambda e_, ps=ps, a=a, cb=cb: e_.tensor_tensor(out=a[:], in0=ps[0:120, :], in1=A[:, cb * 512:(cb + 1) * 512], op=ALU.add), [psb, nb], [ab])
            P.dma('pool', biasS[half * 120:(half + 1) * 120, cb * 512:(cb + 1) * 512], a[:], reads=[ab])
    P.sb_reset(mark)
    return biasS


def na_attn(P, C, nc, biasS, hqk, hv, yab, T, S, stage=3):
    P.mark('na_attn')
    mark = P.sb_mark()
    rows = S // 64
    NCH = S // 128
    assert rows >= 8
    B2 = P.sb([128, 16, 14, 64], F32, 'nB2'); B2b = Buf()
    bv = biasS.rearrange("(h r) (k q) -> k h r q", r=15, q=64)
    for i2 in range(2):
        for h in range(16):
            P.dma('sp', B2[i2 * 64:(i2 + 1) * 64, h, :, :], bv[:, h, i2:i2 + 14, :], writes=[B2b])
    stg = P.sb([64, 2, S], F32, 'nstg'); stgb = Buf()
    qTb = P.sb([64, 2, S], BF16, 'nq'); kTb = P.sb([64, 2, S], BF16, 'nk'); qkb = Buf()
    vstg = P.sb([128, NCH, 128], F32, 'nvs'); vsb = Buf()
    Vt = P.sb([128, NCH, 2, 80], BF16, 'nV'); Vt2 = P.sb([128, NCH, 2, 80], BF16, 'nV2'); Vb = Buf()
    P.add('pool', lambda e_: e_.memset(Vt[:], 1.0), [], [Vb])
    P.add('pool', lambda e_: e_.memset(Vt2[:], 1.0), [], [Vb])
    sc = [(P.sb([128, 512], F32, 'nsc'), Buf()) for _ in range(3)]
    pT = [(P.sb([128, 512], BF16, 'npT'), Buf()) for _ in range(3)]
    yr = [(P.sb([64, 128], F32, 'nyr'), P.sb([64, 2], F32, 'nrc'), Buf()) for _ in range(2)]
    sbank = [(C.ps[0], C.psb[0]), (C.ps[1], C.psb[1]), (C.ps[4], C.psb[4])]
    obank = [(C.ps[2], C.psb[2]), (C.ps[3], C.psb[3])]
    it = 0
    for q in range(T // S):
        t0 = q * S
        for hp in range(8):
            P.dma('sp', stg[:], hqk[hp * 128:(hp + 1) * 128, t0:t0 + S].rearrange("(a d) t -> d a t", d=64), writes=[stgb])
            P.add('act', lambda e_: e_.mul(out=qTb[:], in_=stg[:], mul=0.125), [stgb], [qkb])
            P.dma('sp', stg[:], hqk[1024 + hp * 128:1024 + (hp + 1) * 128, t0:t0 + S].rearrange("(a d) t -> d a t", d=64), writes=[stgb])
            P.add('dve', lambda e_: e_.tensor_copy(out=kTb[:], in_=stg[:]), [stgb], [qkb])
            P.dma('sp', vstg[:], hv[t0:t0 + S, hp * 128:(hp + 1) * 128].rearrange("(c p) f -> p c f", p=128), writes=[vsb])
            P.add('pool', lambda e_: e_.tensor_copy(out=Vt[:, :, :, 0:64], in_=vstg[:].rearrange("p c (a d) -> p c a d", d=64)), [vsb], [Vb])
            P.dma('sp', vstg[:, 0:NCH - 1, :], hv[t0 + 64:t0 + S - 64, hp * 128:(hp + 1) * 128].rearrange("(c p) f -> p c f", p=128), writes=[vsb])
            P.add('pool', lambda e_: e_.tensor_copy(out=Vt2[:, 0:NCH - 1, :, 0:64], in_=vstg[:, 0:NCH - 1, :].rearrange("p c (a d) -> p c a d", d=64)), [vsb], [Vb])
            rlist = list(range(rows))

            def part_a(r, it):
                rs = min(max(r - 4, 0), rows - 8)
                ro0 = rs - r + 7
                sp_, spb = sbank[it % 3]
                s_, sb_ = sc[it % 3]
                p_, pb = pT[it % 3]
                for hh in range(2):
                    for c in range(4):
                        k0 = rs * 64 + c * 128
                        P.add('pe', lambda e_, sp_=sp_, hh=hh, c=c, k0=k0, r=r: e_.matmul(
                            sp_[:, (hh * 4 + c) * 64:(hh * 4 + c + 1) * 64], kTb[:, hh, k0:k0 + 128], qTb[:, hh, r * 64:(r + 1) * 64],
                            start=True, stop=True), [qkb], [spb])
                for hh in range(2):
                    h = 2 * hp + hh
                    P.add('dve', lambda e_, sp_=sp_, s_=s_, hh=hh, h=h, ro0=ro0: e_.tensor_tensor(
                        out=s_[:, hh * 256:(hh + 1) * 256].rearrange("p (c q) -> p c q", q=64),
                        in0=sp_[:, hh * 256:(hh + 1) * 256].rearrange("p (c q) -> p c q", q=64),
                        in1=B2[:, h, ro0:ro0 + 7:2, :], op=ALU.add), [spb, B2b], [sb_])
                P.add('act', lambda e_, s_=s_, p_=p_: e_.activation(out=p_[:], in_=s_[:], func=AF.Exp), [sb_], [pb])

            def part_b(r, it):
                rs = min(max(r - 4, 0), rows - 8)
                p_, pb = pT[it % 3]
                ob_, obb = obank[it % 2]
                y_, rc_, yb_ = yr[it % 2]
                for hh in range(2):
                    for c in range(4):
                        if rs % 2 == 0:
                            vap = Vt[:, rs // 2 + c, hh, 0:65]
                        else:
                            vap = Vt2[:, (rs - 1) // 2 + c, hh, 0:65]
                        P.add('pe', lambda e_, ob_=ob_, p_=p_, hh=hh, c=c, vap=vap: e_.matmul(
                            ob_[0:64, hh * 128:hh * 128 + 65], p_[:, (hh * 4 + c) * 64:(hh * 4 + c + 1) * 64], vap, start=(c == 0), stop=(c == 3)), [pb, Vb], [obb])
                for hh in range(2):
                    P.add('dve', lambda e_, ob_=ob_, rc_=rc_, hh=hh: e_.reciprocal(out=rc_[:, hh:hh + 1], in_=ob_[0:64, hh * 128 + 64:hh * 128 + 65]), [obb], [yb_])
                    P.add('dve', lambda e_, ob_=ob_, rc_=rc_, y_=y_, hh=hh: e_.tensor_scalar(
                        out=y_[:, hh * 64:(hh + 1) * 64], in0=ob_[0:64, hh * 128:hh * 128 + 64], scalar1=rc_[:, hh:hh + 1], scalar2=None, op0=ALU.mult), [obb, yb_], [yb_])
                P.dma('pool', yab[t0 + r * 64:t0 + (r + 1) * 64, 1024 + hp * 128:1024 + (hp + 1) * 128], y_[:], reads=[yb_])
            part_a(0, it)
            part_a(1, it + 1)
            for r in rlist:
                if r + 2 < rows:
                    part_a(r + 2, it + 2)
                part_b(r, it)
                it += 1
    P.sb_reset(mark)


A_GN_EPS = 64e-5
EHALF = float(np.exp(-0.5))


def bc_ap(t, ncols_src, n1, n2):
    return bass.AP(t, 0, [[ncols_src, t.shape[0]], [1, n1], [0, n2]])


def load_bc(P, row_ap, n, name):
    t = P.sb([128, n], F32, name); b = Buf(name)
    P.dma('sp', t[:], bass.AP(row_ap.tensor, row_ap.offset, [[0, 128], [1, n]]), writes=[b])
    return t, b


def rwkv_prep(P, C, nc, W, e, hA, PR, T, S):
    P.mark('rwkv_prep')
    mark = P.sb_mark()
    sh = [load_bc(P, W['ab_shift'][e, k], 3328, f'sh{k}') for k in range(3)]
    w0 = [load_bc(P, W['ab_w0'][e, d], 1024, f'w0{d}') for d in range(2)]
    a0 = [load_bc(P, W['ab_a0'][e, d], 1024, f'a0{d}') for d in range(2)]
    kk_ = load_bc(P, W['ab_k_k'][e], 1024, 'kk')
    ka_ = load_bc(P, W['ab_k_a'][e], 1024, 'ka')
    rk_ = load_bc(P, W['ab_r_k'][e].rearrange("h d -> (h d)"), 1024, 'rk')
    wst = P.sb([128, 2048], F32, 'wst'); wsb = Buf()
    wup = P.sb([64, 2, 1024], BF16, 'wup'); aup = P.sb([64, 2, 1024], BF16, 'aup'); gup = P.sb([128, 1024], BF16, 'gup'); lwb = Buf()
    P.dma('sp', wst[0:64, :].rearrange("p (d n) -> p d n", d=2), W['ab_w_up'][e].rearrange("d k n -> k d n"), writes=[wsb])
    P.add('dve', lambda e_: e_.tensor_copy(out=wup[:].rearrange("p d n -> p (d n)"), in_=wst[0:64, :]), [wsb], [lwb])
    P.dma('sp', wst[0:64, :].rearrange("p (d n) -> p d n", d=2), W['ab_a_up'][e].rearrange("d k n -> k d n"), writes=[wsb])
    P.add('dve', lambda e_: e_.tensor_copy(out=aup[:].rearrange("p d n -> p (d n)"), in_=wst[0:64, :]), [wsb], [lwb])
    P.dma('sp', wst[:, 0:1024], W['ab_g_up'][e], writes=[wsb])
    P.add('dve', lambda e_: e_.tensor_copy(out=gup[:], in_=wst[:, 0:1024]), [wsb], [lwb])
    hp = P.sb([128, 3328], F32, 'hp'); hc = P.sb([128, 3328], F32, 'hc'); hn = P.sb([128, 3328], F32, 'hn')
    hpb, hcb, hnb = Buf(), Buf(), Buf()
    L = P.sb([128, 256], F32, 'L'); Lb = Buf()
    LT = P.sb([128, 3, 128], BF16, 'LT'); LTb = Buf()
    t1 = P.sb([128, 1024], F32, 't1'); t1b = Buf()
    t2 = P.sb([128, 1024], F32, 't2'); t2b = Buf()
    kap = P.sb([128, 1024], F32, 'kap'); kapb = Buf()
    ad = P.sb([128, 1024], F32, 'ad'); adb = Buf()
    o1 = [(P.sb([128, 1024], F32, 'o1'), Buf()) for _ in range(3)]
    sm = P.sb([128, 16], F32, 'sm'); smb = Buf()
    bon = P.sb([128, 16], F32, 'bon'); bonb = Buf()
    oi = [0]

    def outbuf():
        o = o1[oi[0] % 3]; oi[0] += 1
        return o
    for ti in range(T // 128):
        t0 = ti * 128
        r = slice(t0, t0 + 128)
        first = (t0 % S == 0)
        lastt = ((t0 + 128) % S == 0)
        P.dma('sp', hc[:], hA[r, :], writes=[hcb])
        if first:
            P.add('pool', lambda e_: e_.memset(hp[0:1, :], 0.0), [], [hpb])
            P.dma('sp', hp[1:128, :], hA[t0:t0 + 127, :], writes=[hpb])
        else:
            P.dma('sp', hp[:], hA[t0 - 1:t0 + 127, :], writes=[hpb])
        if lastt:
            P.add('pool', lambda e_: e_.memset(hn[:], 0.0), [], [hnb])
            P.dma('sp', hn[0:127, :], hA[t0 + 1:t0 + 128, :], writes=[hnb])
        else:
            P.dma('sp', hn[:], hA[t0 + 1:t0 + 129, :], writes=[hnb])
        P.add('dve', lambda e_: e_.tensor_tensor(out=hc[:], in0=hc[:], in1=sh[1][0][:], op=ALU.mult), [hcb, sh[1][1]], [hcb])
        P.add('pool', lambda e_: e_.tensor_tensor(out=hp[:], in0=hp[:], in1=sh[0][0][:], op=ALU.mult), [hpb, sh[0][1]], [hpb])
        P.add('pool', lambda e_: e_.tensor_tensor(out=hn[:], in0=hn[:], in1=sh[2][0][:], op=ALU.mult), [hnb, sh[2][1]], [hnb])
        P.add('dve', lambda e_: e_.tensor_tensor(out=hc[:], in0=hc[:], in1=hp[:], op=ALU.add), [hcb, hpb], [hcb])
        P.add('dve', lambda e_: e_.tensor_tensor(out=hc[:], in0=hc[:], in1=hn[:], op=ALU.add), [hcb, hnb], [hcb])
        R_ = hc[:, 0:1024]; K_ = hc[:, 1024:2048]; V_ = hc[:, 2048:3072]
        P.dma('pool', PR['R'][r, :], R_, reads=[hcb])
        P.dma('pool', PR['V'][r, :], V_, reads=[hcb])
        P.add('act', lambda e_: e_.activation(out=L[:, 0:64], in_=hc[:, 3072:3136], func=AF.Tanh), [hcb], [Lb])
        P.add('act', lambda e_: e_.activation(out=L[:, 64:128], in_=hc[:, 3136:3200], func=AF.Identity), [hcb], [Lb])
        P.add('act', lambda e_: e_.activation(out=L[:, 128:256], in_=hc[:, 3200:3328], func=AF.Sigmoid), [hcb], [Lb])
        ps, psb = next_ps(C)
        P.add('pe', lambda e_, ps=ps: e_.transpose(ps[0:64, 0:128], L[:, 0:64], C.ident[:]), [Lb, C.b_ident], [psb])
        P.add('pe', lambda e_, ps=ps: e_.transpose(ps[0:64, 128:256], L[:, 64:128], C.ident[:]), [Lb, C.b_ident], [psb])
        P.add('pe', lambda e_, ps=ps: e_.transpose(ps[:, 256:384], L[:, 128:256], C.ident[:]), [Lb, C.b_ident], [psb])
        P.add('dve', lambda e_, ps=ps: e_.tensor_copy(out=LT[0:64, 0:2, :], in_=ps[0:64, 0:256].rearrange("p (a t) -> p a t", t=128)), [psb], [LTb])
        P.add('dve', lambda e_, ps=ps: e_.tensor_copy(out=LT[:, 2, :], in_=ps[:, 256:384]), [psb], [LTb])
        P.add('pool', lambda e_: e_.tensor_tensor(out=kap[:], in0=K_, in1=kk_[0][:], op=ALU.mult), [hcb, kk_[1]], [kapb])
        P.add('dve', lambda e_: e_.tensor_tensor(out=t1[:], in0=kap[:], in1=kap[:], op=ALU.mult), [kapb], [t1b])
        P.add('dve', lambda e_: e_.tensor_reduce(out=sm[:], in_=t1[:].rearrange("p (h d) -> p h d", d=64), axis=AX.X, op=ALU.add), [t1b], [smb])
        P.add('act', lambda e_: e_.activation(out=sm[:], in_=sm[:], func=AF.Sqrt), [smb], [smb])
        P.add('dve', lambda e_: e_.tensor_single_scalar(out=sm[:], in_=sm[:], scalar=1e-12, op=ALU.max), [smb], [smb])
        P.add('dve', lambda e_: e_.reciprocal(out=sm[:], in_=sm[:]), [smb], [smb])
        P.add('dve', lambda e_: e_.tensor_tensor(out=kap[:].rearrange("p (h d) -> p h d", d=64), in0=kap[:].rearrange("p (h d) -> p h d", d=64),
                                                 in1=bc_ap(sm, 16, 16, 64), op=ALU.mult), [kapb, smb], [kapb])
        P.dma('pool', PR['KAP'][r, :], kap[:], reads=[kapb])
        o, ob = outbuf()
        for hf in range(2):
            ps, psb = next_ps(C)
            P.add('pe', lambda e_, ps=ps, hf=hf: e_.matmul(ps[:, :], LT[:, 2, :], gup[:, hf * 512:(hf + 1) * 512], start=True, stop=True), [LTb, lwb], [psb])
            P.add('act', lambda e_, ps=ps, hf=hf, o=o: e_.activation(out=o[:, hf * 512:(hf + 1) * 512], in_=ps[:, :], func=AF.Identity), [psb], [ob])
        P.dma('pool', PR['GATE'][r, :], o[:], reads=[ob])
        for d in range(2):
            o, ob = outbuf()
            for hf in range(2):
                ps, psb = next_ps(C)
                P.add('pe', lambda e_, ps=ps, hf=hf, d=d: e_.matmul(ps[:, :], LT[0:64, 0, :], wup[:, d, hf * 512:(hf + 1) * 512], start=True, stop=True), [LTb, lwb], [psb])
                P.add('dve', lambda e_, ps=ps, hf=hf, d=d: e_.tensor_tensor(out=t1[:, hf * 512:(hf + 1) * 512], in0=ps[:, :], in1=w0[d][0][:, hf * 512:(hf + 1) * 512], op=ALU.add), [psb, w0[d][1]], [t1b])
            P.add('act', lambda e_: e_.activation(out=t1[:], in_=t1[:], func=AF.Sigmoid), [t1b], [t1b])
            P.add('act', lambda e_, o=o: e_.mul(out=o[:], in_=t1[:], mul=-EHALF), [t1b], [ob])
            P.dma('pool', PR[f'LW{d}'][r, :], o[:], reads=[ob])
            for hf in range(2):
                ps, psb = next_ps(C)
                P.add('pe', lambda e_, ps=ps, hf=hf, d=d: e_.matmul(ps[:, :], LT[0:64, 1, :], aup[:, d, hf * 512:(hf + 1) * 512], start=True, stop=True), [LTb, lwb], [psb])
                P.add('dve', lambda e_, ps=ps, hf=hf, d=d: e_.tensor_tensor(out=ad[:, hf * 512:(hf + 1) * 512], in0=ps[:, :], in1=a0[d][0][:, hf * 512:(hf + 1) * 512], op=ALU.add), [psb, a0[d][1]], [adb])
            P.add('act', lambda e_: e_.activation(out=ad[:], in_=ad[:], func=AF.Sigmoid), [adb], [adb])
            o, ob = outbuf()
            P.add('pool', lambda e_, o=o: e_.tensor_tensor(out=o[:], in0=kap[:], in1=ad[:], op=ALU.mult), [kapb, adb], [ob])
            P.dma('pool', PR[f'B{d}'][r, :], o[:], reads=[ob])
            P.add('dve', lambda e_: e_.scalar_tensor_tensor(out=t2[:], in0=ad[:], scalar=-1.0, in1=ka_[0][:], op0=ALU.add, op1=ALU.mult), [adb, ka_[1]], [t2b])
            o, ob = outbuf()
            P.add('dve', lambda e_, o=o: e_.scalar_tensor_tensor(out=o[:], in0=t2[:], scalar=1.0, in1=K_, op0=ALU.add, op1=ALU.mult), [t2b, hcb], [ob])
            P.dma('pool', PR[f'KD{d}'][r, :], o[:], reads=[ob])
            if d == 0:
                P.add('pool', lambda e_, o=o: e_.tensor_tensor(out=t2[:], in0=o[:], in1=R_, op=ALU.mult), [ob, hcb, t2b], [t2b])
                P.add('pool', lambda e_: e_.tensor_tensor(out=t2[:], in0=t2[:], in1=rk_[0][:], op=ALU.mult), [t2b, rk_[1]], [t2b])
                P.add('dve', lambda e_: e_.tensor_reduce(out=bon[:], in_=t2[:].rearrange("p (h d) -> p h d", d=64), axis=AX.X, op=ALU.add), [t2b], [bonb])
                P.dma('pool', PR['BON'][r, :], bon[:], reads=[bonb])
    P.sb_reset(mark)


def rwkv_consts(P, C, d):
    K = {}
    D = P.sb([128, 128], F32, 'cD'); Db = Buf()
    P.add('pool', lambda e_: e_.iota(D[:], [[1, 128]], base=0, channel_multiplier=-1, allow_small_or_imprecise_dtypes=True), [], [Db])
    PI = P.sb([128, 128], F32, 'cPI'); PIb = Buf()
    P.add('pool', lambda e_: e_.iota(PI[:], [[0, 128]], base=0, channel_multiplier=1, allow_small_or_imprecise_dtypes=True), [], [PIb])
    cb = Buf()

    def mk(name, src, srcb, scalar, op):
        t = P.sb([128, 128], F32, name)
        P.add('dve', lambda e_: e_.tensor_single_scalar(out=t[:], in_=src[:], scalar=scalar, op=op), [srcb], [cb])
        return t
    TRI = mk('cTRI', D, Db, 0.0, ALU.is_ge if d == 0 else ALU.is_le)
    TRIS = mk('cTRIS', D, Db, 0.0, ALU.is_gt if d == 0 else ALU.is_lt)
    HALF = mk('cHALF', PI, PIb, 63.5, ALU.is_lt if d == 0 else ALU.is_gt)
    A1 = P.sb([128, 128], F32, 'cA1'); A2 = P.sb([128, 128], F32, 'cA2'); A5 = P.sb([128, 128], F32, 'cA5')
    P.add('dve', lambda e_: e_.tensor_tensor(out=A1[:], in0=TRIS[:], in1=HALF[:], op=ALU.subtract), [cb], [cb])
    P.add('dve', lambda e_: e_.tensor_tensor(out=A2[:], in0=TRI[:], in1=HALF[:], op=ALU.subtract), [cb], [cb])
    P.add('dve', lambda e_: e_.tensor_scalar(out=A5[:], in0=TRI[:], scalar1=-1.0, scalar2=None, op0=ALU.mult), [cb], [cb])
    P.add('dve', lambda e_: e_.tensor_single_scalar(out=A5[:], in_=A5[:], scalar=1.0, op=ALU.add), [cb], [cb])
    K['A1'], K['A2'], K['A3'], K['A4'], K['A5'] = A1, A2, TRIS, TRI, A5
    ones = P.sb([128, 1], F32, 'cone')
    P.add('dve', lambda e_: e_.memset(ones[:], 1.0), [], [cb])
    K['ones'] = ones
    D4 = P.sb([128, 4, 128], F32, 'cD4'); D4b = Buf()
    P.add('pool', lambda e_: e_.iota(D4[:], [[0, 4], [1, 128]], base=0, channel_multiplier=-1, allow_small_or_imprecise_dtypes=True), [], [D4b])

    def mk4(name, op, neg=False):
        t = P.sb([128, 4, 128], F32, name)
        P.add('dve', lambda e_: e_.tensor_single_scalar(out=t[:], in_=D4[:], scalar=0.0, op=op), [D4b], [cb])
        if neg:
            P.add('dve', lambda e_: e_.tensor_single_scalar(out=t[:], in_=t[:], scalar=-1.0, op=ALU.mult), [cb], [cb])
        return t
    K['mT_strict'] = mk4('mTs', ALU.is_gt if d == 0 else ALU.is_lt)
    K['mT_M'] = mk4('mTm', ALU.is_ge, False) if d == 0 else K['mT_strict']
    K['mT_negstrict'] = mk4('mTn', ALU.is_gt if d == 0 else ALU.is_lt, True)
    K['mN_negstrict'] = mk4('mNn', ALU.is_lt if d == 0 else ALU.is_gt, True)
    K['b'] = cb
    return K


def rwkv_scan(P, C, nc, W, e, PR, d, YF, yab, T, S):
    P.mark('rwkv_scan')
    mark = P.sb_mark()
    K = rwkv_consts(P, C, d)
    cb = K['b']
    NCK = S // 128
    f32 = lambda n, nm: P.sb([128, n], F32, nm)
    inp = {k: (f32(1024, 'i' + k), Buf()) for k in ('R', 'V', 'KAP', 'LW', 'B', 'KD')}
    src = {'R': PR['R'], 'V': PR['V'], 'KAP': PR['KAP'], 'LW': PR[f'LW{d}'], 'B': PR[f'B{d}'], 'KD': PR[f'KD{d}']}
    E = [(f32(1024, 'E'), Buf()) for _ in range(3)]
    prod = {k: (f32(1024, 'p' + k), Buf()) for k in ('Kr', 'Br', 'Kdr', 'Rr', 'Ra')}
    tm = {k: (P.sb([128, 1024], BF16, 'b' + k), Buf()) for k in ('Bend', 'Kend', 'V')}
    tr = {k: (P.sb([64, 16, 128], BF16, 't' + k), Buf()) for k in ('Kr', 'Br', 'Kdr', 'Rr', 'Ra')}
    mat = {k: (P.sb([128, 16, 128], BF16, 'm' + k), Buf()) for k in ('P0', 'P0T', 'LkT', 'MrbT', 'MrkT', 'Pa', 'PaT', 'Pb', 'PbT')}
    X32 = P.sb([128, 16, 128], F32, 'X32'); X32b = Buf()
    Xbf = P.sb([128, 16, 128], BF16, 'Xbf'); Xbfb = Buf()
    WT = P.sb([64, 16, 128], BF16, 'WT'); WTb = Buf()
    Ubf = P.sb([128, 16, 64], BF16, 'Ubf'); Ubb = Buf()
    Z32 = P.sb([64, 16, 64], F32, 'Z32'); Zbf = P.sb([64, 16, 64], BF16, 'Zbf'); Zb = Buf()
    ztmp = P.sb([64, 16, 64], F32, 'ztmp'); ztb = Buf()
    cC = P.sb([64, 16], F32, 'cC'); cCb = Buf()
    Y = f32(1024, 'Y'); Yb = Buf()
    if d == 1:
        YFt = f32(1024, 'YFt'); YFb = Buf()
        gat = f32(1024, 'gat'); gatb = Buf()
        bon = P.sb([128, 16], F32, 'bon'); bonb = Buf()
        gng = load_bc(P, W['ab_gn_g'][e], 1024, 'gng'); gnb = load_bc(P, W['ab_gn_b'][e], 1024, 'gnb')
        s1 = P.sb([128, 16], F32, 's1'); s2 = P.sb([128, 16], F32, 's2'); stb = Buf()
        sq = f32(1024, 'sq'); sqb = Buf()
    evi = [0]

    def evac(out_ap, in_ap, reads, writes):
        if evi[0] % 2:
            P.add('act', lambda e_: e_.activation(out=out_ap, in_=in_ap, func=AF.Identity), reads, writes)
        else:
            P.add('dve', lambda e_: e_.tensor_copy(out=out_ap, in_=in_ap), reads, writes)
        evi[0] += 1

    h3 = lambda t: t[:].rearrange("p (h d) -> p h d", d=64)
    for q in range(T // S):
        P.add('pool', lambda e_: e_.memset(Z32[:], 0.0), [], [Zb])
        P.add('pool', lambda e_: e_.memset(Zbf[:], 0.0), [], [Zb])
        order = range(NCK) if d == 0 else range(NCK - 1, -1, -1)
        for ck in order:
            t0 = q * S + ck * 128
            r = slice(t0, t0 + 128)
            for k in inp:
                P.dma('sp', inp[k][0][:], src[k][r, :], writes=[inp[k][1]])
            LW, LWb = inp['LW']
            ei = [0]

            def expo(Akey, scale):
                t, tb = E[ei[0] % 3]; ei[0] += 1
                for hf in range(2):
                    ps, psb = next_ps(C)
                    P.add('pe', lambda e_, ps=ps, hf=hf: e_.matmul(ps[:, :], K[Akey][:], LW[:, hf * 512:(hf + 1) * 512], start=True, stop=True), [cb, LWb], [psb])
                    P.add('act', lambda e_, ps=ps, hf=hf, t=t: e_.activation(out=t[:, hf * 512:(hf + 1) * 512], in_=ps[:, :], func=AF.Exp, scale=scale), [psb], [tb])
                return t, tb

            def mul(out, outb, a, ab, b, bb, eng):
                P.add(eng, lambda e_: e_.tensor_tensor(out=out[:], in0=a[:], in1=b[:], op=ALU.mult), [ab, bb], [outb])
            E1, E1b = expo('A1', 1.0)
            mul(prod['Kr'][0], prod['Kr'][1], inp['KAP'][0], inp['KAP'][1], E1, E1b, 'dve')
            if d == 1:
                mul(prod['Rr'][0], prod['Rr'][1], inp['R'][0], inp['R'][1], E1, E1b, 'pool')
            E2, E2b = expo('A2', -1.0)
            mul(prod['Br'][0], prod['Br'][1], inp['B'][0], inp['B'][1], E2, E2b, 'dve')
            mul(prod['Kdr'][0], prod['Kdr'][1], inp['KD'][0], inp['KD'][1], E2, E2b, 'pool')
            if d == 0:
                E2p, E2pb = expo('A2', 1.0)
                mul(prod['Rr'][0], prod['Rr'][1], inp['R'][0], inp['R'][1], E2p, E2pb, 'dve')
            E3, E3b = expo('A3', 1.0)
            P.add('dve', lambda e_, E3=E3: e_.tensor_tensor(out=X32[:, :, 64:128], in0=h3(inp['KAP'][0]), in1=h3(E3), op=ALU.mult), [inp['KAP'][1], E3b], [X32b])
            P.add('pool', lambda e_: e_.tensor_copy(out=Xbf[:, :, 64:128], in_=X32[:, :, 64:128]), [X32b], [Xbfb])
            if d == 1:
                mul(prod['Ra'][0], prod['Ra'][1], inp['R'][0], inp['R'][1], E3, E3b, 'pool')
            else:
                E4, E4b = expo('A4', 1.0)
                mul(prod['Ra'][0], prod['Ra'][1], inp['R'][0], inp['R'][1], E4, E4b, 'pool')
            E5, E5b = expo('A5', 1.0)
            mul(tm['Bend'][0], tm['Bend'][1], inp['B'][0], inp['B'][1], E5, E5b, 'dve')
            mul(tm['Kend'][0], tm['Kend'][1], inp['KD'][0], inp['KD'][1], E5, E5b, 'pool')
            P.add('pool', lambda e_: e_.tensor_copy(out=tm['V'][0][:], in_=inp['V'][0][:]), [inp['V'][1]], [tm['V'][1]])
            ps, psb = next_ps(C)
            for h in range(16):
                P.add('pe', lambda e_, ps=ps, h=h: e_.matmul(ps[0:64, h:h + 1], LW[:, h * 64:(h + 1) * 64], K['ones'][:], start=True, stop=True), [LWb, cb], [psb])
            P.add('act', lambda e_, ps=ps: e_.activation(out=cC[:], in_=ps[0:64, 0:16], func=AF.Exp), [psb], [cCb])
            for k in ('Kr', 'Br', 'Kdr', 'Rr', 'Ra'):
                pa, pab = prod[k]
                ta, tab = tr[k]
                for g in range(4):
                    ps, psb = next_ps(C)
                    for hi in range(4):
                        h = 4 * g + hi
                        P.add('pe', lambda e_, ps=ps, hi=hi, h=h, pa=pa: e_.transpose(ps[0:64, hi * 128:(hi + 1) * 128], pa[:, h * 64:(h + 1) * 64], C.ident[:]),
                              [pab, C.b_ident], [psb])
                    evac(ta[:, 4 * g:4 * g + 4, :], ps[0:64, :].rearrange("p (a t) -> p a t", t=128), [psb], [tab])
            def score(dst, lk, rk_, mask):
                da, dab = mat[dst]
                for g in range(4):
                    ps, psb = next_ps(C)
                    for hi in range(4):
                        h = 4 * g + hi
                        P.add('pe', lambda e_, ps=ps, hi=hi, h=h: e_.matmul(ps[:, hi * 128:(hi + 1) * 128], tr[lk][0][:, h, :], tr[rk_][0][:, h, :], start=True, stop=True),
                              [tr[lk][1], tr[rk_][1]], [psb])
                    P.add('dve' if g % 2 else 'pool' if False else 'dve', lambda e_, ps=ps, g=g, da=da: e_.tensor_tensor(
                        out=da[:, 4 * g:4 * g + 4, :], in0=ps[:, :].rearrange("p (a t) -> p a t", t=128), in1=K[mask][:], op=ALU.mult), [psb, cb], [dab])
            score('P0', 'Kr', 'Br', 'mN_negstrict')
            score('P0T', 'Br', 'Kr', 'mT_negstrict')
            score('LkT', 'Kdr', 'Kr', 'mT_strict')
            score('MrbT', 'Br', 'Rr', 'mT_M')
            score('MrkT', 'Kdr', 'Rr', 'mT_M')
            for g in range(2):
                ps, psb = next_ps(C)
                for hi in range(8):
                    h = 8 * g + hi
                    P.add('pe', lambda e_, ps=ps, hi=hi, h=h: e_.matmul(ps[:, hi * 64:(hi + 1) * 64], mat['LkT'][0][:, h, :], tm['V'][0][:, h * 64:(h + 1) * 64], start=True, stop=True),
                          [mat['LkT'][1], tm['V'][1]], [psb])
                P.add('dve', lambda e_, ps=ps, g=g: e_.tensor_copy(out=X32[:, 8 * g:8 * g + 8, 0:64], in_=ps[:, :].rearrange("p (a i) -> p a i", i=64)), [psb], [X32b])
                P.add('act', lambda e_, ps=ps, g=g: e_.activation(out=Xbf[:, 8 * g:8 * g + 8, 0:64], in_=ps[:, :].rearrange("p (a i) -> p a i", i=64), func=AF.Identity), [psb], [Xbfb])
            cur, curT = 'P0', 'P0T'
            nxt = [('Pa', 'PaT'), ('Pb', 'PbT')]
            for lev in range(7):
                for g in range(4):
                    ps, psb = next_ps(C)
                    for hi in range(4):
                        h = 4 * g + hi
                        P.add('pe', lambda e_, ps=ps, hi=hi, h=h, curT=curT: e_.matmul(ps[:, hi * 128:(hi + 1) * 128], mat[curT][0][:, h, :], Xbf[:, h, :], start=True, stop=True),
                              [mat[curT][1], Xbfb], [psb])
                    P.add('dve', lambda e_, ps=ps, g=g: e_.tensor_tensor(out=X32[:, 4 * g:4 * g + 4, :], in0=X32[:, 4 * g:4 * g + 4, :],
                                                                        in1=ps[:, :].rearrange("p (a t) -> p a t", t=128), op=ALU.add), [psb, X32b], [X32b])
                for g in range(4):
                    P.add('act', lambda e_, g=g: e_.activation(out=Xbf[:, 4 * g:4 * g + 4, :], in_=X32[:, 4 * g:4 * g + 4, :], func=AF.Identity), [X32b], [Xbfb])
                if lev < 6:
                    n, nT = nxt[lev % 2]
                    for g in range(4):
                        psA, psAb = next_ps(C)
                        psB, psBb = next_ps(C)
                        for hi in range(4):
                            h = 4 * g + hi
                            P.add('pe', lambda e_, psA=psA, hi=hi, h=h, cur=cur, curT=curT: e_.matmul(psA[:, hi * 128:(hi + 1) * 128], mat[curT][0][:, h, :], mat[cur][0][:, h, :], start=True, stop=True),
                                  [mat[cur][1], mat[curT][1]], [psAb])
                            P.add('pe', lambda e_, psB=psB, hi=hi, h=h, cur=cur, curT=curT: e_.matmul(psB[:, hi * 128:(hi + 1) * 128], mat[cur][0][:, h, :], mat[curT][0][:, h, :], start=True, stop=True),
                                  [mat[cur][1], mat[curT][1]], [psBb])
                        evac(mat[n][0][:, 4 * g:4 * g + 4, :], psA[:, :].rearrange("p (a t) -> p a t", t=128), [psAb], [mat[n][1]])
                        evac(mat[nT][0][:, 4 * g:4 * g + 4, :], psB[:, :].rearrange("p (a t) -> p a t", t=128), [psBb], [mat[nT][1]])
                    cur, curT = n, nT
            for g in range(4):
                ps, psb = next_ps(C)
                for hi in range(4):
                    h = 4 * g + hi
                    P.add('pe', lambda e_, ps=ps, hi=hi, h=h: e_.transpose(ps[0:64, hi * 128:(hi + 1) * 128], X32[:, h, 64:128], C.ident[:]), [X32b, C.b_ident], [psb])
                evac(WT[:, 4 * g:4 * g + 4, :], ps[0:64, :].rearrange("p (a t) -> p a t", t=128), [psb], [WTb])
            for g in range(2):
                ps, psb = next_ps(C)
                for hi in range(8):
                    h = 8 * g + hi
                    P.add('pe', lambda e_, ps=ps, hi=hi, h=h: e_.matmul(ps[:, hi * 64:(hi + 1) * 64], WT[:, h, :], Zbf[:, h, :], start=True, stop=True), [WTb, Zb], [psb])
                P.add('dve', lambda e_, ps=ps, g=g: e_.scalar_tensor_tensor(out=Ubf[:, 8 * g:8 * g + 8, :], in0=ps[:, :].rearrange("p (a i) -> p a i", i=64), scalar=-1.0,
                                                                          in1=X32[:, 8 * g:8 * g + 8, 0:64], op0=ALU.mult, op1=ALU.subtract), [psb, X32b], [Ubb])
            for g in range(2):
                ps, psb = next_ps(C)
                for hi in range(8):
                    h = 8 * g + hi
                    o = ps[:, hi * 64:(hi + 1) * 64]
                    P.add('pe', lambda e_, o=o, h=h: e_.matmul(o, tr['Ra'][0][:, h, :], Zbf[:, h, :], start=True, stop=False), [tr['Ra'][1], Zb], [psb])
                    P.add('pe', lambda e_, o=o, h=h: e_.matmul(o, mat['MrbT'][0][:, h, :], Ubf[:, h, :], start=False, stop=False), [mat['MrbT'][1], Ubb], [psb])
                    P.add('pe', lambda e_, o=o, h=h: e_.matmul(o, mat['MrkT'][0][:, h, :], tm['V'][0][:, h * 64:(h + 1) * 64], start=False, stop=True), [mat['MrkT'][1], tm['V'][1]], [psb])
                evac(Y[:, g * 512:(g + 1) * 512], ps[:, :], [psb], [Yb])
            for g in range(2):
                ps, psb = next_ps(C)
                for hi in range(8):
                    h = 8 * g + hi
                    o = ps[0:64, hi * 64:(hi + 1) * 64]
                    P.add('pe', lambda e_, o=o, h=h: e_.matmul(o, tm['Bend'][0][:, h * 64:(h + 1) * 64], Ubf[:, h, :], start=True, stop=False), [tm['Bend'][1], Ubb], [psb])
                    P.add('pe', lambda e_, o=o, h=h: e_.matmul(o, tm['Kend'][0][:, h * 64:(h + 1) * 64], tm['V'][0][:, h * 64:(h + 1) * 64], start=False, stop=True), [tm['Kend'][1], tm['V'][1]], [psb])
                P.add('dve', lambda e_, g=g: e_.tensor_tensor(out=ztmp[:, 8 * g:8 * g + 8, :], in0=Z32[:, 8 * g:8 * g + 8, :],
                                                             in1=bass.AP(cC, 8 * g, [[16, 64], [1, 8], [0, 64]]), op=ALU.mult), [Zb, cCb], [ztb])
                P.add('dve', lambda e_, ps=ps, g=g: e_.tensor_tensor(out=Z32[:, 8 * g:8 * g + 8, :], in0=ztmp[:, 8 * g:8 * g + 8, :],
                                                                    in1=ps[0:64, :].rearrange("p (a i) -> p a i", i=64), op=ALU.add), [ztb, psb], [Zb])
            P.add('act', lambda e_: e_.activation(out=Zbf[:], in_=Z32[:], func=AF.Identity), [Zb], [Zb])
            if d == 0:
                P.dma('pool', YF[r, :], Y[:], reads=[Yb])
            else:
                P.dma('sp', YFt[:], YF[r, :], writes=[YFb])
                P.dma('sp', gat[:], PR['GATE'][r, :], writes=[gatb])
                P.dma('sp', bon[:], PR['BON'][r, :], writes=[bonb])
                P.add('dve', lambda e_: e_.tensor_tensor(out=Y[:], in0=Y[:], in1=YFt[:], op=ALU.add), [Yb, YFb], [Yb])
                P.add('dve', lambda e_: e_.tensor_reduce(out=s1[:], in_=h3(Y), axis=AX.X, op=ALU.add), [Yb], [stb])
                P.add('pool', lambda e_: e_.tensor_tensor(out=sq[:], in0=Y[:], in1=Y[:], op=ALU.mult), [Yb], [sqb])
                P.add('dve', lambda e_: e_.tensor_reduce(out=s2[:], in_=h3(sq), axis=AX.X, op=ALU.add), [sqb], [stb])
                P.add('dve', lambda e_: e_.tensor_single_scalar(out=s1[:], in_=s1[:], scalar=1.0 / 64, op=ALU.mult), [stb], [stb])
                P.add('dve', lambda e_: e_.tensor_single_scalar(out=s2[:], in_=s2[:], scalar=1.0 / 64, op=ALU.mult), [stb], [stb])
                P.add('dve', lambda e_: e_.tensor_tensor(out=sq[:, 0:16], in0=s1[:], in1=s1[:], op=ALU.mult), [stb, sqb], [sqb])
                P.add('dve', lambda e_: e_.tensor_tensor(out=s2[:], in0=s2[:], in1=sq[:, 0:16], op=ALU.subtract), [stb, sqb], [stb])
                P.add('dve', lambda e_: e_.tensor_single_scalar(out=s2[:], in_=s2[:], scalar=A_GN_EPS, op=ALU.add), [stb], [stb])
                P.add('act', lambda e_: e_.activation(out=s2[:], in_=s2[:], func=AF.Sqrt), [stb], [stb])
                P.add('dve', lambda e_: e_.reciprocal(out=s2[:], in_=s2[:]), [stb], [stb])
                P.add('dve', lambda e_: e_.tensor_tensor(out=h3(Y), in0=h3(Y), in1=bc_ap(s1, 16, 16, 64), op=ALU.subtract), [Yb, stb], [Yb])
                P.add('dve', lambda e_: e_.tensor_tensor(out=h3(Y), in0=h3(Y), in1=bc_ap(s2, 16, 16, 64), op=ALU.mult), [Yb, stb], [Yb])
                P.add('pool', lambda e_: e_.tensor_tensor(out=Y[:], in0=Y[:], in1=gng[0][:], op=ALU.mult), [Yb, gng[1]], [Yb])
                P.add('pool', lambda e_: e_.tensor_tensor(out=Y[:], in0=Y[:], in1=gnb[0][:], op=ALU.add), [Yb, gnb[1]], [Yb])
                P.add('dve', lambda e_: e_.tensor_tensor(out=h3(sq), in0=h3(inp['V'][0]), in1=bc_ap(bon, 16, 16, 64), op=ALU.mult), [inp['V'][1], bonb, sqb], [sqb])
                P.add('dve', lambda e_: e_.tensor_tensor(out=Y[:], in0=Y[:], in1=sq[:], op=ALU.add), [Yb, sqb], [Yb])
                P.add('dve', lambda e_: e_.tensor_tensor(out=Y[:], in0=Y[:], in1=gat[:], op=ALU.mult), [Yb, gatb], [Yb])
                P.dma('pool', yab[r, 0:1024], Y[:], reads=[Yb])
    P.sb_reset(mark)


WSPEC = [
    ('ab_w_in', (2, 2048, 6400)), ('ab_shift', (2, 3, 3328)), ('ab_w0', (2, 2, 1024)), ('ab_w_up', (2, 2, 64, 1024)),
    ('ab_a0', (2, 2, 1024)), ('ab_a_up', (2, 2, 64, 1024)), ('ab_g_up', (2, 128, 1024)), ('ab_k_k', (2, 1024)),
    ('ab_k_a', (2, 1024)), ('ab_r_k', (2, 16, 64)), ('ab_gn_g', (2, 1024)), ('ab_gn_b', (2, 1024)),
    ('ab_rpb', (2, 16, 15, 31)), ('ab_w_out', (2, 2048, 2048)), ('c_w_in', (2, 2048, 12288)), ('c_gn_g', (2, 4096)),
    ('c_gn_b', (2, 4096)), ('c_w_out', (2, 4096, 2048)), ('ln1_g', (4, 2048)), ('ln1_b', (4, 2048)),
    ('ffn_w_up', (4, 2048, 11008)), ('ffn_conv', (4, 3, 5504)), ('ffn_conv_b', (4, 5504)), ('ffn_w_down', (4, 5504, 2048)),
    ('ln2_g', (4, 2048)), ('ln2_b', (4, 2048)),
]
BIGW = ['ab_w_in', 'ab_w_out', 'c_w_in', 'c_w_out', 'ffn_w_up', 'ffn_w_down']


def build(NSEQ, S, layers, debug_outs=()):
    T = NSEQ * S
    nc = bass.Bass("TRN2", target_bir_lowering=False)
    W = {}
    for name, shp in WSPEC:
        W[name] = nc.dram_tensor(name, list(shp), F32, kind="ExternalInput").ap()
    x_in = nc.dram_tensor("x", [T, D], F32, kind="ExternalInput").ap()
    y_out = nc.dram_tensor("y", [T, D], F32, kind="ExternalOutput").ap()

    def scr(name, shape, dt):
        kind = "ExternalOutput" if name in debug_outs else "Internal"
        return nc.dram_tensor(name, shape, dt, kind=kind).ap()
    WB = {}
    for name in BIGW:
        shp = dict(WSPEC)[name]
        WB[name] = scr(name + "_b", list(shp), BF16)
    xs = [scr("xs0", [T, D], F32), scr("xs1", [T, D], F32)]
    xTs = [scr("xT0", [D, T], BF16), scr("xT1", [D, T], BF16)]
    hTs = [scr(f"hT{q}", [11008, S], F32) for q in range(NSEQ)]
    vgs = [scr(f"vg{q}", [S, 8192], F32) for q in range(NSEQ)]
    gT = scr("gT", [5504, T], BF16)
    qkr = scr("qkr", [4096, T], BF16)
    ytok = scr("ytok", [T, 4096], F32)
    yT = scr("yT", [4096, T], BF16)
    mix = scr("mix", [T, D], F32)

    P = Prog(nc); C = Ctx()
    setup_consts(P, C)
    TT = min(512, S)
    for l in layers:
        e = l // 2
        if l % 2 == 0:
            cast_weight(P, C, W['ab_w_in'][e], WB['ab_w_in'][e], 2048, 6400)
            cast_weight(P, C, W['ab_w_out'][e], WB['ab_w_out'][e], 2048, 2048)
        else:
            cast_weight(P, C, W['c_w_in'][e], WB['c_w_in'][e], 2048, 12288)
            cast_weight(P, C, W['c_w_out'][e], WB['c_w_out'][e], 4096, 2048)
        cast_weight(P, C, W['ffn_w_up'][l], WB['ffn_w_up'][l], 2048, 11008)
        cast_weight(P, C, W['ffn_w_down'][l], WB['ffn_w_down'][l], 5504, 2048)
    xprep(P, C, x_in, xTs[0], T)
    xcur, xTcur, pp = x_in, xTs[0], 0

    def G(xT_, w_, K, N, mode, out, n_off=0, TG=T):
        m = P.sb_mark()
        gemm(P, C, xT_, w_, K, TG, N, mode, StoreEpi(P, C, out, mode), TT=TT, n_off=n_off)
        P.sb_reset(m)

    for li, l in enumerate(layers):
        e = l // 2
        if l % 2 == 0:
            mixer_ab(P, C, nc, W, WB, e, xTcur, mix, T, S, NSEQ, scr_fn=scr, G=G)
        else:
            for q in range(NSEQ):
                sq = slice(q * S, (q + 1) * S)
                G(xTcur[:, sq], WB['c_w_in'][e], 2048, 4096, 'feat', hTs[q][0:4096, :], TG=S)
                G(xTcur[:, sq], WB['c_w_in'][e], 2048, 8192, 'tok', vgs[q], n_off=4096, TG=S)
                ret_rotary(P, C, hTs[q][0:4096, :], qkr[:, sq], S, S)
                ret_attn(P, C, qkr[:, sq], vgs[q], W['c_gn_g'][e], W['c_gn_b'][e], ytok[sq, :], S, S)
            xprep_n(P, C, ytok, yT, T, 4096)
            G(yT, WB['c_w_out'][e], 4096, 2048, 'tok', mix)
        x1, x1T = xs[pp], xTs[1 - pp]
        ln_phase(P, C, xcur, mix, W['ln1_g'][l], W['ln1_b'][l], x1, x1T, T)
        for q in range(NSEQ):
            sq = slice(q * S, (q + 1) * S)
            G(x1T[:, sq], WB['ffn_w_up'][l], 2048, 11008, 'feat', hTs[q], TG=S)
            ffn_mid(P, C, hTs[q], W['ffn_conv'][l], W['ffn_conv_b'][l], gT[:, sq], S, S)
        G(gT, WB['ffn_w_down'][l], 5504, 2048, 'tok', mix)
        last = (li == len(layers) - 1)
        x2 = y_out if last else xs[1 - pp]
        x2T = xTs[pp]
        ln_phase(P, C, x1, mix, W['ln2_g'][l], W['ln2_b'][l], x2, None if last else x2T, T)
        xcur, xTcur = x2, x2T
        pp = 1 - pp
    P.emit()
    return nc, P


def mixer_ab(P, C, nc, W, WB, e, xTcur, mix, T, S, NSEQ, scr_fn, G):
    if not hasattr(C, 'ab_scr'):
        d = {}
        d['hA'] = scr_fn("hA", [T, 3328], F32)
        d['hqk'] = scr_fn("hqk", [2048, T], F32)
        d['hv'] = scr_fn("hv", [T, 1024], F32)
        d['yab'] = scr_fn("yab", [T, 2048], F32)
        d['yabT'] = scr_fn("yabT", [2048, T], BF16)
        d['YF'] = scr_fn("YF", [T, 1024], F32)
        d['PR'] = {k: scr_fn("pr_" + k, [T, 1024], F32) for k in ('R', 'V', 'KAP', 'GATE', 'LW0', 'LW1', 'B0', 'B1', 'KD0', 'KD1')}
        d['PR']['BON'] = scr_fn("pr_BON", [T, 16], F32)
        C.ab_scr = d
    d = C.ab_scr
    G(xTcur, WB['ab_w_in'][e], 2048, 3328, 'tok', d['hA'])
    G(xTcur, WB['ab_w_in'][e], 2048, 2048, 'feat', d['hqk'], n_off=3328)
    G(xTcur, WB['ab_w_in'][e], 2048, 1024, 'tok', d['hv'], n_off=3328 + 2048)
    rwkv_prep(P, C, nc, W, e, d['hA'], d['PR'], T, S)
    rwkv_scan(P, C, nc, W, e, d['PR'], 0, d['YF'], d['yab'], T, S)
    rwkv_scan(P, C, nc, W, e, d['PR'], 1, d['YF'], d['yab'], T, S)
    bS = na_bias_build(P, C, nc, W['ab_rpb'][e], scr_fn, e)
    na_attn(P, C, nc, bS, d['hqk'], d['hv'], d['yab'], T, S)
    xprep_n(P, C, d['yab'], d['yabT'], T, 2048)
    G(d['yabT'], WB['ab_w_out'][e], 2048, 2048, 'tok', mix)


_CACHE = {}
NSEQ_CORE = 2
SEQ = 4096


def _assign():
    return [(('p', c), ('s', c) if c < 4 else ('p', c)) for c in range(8)]


def kernel(**inputs):
    if 'nc' not in _CACHE:
        _CACHE['nc'] = build(NSEQ_CORE, SEQ, [0, 1, 2, 3])[0]
    nc = _CACHE['nc']
    xp = np.asarray(inputs['x_prompt'], dtype=np.float32)
    xs_ = np.asarray(inputs['x_sample'], dtype=np.float32)
    wd = {n: np.ascontiguousarray(np.asarray(inputs[n], dtype=np.float32)) for n, _ in WSPEC}
    in_maps = []
    asg = _assign()
    for c in range(8):
        rows = []
        for kind, i in asg[c]:
            rows.append(xp[i] if kind == 'p' else xs_[i])
        m = dict(wd)
        m['x'] = np.ascontiguousarray(np.concatenate(rows, axis=0))
        in_maps.append(m)
    res = run_bass_kernel_spmd(nc, in_maps, core_ids=list(range(8)))
    yp = np.empty_like(xp)
    ys = np.empty_like(xs_)
    for c in range(8):
        y = np.asarray(res.results[c]['y'])
        for slot, (kind, i) in enumerate(asg[c]):
            blk = y[slot * SEQ:(slot + 1) * SEQ]
            if slot == 1 and c >= 4:
                continue
            if kind == 'p':
                yp[i] = blk
            else:
                ys[i] = blk
    return (yp, ys)
```

```python
import numpy as np
import concourse.bass as bass
import concourse.mybir as mybir
from concourse.bass_utils import run_bass_kernel_spmd

F32 = mybir.dt.float32
BF16 = mybir.dt.bfloat16
I32 = mybir.dt.int32
AF = mybir.ActivationFunctionType
ALU = mybir.AluOpType
AX = mybir.AxisListType

ENGS = ('pe', 'act', 'dve', 'pool', 'sp')
DMAQ = {'sp': 16, 'act': 8, 'pool': 8}


class Buf:
    __slots__ = ('name', 'lw', 'rd')

    def __init__(self, name=''):
        self.name = name
        self.lw = None
        self.rd = []


class Op:
    __slots__ = ('eng', 'fn', 'dma', 'deps', 'sig', 'slot', 'slotval', 'idx', 'bar')

    def __init__(self, eng, fn, dma):
        self.eng = eng
        self.fn = fn
        self.dma = dma
        self.deps = ()
        self.sig = False
        self.slot = None
        self.slotval = 0
        self.idx = 0
        self.bar = False


class Prog:
    def __init__(self, nc):
        self.nc = nc
        self.ops = []
        self.last = {e: None for e in ENGS}
        self.sb_off = 16640
        self.sb_base = 16640
        self.nalloc = 0

    def sb(self, shape, dtype, name='t'):
        nbytes = int(np.prod(shape[1:])) * (4 if dtype in (F32, I32) else 2)
        off = (self.sb_off + 63) // 64 * 64
        assert off + nbytes <= 229000, (off, nbytes, name)
        self.sb_off = off + nbytes
        self.nalloc += 1
        return self.nc.alloc_sbuf_tensor_at(f"{name}{self.nalloc}", list(shape), dtype, offset=off)

    def sb_mark(self):
        return self.sb_off

    def sb_reset(self, mark):
        self.sb_off = mark
        self.barrier()

    def add(self, eng, fn, reads=(), writes=(), dma=False):
        op = Op(eng, fn, dma)
        deps = set()
        for b in reads:
            if b.lw is not None:
                deps.add(b.lw)
        for b in writes:
            if b.lw is not None:
                deps.add(b.lw)
            deps.update(b.rd)
        dl = []
        for d in deps:
            if d is op:
                continue
            if (not dma) and (not d.dma) and d.eng == 'pe' and eng == 'pe':
                continue
            d.sig = True
            dl.append(d)
        op.deps = dl
        for b in reads:
            b.rd.append(op)
        for b in writes:
            b.lw = op
            b.rd = []
        self.ops.append(op)
        if not dma:
            self.last[eng] = op
        return op

    def dma(self, q, out, in_, reads=(), writes=(), **kw):
        return self.add(q, lambda e: e.dma_start(out=out, in_=in_, **kw), reads, writes, dma=True)

    def mark(self, name):
        op = Op('mark', None, False)
        op.bar = name
        self.ops.append(op)

    def barrier(self):
        op = Op('all', None, False)
        op.bar = True
        for e in ENGS:
            if self.last[e] is not None:
                self.last[e].sig = True
        self.ops.append(op)

    def emit(self):
        nc = self.nc
        self.barrier()
        engsem = {e: nc.alloc_semaphore(f"s_{e}") for e in ENGS}
        slotsem = {q: [nc.alloc_semaphore(f"d_{q}{i}") for i in range(n)] for q, n in DMAQ.items()}
        slotuse = {q: [0] * n for q, n in DMAQ.items()}
        rr = {q: 0 for q in DMAQ}
        cnt = {e: 0 for e in ENGS}
        waited = {e: {} for e in ENGS}
        streams = {e: [] for e in ENGS}

        def want(e, sem, val):
            if val <= 0:
                return
            w = waited[e]
            k = id(sem)
            if w.get(k, 0) >= val:
                return
            w[k] = val
            streams[e].append(('w', sem, val))

        self.marks = []
        for op in self.ops:
            if op.eng == 'mark':
                self.marks.append((op.bar, dict(cnt)))
                continue
            if op.bar:
                for e in ENGS:
                    for x in ENGS:
                        if x != e:
                            want(e, engsem[x], cnt[x])
                    for q in DMAQ:
                        for i, s in enumerate(slotsem[q]):
                            want(e, s, 16 * slotuse[q][i])
                continue
            e = op.eng
            for d in op.deps:
                if d.dma:
                    want(e, slotsem[d.eng][d.slot], d.slotval)
                else:
                    want(e, engsem[d.eng], d.idx)
            if op.dma:
                k = rr[e]
                rr[e] = (k + 1) % DMAQ[e]
                want(e, slotsem[e][k], 16 * slotuse[e][k])
                slotuse[e][k] += 1
                op.slot = k
                op.slotval = 16 * slotuse[e][k]
                streams[e].append(('d', op, slotsem[e][k]))
            else:
                if op.sig:
                    cnt[e] += 1
                    op.idx = cnt[e]
                streams[e].append(('o', op, engsem[e]))

        def run_stream(eng_handle, items):
            for it in items:
                if it[0] == 'w':
                    eng_handle.wait_ge(it[1], it[2])
                elif it[0] == 'd':
                    it[1].fn(eng_handle).then_inc(it[2], 16)
                else:
                    ins = it[1].fn(eng_handle)
                    if it[1].sig:
                        ins.then_inc(it[2], 1)

        with nc.Block() as block:
            @block.sync
            def _(e):
                run_stream(e, streams['sp'])

            @block.tensor
            def _(e):
                run_stream(e, streams['pe'])

            @block.scalar
            def _(e):
                run_stream(e, streams['act'])

            @block.vector
            def _(e):
                run_stream(e, streams['dve'])

            @block.gpsimd
            def _(e):
                run_stream(e, streams['pool'])
        self.nitems = {e: len(streams[e]) for e in ENGS}


D = 2048
DFF = 5504
ALPHA = 8.0 ** 0.25
LN_EPS = 1e-5


class Ctx:
    pass


def setup_consts(P, C):
    nc = P.nc
    idn = nc.inline_tensor(np.eye(128, dtype=np.float32), "ident_d").ap()
    C.ident = P.sb([128, 128], F32, 'ident')
    C.identb = P.sb([128, 128], BF16, 'identb')
    C.b_ident = Buf('ident')
    P.dma('sp', C.ident[:], idn[:, :], writes=[C.b_ident])
    P.add('dve', lambda e: e.tensor_copy(out=C.identb[:], in_=C.ident[:]), [C.b_ident], [C.b_ident])
    C.ps = [nc.alloc_psum_tensor(f"psb{i}", [128, 512], F32) for i in range(8)]
    C.psb = [Buf(f'ps{i}') for i in range(8)]
    C.psi = 0


def next_ps(C, n=8, base=0):
    if n == 8 and base == 0:
        n, base = getattr(C, 'ps_pool', (8, 0))
    i = base + (C.psi % n)
    C.psi += 1
    return C.ps[i], C.psb[i]


def next_ps_t(C):
    C.psti = getattr(C, 'psti', 0) + 1
    i = 6 + (C.psti % 2)
    return C.ps[i], C.psb[i]


def cast_weight(P, C, src, dst, K, N):
    P.mark('cast_weight')
    mark = P.sb_mark()
    CB = 2048
    NBUF = 3
    st = [(P.sb([128, CB], F32, 'cw'), P.sb([128, CB], BF16, 'cwb'), Buf(), Buf()) for _ in range(NBUF)]
    engs = ['dve', 'act', 'pool']
    i = 0
    for r0 in range(0, K, 128):
        for c0 in range(0, N, CB):
            cn = min(CB, N - c0)
            a, b, ba, bb = st[i % NBUF]
            eng = engs[i % 3]
            P.dma('sp', a[:, 0:cn], src[r0:r0 + 128, c0:c0 + cn], writes=[ba])
            if eng == 'act':
                P.add('act', lambda e, a=a, b=b, cn=cn: e.copy(out=b[:, 0:cn], in_=a[:, 0:cn]), [ba], [bb])
            else:
                P.add(eng, lambda e, a=a, b=b, cn=cn: e.tensor_copy(out=b[:, 0:cn], in_=a[:, 0:cn]), [ba], [bb])
            P.dma('pool', dst[r0:r0 + 128, c0:c0 + cn], b[:, 0:cn], reads=[bb])
            i += 1
    P.sb_reset(mark)


def xprep(P, C, x, xT, T):
    P.mark('xprep')
    mark = P.sb_mark()
    KC = D // 128
    NB = 2
    st = [(P.sb([128, D], F32, 'xp'), P.sb([128, KC, 128], BF16, 'xpt'), Buf(), Buf()) for _ in range(NB)]
    xTv = xT.rearrange("(kc p) t -> p kc t", p=128)
    for ti in range(T // 128):
        a, b, ba, bb = st[ti % NB]
        P.dma('sp', a[:], x[ti * 128:(ti + 1) * 128, :], writes=[ba])
        transpose_to(P, C, a, ba, b, bb, KC)
        P.dma('pool', xTv[:, :, ti * 128:(ti + 1) * 128], b[:], reads=[bb])
    P.sb_reset(mark)


def transpose_to(P, C, a, ba, b, bb, KC, evi=[0], tbanks=False):
    for g in range(0, KC, 4):
        ps, psb = next_ps_t(C) if tbanks else next_ps(C)
        ng = min(4, KC - g)
        for j in range(ng):
            kc = g + j
            P.add('pe', lambda e, ps=ps, j=j, kc=kc: e.transpose(ps[:, j * 128:(j + 1) * 128], a[:, kc * 128:(kc + 1) * 128], C.ident[:]),
                  [ba, C.b_ident], [psb])
        eng = 'act' if evi[0] % 2 else 'dve'
        evi[0] += 1
        src = lambda ps=ps, ng=ng: ps[:, 0:ng * 128].rearrange("p (j t) -> p j t", t=128)
        if eng == 'act':
            P.add('act', lambda e, g=g, ng=ng, src=src: e.copy(out=b[:, g:g + ng, :], in_=src()), [psb], [bb])
        else:
            P.add('dve', lambda e, g=g, ng=ng, src=src: e.tensor_copy(out=b[:, g:g + ng, :], in_=src()), [psb], [bb])


def gemm(P, C, xT, w, K, T, N, mode, epi, TT=512, n_off=0, KS=1):
    P.mark('gemm')
    mark = P.sb_mark()
    KC = K // 128
    NBW = 512
    KW = (KC + KS - 1) // KS
    xt = [(P.sb([128, KC, TT], BF16, 'gx'), Buf()) for _ in range(2)]
    wt = [(P.sb([128, KW, NBW], BF16, 'gw'), Buf()) for _ in range(2)]
    xTv = xT.rearrange("(kc p) t -> p kc t", p=128)
    wv = w.rearrange("(kc p) n -> p kc n", p=128)
    wi = 0
    for tti, t0 in enumerate(range(0, T, TT)):
        tt = min(TT, T - t0)
        xa, xb = xt[tti % 2]
        h = KC // 2
        P.dma('sp', xa[:, 0:h, 0:tt], xTv[:, 0:h, t0:t0 + tt], writes=[xb])
        P.dma('sp', xa[:, h:KC, 0:tt], xTv[:, h:KC, t0:t0 + tt], writes=[xb])
        if hasattr(epi, 'pre_tile'):
            epi.pre_tile(t0, tt)
        for n0 in range(0, N, NBW):
            nn = min(NBW, N - n0)
            if mode == 'tok':
                banks = [next_ps(C) for _ in range(0, tt, 128)]
                for ks in range(KS):
                    k0, k1 = ks * KW, min(KC, (ks + 1) * KW)
                    wa, wb = wt[wi % 2]
                    wi += 1
                    hh = (k0 + k1) // 2
                    P.dma('sp', wa[:, 0:hh - k0, 0:nn], wv[:, k0:hh, n_off + n0:n_off + n0 + nn], writes=[wb])
                    P.dma('sp', wa[:, hh - k0:k1 - k0, 0:nn], wv[:, hh:k1, n_off + n0:n_off + n0 + nn], writes=[wb])
                    for si, s0 in enumerate(range(0, tt, 128)):
                        ps, psb = banks[si]
                        for kc in range(k0, k1):
                            P.add('pe', lambda e, ps=ps, kc=kc, k0=k0, s0=s0, xa=xa, wa=wa, nn=nn: e.matmul(
                                ps[:, 0:nn], xa[:, kc, s0:s0 + 128], wa[:, kc - k0, 0:nn], start=(kc == 0), stop=(kc == KC - 1)),
                                [xb, wb], [psb])
                        if ks == KS - 1:
                            epi(ps, psb, t0 + s0, 128, n0, nn)
            else:
                wa, wb = wt[wi % 2]
                wi += 1
                P.dma('sp', wa[:, 0:h, 0:nn], wv[:, 0:h, n_off + n0:n_off + n0 + nn], writes=[wb])
                P.dma('sp', wa[:, h:KC, 0:nn], wv[:, h:KC, n_off + n0:n_off + n0 + nn], writes=[wb])
                for m0 in range(0, nn, 128):
                    ps, psb = next_ps(C)
                    for kc in range(KC):
                        P.add('pe', lambda e, ps=ps, kc=kc, m0=m0, xa=xa, wa=wa, tt=tt: e.matmul(
                            ps[:, 0:tt], wa[:, kc, m0:m0 + 128], xa[:, kc, 0:tt], start=(kc == 0), stop=(kc == KC - 1)),
                            [xb, wb], [psb])
                    epi(ps, psb, t0, tt, n0 + m0, 128)
    P.sb_reset(mark)


class LNEpi:
    def __init__(self, P, C, xin, g_row, b_row, xout, xTout):
        self.P, self.C, self.xin, self.xout = P, C, xin, xout
        self.KC = D // 128
        self.gt = P.sb([128, D], F32, 'lng'); self.gb = Buf()
        self.bt = P.sb([128, D], F32, 'lnb'); self.bb = Buf()
        P.dma('sp', self.gt[:], bass.AP(g_row.tensor, g_row.offset, [[0, 128], [1, D]]), writes=[self.gb])
        P.dma('sp', self.bt[:], bass.AP(b_row.tensor, b_row.offset, [[0, 128], [1, D]]), writes=[self.bb])
        self.z = [dict(z=P.sb([128, D], F32, 'lz'), zb=Buf(), stats=P.sb([128, 4, 6], F32, 'ls'), mv=P.sb([128, 2], F32, 'lmv'),
                       rstd=P.sb([128, 1], F32, 'lr'), nmr=P.sb([128, 1], F32, 'ln'), sb=Buf()) for _ in range(4)]
        self.xt = [(P.sb([128, self.KC, 128], BF16, 'lxt'), Buf()) for _ in range(2)]
        self.xti = 0
        self.xTv = xTout.rearrange("(kc p) t -> p kc t", p=128) if xTout is not None else None

    def pre_tile(self, t0, tt):
        for si, s0 in enumerate(range(0, tt, 128)):
            s = self.z[si]
            self.P.dma('sp', s['z'][:], self.xin[t0 + s0:t0 + s0 + 128, :], writes=[s['zb']])

    def __call__(self, ps, psb, t0, nt, n0, nn):
        P = self.P
        s = self.z[(t0 // 128) % 4]
        z = s['z']
        P.add('dve', lambda e: e.scalar_tensor_tensor(out=z[:, n0:n0 + nn], in0=z[:, n0:n0 + nn], scalar=ALPHA, in1=ps[:, 0:nn], op0=ALU.mult, op1=ALU.add),
              [psb, s['zb']], [s['zb']])
        if n0 + nn < D:
            return
        r = slice(t0, t0 + 128)
        for c in range(4):
            P.add('dve', lambda e, c=c: e.bn_stats(out=s['stats'][:, c, :], in_=z[:, c * 512:(c + 1) * 512]), [s['zb']], [s['sb']])
        P.add('dve', lambda e: e.bn_aggr(out=s['mv'][:], in_=s['stats'][:].rearrange("p a b -> p (a b)")), [s['sb']], [s['sb']])
        P.add('dve', lambda e: e.tensor_scalar(out=s['rstd'][:], in0=s['mv'][:, 1:2], scalar1=LN_EPS, scalar2=None, op0=ALU.add), [s['sb']], [s['sb']])
        P.add('act', lambda e: e.activation(out=s['rstd'][:], in_=s['rstd'][:], func=AF.Sqrt), [s['sb']], [s['sb']])
        P.add('dve', lambda e: e.reciprocal(out=s['rstd'][:], in_=s['rstd'][:]), [s['sb']], [s['sb']])
        P.add('dve', lambda e: e.scalar_tensor_tensor(out=s['nmr'][:], in0=s['mv'][:, 0:1], scalar=-1.0, in1=s['rstd'][:], op0=ALU.mult, op1=ALU.mult), [s['sb']], [s['sb']])
        P.add('act', lambda e: e.activation(out=z[:], in_=z[:], func=AF.Identity, bias=s['nmr'][:], scale=s['rstd'][:]), [s['zb'], s['sb']], [s['zb']])
        P.add('pool', lambda e: e.tensor_tensor(out=z[:], in0=z[:], in1=self.gt[:], op=ALU.mult), [s['zb'], self.gb], [s['zb']])
        P.add('pool', lambda e: e.tensor_tensor(out=z[:], in0=z[:], in1=self.bt[:], op=ALU.add), [s['zb'], self.bb], [s['zb']])
        P.dma('pool', self.xout[r, :], z[:], reads=[s['zb']])
        if self.xTv is not None:
            xt, xtb = self.xt[self.xti % 2]
            self.xti += 1
            transpose_to(P, self.C, z, s['zb'], xt, xtb, self.KC, tbanks=True)
            P.dma('pool', self.xTv[:, :, r], xt[:], reads=[xtb])


class StoreEpi:
    def __init__(self, P, C, out, mode, dtype=F32):
        self.P, self.C, self.out, self.mode = P, C, out, mode
        self.st = [(P.sb([128, 512], dtype, 'se'), Buf()) for _ in range(4)]
        self.i = 0

    def __call__(self, ps, psb, t0, nt, n0, nn):
        P = self.P
        a, ab = self.st[self.i % 4]
        eng = 'act' if self.i % 2 else 'dve'
        self.i += 1
        if self.mode == 'tok':
            w = nn
            dst = self.out[t0:t0 + nt, n0:n0 + nn]
        else:
            w = nt
            dst = self.out[n0:n0 + nn, t0:t0 + nt]
        if eng == 'act':
            P.add('act', lambda e: e.copy(out=a[:, 0:w], in_=ps[:, 0:w]), [psb], [ab])
        else:
            P.add('dve', lambda e: e.tensor_copy(out=a[:, 0:w], in_=ps[:, 0:w]), [psb], [ab])
        P.dma('pool', dst, a[:, 0:w], reads=[ab])


def bcast_rows(P, C, src_row, n, name):
    t = P.sb([128, n], F32, name)
    b = Buf(name)
    P.dma('sp', t[:], src_row.partition_broadcast(128) if hasattr(src_row, 'partition_broadcast') else src_row, writes=[b])
    return t, b


def ln_phase(P, C, xin, mix, g_row, b_row, xout, xTout, T, final_out=None):
    P.mark('ln_phase')
    mark = P.sb_mark()
    nc = P.nc
    KC = D // 128
    gt = P.sb([128, D], F32, 'lng'); gb = Buf()
    bt = P.sb([128, D], F32, 'lnb'); bb_ = Buf()
    P.dma('sp', gt[:], bass.AP(g_row.tensor, g_row.offset, [[0, 128], [1, D]]), writes=[gb])
    P.dma('sp', bt[:], bass.AP(b_row.tensor, b_row.offset, [[0, 128], [1, D]]), writes=[bb_])
    NB = 2
    st = []
    for _ in range(NB):
        st.append(dict(x=P.sb([128, D], F32, 'lx'), m=P.sb([128, D], F32, 'lm'), xb=Buf(), mb=Buf(),
                       stats=P.sb([128, 4, 6], F32, 'ls'), mv=P.sb([128, 2], F32, 'lmv'), sb=Buf(),
                       rstd=P.sb([128, 1], F32, 'lr'), nmr=P.sb([128, 1], F32, 'ln'),
                       xt=P.sb([128, KC, 128], BF16, 'lxt'), xtb=Buf()))
    xTv = xTout.rearrange("(kc p) t -> p kc t", p=128) if xTout is not None else None
    for ti in range(T // 128):
        s = st[ti % NB]
        x, m = s['x'], s['m']
        r = slice(ti * 128, (ti + 1) * 128)
        P.dma('sp', x[:], xin[r, :], writes=[s['xb']])
        P.dma('sp', m[:], mix[r, :], writes=[s['mb']])
        P.add('dve', lambda e, x=x, m=m: e.scalar_tensor_tensor(out=m[:], in0=x[:], scalar=ALPHA, in1=m[:], op0=ALU.mult, op1=ALU.add),
              [s['xb'], s['mb']], [s['mb']])
        for c in range(4):
            P.add('dve', lambda e, s=s, m=m, c=c: e.bn_stats(out=s['stats'][:, c, :], in_=m[:, c * 512:(c + 1) * 512]), [s['mb']], [s['sb']])
        P.add('dve', lambda e, s=s: e.bn_aggr(out=s['mv'][:], in_=s['stats'][:].rearrange("p a b -> p (a b)")), [s['sb']], [s['sb']])
        P.add('dve', lambda e, s=s: e.tensor_scalar(out=s['rstd'][:], in0=s['mv'][:, 1:2], scalar1=LN_EPS, scalar2=None, op0=ALU.add),
              [s['sb']], [s['sb']])
        P.add('act', lambda e, s=s: e.activation(out=s['rstd'][:], in_=s['rstd'][:], func=AF.Sqrt), [s['sb']], [s['sb']])
        P.add('dve', lambda e, s=s: e.reciprocal(out=s['rstd'][:], in_=s['rstd'][:]), [s['sb']], [s['sb']])
        P.add('dve', lambda e, s=s: e.scalar_tensor_tensor(out=s['nmr'][:], in0=s['mv'][:, 0:1], scalar=-1.0, in1=s['rstd'][:], op0=ALU.mult, op1=ALU.mult),
              [s['sb']], [s['sb']])
        P.add('act', lambda e, s=s, x=x, m=m: e.activation(out=x[:], in_=m[:], func=AF.Identity, bias=s['nmr'][:], scale=s['rstd'][:]),
              [s['mb'], s['sb']], [s['xb']])
        P.add('pool', lambda e, x=x: e.tensor_tensor(out=x[:], in0=x[:], in1=gt[:], op=ALU.mult), [s['xb'], gb], [s['xb']])
        P.add('dve', lambda e, x=x: e.tensor_tensor(out=x[:], in0=x[:], in1=bt[:], op=ALU.add), [s['xb'], bb_], [s['xb']])
        P.dma('pool', xout[r, :], x[:], reads=[s['xb']])
        if xTv is not None:
            transpose_to(P, C, x, s['xb'], s['xt'], s['xtb'], KC)
            P.dma('pool', xTv[:, :, r], s['xt'][:], reads=[s['xtb']])
    P.sb_reset(mark)


def ffn_mid(P, C, hT, conv_w, conv_b, gT, T, S):
    P.mark('ffn_mid')
    mark = P.sb_mark()
    NCH = DFF // 128
    cw = P.sb([128, NCH, 3], F32, 'fcw'); cwb = Buf()
    cb = P.sb([128, NCH], F32, 'fcb'); cbb = Buf()
    for k in range(3):
        P.dma('sp', cw[:, :, k], conv_w[k, :].rearrange("(c p) -> p c", p=128), writes=[cwb], allow_slow_non_contiguous=True)
    P.dma('sp', cb[:], conv_b.rearrange("(c p) -> p c", p=128), writes=[cbb], allow_slow_non_contiguous=True)
    NB = 2
    st = [dict(g=P.sb([128, S + 2], F32, 'fg'), u=P.sb([128, S], F32, 'fu'), a=P.sb([128, S], F32, 'fa'),
               o=P.sb([128, S], BF16, 'fo'), gb=Buf(), ub=Buf(), ab=Buf(), ob=Buf()) for _ in range(NB)]
    for s in st:
        P.add('pool', lambda e, s=s: e.memset(s['g'][:, 0:1], 0.0), [], [s['gb']])
        P.add('pool', lambda e, s=s: e.memset(s['g'][:, S + 1:S + 2], 0.0), [], [s['gb']])
    i = 0
    for q in range(T // S):
        for ch in range(NCH):
            s = st[i % NB]
            i += 1
            g, u, a, o = s['g'], s['u'], s['a'], s['o']
            tr = slice(q * S, (q + 1) * S)
            P.dma('sp', g[:, 1:S + 1], hT[ch * 128:(ch + 1) * 128, tr], writes=[s['gb']])
            P.dma('sp', u[:], hT[DFF + ch * 128:DFF + (ch + 1) * 128, tr], writes=[s['ub']])
            P.add('act', lambda e, g=g, a=a, ch=ch: e.activation(out=a[:], in_=g[:, 1:S + 1], func=AF.Identity, bias=cb[:, ch:ch + 1], scale=cw[:, ch, 1:2]),
                  [s['gb'], cwb, cbb], [s['ab']])
            P.add('dve', lambda e, g=g, a=a, ch=ch: e.scalar_tensor_tensor(out=a[:], in0=g[:, 0:S], scalar=cw[:, ch, 0:1], in1=a[:], op0=ALU.mult, op1=ALU.add),
                  [s['gb'], s['ab'], cwb], [s['ab']])
            P.add('dve', lambda e, g=g, a=a, ch=ch: e.scalar_tensor_tensor(out=a[:], in0=g[:, 2:S + 2], scalar=cw[:, ch, 2:3], in1=a[:], op0=ALU.mult, op1=ALU.add),
                  [s['gb'], s['ab'], cwb], [s['ab']])
            P.add('act', lambda e, a=a: e.activation(out=a[:], in_=a[:], func=AF.Gelu), [s['ab']], [s['ab']])
            P.add('pool', lambda e, a=a, u=u, o=o: e.tensor_tensor(out=o[:], in0=a[:], in1=u[:], op=ALU.mult), [s['ab'], s['ub']], [s['ob']])
            P.dma('pool', gT[ch * 128:(ch + 1) * 128, tr], o[:], reads=[s['ob']])
    P.sb_reset(mark)


def xprep_n(P, C, x, xT, T, DD):
    P.mark('xprep_n')
    mark = P.sb_mark()
    KC = DD // 128
    NB = 2
    st = [(P.sb([128, DD], F32, 'xp'), P.sb([128, KC, 128], BF16, 'xpt'), Buf(), Buf()) for _ in range(NB)]
    xTv = xT.rearrange("(kc p) t -> p kc t", p=128)
    for ti in range(T // 128):
        a, b, ba, bb = st[ti % NB]
        P.dma('sp', a[:], x[ti * 128:(ti + 1) * 128, :], writes=[ba])
        transpose_to(P, C, a, ba, b, bb, KC)
        P.dma('pool', xTv[:, :, ti * 128:(ti + 1) * 128], b[:], reads=[bb])
    P.sb_reset(mark)


def rot_tables(S):
    half = 128
    inv = (10000.0 ** (-np.arange(half, dtype=np.float32) / half)).astype(np.float32)
    pos = np.arange(S, dtype=np.float32)
    ang = (pos[:, None] * inv[None, :]).astype(np.float32)
    return np.cos(ang).T.astype(np.float32).copy(), np.sin(ang).T.astype(np.float32).copy()


def ret_rotary(P, C, qkT, qkr, T, S):
    P.mark('ret_rotary')
    nc = P.nc
    mark = P.sb_mark()
    if not hasattr(C, 'rot'):
        ct, sn = rot_tables(S)
        C.rot = (nc.inline_tensor(ct, "rot_cos").ap(), nc.inline_tensor(sn, "rot_sin").ap())
    cd, sd = C.rot
    cos = P.sb([128, S], F32, 'cos'); sin = P.sb([128, S], F32, 'sin')
    cosk = P.sb([128, S], F32, 'cosk'); sink = P.sb([128, S], F32, 'sink')
    tb = Buf()
    P.dma('sp', cos[:], cd[:, :], writes=[tb])
    P.dma('sp', sin[:], sd[:, :], writes=[tb])
    P.add('act', lambda e: e.mul(out=cosk[:], in_=cos[:], mul=1.0 / 16), [tb], [tb])
    P.add('act', lambda e: e.mul(out=sink[:], in_=sin[:], mul=1.0 / 16), [tb], [tb])
    NB = 2 if S <= 2048 else 1
    st = [dict(x1=P.sb([128, S], F32, 'r1'), x2=P.sb([128, S], F32, 'r2'), a=P.sb([128, S], F32, 'ra'), b=P.sb([128, S], F32, 'rb'),
               o1=P.sb([128, S], BF16, 'ro1'), o2=P.sb([128, S], BF16, 'ro2'), xb=Buf(), ab=Buf(), ob=Buf()) for _ in range(NB)]
    i = 0
    for q in range(T // S):
        tr = slice(q * S, (q + 1) * S)
        for hh in range(16):
            s = st[i % NB]; i += 1
            c_, s_ = (cos, sin) if hh < 8 else (cosk, sink)
            r0 = hh * 256
            x1, x2, a, b, o1, o2 = s['x1'], s['x2'], s['a'], s['b'], s['o1'], s['o2']
            P.dma('sp', x1[:], qkT[r0:r0 + 128, tr], writes=[s['xb']])
            P.dma('sp', x2[:], qkT[r0 + 128:r0 + 256, tr], writes=[s['xb']])
            P.add('dve', lambda e, x1=x1, a=a, c_=c_: e.tensor_tensor(out=a[:], in0=x1[:], in1=c_[:], op=ALU.mult), [s['xb'], tb], [s['ab']])
            P.add('pool', lambda e, x2=x2, b=b, s_=s_: e.tensor_tensor(out=b[:], in0=x2[:], in1=s_[:], op=ALU.mult), [s['xb'], tb], [s['ab']])
            P.add('dve', lambda e, a=a, b=b, o1=o1: e.tensor_tensor(out=o1[:], in0=a[:], in1=b[:], op=ALU.subtract), [s['ab']], [s['ob']])
            P.add('pool', lambda e, x1=x1, a=a, s_=s_: e.tensor_tensor(out=a[:], in0=x1[:], in1=s_[:], op=ALU.mult), [s['xb'], tb, s['ab']], [s['ab']])
            P.add('dve', lambda e, x2=x2, b=b, c_=c_: e.tensor_tensor(out=b[:], in0=x2[:], in1=c_[:], op=ALU.mult), [s['xb'], tb, s['ab']], [s['ab']])
            P.add('pool', lambda e, a=a, b=b, o2=o2: e.tensor_tensor(out=o2[:], in0=a[:], in1=b[:], op=ALU.add), [s['ab']], [s['ob']])
            P.dma('pool', qkr[r0:r0 + 128, tr], o1[:], reads=[s['ob']])
            P.dma('pool', qkr[r0 + 128:r0 + 256, tr], o2[:], reads=[s['ob']])
    P.sb_reset(mark)


def ret_attn(P, C, qkr, vg, gn_g, gn_b, ytok, T, S):
    P.mark('ret_attn')
    nc = P.nc
    mark = P.sb_mark()
    NQB = S // 512
    NKC = S // 128
    lg = [float(np.log(np.float32(1.0) - np.float32(2.0) ** np.float32(-5.0 - h))) for h in range(8)]
    d0 = P.sb([128, 512], F32, 'd0'); d0b = Buf()
    P.add('pool', lambda e: e.iota(d0[:], [[1, 512]], base=0, channel_multiplier=-1, allow_small_or_imprecise_dtypes=True), [], [d0b])
    deltas = sorted({qb * 512 - kc * 128 for qb in range(NQB) for kc in range(NKC)})
    gam = {dl: P.sb([128, 512], BF16, 'gam') for dl in deltas}
    gamb = Buf()
    gtmp = [(P.sb([128, 512], F32, 'gt'), Buf()) for _ in range(2)]
    gg = P.sb([128, 4096], F32, 'gng'); gbt = P.sb([128, 4096], F32, 'gnb'); ggb = Buf()
    P.dma('sp', gg[:], bass.AP(gn_g.tensor, gn_g.offset, [[0, 128], [1, 4096]]), writes=[ggb])
    P.dma('sp', gbt[:], bass.AP(gn_b.tensor, gn_b.offset, [[0, 128], [1, 4096]]), writes=[ggb])
    qt = P.sb([128, 2, S], BF16, 'qt'); kt = P.sb([128, 2, S], BF16, 'kt'); qkb = Buf()
    vt = P.sb([128, NKC, 512], BF16, 'vt'); vb = Buf()
    vst = [(P.sb([128, 512], F32, 'vs'), Buf()) for _ in range(2)]
    pt = [(P.sb([128, 512], BF16, 'pt'), Buf()) for _ in range(3)]
    ep = [dict(o=P.sb([128, 512], F32, 'eo'), g=P.sb([128, 512], F32, 'eg'), st=P.sb([128, 6], F32, 'es'), mv=P.sb([128, 2], F32, 'em'),
               r=P.sb([128, 1], F32, 'er'), n=P.sb([128, 1], F32, 'en'), ob=Buf(), gb=Buf(), sb=Buf()) for _ in range(2)]
    obank = [(C.ps[i], C.psb[i]) for i in range(4)]
    sbank = [(C.ps[4 + i], C.psb[4 + i]) for i in range(3)]
    cnt = 0
    ec = 0
    for h in range(8):
        for gi, dl in enumerate(deltas):
            t, tbuf = gtmp[gi % 2]
            P.add('act', lambda e, t=t, dl=dl: e.activation(out=t[:], in_=d0[:], func=AF.Abs, bias=float(dl), scale=1.0), [d0b], [tbuf])
            P.add('act', lambda e, t=t, dl=dl, h=h: e.activation(out=gam[dl][:], in_=t[:], func=AF.Exp, scale=lg[h]), [tbuf], [gamb])
        for q in range(T // S):
            tb0 = q * S
            P.dma('sp', qt[:], qkr[h * 256:(h + 1) * 256, tb0:tb0 + S].rearrange("(c p) t -> p c t", p=128), writes=[qkb])
            P.dma('sp', kt[:], qkr[2048 + h * 256:2048 + (h + 1) * 256, tb0:tb0 + S].rearrange("(c p) t -> p c t", p=128), writes=[qkb])
            for kc in range(NKC):
                a, ab = vst[kc % 2]
                P.dma('sp', a[:], vg[tb0 + kc * 128:tb0 + (kc + 1) * 128, h * 512:(h + 1) * 512], writes=[ab])
                P.add('pool', lambda e, a=a, kc=kc: e.tensor_copy(out=vt[:, kc, :], in_=a[:]), [ab], [vb])
            for qb in range(NQB):
                def emit_s(kc, cnt):
                    sp_, spb = sbank[cnt % 3]
                    p_, pb = pt[cnt % 3]
                    for c in range(2):
                        P.add('pe', lambda e, sp_=sp_, c=c, kc=kc, qb=qb: e.matmul(sp_[:, :], kt[:, c, kc * 128:(kc + 1) * 128], qt[:, c, qb * 512:(qb + 1) * 512],
                                                                                  start=(c == 0), stop=(c == 1)), [qkb], [spb])
                    dl = qb * 512 - kc * 128
                    P.add('dve', lambda e, sp_=sp_, p_=p_, dl=dl: e.tensor_tensor(out=p_[:], in0=sp_[:, :], in1=gam[dl][:], op=ALU.mult), [spb, gamb], [pb])

                def emit_pv(kc, cnt):
                    p_, pb = pt[cnt % 3]
                    for sub in range(4):
                        ob_, obb = obank[sub]
                        P.add('pe', lambda e, ob_=ob_, p_=p_, sub=sub, kc=kc: e.matmul(ob_[:, :], p_[:, sub * 128:(sub + 1) * 128], vt[:, kc, :],
                                                                                     start=(kc == 0), stop=(kc == NKC - 1)), [pb, vb], [obb])
                emit_s(0, cnt)
                emit_s(1, cnt + 1)
                for kc in range(NKC):
                    if kc + 2 < NKC:
                        emit_s(kc + 2, cnt + 2)
                    emit_pv(kc, cnt)
                    cnt += 1
                for sub in range(4):
                    ob_, obb = obank[sub]
                    s = ep[ec % 2]; ec += 1
                    r = slice(tb0 + qb * 512 + sub * 128, tb0 + qb * 512 + (sub + 1) * 128)
                    o, g = s['o'], s['g']
                    P.dma('sp', g[:], vg[r, 4096 + h * 512:4096 + (h + 1) * 512], writes=[s['gb']])
                    P.add('act', lambda e, g=g: e.activation(out=g[:], in_=g[:], func=AF.Silu), [s['gb']], [s['gb']])
                    P.add('dve', lambda e, s=s, ob_=ob_: e.bn_stats(out=s['st'][:], in_=ob_[:, :]), [obb], [s['sb']])
                    P.add('dve', lambda e, s=s: e.bn_aggr(out=s['mv'][:], in_=s['st'][:]), [s['sb']], [s['sb']])
                    P.add('dve', lambda e, s=s: e.tensor_scalar(out=s['r'][:], in0=s['mv'][:, 1:2], scalar1=LN_EPS, scalar2=None, op0=ALU.add), [s['sb']], [s['sb']])
                    P.add('act', lambda e, s=s: e.activation(out=s['r'][:], in_=s['r'][:], func=AF.Sqrt), [s['sb']], [s['sb']])
                    P.add('dve', lambda e, s=s: e.reciprocal(out=s['r'][:], in_=s['r'][:]), [s['sb']], [s['sb']])
                    P.add('dve', lambda e, s=s: e.scalar_tensor_tensor(out=s['n'][:], in0=s['mv'][:, 0:1], scalar=-1.0, in1=s['r'][:], op0=ALU.mult, op1=ALU.mult), [s['sb']], [s['sb']])
                    P.add('act', lambda e, s=s, o=o, ob_=ob_: e.activation(out=o[:], in_=ob_[:, :], func=AF.Identity, bias=s['n'][:], scale=s['r'][:]), [obb, s['sb']], [s['ob']])
                    P.add('pool', lambda e, o=o, h=h: e.tensor_tensor(out=o[:], in0=o[:], in1=gg[:, h * 512:(h + 1) * 512], op=ALU.mult), [s['ob'], ggb], [s['ob']])
                    P.add('pool', lambda e, o=o, h=h: e.tensor_tensor(out=o[:], in0=o[:], in1=gbt[:, h * 512:(h + 1) * 512], op=ALU.add), [s['ob'], ggb], [s['ob']])
                    P.add('dve', lambda e, o=o, g=g: e.tensor_tensor(out=o[:], in0=o[:], in1=g[:], op=ALU.mult), [s['ob'], s['gb']], [s['ob']])
                    P.dma('pool', ytok[r, h * 512:(h + 1) * 512], o[:], reads=[s['ob']])
    P.sb_reset(mark)


NEG = -30000.0


def na_bias_build(P, C, nc, rpb, scr_fn, e):
    P.mark('na_bias_build')
    mark = P.sb_mark()
    biasS = scr_fn(f"na_bias{e}", [240, 4096], F32)
    G = P.sb([31, 240], F32, 'nG'); Gb = Buf()
    for h_ in range(16):
        P.dma('sp', G[:, h_ * 15:(h_ + 1) * 15], rpb[h_].rearrange("r d -> d r"), writes=[Gb], allow_slow_non_contiguous=True)
    M = P.sb([31, 4096], F32, 'nM'); Mb = Buf()
    P.add('pool', lambda e_: e_.iota(M[:].rearrange("p (a b) -> p a b", b=64), [[1, 64], [-1, 64]], base=15, channel_multiplier=-1,
                                     allow_small_or_imprecise_dtypes=True), [], [Mb])
    P.add('dve', lambda e_: e_.tensor_single_scalar(out=M[:], in_=M[:], scalar=0.0, op=ALU.is_equal), [Mb], [Mb])
    A = P.sb([120, 4096], F32, 'nA'); Q = P.sb([120, 4096], F32, 'nQ'); nb = Buf()
    P.add('pool', lambda e_: e_.iota(A[:].rearrange("p (a b) -> p a b", b=64), [[1, 64], [0, 64]], base=0, channel_multiplier=0,
                                     allow_small_or_imprecise_dtypes=True), [], [nb])
    P.add('pool', lambda e_: e_.iota(Q[:].rearrange("p (a b) -> p a b", b=64), [[0, 64], [1, 64]], base=0, channel_multiplier=0,
                                     allow_small_or_imprecise_dtypes=True), [nb], [nb])
    for (s1, op) in ((-8.0, ALU.add), (0.0, ALU.max), (48.0, ALU.min)):
        P.add('dve', lambda e_, s1=s1, op=op: e_.tensor_single_scalar(out=Q[:], in_=Q[:], scalar=s1, op=op), [nb], [nb])
    P.add('dve', lambda e_: e_.tensor_tensor(out=A[:], in0=A[:], in1=Q[:], op=ALU.subtract), [nb], [nb])
    P.add('dve', lambda e_: e_.tensor_single_scalar(out=Q[:], in_=A[:], scalar=0.0, op=ALU.is_ge), [nb], [nb])
    P.add('dve', lambda e_: e_.tensor_single_scalar(out=A[:], in_=A[:], scalar=15.0, op=ALU.is_le), [nb], [nb])
    P.add('dve', lambda e_: e_.tensor_tensor(out=A[:], in0=A[:], in1=Q[:], op=ALU.mult), [nb], [nb])
    P.add('dve', lambda e_: e_.tensor_single_scalar(out=A[:], in_=A[:], scalar=-1.0, op=ALU.add), [nb], [nb])
    P.add('dve', lambda e_: e_.tensor_single_scalar(out=A[:], in_=A[:], scalar=-NEG, op=ALU.mult), [nb], [nb])
    st = [(P.sb([120, 512], F32, 'nbs'), Buf()) for _ in range(2)]
    i = 0
    for half in range(2):
        for cb in range(8):
            ps, psb = next_ps(C)
            P.add('pe', lambda e_, ps=ps, half=half, cb=cb: e_.matmul(ps[0:120, :], G[:, half * 120:(half + 1) * 120], M[:, cb * 512:(cb + 1) * 512],
                                                                     start=True, stop=True), [Gb, Mb], [psb])
            a, ab = st[i % 2]; i += 1
            P.add('dve', lambda e_, ps=ps, a=a, cb=cb: e_.tensor_tensor(out=a[:], in0=ps[0:120, :], in1=A[:, cb * 512:(cb + 1) * 512], op=ALU.add), [psb, nb], [ab])
            P.dma('pool', biasS[half * 120:(half + 1) * 120, cb * 512:(cb + 1) * 512], a[:], reads=[ab])
    P.sb_reset(mark)
    return biasS


def na_attn(P, C, nc, biasS, hqk, hv, yab, T, S, stage=3):
    P.mark('na_attn')
    mark = P.sb_mark()
    rows = S // 64
    NCH = S // 128
    assert rows >= 8
    B2 = P.sb([128, 16, 14, 64], F32, 'nB2'); B2b = Buf()
    bv = biasS.rearrange("(h r) (k q) -> k h r q", r=15, q=64)
    for i2 in range(2):
        for h in range(16):
            P.dma('sp', B2[i2 * 64:(i2 + 1) * 64, h, :, :], bv[:, h, i2:i2 + 14, :], writes=[B2b])
    stg = P.sb([64, 2, S], F32, 'nstg'); stgb = Buf()
    qTb = P.sb([64, 2, S], BF16, 'nq'); kTb = P.sb([64, 2, S], BF16, 'nk'); qkb = Buf()
    vstg = P.sb([128, NCH, 128], F32, 'nvs'); vsb = Buf()
    Vt = P.sb([128, NCH, 2, 80], BF16, 'nV'); Vt2 = P.sb([128, NCH, 2, 80], BF16, 'nV2'); Vb = Buf()
    P.add('pool', lambda e_: e_.memset(Vt[:], 1.0), [], [Vb])
    P.add('pool', lambda e_: e_.memset(Vt2[:], 1.0), [], [Vb])
    sc = [(P.sb([128, 512], F32, 'nsc'), Buf()) for _ in range(3)]
    pT = [(P.sb([128, 512], BF16, 'npT'), Buf()) for _ in range(3)]
    yr = [(P.sb([64, 128], F32, 'nyr'), P.sb([64, 2], F32, 'nrc'), Buf()) for _ in range(2)]
    sbank = [(C.ps[0], C.psb[0]), (C.ps[1], C.psb[1]), (C.ps[4], C.psb[4])]
    obank = [(C.ps[2], C.psb[2]), (C.ps[3], C.psb[3])]
    it = 0
    for q in range(T // S):
        t0 = q * S
        for hp in range(8):
            P.dma('sp', stg[:], hqk[hp * 128:(hp + 1) * 128, t0:t0 + S].rearrange("(a d) t -> d a t", d=64), writes=[stgb])
            P.add('act', lambda e_: e_.mul(out=qTb[:], in_=stg[:], mul=0.125), [stgb], [qkb])
            P.dma('sp', stg[:], hqk[1024 + hp * 128:1024 + (hp + 1) * 128, t0:t0 + S].rearrange("(a d) t -> d a t", d=64), writes=[stgb])
            P.add('dve', lambda e_: e_.tensor_copy(out=kTb[:], in_=stg[:]), [stgb], [qkb])
            P.dma('sp', vstg[:], hv[t0:t0 + S, hp * 128:(hp + 1) * 128].rearrange("(c p) f -> p c f", p=128), writes=[vsb])
            P.add('pool', lambda e_: e_.tensor_copy(out=Vt[:, :, :, 0:64], in_=vstg[:].rearrange("p c (a d) -> p c a d", d=64)), [vsb], [Vb])
            P.dma('sp', vstg[:, 0:NCH - 1, :], hv[t0 + 64:t0 + S - 64, hp * 128:(hp + 1) * 128].rearrange("(c p) f -> p c f", p=128), writes=[vsb])
            P.add('pool', lambda e_: e_.tensor_copy(out=Vt2[:, 0:NCH - 1, :, 0:64], in_=vstg[:, 0:NCH - 1, :].rearrange("p c (a d) -> p c a d", d=64)), [vsb], [Vb])
            rlist = list(range(rows))

            def part_a(r, it):
                rs = min(max(r - 4, 0), rows - 8)
                ro0 = rs - r + 7
                sp_, spb = sbank[it % 3]
                s_, sb_ = sc[it % 3]
                p_, pb = pT[it % 3]
                for hh in range(2):
                    for c in range(4):
                        k0 = rs * 64 + c * 128
                        P.add('pe', lambda e_, sp_=sp_, hh=hh, c=c, k0=k0, r=r: e_.matmul(
                            sp_[:, (hh * 4 + c) * 64:(hh * 4 + c + 1) * 64], kTb[:, hh, k0:k0 + 128], qTb[:, hh, r * 64:(r + 1) * 64],
                            start=True, stop=True), [qkb], [spb])
                for hh in range(2):
                    h = 2 * hp + hh
                    P.add('dve', lambda e_, sp_=sp_, s_=s_, hh=hh, h=h, ro0=ro0: e_.tensor_tensor(
                        out=s_[:, hh * 256:(hh + 1) * 256].rearrange("p (c q) -> p c q", q=64),
                        in0=sp_[:, hh * 256:(hh + 1) * 256].rearrange("p (c q) -> p c q", q=64),
                        in1=B2[:, h, ro0:ro0 + 7:2, :], op=ALU.add), [spb, B2b], [sb_])
                P.add('act', lambda e_, s_=s_, p_=p_: e_.activation(out=p_[:], in_=s_[:], func=AF.Exp), [sb_], [pb])

            def part_b(r, it):
                rs = min(max(r - 4, 0), rows - 8)
                p_, pb = pT[it % 3]
                ob_, obb = obank[it % 2]
                y_, rc_, yb_ = yr[it % 2]
                for hh in range(2):
                    for c in range(4):
                        if rs % 2 == 0:
                            vap = Vt[:, rs // 2 + c, hh, 0:65]
                        else:
                            vap = Vt2[:, (rs - 1) // 2 + c, hh, 0:65]
                        P.add('pe', lambda e_, ob_=ob_, p_=p_, hh=hh, c=c, vap=vap: e_.matmul(
                            ob_[0:64, hh * 128:hh * 128 + 65], p_[:, (hh * 4 + c) * 64:(hh * 4 + c + 1) * 64], vap, start=(c == 0), stop=(c == 3)), [pb, Vb], [obb])
                for hh in range(2):
                    P.add('dve', lambda e_, ob_=ob_, rc_=rc_, hh=hh: e_.reciprocal(out=rc_[:, hh:hh + 1], in_=ob_[0:64, hh * 128 + 64:hh * 128 + 65]), [obb], [yb_])
                    P.add('dve', lambda e_, ob_=ob_, rc_=rc_, y_=y_, hh=hh: e_.tensor_scalar(
                        out=y_[:, hh * 64:(hh + 1) * 64], in0=ob_[0:64, hh * 128:hh * 128 + 64], scalar1=rc_[:, hh:hh + 1], scalar2=None, op0=ALU.mult), [obb, yb_], [yb_])
                P.dma('pool', yab[t0 + r * 64:t0 + (r + 1) * 64, 1024 + hp * 128:1024 + (hp + 1) * 128], y_[:], reads=[yb_])
            part_a(0, it)
            part_a(1, it + 1)
            for r in rlist:
                if r + 2 < rows:
                    part_a(r + 2, it + 2)
                part_b(r, it)
                it += 1
    P.sb_reset(mark)


A_GN_EPS = 64e-5
EHALF = float(np.exp(-0.5))


def bc_ap(t, ncols_src, n1, n2):
    return bass.AP(t, 0, [[ncols_src, t.shape[0]], [1, n1], [0, n2]])


def load_bc(P, row_ap, n, name):
    t = P.sb([128, n], F32, name); b = Buf(name)
    P.dma('sp', t[:], bass.AP(row_ap.tensor, row_ap.offset, [[0, 128], [1, n]]), writes=[b])
    return t, b


def rwkv_prep(P, C, nc, W, e, hA, PR, T, S):
    P.mark('rwkv_prep')
    mark = P.sb_mark()
    sh = [load_bc(P, W['ab_shift'][e, k], 3328, f'sh{k}') for k in range(3)]
    w0 = [load_bc(P, W['ab_w0'][e, d], 1024, f'w0{d}') for d in range(2)]
    a0 = [load_bc(P, W['ab_a0'][e, d], 1024, f'a0{d}') for d in range(2)]
    kk_ = load_bc(P, W['ab_k_k'][e], 1024, 'kk')
    ka_ = load_bc(P, W['ab_k_a'][e], 1024, 'ka')
    rk_ = load_bc(P, W['ab_r_k'][e].rearrange("h d -> (h d)"), 1024, 'rk')
    wst = P.sb([128, 2048], F32, 'wst'); wsb = Buf()
    wup = P.sb([64, 2, 1024], BF16, 'wup'); aup = P.sb([64, 2, 1024], BF16, 'aup'); gup = P.sb([128, 1024], BF16, 'gup'); lwb = Buf()
    P.dma('sp', wst[0:64, :].rearrange("p (d n) -> p d n", d=2), W['ab_w_up'][e].rearrange("d k n -> k d n"), writes=[wsb])
    P.add('dve', lambda e_: e_.tensor_copy(out=wup[:].rearrange("p d n -> p (d n)"), in_=wst[0:64, :]), [wsb], [lwb])
    P.dma('sp', wst[0:64, :].rearrange("p (d n) -> p d n", d=2), W['ab_a_up'][e].rearrange("d k n -> k d n"), writes=[wsb])
    P.add('dve', lambda e_: e_.tensor_copy(out=aup[:].rearrange("p d n -> p (d n)"), in_=wst[0:64, :]), [wsb], [lwb])
    P.dma('sp', wst[:, 0:1024], W['ab_g_up'][e], writes=[wsb])
    P.add('dve', lambda e_: e_.tensor_copy(out=gup[:], in_=wst[:, 0:1024]), [wsb], [lwb])
    hbufs = [(P.sb([128, 3328], F32, 'hp'), P.sb([128, 3328], F32, 'hc'), P.sb([128, 3328], F32, 'hn'), Buf(), Buf(), Buf()) for _ in range(2)]
    L = P.sb([128, 256], F32, 'L'); Lb = Buf()
    LT = P.sb([128, 3, 128], BF16, 'LT'); LTb = Buf()
    t1 = P.sb([128, 1024], F32, 't1'); t1b = Buf()
    t2 = P.sb([128, 1024], F32, 't2'); t2b = Buf()
    kap = P.sb([128, 1024], F32, 'kap'); kapb = Buf()
    ad = P.sb([128, 1024], F32, 'ad'); adb = Buf()
    o1 = [(P.sb([128, 1024], F32, 'o1'), Buf()) for _ in range(3)]
    sm = P.sb([128, 16], F32, 'sm'); smb = Buf()
    bon = P.sb([128, 16], F32, 'bon'); bonb = Buf()
    oi = [0]

    def outbuf():
        o = o1[oi[0] % 3]; oi[0] += 1
        return o
    def _tile(ti):
        hp, hc, hn, hpb, hcb, hnb = hbufs[ti % 2]
        t0 = ti * 128
        r = slice(t0, t0 + 128)
        first = (t0 % S == 0)
        lastt = ((t0 + 128) % S == 0)
        P.dma('sp', hc[:], hA[r, :], writes=[hcb])
        if first:
            P.add('pool', lambda e_: e_.memset(hp[0:1, :], 0.0), [], [hpb])
            P.dma('sp', hp[1:128, :], hA[t0:t0 + 127, :], writes=[hpb])
        else:
            P.dma('sp', hp[:], hA[t0 - 1:t0 + 127, :], writes=[hpb])
        if lastt:
            P.add('pool', lambda e_: e_.memset(hn[:], 0.0), [], [hnb])
            P.dma('sp', hn[0:127, :], hA[t0 + 1:t0 + 128, :], writes=[hnb])
        else:
            P.dma('sp', hn[:], hA[t0 + 1:t0 + 129, :], writes=[hnb])
        P.add('dve', lambda e_: e_.tensor_tensor(out=hc[:], in0=hc[:], in1=sh[1][0][:], op=ALU.mult), [hcb, sh[1][1]], [hcb])
        P.add('pool', lambda e_: e_.tensor_tensor(out=hp[:], in0=hp[:], in1=sh[0][0][:], op=ALU.mult), [hpb, sh[0][1]], [hpb])
        P.add('pool', lambda e_: e_.tensor_tensor(out=hn[:], in0=hn[:], in1=sh[2][0][:], op=ALU.mult), [hnb, sh[2][1]], [hnb])
        P.add('dve', lambda e_: e_.tensor_tensor(out=hc[:], in0=hc[:], in1=hp[:], op=ALU.add), [hcb, hpb], [hcb])
        P.add('dve', lambda e_: e_.tensor_tensor(out=hc[:], in0=hc[:], in1=hn[:], op=ALU.add), [hcb, hnb], [hcb])
        R_ = hc[:, 0:1024]; K_ = hc[:, 1024:2048]; V_ = hc[:, 2048:3072]
        P.dma('pool', PR['R'][r, :], R_, reads=[hcb])
        P.dma('pool', PR['V'][r, :], V_, reads=[hcb])
        P.add('act', lambda e_: e_.activation(out=L[:, 0:64], in_=hc[:, 3072:3136], func=AF.Tanh), [hcb], [Lb])
        P.add('act', lambda e_: e_.activation(out=L[:, 64:128], in_=hc[:, 3136:3200], func=AF.Identity), [hcb], [Lb])
        P.add('act', lambda e_: e_.activation(out=L[:, 128:256], in_=hc[:, 3200:3328], func=AF.Sigmoid), [hcb], [Lb])
        ps, psb = next_ps(C)
        P.add('pe', lambda e_, ps=ps: e_.transpose(ps[0:64, 0:128], L[:, 0:64], C.ident[:]), [Lb, C.b_ident], [psb])
        P.add('pe', lambda e_, ps=ps: e_.transpose(ps[0:64, 128:256], L[:, 64:128], C.ident[:]), [Lb, C.b_ident], [psb])
        P.add('pe', lambda e_, ps=ps: e_.transpose(ps[:, 256:384], L[:, 128:256], C.ident[:]), [Lb, C.b_ident], [psb])
        P.add('dve', lambda e_, ps=ps: e_.tensor_copy(out=LT[0:64, 0:2, :], in_=ps[0:64, 0:256].rearrange("p (a t) -> p a t", t=128)), [psb], [LTb])
        P.add('dve', lambda e_, ps=ps: e_.tensor_copy(out=LT[:, 2, :], in_=ps[:, 256:384]), [psb], [LTb])
        P.add('pool', lambda e_: e_.tensor_tensor(out=kap[:], in0=K_, in1=kk_[0][:], op=ALU.mult), [hcb, kk_[1]], [kapb])
        P.add('dve', lambda e_: e_.tensor_tensor(out=t1[:], in0=kap[:], in1=kap[:], op=ALU.mult), [kapb], [t1b])
        P.add('dve', lambda e_: e_.tensor_reduce(out=sm[:], in_=t1[:].rearrange("p (h d) -> p h d", d=64), axis=AX.X, op=ALU.add), [t1b], [smb])
        P.add('act', lambda e_: e_.activation(out=sm[:], in_=sm[:], func=AF.Sqrt), [smb], [smb])
        P.add('dve', lambda e_: e_.tensor_single_scalar(out=sm[:], in_=sm[:], scalar=1e-12, op=ALU.max), [smb], [smb])
        P.add('dve', lambda e_: e_.reciprocal(out=sm[:], in_=sm[:]), [smb], [smb])
        P.add('dve', lambda e_: e_.tensor_tensor(out=kap[:].rearrange("p (h d) -> p h d", d=64), in0=kap[:].rearrange("p (h d) -> p h d", d=64),
                                                 in1=bc_ap(sm, 16, 16, 64), op=ALU.mult), [kapb, smb], [kapb])
        P.dma('pool', PR['KAP'][r, :], kap[:], reads=[kapb])
        o, ob = outbuf()
        for hf in range(2):
            ps, psb = next_ps(C)
            P.add('pe', lambda e_, ps=ps, hf=hf: e_.matmul(ps[:, :], LT[:, 2, :], gup[:, hf * 512:(hf + 1) * 512], start=True, stop=True), [LTb, lwb], [psb])
            P.add('act', lambda e_, ps=ps, hf=hf, o=o: e_.activation(out=o[:, hf * 512:(hf + 1) * 512], in_=ps[:, :], func=AF.Identity), [psb], [ob])
        P.dma('pool', PR['GATE'][r, :], o[:], reads=[ob])
        for d in range(2):
            o, ob = outbuf()
            for hf in range(2):
                ps, psb = next_ps(C)
                P.add('pe', lambda e_, ps=ps, hf=hf, d=d: e_.matmul(ps[:, :], LT[0:64, 0, :], wup[:, d, hf * 512:(hf + 1) * 512], start=True, stop=True), [LTb, lwb], [psb])
                P.add('dve', lambda e_, ps=ps, hf=hf, d=d: e_.tensor_tensor(out=t1[:, hf * 512:(hf + 1) * 512], in0=ps[:, :], in1=w0[d][0][:, hf * 512:(hf + 1) * 512], op=ALU.add), [psb, w0[d][1]], [t1b])
            P.add('act', lambda e_: e_.activation(out=t1[:], in_=t1[:], func=AF.Sigmoid), [t1b], [t1b])
            P.add('act', lambda e_, o=o: e_.mul(out=o[:], in_=t1[:], mul=-EHALF), [t1b], [ob])
            P.dma('pool', PR[f'LW{d}'][r, :], o[:], reads=[ob])
            for hf in range(2):
                ps, psb = next_ps(C)
                P.add('pe', lambda e_, ps=ps, hf=hf, d=d: e_.matmul(ps[:, :], LT[0:64, 1, :], aup[:, d, hf * 512:(hf + 1) * 512], start=True, stop=True), [LTb, lwb], [psb])
                P.add('dve', lambda e_, ps=ps, hf=hf, d=d: e_.tensor_tensor(out=ad[:, hf * 512:(hf + 1) * 512], in0=ps[:, :], in1=a0[d][0][:, hf * 512:(hf + 1) * 512], op=ALU.add), [psb, a0[d][1]], [adb])
            P.add('act', lambda e_: e_.activation(out=ad[:], in_=ad[:], func=AF.Sigmoid), [adb], [adb])
            o, ob = outbuf()
            P.add('pool', lambda e_, o=o: e_.tensor_tensor(out=o[:], in0=kap[:], in1=ad[:], op=ALU.mult), [kapb, adb], [ob])
            P.dma('pool', PR[f'B{d}'][r, :], o[:], reads=[ob])
            P.add('dve', lambda e_: e_.scalar_tensor_tensor(out=t2[:], in0=ad[:], scalar=-1.0, in1=ka_[0][:], op0=ALU.add, op1=ALU.mult), [adb, ka_[1]], [t2b])
            o, ob = outbuf()
            P.add('dve', lambda e_, o=o: e_.scalar_tensor_tensor(out=o[:], in0=t2[:], scalar=1.0, in1=K_, op0=ALU.add, op1=ALU.mult), [t2b, hcb], [ob])
            P.dma('pool', PR[f'KD{d}'][r, :], o[:], reads=[ob])
            if d == 0:
                P.add('pool', lambda e_, o=o: e_.tensor_tensor(out=t2[:], in0=o[:], in1=R_, op=ALU.mult), [ob, hcb, t2b], [t2b])
                P.add('pool', lambda e_: e_.tensor_tensor(out=t2[:], in0=t2[:], in1=rk_[0][:], op=ALU.mult), [t2b, rk_[1]], [t2b])
                P.add('dve', lambda e_: e_.tensor_reduce(out=bon[:], in_=t2[:].rearrange("p (h d) -> p h d", d=64), axis=AX.X, op=ALU.add), [t2b], [bonb])
                P.dma('pool', PR['BON'][r, :], bon[:], reads=[bonb])
    for ti in range(T // 128):
        _tile(ti)
    P.sb_reset(mark)


def rwkv_consts(P, C, d):
    K = {}
    D = P.sb([128, 128], F32, 'cD'); Db = Buf()
    P.add('pool', lambda e_: e_.iota(D[:], [[1, 128]], base=0, channel_multiplier=-1, allow_small_or_imprecise_dtypes=True), [], [Db])
    PI = P.sb([128, 128], F32, 'cPI'); PIb = Buf()
    P.add('pool', lambda e_: e_.iota(PI[:], [[0, 128]], base=0, channel_multiplier=1, allow_small_or_imprecise_dtypes=True), [], [PIb])
    cb = Buf()

    def mk(name, src, srcb, scalar, op):
        t = P.sb([128, 128], F32, name)
        P.add('dve', lambda e_: e_.tensor_single_scalar(out=t[:], in_=src[:], scalar=scalar, op=op), [srcb], [cb])
        return t
    TRI = mk('cTRI', D, Db, 0.0, ALU.is_ge if d == 0 else ALU.is_le)
    TRIS = mk('cTRIS', D, Db, 0.0, ALU.is_gt if d == 0 else ALU.is_lt)
    HALF = mk('cHALF', PI, PIb, 63.5, ALU.is_lt if d == 0 else ALU.is_gt)
    A1 = P.sb([128, 128], F32, 'cA1'); A2 = P.sb([128, 128], F32, 'cA2'); A5 = P.sb([128, 128], F32, 'cA5')
    P.add('dve', lambda e_: e_.tensor_tensor(out=A1[:], in0=TRIS[:], in1=HALF[:], op=ALU.subtract), [cb], [cb])
    P.add('dve', lambda e_: e_.tensor_tensor(out=A2[:], in0=TRI[:], in1=HALF[:], op=ALU.subtract), [cb], [cb])
    P.add('dve', lambda e_: e_.tensor_scalar(out=A5[:], in0=TRI[:], scalar1=-1.0, scalar2=None, op0=ALU.mult), [cb], [cb])
    P.add('dve', lambda e_: e_.tensor_single_scalar(out=A5[:], in_=A5[:], scalar=1.0, op=ALU.add), [cb], [cb])
    K['A1'], K['A2'], K['A3'], K['A4'], K['A5'] = A1, A2, TRIS, TRI, A5
    ones = P.sb([128, 1], F32, 'cone')
    P.add('dve', lambda e_: e_.memset(ones[:], 1.0), [], [cb])
    K['ones'] = ones
    D4 = P.sb([128, 4, 128], F32, 'cD4'); D4b = Buf()
    P.add('pool', lambda e_: e_.iota(D4[:], [[0, 4], [1, 128]], base=0, channel_multiplier=-1, allow_small_or_imprecise_dtypes=True), [], [D4b])

    def mk4(name, op, neg=False):
        t = P.sb([128, 4, 128], F32, name)
        P.add('dve', lambda e_: e_.tensor_single_scalar(out=t[:], in_=D4[:], scalar=0.0, op=op), [D4b], [cb])
        if neg:
            P.add('dve', lambda e_: e_.tensor_single_scalar(out=t[:], in_=t[:], scalar=-1.0, op=ALU.mult), [cb], [cb])
        return t
    K['mT_strict'] = mk4('mTs', ALU.is_gt if d == 0 else ALU.is_lt)
    K['mT_M'] = mk4('mTm', ALU.is_ge, False) if d == 0 else K['mT_strict']
    K['mT_negstrict'] = mk4('mTn', ALU.is_gt if d == 0 else ALU.is_lt, True)
    K['mN_negstrict'] = mk4('mNn', ALU.is_lt if d == 0 else ALU.is_gt, True)
    K['b'] = cb
    return K


def rwkv_scan(P, C, nc, W, e, PR, d, YF, yab, T, S):
    P.mark('rwkv_scan')
    mark = P.sb_mark()
    K = rwkv_consts(P, C, d)
    cb = K['b']
    NCK = S // 128
    f32 = lambda n, nm: P.sb([128, n], F32, nm)
    inp = {k: (f32(1024, 'i' + k), Buf()) for k in ('R', 'V', 'KAP', 'LW', 'B', 'KD')}
    src = {'R': PR['R'], 'V': PR['V'], 'KAP': PR['KAP'], 'LW': PR[f'LW{d}'], 'B': PR[f'B{d}'], 'KD': PR[f'KD{d}']}
    E = [(f32(1024, 'E'), Buf()) for _ in range(3)]
    prod = {k: (f32(1024, 'p' + k), Buf()) for k in ('Kr', 'Br', 'Kdr', 'Rr', 'Ra')}
    tm = {k: (P.sb([128, 1024], BF16, 'b' + k), Buf()) for k in ('Bend', 'Kend', 'V')}
    tr = {k: (P.sb([64, 16, 128], BF16, 't' + k), Buf()) for k in ('Kr', 'Br', 'Kdr', 'Rr', 'Ra')}
    mat = {k: (P.sb([128, 16, 128], BF16, 'm' + k), Buf()) for k in ('P0', 'P0T', 'LkT', 'MrbT', 'MrkT', 'Pa', 'PaT', 'Pb', 'PbT')}
    X32 = P.sb([128, 16, 128], F32, 'X32'); X32b = Buf()
    Xbf = P.sb([128, 16, 128], BF16, 'Xbf'); Xbfb = Buf()
    WT = P.sb([64, 16, 128], BF16, 'WT'); WTb = Buf()
    Ubf = P.sb([128, 16, 64], BF16, 'Ubf'); Ubb = Buf()
    Z32 = P.sb([64, 16, 64], F32, 'Z32'); Zbf = P.sb([64, 16, 64], BF16, 'Zbf'); Zb = Buf()
    ztmp = P.sb([64, 16, 64], F32, 'ztmp'); ztb = Buf()
    cC = P.sb([64, 16], F32, 'cC'); cCb = Buf()
    Y = f32(1024, 'Y'); Yb = Buf()
    if d == 1:
        YFt = f32(1024, 'YFt'); YFb = Buf()
        gat = f32(1024, 'gat'); gatb = Buf()
        bon = P.sb([128, 16], F32, 'bon'); bonb = Buf()
        gng = load_bc(P, W['ab_gn_g'][e], 1024, 'gng'); gnb = load_bc(P, W['ab_gn_b'][e], 1024, 'gnb')
        s1 = P.sb([128, 16], F32, 's1'); s2 = P.sb([128, 16], F32, 's2'); stb = Buf()
        sq = f32(1024, 'sq'); sqb = Buf()
    evi = [0]

    def evac(out_ap, in_ap, reads, writes):
        if evi[0] % 2:
            P.add('act', lambda e_: e_.activation(out=out_ap, in_=in_ap, func=AF.Identity), reads, writes)
        else:
            P.add('dve', lambda e_: e_.tensor_copy(out=out_ap, in_=in_ap), reads, writes)
        evi[0] += 1

    h3 = lambda t: t[:].rearrange("p (h d) -> p h d", d=64)
    for q in range(T // S):
        P.add('pool', lambda e_: e_.memset(Z32[:], 0.0), [], [Zb])
        P.add('pool', lambda e_: e_.memset(Zbf[:], 0.0), [], [Zb])
        order = range(NCK) if d == 0 else range(NCK - 1, -1, -1)
        for ck in order:
            t0 = q * S + ck * 128
            r = slice(t0, t0 + 128)
            for k in inp:
                P.dma('sp', inp[k][0][:], src[k][r, :], writes=[inp[k][1]])
            LW, LWb = inp['LW']
            ei = [0]

            def expo(Akey, scale):
                t, tb = E[ei[0] % 3]; ei[0] += 1
                for hf in range(2):
                    ps, psb = next_ps(C)
                    P.add('pe', lambda e_, ps=ps, hf=hf: e_.matmul(ps[:, :], K[Akey][:], LW[:, hf * 512:(hf + 1) * 512], start=True, stop=True), [cb, LWb], [psb])
                    P.add('act', lambda e_, ps=ps, hf=hf, t=t: e_.activation(out=t[:, hf * 512:(hf + 1) * 512], in_=ps[:, :], func=AF.Exp, scale=scale), [psb], [tb])
                return t, tb

            def mul(out, outb, a, ab, b, bb, eng):
                P.add(eng, lambda e_: e_.tensor_tensor(out=out[:], in0=a[:], in1=b[:], op=ALU.mult), [ab, bb], [outb])
            E1, E1b = expo('A1', 1.0)
            mul(prod['Kr'][0], prod['Kr'][1], inp['KAP'][0], inp['KAP'][1], E1, E1b, 'dve')
            if d == 1:
                mul(prod['Rr'][0], prod['Rr'][1], inp['R'][0], inp['R'][1], E1, E1b, 'pool')
            E2, E2b = expo('A2', -1.0)
            mul(prod['Br'][0], prod['Br'][1], inp['B'][0], inp['B'][1], E2, E2b, 'dve')
            mul(prod['Kdr'][0], prod['Kdr'][1], inp['KD'][0], inp['KD'][1], E2, E2b, 'pool')
            if d == 0:
                E2p, E2pb = expo('A2', 1.0)
                mul(prod['Rr'][0], prod['Rr'][1], inp['R'][0], inp['R'][1], E2p, E2pb, 'dve')
            E3, E3b = expo('A3', 1.0)
            P.add('dve', lambda e_, E3=E3: e_.tensor_tensor(out=X32[:, :, 64:128], in0=h3(inp['KAP'][0]), in1=h3(E3), op=ALU.mult), [inp['KAP'][1], E3b], [X32b])
            P.add('pool', lambda e_: e_.tensor_copy(out=Xbf[:, :, 64:128], in_=X32[:, :, 64:128]), [X32b], [Xbfb])
            if d == 1:
                mul(prod['Ra'][0], prod['Ra'][1], inp['R'][0], inp['R'][1], E3, E3b, 'pool')
            else:
                E4, E4b = expo('A4', 1.0)
                mul(prod['Ra'][0], prod['Ra'][1], inp['R'][0], inp['R'][1], E4, E4b, 'pool')
            E5, E5b = expo('A5', 1.0)
            mul(tm['Bend'][0], tm['Bend'][1], inp['B'][0], inp['B'][1], E5, E5b, 'dve')
            mul(tm['Kend'][0], tm['Kend'][1], inp['KD'][0], inp['KD'][1], E5, E5b, 'pool')
            P.add('pool', lambda e_: e_.tensor_copy(out=tm['V'][0][:], in_=inp['V'][0][:]), [inp['V'][1]], [tm['V'][1]])
            ps, psb = next_ps(C)
            for h in range(16):
                P.add('pe', lambda e_, ps=ps, h=h: e_.matmul(ps[0:64, h:h + 1], LW[:, h * 64:(h + 1) * 64], K['ones'][:], start=True, stop=True), [LWb, cb], [psb])
            P.add('act', lambda e_, ps=ps: e_.activation(out=cC[:], in_=ps[0:64, 0:16], func=AF.Exp), [psb], [cCb])
            for k in ('Kr', 'Br', 'Kdr', 'Rr', 'Ra'):
                pa, pab = prod[k]
                ta, tab = tr[k]
                for g in range(4):
                    ps, psb = next_ps(C)
                    for hi in range(4):
                        h = 4 * g + hi
                        P.add('pe', lambda e_, ps=ps, hi=hi, h=h, pa=pa: e_.transpose(ps[0:64, hi * 128:(hi + 1) * 128], pa[:, h * 64:(h + 1) * 64], C.ident[:]),
                              [pab, C.b_ident], [psb])
                    evac(ta[:, 4 * g:4 * g + 4, :], ps[0:64, :].rearrange("p (a t) -> p a t", t=128), [psb], [tab])
            def score(dst, lk, rk_, mask):
                da, dab = mat[dst]
                for g in range(4):
                    ps, psb = next_ps(C)
                    for hi in range(4):
                        h = 4 * g + hi
                        P.add('pe', lambda e_, ps=ps, hi=hi, h=h: e_.matmul(ps[:, hi * 128:(hi + 1) * 128], tr[lk][0][:, h, :], tr[rk_][0][:, h, :], start=True, stop=True),
                              [tr[lk][1], tr[rk_][1]], [psb])
                    P.add('dve' if g % 2 else 'pool' if False else 'dve', lambda e_, ps=ps, g=g, da=da: e_.tensor_tensor(
                        out=da[:, 4 * g:4 * g + 4, :], in0=ps[:, :].rearrange("p (a t) -> p a t", t=128), in1=K[mask][:], op=ALU.mult), [psb, cb], [dab])
            score('P0', 'Kr', 'Br', 'mN_negstrict')
            score('P0T', 'Br', 'Kr', 'mT_negstrict')
            score('LkT', 'Kdr', 'Kr', 'mT_strict')
            score('MrbT', 'Br', 'Rr', 'mT_M')
            score('MrkT', 'Kdr', 'Rr', 'mT_M')
            for g in range(2):
                ps, psb = next_ps(C)
                for hi in range(8):
                    h = 8 * g + hi
                    P.add('pe', lambda e_, ps=ps, hi=hi, h=h: e_.matmul(ps[:, hi * 64:(hi + 1) * 64], mat['LkT'][0][:, h, :], tm['V'][0][:, h * 64:(h + 1) * 64], start=True, stop=True),
                          [mat['LkT'][1], tm['V'][1]], [psb])
                P.add('dve', lambda e_, ps=ps, g=g: e_.tensor_copy(out=X32[:, 8 * g:8 * g + 8, 0:64], in_=ps[:, :].rearrange("p (a i) -> p a i", i=64)), [psb], [X32b])
                P.add('act', lambda e_, ps=ps, g=g: e_.activation(out=Xbf[:, 8 * g:8 * g + 8, 0:64], in_=ps[:, :].rearrange("p (a i) -> p a i", i=64), func=AF.Identity), [psb], [Xbfb])
            cur, curT = 'P0', 'P0T'
            nxt = [('Pa', 'PaT'), ('Pb', 'PbT')]
            for lev in range(7):
                for g in range(4):
                    ps, psb = next_ps(C)
                    for hi in range(4):
                        h = 4 * g + hi
                        P.add('pe', lambda e_, ps=ps, hi=hi, h=h, curT=curT: e_.matmul(ps[:, hi * 128:(hi + 1) * 128], mat[curT][0][:, h, :], Xbf[:, h, :], start=True, stop=True),
                              [mat[curT][1], Xbfb], [psb])
                    P.add('dve', lambda e_, ps=ps, g=g: e_.tensor_tensor(out=X32[:, 4 * g:4 * g + 4, :], in0=X32[:, 4 * g:4 * g + 4, :],
                                                                        in1=ps[:, :].rearrange("p (a t) -> p a t", t=128), op=ALU.add), [psb, X32b], [X32b])
                for g in range(4):
                    P.add('act', lambda e_, g=g: e_.activation(out=Xbf[:, 4 * g:4 * g + 4, :], in_=X32[:, 4 * g:4 * g + 4, :], func=AF.Identity), [X32b], [Xbfb])
                if lev < 6:
                    n, nT = nxt[lev % 2]
                    for g in range(4):
                        psA, psAb = next_ps(C)
                        psB, psBb = next_ps(C)
                        for hi in range(4):
                            h = 4 * g + hi
                            P.add('pe', lambda e_, psA=psA, hi=hi, h=h, cur=cur, curT=curT: e_.matmul(psA[:, hi * 128:(hi + 1) * 128], mat[curT][0][:, h, :], mat[cur][0][:, h, :], start=True, stop=True),
                                  [mat[cur][1], mat[curT][1]], [psAb])
                            P.add('pe', lambda e_, psB=psB, hi=hi, h=h, cur=cur, curT=curT: e_.matmul(psB[:, hi * 128:(hi + 1) * 128], mat[cur][0][:, h, :], mat[curT][0][:, h, :], start=True, stop=True),
                                  [mat[cur][1], mat[curT][1]], [psBb])
                        evac(mat[n][0][:, 4 * g:4 * g + 4, :], psA[:, :].rearrange("p (a t) -> p a t", t=128), [psAb], [mat[n][1]])
                        evac(mat[nT][0][:, 4 * g:4 * g + 4, :], psB[:, :].rearrange("p (a t) -> p a t", t=128), [psBb], [mat[nT][1]])
                    cur, curT = n, nT
            for g in range(4):
                ps, psb = next_ps(C)
                for hi in range(4):
                    h = 4 * g + hi
                    P.add('pe', lambda e_, ps=ps, hi=hi, h=h: e_.transpose(ps[0:64, hi * 128:(hi + 1) * 128], X32[:, h, 64:128], C.ident[:]), [X32b, C.b_ident], [psb])
                evac(WT[:, 4 * g:4 * g + 4, :], ps[0:64, :].rearrange("p (a t) -> p a t", t=128), [psb], [WTb])
            for g in range(2):
                ps, psb = next_ps(C)
                for hi in range(8):
                    h = 8 * g + hi
                    P.add('pe', lambda e_, ps=ps, hi=hi, h=h: e_.matmul(ps[:, hi * 64:(hi + 1) * 64], WT[:, h, :], Zbf[:, h, :], start=True, stop=True), [WTb, Zb], [psb])
                P.add('dve', lambda e_, ps=ps, g=g: e_.scalar_tensor_tensor(out=Ubf[:, 8 * g:8 * g + 8, :], in0=ps[:, :].rearrange("p (a i) -> p a i", i=64), scalar=-1.0,
                                                                          in1=X32[:, 8 * g:8 * g + 8, 0:64], op0=ALU.mult, op1=ALU.subtract), [psb, X32b], [Ubb])
            for g in range(2):
                ps, psb = next_ps(C)
                for hi in range(8):
                    h = 8 * g + hi
                    o = ps[:, hi * 64:(hi + 1) * 64]
                    P.add('pe', lambda e_, o=o, h=h: e_.matmul(o, tr['Ra'][0][:, h, :], Zbf[:, h, :], start=True, stop=False), [tr['Ra'][1], Zb], [psb])
                    P.add('pe', lambda e_, o=o, h=h: e_.matmul(o, mat['MrbT'][0][:, h, :], Ubf[:, h, :], start=False, stop=False), [mat['MrbT'][1], Ubb], [psb])
                    P.add('pe', lambda e_, o=o, h=h: e_.matmul(o, mat['MrkT'][0][:, h, :], tm['V'][0][:, h * 64:(h + 1) * 64], start=False, stop=True), [mat['MrkT'][1], tm['V'][1]], [psb])
                evac(Y[:, g * 512:(g + 1) * 512], ps[:, :], [psb], [Yb])
            for g in range(2):
                ps, psb = next_ps(C)
                for hi in range(8):
                    h = 8 * g + hi
                    o = ps[0:64, hi * 64:(hi + 1) * 64]
                    P.add('pe', lambda e_, o=o, h=h: e_.matmul(o, tm['Bend'][0][:, h * 64:(h + 1) * 64], Ubf[:, h, :], start=True, stop=False), [tm['Bend'][1], Ubb], [psb])
                    P.add('pe', lambda e_, o=o, h=h: e_.matmul(o, tm['Kend'][0][:, h * 64:(h + 1) * 64], tm['V'][0][:, h * 64:(h + 1) * 64], start=False, stop=True), [tm['Kend'][1], tm['V'][1]], [psb])
                P.add('dve', lambda e_, g=g: e_.tensor_tensor(out=ztmp[:, 8 * g:8 * g + 8, :], in0=Z32[:, 8 * g:8 * g + 8, :],
                                                             in1=bass.AP(cC, 8 * g, [[16, 64], [1, 8], [0, 64]]), op=ALU.mult), [Zb, cCb], [ztb])
                P.add('dve', lambda e_, ps=ps, g=g: e_.tensor_tensor(out=Z32[:, 8 * g:8 * g + 8, :], in0=ztmp[:, 8 * g:8 * g + 8, :],
                                                                    in1=ps[0:64, :].rearrange("p (a i) -> p a i", i=64), op=ALU.add), [ztb, psb], [Zb])
            P.add('act', lambda e_: e_.activation(out=Zbf[:], in_=Z32[:], func=AF.Identity), [Zb], [Zb])
            if d == 0:
                P.dma('pool', YF[r, :], Y[:], reads=[Yb])
            else:
                P.dma('sp', YFt[:], YF[r, :], writes=[YFb])
                P.dma('sp', gat[:], PR['GATE'][r, :], writes=[gatb])
                P.dma('sp', bon[:], PR['BON'][r, :], writes=[bonb])
                P.add('dve', lambda e_: e_.tensor_tensor(out=Y[:], in0=Y[:], in1=YFt[:], op=ALU.add), [Yb, YFb], [Yb])
                P.add('dve', lambda e_: e_.tensor_reduce(out=s1[:], in_=h3(Y), axis=AX.X, op=ALU.add), [Yb], [stb])
                P.add('pool', lambda e_: e_.tensor_tensor(out=sq[:], in0=Y[:], in1=Y[:], op=ALU.mult), [Yb], [sqb])
                P.add('dve', lambda e_: e_.tensor_reduce(out=s2[:], in_=h3(sq), axis=AX.X, op=ALU.add), [sqb], [stb])
                P.add('dve', lambda e_: e_.tensor_single_scalar(out=s1[:], in_=s1[:], scalar=1.0 / 64, op=ALU.mult), [stb], [stb])
                P.add('dve', lambda e_: e_.tensor_single_scalar(out=s2[:], in_=s2[:], scalar=1.0 / 64, op=ALU.mult), [stb], [stb])
                P.add('dve', lambda e_: e_.tensor_tensor(out=sq[:, 0:16], in0=s1[:], in1=s1[:], op=ALU.mult), [stb, sqb], [sqb])
                P.add('dve', lambda e_: e_.tensor_tensor(out=s2[:], in0=s2[:], in1=sq[:, 0:16], op=ALU.subtract), [stb, sqb], [stb])
                P.add('dve', lambda e_: e_.tensor_single_scalar(out=s2[:], in_=s2[:], scalar=A_GN_EPS, op=ALU.add), [stb], [stb])
                P.add('act', lambda e_: e_.activation(out=s2[:], in_=s2[:], func=AF.Sqrt), [stb], [stb])
                P.add('dve', lambda e_: e_.reciprocal(out=s2[:], in_=s2[:]), [stb], [stb])
                P.add('dve', lambda e_: e_.tensor_tensor(out=h3(Y), in0=h3(Y), in1=bc_ap(s1, 16, 16, 64), op=ALU.subtract), [Yb, stb], [Yb])
                P.add('dve', lambda e_: e_.tensor_tensor(out=h3(Y), in0=h3(Y), in1=bc_ap(s2, 16, 16, 64), op=ALU.mult), [Yb, stb], [Yb])
                P.add('pool', lambda e_: e_.tensor_tensor(out=Y[:], in0=Y[:], in1=gng[0][:], op=ALU.mult), [Yb, gng[1]], [Yb])
                P.add('pool', lambda e_: e_.tensor_tensor(out=Y[:], in0=Y[:], in1=gnb[0][:], op=ALU.add), [Yb, gnb[1]], [Yb])
                P.add('dve', lambda e_: e_.tensor_tensor(out=h3(sq), in0=h3(inp['V'][0]), in1=bc_ap(bon, 16, 16, 64), op=ALU.mult), [inp['V'][1], bonb, sqb], [sqb])
                P.add('dve', lambda e_: e_.tensor_tensor(out=Y[:], in0=Y[:], in1=sq[:], op=ALU.add), [Yb, sqb], [Yb])
                P.add('dve', lambda e_: e_.tensor_tensor(out=Y[:], in0=Y[:], in1=gat[:], op=ALU.mult), [Yb, gatb], [Yb])
                P.dma('pool', yab[r, 0:1024], Y[:], reads=[Yb])
    P.sb_reset(mark)


WSPEC = [
    ('ab_w_in', (2, 2048, 6400)), ('ab_shift', (2, 3, 3328)), ('ab_w0', (2, 2, 1024)), ('ab_w_up', (2, 2, 64, 1024)),
    ('ab_a0', (2, 2, 1024)), ('ab_a_up', (2, 2, 64, 1024)), ('ab_g_up', (2, 128, 1024)), ('ab_k_k', (2, 1024)),
    ('ab_k_a', (2, 1024)), ('ab_r_k', (2, 16, 64)), ('ab_gn_g', (2, 1024)), ('ab_gn_b', (2, 1024)),
    ('ab_rpb', (2, 16, 15, 31)), ('ab_w_out', (2, 2048, 2048)), ('c_w_in', (2, 2048, 12288)), ('c_gn_g', (2, 4096)),
    ('c_gn_b', (2, 4096)), ('c_w_out', (2, 4096, 2048)), ('ln1_g', (4, 2048)), ('ln1_b', (4, 2048)),
    ('ffn_w_up', (4, 2048, 11008)), ('ffn_conv', (4, 3, 5504)), ('ffn_conv_b', (4, 5504)), ('ffn_w_down', (4, 5504, 2048)),
    ('ln2_g', (4, 2048)), ('ln2_b', (4, 2048)),
]
BIGW = ['ab_w_in', 'ab_w_out', 'c_w_in', 'c_w_out', 'ffn_w_up', 'ffn_w_down']


def build(NSEQ, S, layers, debug_outs=()):
    T = NSEQ * S
    nc = bass.Bass("TRN2", target_bir_lowering=False)
    W = {}
    for name, shp in WSPEC:
        W[name] = nc.dram_tensor(name, list(shp), F32, kind="ExternalInput").ap()
    x_in = nc.dram_tensor("x", [T, D], F32, kind="ExternalInput").ap()
    y_out = nc.dram_tensor("y", [T, D], F32, kind="ExternalOutput").ap()

    def scr(name, shape, dt):
        kind = "ExternalOutput" if name in debug_outs else "Internal"
        return nc.dram_tensor(name, shape, dt, kind=kind).ap()
    WB = {}
    for name in BIGW:
        shp = dict(WSPEC)[name]
        WB[name] = scr(name + "_b", list(shp), BF16)
    xs = [scr("xs0", [T, D], F32), scr("xs1", [T, D], F32)]
    xTs = [scr("xT0", [D, T], BF16), scr("xT1", [D, T], BF16)]
    hTs = [scr(f"hT{q}", [11008, S], F32) for q in range(NSEQ)]
    vgs = [scr(f"vg{q}", [S, 8192], F32) for q in range(NSEQ)]
    gT = scr("gT", [5504, T], BF16)
    qkr = scr("qkr", [4096, T], BF16)
    ytok = scr("ytok", [T, 4096], F32)
    yT = scr("yT", [4096, T], BF16)
    mix = scr("mix", [T, D], F32)

    P = Prog(nc); C = Ctx()
    setup_consts(P, C)
    TT = min(512, S)
    for l in layers:
        e = l // 2
        if l % 2 == 0:
            cast_weight(P, C, W['ab_w_in'][e], WB['ab_w_in'][e], 2048, 6400)
            cast_weight(P, C, W['ab_w_out'][e], WB['ab_w_out'][e], 2048, 2048)
        else:
            cast_weight(P, C, W['c_w_in'][e], WB['c_w_in'][e], 2048, 12288)
            cast_weight(P, C, W['c_w_out'][e], WB['c_w_out'][e], 4096, 2048)
        cast_weight(P, C, W['ffn_w_up'][l], WB['ffn_w_up'][l], 2048, 11008)
        cast_weight(P, C, W['ffn_w_down'][l], WB['ffn_w_down'][l], 5504, 2048)
    xprep(P, C, x_in, xTs[0], T)
    xcur, xTcur, pp = x_in, xTs[0], 0

    def G(xT_, w_, K, N, mode, out, n_off=0, TG=T):
        m = P.sb_mark()
        gemm(P, C, xT_, w_, K, TG, N, mode, StoreEpi(P, C, out, mode), TT=TT, n_off=n_off)
        P.sb_reset(m)

    def GLN(xT_, w_, K, xin, g_row, b_row, xout, xTout, KS=1):
        m = P.sb_mark()
        C.ps_pool = (6, 0)
        gemm(P, C, xT_, w_, K, T, 2048, 'tok', LNEpi(P, C, xin, g_row, b_row, xout, xTout), TT=TT, KS=KS)
        C.ps_pool = (8, 0)
        P.sb_reset(m)

    for li, l in enumerate(layers):
        e = l // 2
        if l % 2 == 0:
            yabT_ = mixer_ab(P, C, nc, W, WB, e, xTcur, mix, T, S, NSEQ, scr_fn=scr, G=G)
        else:
            for q in range(NSEQ):
                sq = slice(q * S, (q + 1) * S)
                G(xTcur[:, sq], WB['c_w_in'][e], 2048, 4096, 'feat', hTs[q][0:4096, :], TG=S)
                G(xTcur[:, sq], WB['c_w_in'][e], 2048, 8192, 'tok', vgs[q], n_off=4096, TG=S)
                ret_rotary(P, C, hTs[q][0:4096, :], qkr[:, sq], S, S)
                ret_attn(P, C, qkr[:, sq], vgs[q], W['c_gn_g'][e], W['c_gn_b'][e], ytok[sq, :], S, S)
            xprep_n(P, C, ytok, yT, T, 4096)
        x1, x1T = xs[pp], xTs[1 - pp]
        if l % 2 == 0:
            GLN(yabT_, WB['ab_w_out'][e], 2048, xcur, W['ln1_g'][l], W['ln1_b'][l], x1, x1T)
        else:
            GLN(yT, WB['c_w_out'][e], 4096, xcur, W['ln1_g'][l], W['ln1_b'][l], x1, x1T)
        for q in range(NSEQ):
            sq = slice(q * S, (q + 1) * S)
            G(x1T[:, sq], WB['ffn_w_up'][l], 2048, 11008, 'feat', hTs[q], TG=S)
            ffn_mid(P, C, hTs[q], W['ffn_conv'][l], W['ffn_conv_b'][l], gT[:, sq], S, S)
        last = (li == len(layers) - 1)
        x2 = y_out if last else xs[1 - pp]
        x2T = xTs[pp]
        GLN(gT, WB['ffn_w_down'][l], 5504, x1, W['ln2_g'][l], W['ln2_b'][l], x2, None if last else x2T, KS=2)
        xcur, xTcur = x2, x2T
        pp = 1 - pp
    P.emit()
    return nc, P


def mixer_ab(P, C, nc, W, WB, e, xTcur, mix, T, S, NSEQ, scr_fn, G):
    if not hasattr(C, 'ab_scr'):
        d = {}
        d['hA'] = scr_fn("hA", [T, 3328], F32)
        d['hqk'] = scr_fn("hqk", [2048, T], F32)
        d['hv'] = scr_fn("hv", [T, 1024], F32)
        d['yab'] = scr_fn("yab", [T, 2048], F32)
        d['yabT'] = scr_fn("yabT", [2048, T], BF16)
        d['YF'] = scr_fn("YF", [T, 1024], F32)
        d['PR'] = {k: scr_fn("pr_" + k, [T, 1024], F32) for k in ('R', 'V', 'KAP', 'GATE', 'LW0', 'LW1', 'B0', 'B1', 'KD0', 'KD1')}
        d['PR']['BON'] = scr_fn("pr_BON", [T, 16], F32)
        C.ab_scr = d
    d = C.ab_scr
    G(xTcur, WB['ab_w_in'][e], 2048, 3328, 'tok', d['hA'])
    G(xTcur, WB['ab_w_in'][e], 2048, 2048, 'feat', d['hqk'], n_off=3328)
    G(xTcur, WB['ab_w_in'][e], 2048, 1024, 'tok', d['hv'], n_off=3328 + 2048)
    rwkv_prep(P, C, nc, W, e, d['hA'], d['PR'], T, S)
    rwkv_scan(P, C, nc, W, e, d['PR'], 0, d['YF'], d['yab'], T, S)
    rwkv_scan(P, C, nc, W, e, d['PR'], 1, d['YF'], d['yab'], T, S)
    bS = na_bias_build(P, C, nc, W['ab_rpb'][e], scr_fn, e)
    na_attn(P, C, nc, bS, d['hqk'], d['hv'], d['yab'], T, S)
    xprep_n(P, C, d['yab'], d['yabT'], T, 2048)
    return d['yabT']


_CACHE = {}
NSEQ_CORE = 2
SEQ = 4096


def _assign():
    return [(('p', c), ('s', c) if c < 4 else ('p', c)) for c in range(8)]


def kernel(**inputs):
    if 'nc' not in _CACHE:
        _CACHE['nc'] = build(NSEQ_CORE, SEQ, [0, 1, 2, 3])[0]
    nc = _CACHE['nc']
    xp = np.asarray(inputs['x_prompt'], dtype=np.float32)
    xs_ = np.asarray(inputs['x_sample'], dtype=np.float32)
    wd = {n: np.ascontiguousarray(np.asarray(inputs[n], dtype=np.float32)) for n, _ in WSPEC}
    in_maps = []
    asg = _assign()
    for c in range(8):
        rows = []
        for kind, i in asg[c]:
            rows.append(xp[i] if kind == 'p' else xs_[i])
        m = dict(wd)
        m['x'] = np.ascontiguousarray(np.concatenate(rows, axis=0))
        in_maps.append(m)
    res = run_bass_kernel_spmd(nc, in_maps, core_ids=list(range(8)))
    yp = np.empty_like(xp)
    ys = np.empty_like(xs_)
    for c in range(8):
        y = np.asarray(res.results[c]['y'])
        for slot, (kind, i) in enumerate(asg[c]):
            blk = y[slot * SEQ:(slot + 1) * SEQ]
            if slot == 1 and c >= 4:
                continue
            if kind == 'p':
                yp[i] = blk
            else:
                ys[i] = blk
    return (yp, ys)
```

```python
import numpy as np
import concourse.bass as bass
import concourse.mybir as mybir
from concourse.bass_utils import run_bass_kernel_spmd

F32 = mybir.dt.float32
BF16 = mybir.dt.bfloat16
I32 = mybir.dt.int32
AF = mybir.ActivationFunctionType
ALU = mybir.AluOpType
AX = mybir.AxisListType

ENGS = ('pe', 'act', 'dve', 'pool', 'sp')
DMAQ = {'sp': 16, 'act': 8, 'pool': 8}


class Buf:
    __slots__ = ('name', 'lw', 'rd')

    def __init__(self, name=''):
        self.name = name
        self.lw = None
        self.rd = []


class Op:
    __slots__ = ('eng', 'fn', 'dma', 'deps', 'sig', 'slot', 'slotval', 'idx', 'bar')

    def __init__(self, eng, fn, dma):
        self.eng = eng
        self.fn = fn
        self.dma = dma
        self.deps = ()
        self.sig = False
        self.slot = None
        self.slotval = 0
        self.idx = 0
        self.bar = False


class Prog:
    def __init__(self, nc):
        self.nc = nc
        self.ops = []
        self.last = {e: None for e in ENGS}
        self.sb_off = 16640
        self.sb_base = 16640
        self.nalloc = 0

    def sb(self, shape, dtype, name='t'):
        nbytes = int(np.prod(shape[1:])) * (4 if dtype in (F32, I32) else 2)
        off = (self.sb_off + 63) // 64 * 64
        assert off + nbytes <= 229000, (off, nbytes, name)
        self.sb_off = off + nbytes
        self.nalloc += 1
        return self.nc.alloc_sbuf_tensor_at(f"{name}{self.nalloc}", list(shape), dtype, offset=off)

    def sb_mark(self):
        return self.sb_off

    def sb_reset(self, mark):
        self.sb_off = mark
        self.barrier()

    def add(self, eng, fn, reads=(), writes=(), dma=False):
        op = Op(eng, fn, dma)
        deps = set()
        for b in reads:
            if b.lw is not None:
                deps.add(b.lw)
        for b in writes:
            if b.lw is not None:
                deps.add(b.lw)
            deps.update(b.rd)
        dl = []
        for d in deps:
            if d is op:
                continue
            if (not dma) and (not d.dma) and d.eng == 'pe' and eng == 'pe':
                continue
            d.sig = True
            dl.append(d)
        op.deps = dl
        for b in reads:
            b.rd.append(op)
        for b in writes:
            b.lw = op
            b.rd = []
        self.ops.append(op)
        if not dma:
            self.last[eng] = op
        return op

    def dma(self, q, out, in_, reads=(), writes=(), **kw):
        return self.add(q, lambda e: e.dma_start(out=out, in_=in_, **kw), reads, writes, dma=True)

    def mark(self, name):
        op = Op('mark', None, False)
        op.bar = name
        self.ops.append(op)

    def barrier(self):
        op = Op('all', None, False)
        op.bar = True
        for e in ENGS:
            if self.last[e] is not None:
                self.last[e].sig = True
        self.ops.append(op)

    def emit(self):
        nc = self.nc
        self.barrier()
        engsem = {e: nc.alloc_semaphore(f"s_{e}") for e in ENGS}
        slotsem = {q: [nc.alloc_semaphore(f"d_{q}{i}") for i in range(n)] for q, n in DMAQ.items()}
        slotuse = {q: [0] * n for q, n in DMAQ.items()}
        rr = {q: 0 for q in DMAQ}
        cnt = {e: 0 for e in ENGS}
        waited = {e: {} for e in ENGS}
        streams = {e: [] for e in ENGS}

        def want(e, sem, val):
            if val <= 0:
                return
            w = waited[e]
            k = id(sem)
            if w.get(k, 0) >= val:
                return
            w[k] = val
            streams[e].append(('w', sem, val))

        self.marks = []
        for op in self.ops:
            if op.eng == 'mark':
                self.marks.append((op.bar, dict(cnt)))
                continue
            if op.bar:
                for e in ENGS:
                    for x in ENGS:
                        if x != e:
                            want(e, engsem[x], cnt[x])
                    for q in DMAQ:
                        for i, s in enumerate(slotsem[q]):
                            want(e, s, 16 * slotuse[q][i])
                continue
            e = op.eng
            for d in op.deps:
                if d.dma:
                    want(e, slotsem[d.eng][d.slot], d.slotval)
                else:
                    want(e, engsem[d.eng], d.idx)
            if op.dma:
                k = rr[e]
                rr[e] = (k + 1) % DMAQ[e]
                want(e, slotsem[e][k], 16 * slotuse[e][k])
                slotuse[e][k] += 1
                op.slot = k
                op.slotval = 16 * slotuse[e][k]
                streams[e].append(('d', op, slotsem[e][k]))
            else:
                if op.sig:
                    cnt[e] += 1
                    op.idx = cnt[e]
                streams[e].append(('o', op, engsem[e]))

        def run_stream(eng_handle, items):
            for it in items:
                if it[0] == 'w':
                    eng_handle.wait_ge(it[1], it[2])
                elif it[0] == 'd':
                    it[1].fn(eng_handle).then_inc(it[2], 16)
                else:
                    ins = it[1].fn(eng_handle)
                    if it[1].sig:
                        ins.then_inc(it[2], 1)

        with nc.Block() as block:
            @block.sync
            def _(e):
                run_stream(e, streams['sp'])

            @block.tensor
            def _(e):
                run_stream(e, streams['pe'])

            @block.scalar
            def _(e):
                run_stream(e, streams['act'])

            @block.vector
            def _(e):
                run_stream(e, streams['dve'])

            @block.gpsimd
            def _(e):
                run_stream(e, streams['pool'])
        self.nitems = {e: len(streams[e]) for e in ENGS}


D = 2048
DFF = 5504
ALPHA = 8.0 ** 0.25
LN_EPS = 1e-5


class Ctx:
    pass


def setup_consts(P, C):
    nc = P.nc
    idn = nc.inline_tensor(np.eye(128, dtype=np.float32), "ident_d").ap()
    C.ident = P.sb([128, 128], F32, 'ident')
    C.identb = P.sb([128, 128], BF16, 'identb')
    C.b_ident = Buf('ident')
    P.dma('sp', C.ident[:], idn[:, :], writes=[C.b_ident])
    P.add('dve', lambda e: e.tensor_copy(out=C.identb[:], in_=C.ident[:]), [C.b_ident], [C.b_ident])
    C.ps = [nc.alloc_psum_tensor(f"psb{i}", [128, 512], F32) for i in range(8)]
    C.psb = [Buf(f'ps{i}') for i in range(8)]
    C.psi = 0


def next_ps(C, n=8, base=0):
    if n == 8 and base == 0:
        n, base = getattr(C, 'ps_pool', (8, 0))
    i = base + (C.psi % n)
    C.psi += 1
    return C.ps[i], C.psb[i]


def next_ps_t(C):
    C.psti = getattr(C, 'psti', 0) + 1
    i = 6 + (C.psti % 2)
    return C.ps[i], C.psb[i]


def cast_weight(P, C, src, dst, K, N):
    P.mark('cast_weight')
    mark = P.sb_mark()
    CB = 2048
    NBUF = 3
    st = [(P.sb([128, CB], F32, 'cw'), P.sb([128, CB], BF16, 'cwb'), Buf(), Buf()) for _ in range(NBUF)]
    engs = ['dve', 'act', 'pool']
    i = 0
    for r0 in range(0, K, 128):
        for c0 in range(0, N, CB):
            cn = min(CB, N - c0)
            a, b, ba, bb = st[i % NBUF]
            eng = engs[i % 3]
            P.dma('sp', a[:, 0:cn], src[r0:r0 + 128, c0:c0 + cn], writes=[ba])
            if eng == 'act':
                P.add('act', lambda e, a=a, b=b, cn=cn: e.copy(out=b[:, 0:cn], in_=a[:, 0:cn]), [ba], [bb])
            else:
                P.add(eng, lambda e, a=a, b=b, cn=cn: e.tensor_copy(out=b[:, 0:cn], in_=a[:, 0:cn]), [ba], [bb])
            P.dma('pool', dst[r0:r0 + 128, c0:c0 + cn], b[:, 0:cn], reads=[bb])
            i += 1
    P.sb_reset(mark)


def xprep(P, C, x, xT, T):
    P.mark('xprep')
    mark = P.sb_mark()
    KC = D // 128
    NB = 2
    st = [(P.sb([128, D], F32, 'xp'), P.sb([128, KC, 128], BF16, 'xpt'), Buf(), Buf()) for _ in range(NB)]
    xTv = xT.rearrange("(kc p) t -> p kc t", p=128)
    for ti in range(T // 128):
        a, b, ba, bb = st[ti % NB]
        P.dma('sp', a[:], x[ti * 128:(ti + 1) * 128, :], writes=[ba])
        transpose_to(P, C, a, ba, b, bb, KC)
        P.dma('pool', xTv[:, :, ti * 128:(ti + 1) * 128], b[:], reads=[bb])
    P.sb_reset(mark)


def transpose_to(P, C, a, ba, b, bb, KC, evi=[0], tbanks=False):
    for g in range(0, KC, 4):
        ps, psb = next_ps_t(C) if tbanks else next_ps(C)
        ng = min(4, KC - g)
        for j in range(ng):
            kc = g + j
            P.add('pe', lambda e, ps=ps, j=j, kc=kc: e.transpose(ps[:, j * 128:(j + 1) * 128], a[:, kc * 128:(kc + 1) * 128], C.ident[:]),
                  [ba, C.b_ident], [psb])
        eng = 'act' if evi[0] % 2 else 'dve'
        evi[0] += 1
        src = lambda ps=ps, ng=ng: ps[:, 0:ng * 128].rearrange("p (j t) -> p j t", t=128)
        if eng == 'act':
            P.add('act', lambda e, g=g, ng=ng, src=src: e.copy(out=b[:, g:g + ng, :], in_=src()), [psb], [bb])
        else:
            P.add('dve', lambda e, g=g, ng=ng, src=src: e.tensor_copy(out=b[:, g:g + ng, :], in_=src()), [psb], [bb])


def gemm(P, C, xT, w, K, T, N, mode, epi, TT=512, n_off=0, KS=1):
    P.mark('gemm')
    mark = P.sb_mark()
    KC = K // 128
    NBW = 512
    KW = (KC + KS - 1) // KS
    xt = [(P.sb([128, KC, TT], BF16, 'gx'), Buf()) for _ in range(2)]
    wt = [(P.sb([128, KW, NBW], BF16, 'gw'), Buf()) for _ in range(2)]
    xTv = xT.rearrange("(kc p) t -> p kc t", p=128)
    wv = w.rearrange("(kc p) n -> p kc n", p=128)
    wi = 0
    for tti, t0 in enumerate(range(0, T, TT)):
        tt = min(TT, T - t0)
        xa, xb = xt[tti % 2]
        h = KC // 2
        P.dma('sp', xa[:, 0:h, 0:tt], xTv[:, 0:h, t0:t0 + tt], writes=[xb])
        P.dma('sp', xa[:, h:KC, 0:tt], xTv[:, h:KC, t0:t0 + tt], writes=[xb])
        if hasattr(epi, 'pre_tile'):
            epi.pre_tile(t0, tt)
        for n0 in range(0, N, NBW):
            nn = min(NBW, N - n0)
            if mode == 'tok':
                banks = [next_ps(C) for _ in range(0, tt, 128)]
                for ks in range(KS):
                    k0, k1 = ks * KW, min(KC, (ks + 1) * KW)
                    wa, wb = wt[wi % 2]
                    wi += 1
                    hh = (k0 + k1) // 2
                    P.dma('sp', wa[:, 0:hh - k0, 0:nn], wv[:, k0:hh, n_off + n0:n_off + n0 + nn], writes=[wb])
                    P.dma('sp', wa[:, hh - k0:k1 - k0, 0:nn], wv[:, hh:k1, n_off + n0:n_off + n0 + nn], writes=[wb])
                    for si, s0 in enumerate(range(0, tt, 128)):
                        ps, psb = banks[si]
                        for kc in range(k0, k1):
                            P.add('pe', lambda e, ps=ps, kc=kc, k0=k0, s0=s0, xa=xa, wa=wa, nn=nn: e.matmul(
                                ps[:, 0:nn], xa[:, kc, s0:s0 + 128], wa[:, kc - k0, 0:nn], start=(kc == 0), stop=(kc == KC - 1)),
                                [xb, wb], [psb])
                        if ks == KS - 1:
                            epi(ps, psb, t0 + s0, 128, n0, nn)
            else:
                wa, wb = wt[wi % 2]
                wi += 1
                P.dma('sp', wa[:, 0:h, 0:nn], wv[:, 0:h, n_off + n0:n_off + n0 + nn], writes=[wb])
                P.dma('sp', wa[:, h:KC, 0:nn], wv[:, h:KC, n_off + n0:n_off + n0 + nn], writes=[wb])
                for m0 in range(0, nn, 128):
                    ps, psb = next_ps(C)
                    for kc in range(KC):
                        P.add('pe', lambda e, ps=ps, kc=kc, m0=m0, xa=xa, wa=wa, tt=tt: e.matmul(
                            ps[:, 0:tt], wa[:, kc, m0:m0 + 128], xa[:, kc, 0:tt], start=(kc == 0), stop=(kc == KC - 1)),
                            [xb, wb], [psb])
                    epi(ps, psb, t0, tt, n0 + m0, 128)
    P.sb_reset(mark)


class LNEpi:
    def __init__(self, P, C, xin, g_row, b_row, xout, xTout):
        self.P, self.C, self.xin, self.xout = P, C, xin, xout
        self.KC = D // 128
        self.gt = P.sb([128, D], F32, 'lng'); self.gb = Buf()
        self.bt = P.sb([128, D], F32, 'lnb'); self.bb = Buf()
        P.dma('sp', self.gt[:], bass.AP(g_row.tensor, g_row.offset, [[0, 128], [1, D]]), writes=[self.gb])
        P.dma('sp', self.bt[:], bass.AP(b_row.tensor, b_row.offset, [[0, 128], [1, D]]), writes=[self.bb])
        self.z = [dict(z=P.sb([128, D], F32, 'lz'), zb=Buf(), stats=P.sb([128, 4, 6], F32, 'ls'), mv=P.sb([128, 2], F32, 'lmv'),
                       rstd=P.sb([128, 1], F32, 'lr'), nmr=P.sb([128, 1], F32, 'ln'), sb=Buf()) for _ in range(4)]
        self.xt = [(P.sb([128, self.KC, 128], BF16, 'lxt'), Buf()) for _ in range(2)]
        self.xti = 0
        self.xTv = xTout.rearrange("(kc p) t -> p kc t", p=128) if xTout is not None else None

    def pre_tile(self, t0, tt):
        for si, s0 in enumerate(range(0, tt, 128)):
            s = self.z[si]
            self.P.dma('sp', s['z'][:], self.xin[t0 + s0:t0 + s0 + 128, :], writes=[s['zb']])

    def __call__(self, ps, psb, t0, nt, n0, nn):
        P = self.P
        s = self.z[(t0 // 128) % 4]
        z = s['z']
        P.add('dve', lambda e: e.scalar_tensor_tensor(out=z[:, n0:n0 + nn], in0=z[:, n0:n0 + nn], scalar=ALPHA, in1=ps[:, 0:nn], op0=ALU.mult, op1=ALU.add),
              [psb, s['zb']], [s['zb']])
        if n0 + nn < D:
            return
        r = slice(t0, t0 + 128)
        for c in range(4):
            P.add('dve', lambda e, c=c: e.bn_stats(out=s['stats'][:, c, :], in_=z[:, c * 512:(c + 1) * 512]), [s['zb']], [s['sb']])
        P.add('dve', lambda e: e.bn_aggr(out=s['mv'][:], in_=s['stats'][:].rearrange("p a b -> p (a b)")), [s['sb']], [s['sb']])
        P.add('dve', lambda e: e.tensor_scalar(out=s['rstd'][:], in0=s['mv'][:, 1:2], scalar1=LN_EPS, scalar2=None, op0=ALU.add), [s['sb']], [s['sb']])
        P.add('act', lambda e: e.activation(out=s['rstd'][:], in_=s['rstd'][:], func=AF.Sqrt), [s['sb']], [s['sb']])
        P.add('dve', lambda e: e.reciprocal(out=s['rstd'][:], in_=s['rstd'][:]), [s['sb']], [s['sb']])
        P.add('dve', lambda e: e.scalar_tensor_tensor(out=s['nmr'][:], in0=s['mv'][:, 0:1], scalar=-1.0, in1=s['rstd'][:], op0=ALU.mult, op1=ALU.mult), [s['sb']], [s['sb']])
        P.add('act', lambda e: e.activation(out=z[:], in_=z[:], func=AF.Identity, bias=s['nmr'][:], scale=s['rstd'][:]), [s['zb'], s['sb']], [s['zb']])
        P.add('pool', lambda e: e.tensor_tensor(out=z[:], in0=z[:], in1=self.gt[:], op=ALU.mult), [s['zb'], self.gb], [s['zb']])
        P.add('pool', lambda e: e.tensor_tensor(out=z[:], in0=z[:], in1=self.bt[:], op=ALU.add), [s['zb'], self.bb], [s['zb']])
        P.dma('pool', self.xout[r, :], z[:], reads=[s['zb']])
        if self.xTv is not None:
            xt, xtb = self.xt[self.xti % 2]
            self.xti += 1
            transpose_to(P, self.C, z, s['zb'], xt, xtb, self.KC, tbanks=True)
            P.dma('pool', self.xTv[:, :, r], xt[:], reads=[xtb])


class StoreEpi:
    def __init__(self, P, C, out, mode, dtype=F32):
        self.P, self.C, self.out, self.mode = P, C, out, mode
        self.st = [(P.sb([128, 512], dtype, 'se'), Buf()) for _ in range(4)]
        self.i = 0

    def __call__(self, ps, psb, t0, nt, n0, nn):
        P = self.P
        a, ab = self.st[self.i % 4]
        eng = 'act' if self.i % 2 else 'dve'
        self.i += 1
        if self.mode == 'tok':
            w = nn
            dst = self.out[t0:t0 + nt, n0:n0 + nn]
        else:
            w = nt
            dst = self.out[n0:n0 + nn, t0:t0 + nt]
        if eng == 'act':
            P.add('act', lambda e: e.copy(out=a[:, 0:w], in_=ps[:, 0:w]), [psb], [ab])
        else:
            P.add('dve', lambda e: e.tensor_copy(out=a[:, 0:w], in_=ps[:, 0:w]), [psb], [ab])
        P.dma('pool', dst, a[:, 0:w], reads=[ab])


def bcast_rows(P, C, src_row, n, name):
    t = P.sb([128, n], F32, name)
    b = Buf(name)
    P.dma('sp', t[:], src_row.partition_broadcast(128) if hasattr(src_row, 'partition_broadcast') else src_row, writes=[b])
    return t, b


def ln_phase(P, C, xin, mix, g_row, b_row, xout, xTout, T, final_out=None):
    P.mark('ln_phase')
    mark = P.sb_mark()
    nc = P.nc
    KC = D // 128
    gt = P.sb([128, D], F32, 'lng'); gb = Buf()
    bt = P.sb([128, D], F32, 'lnb'); bb_ = Buf()
    P.dma('sp', gt[:], bass.AP(g_row.tensor, g_row.offset, [[0, 128], [1, D]]), writes=[gb])
    P.dma('sp', bt[:], bass.AP(b_row.tensor, b_row.offset, [[0, 128], [1, D]]), writes=[bb_])
    NB = 2
    st = []
    for _ in range(NB):
        st.append(dict(x=P.sb([128, D], F32, 'lx'), m=P.sb([128, D], F32, 'lm'), xb=Buf(), mb=Buf(),
                       stats=P.sb([128, 4, 6], F32, 'ls'), mv=P.sb([128, 2], F32, 'lmv'), sb=Buf(),
                       rstd=P.sb([128, 1], F32, 'lr'), nmr=P.sb([128, 1], F32, 'ln'),
                       xt=P.sb([128, KC, 128], BF16, 'lxt'), xtb=Buf()))
    xTv = xTout.rearrange("(kc p) t -> p kc t", p=128) if xTout is not None else None
    for ti in range(T // 128):
        s = st[ti % NB]
        x, m = s['x'], s['m']
        r = slice(ti * 128, (ti + 1) * 128)
        P.dma('sp', x[:], xin[r, :], writes=[s['xb']])
        P.dma('sp', m[:], mix[r, :], writes=[s['mb']])
        P.add('dve', lambda e, x=x, m=m: e.scalar_tensor_tensor(out=m[:], in0=x[:], scalar=ALPHA, in1=m[:], op0=ALU.mult, op1=ALU.add),
              [s['xb'], s['mb']], [s['mb']])
        for c in range(4):
            P.add('dve', lambda e, s=s, m=m, c=c: e.bn_stats(out=s['stats'][:, c, :], in_=m[:, c * 512:(c + 1) * 512]), [s['mb']], [s['sb']])
        P.add('dve', lambda e, s=s: e.bn_aggr(out=s['mv'][:], in_=s['stats'][:].rearrange("p a b -> p (a b)")), [s['sb']], [s['sb']])
        P.add('dve', lambda e, s=s: e.tensor_scalar(out=s['rstd'][:], in0=s['mv'][:, 1:2], scalar1=LN_EPS, scalar2=None, op0=ALU.add),
              [s['sb']], [s['sb']])
        P.add('act', lambda e, s=s: e.activation(out=s['rstd'][:], in_=s['rstd'][:], func=AF.Sqrt), [s['sb']], [s['sb']])
        P.add('dve', lambda e, s=s: e.reciprocal(out=s['rstd'][:], in_=s['rstd'][:]), [s['sb']], [s['sb']])
        P.add('dve', lambda e, s=s: e.scalar_tensor_tensor(out=s['nmr'][:], in0=s['mv'][:, 0:1], scalar=-1.0, in1=s['rstd'][:], op0=ALU.mult, op1=ALU.mult),
              [s['sb']], [s['sb']])
        P.add('act', lambda e, s=s, x=x, m=m: e.activation(out=x[:], in_=m[:], func=AF.Identity, bias=s['nmr'][:], scale=s['rstd'][:]),
              [s['mb'], s['sb']], [s['xb']])
        P.add('pool', lambda e, x=x: e.tensor_tensor(out=x[:], in0=x[:], in1=gt[:], op=ALU.mult), [s['xb'], gb], [s['xb']])
        P.add('dve', lambda e, x=x: e.tensor_tensor(out=x[:], in0=x[:], in1=bt[:], op=ALU.add), [s['xb'], bb_], [s['xb']])
        P.dma('pool', xout[r, :], x[:], reads=[s['xb']])
        if xTv is not None:
            transpose_to(P, C, x, s['xb'], s['xt'], s['xtb'], KC)
            P.dma('pool', xTv[:, :, r], s['xt'][:], reads=[s['xtb']])
    P.sb_reset(mark)


def ffn_mid(P, C, hT, conv_w, conv_b, gT, T, S):
    P.mark('ffn_mid')
    mark = P.sb_mark()
    NCH = DFF // 128
    cw = P.sb([128, NCH, 3], F32, 'fcw'); cwb = Buf()
    cb = P.sb([128, NCH], F32, 'fcb'); cbb = Buf()
    for k in range(3):
        P.dma('sp', cw[:, :, k], conv_w[k, :].rearrange("(c p) -> p c", p=128), writes=[cwb], allow_slow_non_contiguous=True)
    P.dma('sp', cb[:], conv_b.rearrange("(c p) -> p c", p=128), writes=[cbb], allow_slow_non_contiguous=True)
    NB = 2
    st = [dict(g=P.sb([128, S + 2], F32, 'fg'), u=P.sb([128, S], F32, 'fu'), a=P.sb([128, S], F32, 'fa'),
               o=P.sb([128, S], BF16, 'fo'), gb=Buf(), ub=Buf(), ab=Buf(), ob=Buf()) for _ in range(NB)]
    for s in st:
        P.add('pool', lambda e, s=s: e.memset(s['g'][:, 0:1], 0.0), [], [s['gb']])
        P.add('pool', lambda e, s=s: e.memset(s['g'][:, S + 1:S + 2], 0.0), [], [s['gb']])
    i = 0
    for q in range(T // S):
        for ch in range(NCH):
            s = st[i % NB]
            i += 1
            g, u, a, o = s['g'], s['u'], s['a'], s['o']
            tr = slice(q * S, (q + 1) * S)
            P.dma('sp', g[:, 1:S + 1], hT[ch * 128:(ch + 1) * 128, tr], writes=[s['gb']])
            P.dma('sp', u[:], hT[DFF + ch * 128:DFF + (ch + 1) * 128, tr], writes=[s['ub']])
            P.add('act', lambda e, g=g, a=a, ch=ch: e.activation(out=a[:], in_=g[:, 1:S + 1], func=AF.Identity, bias=cb[:, ch:ch + 1], scale=cw[:, ch, 1:2]),
                  [s['gb'], cwb, cbb], [s['ab']])
            P.add('dve', lambda e, g=g, a=a, ch=ch: e.scalar_tensor_tensor(out=a[:], in0=g[:, 0:S], scalar=cw[:, ch, 0:1], in1=a[:], op0=ALU.mult, op1=ALU.add),
                  [s['gb'], s['ab'], cwb], [s['ab']])
            P.add('dve', lambda e, g=g, a=a, ch=ch: e.scalar_tensor_tensor(out=a[:], in0=g[:, 2:S + 2], scalar=cw[:, ch, 2:3], in1=a[:], op0=ALU.mult, op1=ALU.add),
                  [s['gb'], s['ab'], cwb], [s['ab']])
            P.add('act', lambda e, a=a: e.activation(out=a[:], in_=a[:], func=AF.Gelu), [s['ab']], [s['ab']])
            P.add('pool', lambda e, a=a, u=u, o=o: e.tensor_tensor(out=o[:], in0=a[:], in1=u[:], op=ALU.mult), [s['ab'], s['ub']], [s['ob']])
            P.dma('pool', gT[ch * 128:(ch + 1) * 128, tr], o[:], reads=[s['ob']])
    P.sb_reset(mark)


def xprep_n(P, C, x, xT, T, DD):
    P.mark('xprep_n')
    mark = P.sb_mark()
    KC = DD // 128
    NB = 2
    st = [(P.sb([128, DD], F32, 'xp'), P.sb([128, KC, 128], BF16, 'xpt'), Buf(), Buf()) for _ in range(NB)]
    xTv = xT.rearrange("(kc p) t -> p kc t", p=128)
    for ti in range(T // 128):
        a, b, ba, bb = st[ti % NB]
        P.dma('sp', a[:], x[ti * 128:(ti + 1) * 128, :], writes=[ba])
        transpose_to(P, C, a, ba, b, bb, KC)
        P.dma('pool', xTv[:, :, ti * 128:(ti + 1) * 128], b[:], reads=[bb])
    P.sb_reset(mark)


def rot_tables(S):
    half = 128
    inv = (10000.0 ** (-np.arange(half, dtype=np.float32) / half)).astype(np.float32)
    pos = np.arange(S, dtype=np.float32)
    ang = (pos[:, None] * inv[None, :]).astype(np.float32)
    return np.cos(ang).T.astype(np.float32).copy(), np.sin(ang).T.astype(np.float32).copy()


def ret_rotary(P, C, qkT, qkr, T, S):
    P.mark('ret_rotary')
    nc = P.nc
    mark = P.sb_mark()
    if not hasattr(C, 'rot'):
        ct, sn = rot_tables(S)
        C.rot = (nc.inline_tensor(ct, "rot_cos").ap(), nc.inline_tensor(sn, "rot_sin").ap())
    cd, sd = C.rot
    cos = P.sb([128, S], F32, 'cos'); sin = P.sb([128, S], F32, 'sin')
    cosk = P.sb([128, S], F32, 'cosk'); sink = P.sb([128, S], F32, 'sink')
    tb = Buf()
    P.dma('sp', cos[:], cd[:, :], writes=[tb])
    P.dma('sp', sin[:], sd[:, :], writes=[tb])
    P.add('act', lambda e: e.mul(out=cosk[:], in_=cos[:], mul=1.0 / 16), [tb], [tb])
    P.add('act', lambda e: e.mul(out=sink[:], in_=sin[:], mul=1.0 / 16), [tb], [tb])
    NB = 2 if S <= 2048 else 1
    st = [dict(x1=P.sb([128, S], F32, 'r1'), x2=P.sb([128, S], F32, 'r2'), a=P.sb([128, S], F32, 'ra'), b=P.sb([128, S], F32, 'rb'),
               o1=P.sb([128, S], BF16, 'ro1'), o2=P.sb([128, S], BF16, 'ro2'), xb=Buf(), ab=Buf(), ob=Buf()) for _ in range(NB)]
    i = 0
    for q in range(T // S):
        tr = slice(q * S, (q + 1) * S)
        for hh in range(16):
            s = st[i % NB]; i += 1
            c_, s_ = (cos, sin) if hh < 8 else (cosk, sink)
            r0 = hh * 256
            x1, x2, a, b, o1, o2 = s['x1'], s['x2'], s['a'], s['b'], s['o1'], s['o2']
            P.dma('sp', x1[:], qkT[r0:r0 + 128, tr], writes=[s['xb']])
            P.dma('sp', x2[:], qkT[r0 + 128:r0 + 256, tr], writes=[s['xb']])
            P.add('dve', lambda e, x1=x1, a=a, c_=c_: e.tensor_tensor(out=a[:], in0=x1[:], in1=c_[:], op=ALU.mult), [s['xb'], tb], [s['ab']])
            P.add('pool', lambda e, x2=x2, b=b, s_=s_: e.tensor_tensor(out=b[:], in0=x2[:], in1=s_[:], op=ALU.mult), [s['xb'], tb], [s['ab']])
            P.add('dve', lambda e, a=a, b=b, o1=o1: e.tensor_tensor(out=o1[:], in0=a[:], in1=b[:], op=ALU.subtract), [s['ab']], [s['ob']])
            P.add('pool', lambda e, x1=x1, a=a, s_=s_: e.tensor_tensor(out=a[:], in0=x1[:], in1=s_[:], op=ALU.mult), [s['xb'], tb, s['ab']], [s['ab']])
            P.add('dve', lambda e, x2=x2, b=b, c_=c_: e.tensor_tensor(out=b[:], in0=x2[:], in1=c_[:], op=ALU.mult), [s['xb'], tb, s['ab']], [s['ab']])
            P.add('pool', lambda e, a=a, b=b, o2=o2: e.tensor_tensor(out=o2[:], in0=a[:], in1=b[:], op=ALU.add), [s['ab']], [s['ob']])
            P.dma('pool', qkr[r0:r0 + 128, tr], o1[:], reads=[s['ob']])
            P.dma('pool', qkr[r0 + 128:r0 + 256, tr], o2[:], reads=[s['ob']])
    P.sb_reset(mark)


def ret_attn(P, C, qkr, vg, gn_g, gn_b, ytok, T, S):
    P.mark('ret_attn')
    nc = P.nc
    mark = P.sb_mark()
    NQB = S // 512
    NKC = S // 128
    lg = [float(np.log(np.float32(1.0) - np.float32(2.0) ** np.float32(-5.0 - h))) for h in range(8)]
    d0 = P.sb([128, 512], F32, 'd0'); d0b = Buf()
    P.add('pool', lambda e: e.iota(d0[:], [[1, 512]], base=0, channel_multiplier=-1, allow_small_or_imprecise_dtypes=True), [], [d0b])
    deltas = sorted({qb * 512 - kc * 128 for qb in range(NQB) for kc in range(NKC)})
    gam = {dl: P.sb([128, 512], BF16, 'gam') for dl in deltas}
    gamb = Buf()
    gtmp = [(P.sb([128, 512], F32, 'gt'), Buf()) for _ in range(2)]
    gg = P.sb([128, 4096], F32, 'gng'); gbt = P.sb([128, 4096], F32, 'gnb'); ggb = Buf()
    P.dma('sp', gg[:], bass.AP(gn_g.tensor, gn_g.offset, [[0, 128], [1, 4096]]), writes=[ggb])
    P.dma('sp', gbt[:], bass.AP(gn_b.tensor, gn_b.offset, [[0, 128], [1, 4096]]), writes=[ggb])
    qt = P.sb([128, 2, S], BF16, 'qt'); kt = P.sb([128, 2, S], BF16, 'kt'); qkb = Buf()
    vt = P.sb([128, NKC, 512], BF16, 'vt'); vb = Buf()
    vst = [(P.sb([128, 512], F32, 'vs'), Buf()) for _ in range(2)]
    pt = [(P.sb([128, 512], BF16, 'pt'), Buf()) for _ in range(3)]
    ep = [dict(o=P.sb([128, 512], F32, 'eo'), g=P.sb([128, 512], F32, 'eg'), st=P.sb([128, 6], F32, 'es'), mv=P.sb([128, 2], F32, 'em'),
               r=P.sb([128, 1], F32, 'er'), n=P.sb([128, 1], F32, 'en'), ob=Buf(), gb=Buf(), sb=Buf()) for _ in range(2)]
    obank = [(C.ps[i], C.psb[i]) for i in range(4)]
    ostage = [(P.sb([128, 512], F32, 'ost'), Buf()) for _ in range(8)]
    sbank = [(C.ps[4 + i], C.psb[4 + i]) for i in range(3)]
    cnt = 0
    ec = 0
    for h in range(8):
        for gi, dl in enumerate(deltas):
            t, tbuf = gtmp[gi % 2]
            P.add('act', lambda e, t=t, dl=dl: e.activation(out=t[:], in_=d0[:], func=AF.Abs, bias=float(dl), scale=1.0), [d0b], [tbuf])
            P.add('act', lambda e, t=t, dl=dl, h=h: e.activation(out=gam[dl][:], in_=t[:], func=AF.Exp, scale=lg[h]), [tbuf], [gamb])
        for q in range(T // S):
            tb0 = q * S
            P.dma('sp', qt[:], qkr[h * 256:(h + 1) * 256, tb0:tb0 + S].rearrange("(c p) t -> p c t", p=128), writes=[qkb])
            P.dma('sp', kt[:], qkr[2048 + h * 256:2048 + (h + 1) * 256, tb0:tb0 + S].rearrange("(c p) t -> p c t", p=128), writes=[qkb])
            for kc in range(NKC):
                a, ab = vst[kc % 2]
                P.dma('sp', a[:], vg[tb0 + kc * 128:tb0 + (kc + 1) * 128, h * 512:(h + 1) * 512], writes=[ab])
                P.add('pool', lambda e, a=a, kc=kc: e.tensor_copy(out=vt[:, kc, :], in_=a[:]), [ab], [vb])
            for qb in range(NQB):
                def emit_s(kc, cnt):
                    sp_, spb = sbank[cnt % 3]
                    p_, pb = pt[cnt % 3]
                    for c in range(2):
                        P.add('pe', lambda e, sp_=sp_, c=c, kc=kc, qb=qb: e.matmul(sp_[:, :], kt[:, c, kc * 128:(kc + 1) * 128], qt[:, c, qb * 512:(qb + 1) * 512],
                                                                                  start=(c == 0), stop=(c == 1)), [qkb], [spb])
                    dl = qb * 512 - kc * 128
                    P.add('dve', lambda e, sp_=sp_, p_=p_, dl=dl: e.tensor_tensor(out=p_[:], in0=sp_[:, :], in1=gam[dl][:], op=ALU.mult), [spb, gamb], [pb])

                def emit_pv(kc, cnt):
                    p_, pb = pt[cnt % 3]
                    for sub in range(4):
                        ob_, obb = obank[sub]
                        P.add('pe', lambda e, ob_=ob_, p_=p_, sub=sub, kc=kc: e.matmul(ob_[:, :], p_[:, sub * 128:(sub + 1) * 128], vt[:, kc, :],
                                                                                     start=(kc == 0), stop=(kc == NKC - 1)), [pb, vb], [obb])
                emit_s(0, cnt)
                emit_s(1, cnt + 1)
                for kc in range(NKC):
                    if kc + 2 < NKC:
                        emit_s(kc + 2, cnt + 2)
                    emit_pv(kc, cnt)
                    cnt += 1
                for sub in range(4):
                    ob_p, obb_p = obank[sub]
                    ob_, obb = ostage[(ec + sub) % 8]
                    if sub % 2:
                        P.add('act', lambda e, ob_=ob_, ob_p=ob_p: e.activation(out=ob_[:], in_=ob_p[:, :], func=AF.Identity), [obb_p], [obb])
                    else:
                        P.add('dve', lambda e, ob_=ob_, ob_p=ob_p: e.tensor_copy(out=ob_[:], in_=ob_p[:, :]), [obb_p], [obb])
                for sub in range(4):
                    ob_, obb = ostage[ec % 8]
                    s = ep[ec % 2]; ec += 1
                    r = slice(tb0 + qb * 512 + sub * 128, tb0 + qb * 512 + (sub + 1) * 128)
                    o, g = s['o'], s['g']
                    P.dma('sp', g[:], vg[r, 4096 + h * 512:4096 + (h + 1) * 512], writes=[s['gb']])
                    P.add('act', lambda e, g=g: e.activation(out=g[:], in_=g[:], func=AF.Silu), [s['gb']], [s['gb']])
                    P.add('dve', lambda e, s=s, ob_=ob_: e.bn_stats(out=s['st'][:], in_=ob_[:, :]), [obb], [s['sb']])
                    P.add('dve', lambda e, s=s: e.bn_aggr(out=s['mv'][:], in_=s['st'][:]), [s['sb']], [s['sb']])
                    P.add('dve', lambda e, s=s: e.tensor_scalar(out=s['r'][:], in0=s['mv'][:, 1:2], scalar1=LN_EPS, scalar2=None, op0=ALU.add), [s['sb']], [s['sb']])
                    P.add('act', lambda e, s=s: e.activation(out=s['r'][:], in_=s['r'][:], func=AF.Sqrt), [s['sb']], [s['sb']])
                    P.add('dve', lambda e, s=s: e.reciprocal(out=s['r'][:], in_=s['r'][:]), [s['sb']], [s['sb']])
                    P.add('dve', lambda e, s=s: e.scalar_tensor_tensor(out=s['n'][:], in0=s['mv'][:, 0:1], scalar=-1.0, in1=s['r'][:], op0=ALU.mult, op1=ALU.mult), [s['sb']], [s['sb']])
                    P.add('act', lambda e, s=s, o=o, ob_=ob_: e.activation(out=o[:], in_=ob_[:, :], func=AF.Identity, bias=s['n'][:], scale=s['r'][:]), [obb, s['sb']], [s['ob']])
                    P.add('pool', lambda e, o=o, h=h: e.tensor_tensor(out=o[:], in0=o[:], in1=gg[:, h * 512:(h + 1) * 512], op=ALU.mult), [s['ob'], ggb], [s['ob']])
                    P.add('pool', lambda e, o=o, h=h: e.tensor_tensor(out=o[:], in0=o[:], in1=gbt[:, h * 512:(h + 1) * 512], op=ALU.add), [s['ob'], ggb], [s['ob']])
                    P.add('dve', lambda e, o=o, g=g: e.tensor_tensor(out=o[:], in0=o[:], in1=g[:], op=ALU.mult), [s['ob'], s['gb']], [s['ob']])
                    P.dma('pool', ytok[r, h * 512:(h + 1) * 512], o[:], reads=[s['ob']])
    P.sb_reset(mark)


NEG = -30000.0


def na_bias_build(P, C, nc, rpb, scr_fn, e):
    P.mark('na_bias_build')
    mark = P.sb_mark()
    biasS = scr_fn(f"na_bias{e}", [240, 4096], F32)
    G = P.sb([31, 240], F32, 'nG'); Gb = Buf()
    for h_ in range(16):
        P.dma('sp', G[:, h_ * 15:(h_ + 1) * 15], rpb[h_].rearrange("r d -> d r"), writes=[Gb], allow_slow_non_contiguous=True)
    M = P.sb([31, 4096], F32, 'nM'); Mb = Buf()
    P.add('pool', lambda e_: e_.iota(M[:].rearrange("p (a b) -> p a b", b=64), [[1, 64], [-1, 64]], base=15, channel_multiplier=-1,
                                     allow_small_or_imprecise_dtypes=True), [], [Mb])
    P.add('dve', lambda e_: e_.tensor_single_scalar(out=M[:], in_=M[:], scalar=0.0, op=ALU.is_equal), [Mb], [Mb])
    A = P.sb([120, 4096], F32, 'nA'); Q = P.sb([120, 4096], F32, 'nQ'); nb = Buf()
    P.add('pool', lambda e_: e_.iota(A[:].rearrange("p (a b) -> p a b", b=64), [[1, 64], [0, 64]], base=0, channel_multiplier=0,
                                     allow_small_or_imprecise_dtypes=True), [], [nb])
    P.add('pool', lambda e_: e_.iota(Q[:].rearrange("p (a b) -> p a b", b=64), [[0, 64], [1, 64]], base=0, channel_multiplier=0,
                                     allow_small_or_imprecise_dtypes=True), [nb], [nb])
    for (s1, op) in ((-8.0, ALU.add), (0.0, ALU.max), (48.0, ALU.min)):
        P.add('dve', lambda e_, s1=s1, op=op: e_.tensor_single_scalar(out=Q[:], in_=Q[:], scalar=s1, op=op), [nb], [nb])
    P.add('dve', lambda e_: e_.tensor_tensor(out=A[:], in0=A[:], in1=Q[:], op=ALU.subtract), [nb], [nb])
    P.add('dve', lambda e_: e_.tensor_single_scalar(out=Q[:], in_=A[:], scalar=0.0, op=ALU.is_ge), [nb], [nb])
    P.add('dve', lambda e_: e_.tensor_single_scalar(out=A[:], in_=A[:], scalar=15.0, op=ALU.is_le), [nb], [nb])
    P.add('dve', lambda e_: e_.tensor_tensor(out=A[:], in0=A[:], in1=Q[:], op=ALU.mult), [nb], [nb])
    P.add('dve', lambda e_: e_.tensor_single_scalar(out=A[:], in_=A[:], scalar=-1.0, op=ALU.add), [nb], [nb])
    P.add('dve', lambda e_: e_.tensor_single_scalar(out=A[:], in_=A[:], scalar=-NEG, op=ALU.mult), [nb], [nb])
    st = [(P.sb([120, 512], F32, 'nbs'), Buf()) for _ in range(2)]
    i = 0
    for half in range(2):
        for cb in range(8):
            ps, psb = next_ps(C)
            P.add('pe', lambda e_, ps=ps, half=half, cb=cb: e_.matmul(ps[0:120, :], G[:, half * 120:(half + 1) * 120], M[:, cb * 512:(cb + 1) * 512],
                                                                     start=True, stop=True), [Gb, Mb], [psb])
            a, ab = st[i % 2]; i += 1
            P.add('dve', lambda e_, ps=ps, a=a, cb=cb: e_.tensor_tensor(out=a[:], in0=ps[0:120, :], in1=A[:, cb * 512:(cb + 1) * 512], op=ALU.add), [psb, nb], [ab])
            P.dma('pool', biasS[half * 120:(half + 1) * 120, cb * 512:(cb + 1) * 512], a[:], reads=[ab])
    P.sb_reset(mark)
    return biasS


def na_attn(P, C, nc, biasS, hqk, hv, yab, T, S, stage=3):
    P.mark('na_attn')
    mark = P.sb_mark()
    rows = S // 64
    NCH = S // 128
    assert rows >= 8
    B2 = P.sb([128, 16, 14, 64], F32, 'nB2'); B2b = Buf()
    bv = biasS.rearrange("(h r) (k q) -> k h r q", r=15, q=64)
    for i2 in range(2):
        for h in range(16):
            P.dma('sp', B2[i2 * 64:(i2 + 1) * 64, h, :, :], bv[:, h, i2:i2 + 14, :], writes=[B2b])
    stg = P.sb([64, 2, S], F32, 'nstg'); stgb = Buf()
    qTb = P.sb([64, 2, S], BF16, 'nq'); kTb = P.sb([64, 2, S], BF16, 'nk'); qkb = Buf()
    vstg = P.sb([128, NCH, 128], F32, 'nvs'); vsb = Buf()
    Vt = P.sb([128, NCH, 2, 80], BF16, 'nV'); Vt2 = P.sb([128, NCH, 2, 80], BF16, 'nV2'); Vb = Buf()
    P.add('pool', lambda e_: e_.memset(Vt[:], 1.0), [], [Vb])
    P.add('pool', lambda e_: e_.memset(Vt2[:], 1.0), [], [Vb])
    sc = [(P.sb([128, 512], F32, 'nsc'), Buf()) for _ in range(3)]
    pT = [(P.sb([128, 512], BF16, 'npT'), Buf()) for _ in range(3)]
    yr = [(P.sb([64, 128], F32, 'nyr'), P.sb([64, 2], F32, 'nrc'), Buf()) for _ in range(2)]
    sbank = [(C.ps[0], C.psb[0]), (C.ps[1], C.psb[1]), (C.ps[4], C.psb[4])]
    obank = [(C.ps[2], C.psb[2]), (C.ps[3], C.psb[3])]
    it = 0
    for q in range(T // S):
        t0 = q * S
        for hp in range(8):
            P.dma('sp', stg[:], hqk[hp * 128:(hp + 1) * 128, t0:t0 + S].rearrange("(a d) t -> d a t", d=64), writes=[stgb])
            P.add('act', lambda e_: e_.mul(out=qTb[:], in_=stg[:], mul=0.125), [stgb], [qkb])
            P.dma('sp', stg[:], hqk[1024 + hp * 128:1024 + (hp + 1) * 128, t0:t0 + S].rearrange("(a d) t -> d a t", d=64), writes=[stgb])
            P.add('dve', lambda e_: e_.tensor_copy(out=kTb[:], in_=stg[:]), [stgb], [qkb])
            P.dma('sp', vstg[:], hv[t0:t0 + S, hp * 128:(hp + 1) * 128].rearrange("(c p) f -> p c f", p=128), writes=[vsb])
            P.add('pool', lambda e_: e_.tensor_copy(out=Vt[:, :, :, 0:64], in_=vstg[:].rearrange("p c (a d) -> p c a d", d=64)), [vsb], [Vb])
            P.dma('sp', vstg[:, 0:NCH - 1, :], hv[t0 + 64:t0 + S - 64, hp * 128:(hp + 1) * 128].rearrange("(c p) f -> p c f", p=128), writes=[vsb])
            P.add('pool', lambda e_: e_.tensor_copy(out=Vt2[:, 0:NCH - 1, :, 0:64], in_=vstg[:, 0:NCH - 1, :].rearrange("p c (a d) -> p c a d", d=64)), [vsb], [Vb])
            rlist = list(range(rows))

            def part_a(r, it):
                rs = min(max(r - 4, 0), rows - 8)
                ro0 = rs - r + 7
                sp_, spb = sbank[it % 3]
                s_, sb_ = sc[it % 3]
                p_, pb = pT[it % 3]
                for hh in range(2):
                    for c in range(4):
                        k0 = rs * 64 + c * 128
                        P.add('pe', lambda e_, sp_=sp_, hh=hh, c=c, k0=k0, r=r: e_.matmul(
                            sp_[:, (hh * 4 + c) * 64:(hh * 4 + c + 1) * 64], kTb[:, hh, k0:k0 + 128], qTb[:, hh, r * 64:(r + 1) * 64],
                            start=True, stop=True), [qkb], [spb])
                for hh in range(2):
                    h = 2 * hp + hh
                    P.add('dve', lambda e_, sp_=sp_, s_=s_, hh=hh, h=h, ro0=ro0: e_.tensor_tensor(
                        out=s_[:, hh * 256:(hh + 1) * 256].rearrange("p (c q) -> p c q", q=64),
                        in0=sp_[:, hh * 256:(hh + 1) * 256].rearrange("p (c q) -> p c q", q=64),
                        in1=B2[:, h, ro0:ro0 + 7:2, :], op=ALU.add), [spb, B2b], [sb_])
                P.add('act', lambda e_, s_=s_, p_=p_: e_.activation(out=p_[:], in_=s_[:], func=AF.Exp), [sb_], [pb])

            def part_b(r, it):
                rs = min(max(r - 4, 0), rows - 8)
                p_, pb = pT[it % 3]
                ob_, obb = obank[it % 2]
                y_, rc_, yb_ = yr[it % 2]
                for hh in range(2):
                    for c in range(4):
                        if rs % 2 == 0:
                            vap = Vt[:, rs // 2 + c, hh, 0:65]
                        else:
                            vap = Vt2[:, (rs - 1) // 2 + c, hh, 0:65]
                        P.add('pe', lambda e_, ob_=ob_, p_=p_, hh=hh, c=c, vap=vap: e_.matmul(
                            ob_[0:64, hh * 128:hh * 128 + 65], p_[:, (hh * 4 + c) * 64:(hh * 4 + c + 1) * 64], vap, start=(c == 0), stop=(c == 3)), [pb, Vb], [obb])
                for hh in range(2):
                    P.add('dve', lambda e_, ob_=ob_, rc_=rc_, hh=hh: e_.reciprocal(out=rc_[:, hh:hh + 1], in_=ob_[0:64, hh * 128 + 64:hh * 128 + 65]), [obb], [yb_])
                    P.add('dve', lambda e_, ob_=ob_, rc_=rc_, y_=y_, hh=hh: e_.tensor_scalar(
                        out=y_[:, hh * 64:(hh + 1) * 64], in0=ob_[0:64, hh * 128:hh * 128 + 64], scalar1=rc_[:, hh:hh + 1], scalar2=None, op0=ALU.mult), [obb, yb_], [yb_])
                P.dma('pool', yab[t0 + r * 64:t0 + (r + 1) * 64, 1024 + hp * 128:1024 + (hp + 1) * 128], y_[:], reads=[yb_])
            part_a(0, it)
            part_a(1, it + 1)
            for r in rlist:
                if r + 2 < rows:
                    part_a(r + 2, it + 2)
                part_b(r, it)
                it += 1
    P.sb_reset(mark)


A_GN_EPS = 64e-5
EHALF = float(np.exp(-0.5))


def bc_ap(t, ncols_src, n1, n2):
    return bass.AP(t, 0, [[ncols_src, t.shape[0]], [1, n1], [0, n2]])


def load_bc(P, row_ap, n, name):
    t = P.sb([128, n], F32, name); b = Buf(name)
    P.dma('sp', t[:], bass.AP(row_ap.tensor, row_ap.offset, [[0, 128], [1, n]]), writes=[b])
    return t, b


def rwkv_prep(P, C, nc, W, e, hA, PR, T, S):
    P.mark('rwkv_prep')
    mark = P.sb_mark()
    sh = [load_bc(P, W['ab_shift'][e, k], 3328, f'sh{k}') for k in range(3)]
    w0 = [load_bc(P, W['ab_w0'][e, d], 1024, f'w0{d}') for d in range(2)]
    a0 = [load_bc(P, W['ab_a0'][e, d], 1024, f'a0{d}') for d in range(2)]
    kk_ = load_bc(P, W['ab_k_k'][e], 1024, 'kk')
    ka_ = load_bc(P, W['ab_k_a'][e], 1024, 'ka')
    rk_ = load_bc(P, W['ab_r_k'][e].rearrange("h d -> (h d)"), 1024, 'rk')
    wst = P.sb([128, 2048], F32, 'wst'); wsb = Buf()
    wup = P.sb([64, 2, 1024], BF16, 'wup'); aup = P.sb([64, 2, 1024], BF16, 'aup'); gup = P.sb([128, 1024], BF16, 'gup'); lwb = Buf()
    P.dma('sp', wst[0:64, :].rearrange("p (d n) -> p d n", d=2), W['ab_w_up'][e].rearrange("d k n -> k d n"), writes=[wsb])
    P.add('dve', lambda e_: e_.tensor_copy(out=wup[:].rearrange("p d n -> p (d n)"), in_=wst[0:64, :]), [wsb], [lwb])
    P.dma('sp', wst[0:64, :].rearrange("p (d n) -> p d n", d=2), W['ab_a_up'][e].rearrange("d k n -> k d n"), writes=[wsb])
    P.add('dve', lambda e_: e_.tensor_copy(out=aup[:].rearrange("p d n -> p (d n)"), in_=wst[0:64, :]), [wsb], [lwb])
    P.dma('sp', wst[:, 0:1024], W['ab_g_up'][e], writes=[wsb])
    P.add('dve', lambda e_: e_.tensor_copy(out=gup[:], in_=wst[:, 0:1024]), [wsb], [lwb])
    hbufs = [(P.sb([128, 3328], F32, 'hp'), P.sb([128, 3328], F32, 'hc'), P.sb([128, 3328], F32, 'hn'), Buf(), Buf(), Buf()) for _ in range(2)]
    L = P.sb([128, 256], F32, 'L'); Lb = Buf()
    LT = P.sb([128, 3, 128], BF16, 'LT'); LTb = Buf()
    t1 = P.sb([128, 1024], F32, 't1'); t1b = Buf()
    t2 = P.sb([128, 1024], F32, 't2'); t2b = Buf()
    kap = P.sb([128, 1024], F32, 'kap'); kapb = Buf()
    ad = P.sb([128, 1024], F32, 'ad'); adb = Buf()
    o1 = [(P.sb([128, 1024], F32, 'o1'), Buf()) for _ in range(3)]
    sm = P.sb([128, 16], F32, 'sm'); smb = Buf()
    bon = P.sb([128, 16], F32, 'bon'); bonb = Buf()
    oi = [0]

    def outbuf():
        o = o1[oi[0] % 3]; oi[0] += 1
        return o
    def _tile(ti):
        hp, hc, hn, hpb, hcb, hnb = hbufs[ti % 2]
        t0 = ti * 128
        r = slice(t0, t0 + 128)
        first = (t0 % S == 0)
        lastt = ((t0 + 128) % S == 0)
        P.dma('sp', hc[:], hA[r, :], writes=[hcb])
        if first:
            P.add('pool', lambda e_: e_.memset(hp[0:1, :], 0.0), [], [hpb])
            P.dma('sp', hp[1:128, :], hA[t0:t0 + 127, :], writes=[hpb])
        else:
            P.dma('sp', hp[:], hA[t0 - 1:t0 + 127, :], writes=[hpb])
        if lastt:
            P.add('pool', lambda e_: e_.memset(hn[:], 0.0), [], [hnb])
            P.dma('sp', hn[0:127, :], hA[t0 + 1:t0 + 128, :], writes=[hnb])
        else:
            P.dma('sp', hn[:], hA[t0 + 1:t0 + 129, :], writes=[hnb])
        P.add('dve', lambda e_: e_.tensor_tensor(out=hc[:], in0=hc[:], in1=sh[1][0][:], op=ALU.mult), [hcb, sh[1][1]], [hcb])
        P.add('pool', lambda e_: e_.tensor_tensor(out=hp[:], in0=hp[:], in1=sh[0][0][:], op=ALU.mult), [hpb, sh[0][1]], [hpb])
        P.add('pool', lambda e_: e_.tensor_tensor(out=hn[:], in0=hn[:], in1=sh[2][0][:], op=ALU.mult), [hnb, sh[2][1]], [hnb])
        P.add('dve', lambda e_: e_.tensor_tensor(out=hc[:], in0=hc[:], in1=hp[:], op=ALU.add), [hcb, hpb], [hcb])
        P.add('dve', lambda e_: e_.tensor_tensor(out=hc[:], in0=hc[:], in1=hn[:], op=ALU.add), [hcb, hnb], [hcb])
        R_ = hc[:, 0:1024]; K_ = hc[:, 1024:2048]; V_ = hc[:, 2048:3072]
        P.dma('pool', PR['R'][r, :], R_, reads=[hcb])
        P.dma('pool', PR['V'][r, :], V_, reads=[hcb])
        P.add('act', lambda e_: e_.activation(out=L[:, 0:64], in_=hc[:, 3072:3136], func=AF.Tanh), [hcb], [Lb])
        P.add('act', lambda e_: e_.activation(out=L[:, 64:128], in_=hc[:, 3136:3200], func=AF.Identity), [hcb], [Lb])
        P.add('act', lambda e_: e_.activation(out=L[:, 128:256], in_=hc[:, 3200:3328], func=AF.Sigmoid), [hcb], [Lb])
        ps, psb = next_ps(C)
        P.add('pe', lambda e_, ps=ps: e_.transpose(ps[0:64, 0:128], L[:, 0:64], C.ident[:]), [Lb, C.b_ident], [psb])
        P.add('pe', lambda e_, ps=ps: e_.transpose(ps[0:64, 128:256], L[:, 64:128], C.ident[:]), [Lb, C.b_ident], [psb])
        P.add('pe', lambda e_, ps=ps: e_.transpose(ps[:, 256:384], L[:, 128:256], C.ident[:]), [Lb, C.b_ident], [psb])
        P.add('dve', lambda e_, ps=ps: e_.tensor_copy(out=LT[0:64, 0:2, :], in_=ps[0:64, 0:256].rearrange("p (a t) -> p a t", t=128)), [psb], [LTb])
        P.add('dve', lambda e_, ps=ps: e_.tensor_copy(out=LT[:, 2, :], in_=ps[:, 256:384]), [psb], [LTb])
        P.add('pool', lambda e_: e_.tensor_tensor(out=kap[:], in0=K_, in1=kk_[0][:], op=ALU.mult), [hcb, kk_[1]], [kapb])
        P.add('dve', lambda e_: e_.tensor_tensor(out=t1[:], in0=kap[:], in1=kap[:], op=ALU.mult), [kapb], [t1b])
        P.add('dve', lambda e_: e_.tensor_reduce(out=sm[:], in_=t1[:].rearrange("p (h d) -> p h d", d=64), axis=AX.X, op=ALU.add), [t1b], [smb])
        P.add('act', lambda e_: e_.activation(out=sm[:], in_=sm[:], func=AF.Sqrt), [smb], [smb])
        P.add('dve', lambda e_: e_.tensor_single_scalar(out=sm[:], in_=sm[:], scalar=1e-12, op=ALU.max), [smb], [smb])
        P.add('dve', lambda e_: e_.reciprocal(out=sm[:], in_=sm[:]), [smb], [smb])
        P.add('dve', lambda e_: e_.tensor_tensor(out=kap[:].rearrange("p (h d) -> p h d", d=64), in0=kap[:].rearrange("p (h d) -> p h d", d=64),
                                                 in1=bc_ap(sm, 16, 16, 64), op=ALU.mult), [kapb, smb], [kapb])
        P.dma('pool', PR['KAP'][r, :], kap[:], reads=[kapb])
        o, ob = outbuf()
        for hf in range(2):
            ps, psb = next_ps(C)
            P.add('pe', lambda e_, ps=ps, hf=hf: e_.matmul(ps[:, :], LT[:, 2, :], gup[:, hf * 512:(hf + 1) * 512], start=True, stop=True), [LTb, lwb], [psb])
            P.add('act', lambda e_, ps=ps, hf=hf, o=o: e_.activation(out=o[:, hf * 512:(hf + 1) * 512], in_=ps[:, :], func=AF.Identity), [psb], [ob])
        P.dma('pool', PR['GATE'][r, :], o[:], reads=[ob])
        for d in range(2):
            o, ob = outbuf()
            for hf in range(2):
                ps, psb = next_ps(C)
                P.add('pe', lambda e_, ps=ps, hf=hf, d=d: e_.matmul(ps[:, :], LT[0:64, 0, :], wup[:, d, hf * 512:(hf + 1) * 512], start=True, stop=True), [LTb, lwb], [psb])
                P.add('dve', lambda e_, ps=ps, hf=hf, d=d: e_.tensor_tensor(out=t1[:, hf * 512:(hf + 1) * 512], in0=ps[:, :], in1=w0[d][0][:, hf * 512:(hf + 1) * 512], op=ALU.add), [psb, w0[d][1]], [t1b])
            P.add('act', lambda e_: e_.activation(out=t1[:], in_=t1[:], func=AF.Sigmoid), [t1b], [t1b])
            P.add('act', lambda e_, o=o: e_.mul(out=o[:], in_=t1[:], mul=-EHALF), [t1b], [ob])
            P.dma('pool', PR[f'LW{d}'][r, :], o[:], reads=[ob])
            for hf in range(2):
                ps, psb = next_ps(C)
                P.add('pe', lambda e_, ps=ps, hf=hf, d=d: e_.matmul(ps[:, :], LT[0:64, 1, :], aup[:, d, hf * 512:(hf + 1) * 512], start=True, stop=True), [LTb, lwb], [psb])
                P.add('dve', lambda e_, ps=ps, hf=hf, d=d: e_.tensor_tensor(out=ad[:, hf * 512:(hf + 1) * 512], in0=ps[:, :], in1=a0[d][0][:, hf * 512:(hf + 1) * 512], op=ALU.add), [psb, a0[d][1]], [adb])
            P.add('act', lambda e_: e_.activation(out=ad[:], in_=ad[:], func=AF.Sigmoid), [adb], [adb])
            o, ob = outbuf()
            P.add('pool', lambda e_, o=o: e_.tensor_tensor(out=o[:], in0=kap[:], in1=ad[:], op=ALU.mult), [kapb, adb], [ob])
            P.dma('pool', PR[f'B{d}'][r, :], o[:], reads=[ob])
            P.add('dve', lambda e_: e_.scalar_tensor_tensor(out=t2[:], in0=ad[:], scalar=-1.0, in1=ka_[0][:], op0=ALU.add, op1=ALU.mult), [adb, ka_[1]], [t2b])
            o, ob = outbuf()
            P.add('dve', lambda e_, o=o: e_.scalar_tensor_tensor(out=o[:], in0=t2[:], scalar=1.0, in1=K_, op0=ALU.add, op1=ALU.mult), [t2b, hcb], [ob])
            P.dma('pool', PR[f'KD{d}'][r, :], o[:], reads=[ob])
            if d == 0:
                P.add('pool', lambda e_, o=o: e_.tensor_tensor(out=t2[:], in0=o[:], in1=R_, op=ALU.mult), [ob, hcb, t2b], [t2b])
                P.add('pool', lambda e_: e_.tensor_tensor(out=t2[:], in0=t2[:], in1=rk_[0][:], op=ALU.mult), [t2b, rk_[1]], [t2b])
                P.add('dve', lambda e_: e_.tensor_reduce(out=bon[:], in_=t2[:].rearrange("p (h d) -> p h d", d=64), axis=AX.X, op=ALU.add), [t2b], [bonb])
                P.dma('pool', PR['BON'][r, :], bon[:], reads=[bonb])
    for ti in range(T // 128):
        _tile(ti)
    P.sb_reset(mark)


def rwkv_consts(P, C, d):
    K = {}
    D = P.sb([128, 128], F32, 'cD'); Db = Buf()
    P.add('pool', lambda e_: e_.iota(D[:], [[1, 128]], base=0, channel_multiplier=-1, allow_small_or_imprecise_dtypes=True), [], [Db])
    PI = P.sb([128, 128], F32, 'cPI'); PIb = Buf()
    P.add('pool', lambda e_: e_.iota(PI[:], [[0, 128]], base=0, channel_multiplier=1, allow_small_or_imprecise_dtypes=True), [], [PIb])
    cb = Buf()

    def mk(name, src, srcb, scalar, op):
        t = P.sb([128, 128], F32, name)
        P.add('dve', lambda e_: e_.tensor_single_scalar(out=t[:], in_=src[:], scalar=scalar, op=op), [srcb], [cb])
        return t
    TRI = mk('cTRI', D, Db, 0.0, ALU.is_ge if d == 0 else ALU.is_le)
    TRIS = mk('cTRIS', D, Db, 0.0, ALU.is_gt if d == 0 else ALU.is_lt)
    HALF = mk('cHALF', PI, PIb, 63.5, ALU.is_lt if d == 0 else ALU.is_gt)
    A1 = P.sb([128, 128], F32, 'cA1'); A2 = P.sb([128, 128], F32, 'cA2'); A5 = P.sb([128, 128], F32, 'cA5')
    P.add('dve', lambda e_: e_.tensor_tensor(out=A1[:], in0=TRIS[:], in1=HALF[:], op=ALU.subtract), [cb], [cb])
    P.add('dve', lambda e_: e_.tensor_tensor(out=A2[:], in0=TRI[:], in1=HALF[:], op=ALU.subtract), [cb], [cb])
    P.add('dve', lambda e_: e_.tensor_scalar(out=A5[:], in0=TRI[:], scalar1=-1.0, scalar2=None, op0=ALU.mult), [cb], [cb])
    P.add('dve', lambda e_: e_.tensor_single_scalar(out=A5[:], in_=A5[:], scalar=1.0, op=ALU.add), [cb], [cb])
    K['A1'], K['A2'], K['A3'], K['A4'], K['A5'] = A1, A2, TRIS, TRI, A5
    ones = P.sb([128, 1], F32, 'cone')
    P.add('dve', lambda e_: e_.memset(ones[:], 1.0), [], [cb])
    K['ones'] = ones
    D4 = P.sb([128, 4, 128], F32, 'cD4'); D4b = Buf()
    P.add('pool', lambda e_: e_.iota(D4[:], [[0, 4], [1, 128]], base=0, channel_multiplier=-1, allow_small_or_imprecise_dtypes=True), [], [D4b])

    def mk4(name, op, neg=False):
        t = P.sb([128, 4, 128], F32, name)
        P.add('dve', lambda e_: e_.tensor_single_scalar(out=t[:], in_=D4[:], scalar=0.0, op=op), [D4b], [cb])
        if neg:
            P.add('dve', lambda e_: e_.tensor_single_scalar(out=t[:], in_=t[:], scalar=-1.0, op=ALU.mult), [cb], [cb])
        return t
    K['mT_strict'] = mk4('mTs', ALU.is_gt if d == 0 else ALU.is_lt)
    K['mT_M'] = mk4('mTm', ALU.is_ge, False) if d == 0 else K['mT_strict']
    K['mT_negstrict'] = mk4('mTn', ALU.is_gt if d == 0 else ALU.is_lt, True)
    K['mN_negstrict'] = mk4('mNn', ALU.is_lt if d == 0 else ALU.is_gt, True)
    K['b'] = cb
    return K


def rwkv_stream(P, C, K, W, e, PR, d, YF, yab, S, q, h0, NH, gng, gnb):
    cb = K['b']
    NCK = S // 128
    NW = NH * 64
    NG = NH // 4
    c0 = h0 * 64
    f32 = lambda n, nm: P.sb([128, n], F32, nm)
    inp = {k: (f32(NW, 'i' + k), Buf()) for k in ('R', 'V', 'KAP', 'LW', 'B', 'KD')}
    src = {'R': PR['R'], 'V': PR['V'], 'KAP': PR['KAP'], 'LW': PR[f'LW{d}'], 'B': PR[f'B{d}'], 'KD': PR[f'KD{d}']}
    E = [(f32(NW, 'E'), Buf()) for _ in range(3)]
    prod = {k: (f32(NW, 'p' + k), Buf()) for k in ('Kr', 'Br', 'Kdr', 'Rr', 'Ra')}
    tm = {k: (P.sb([128, NW], BF16, 'b' + k), Buf()) for k in ('Bend', 'Kend', 'V')}
    tr = {k: (P.sb([64, NH, 128], BF16, 't' + k), Buf()) for k in ('Kr', 'Br', 'Kdr', 'Rr', 'Ra')}
    mat = {k: (P.sb([128, NH, 128], BF16, 'm' + k), Buf()) for k in ('P0', 'P0T', 'LkT', 'MrbT', 'MrkT', 'Pa', 'PaT', 'Pb', 'PbT')}
    X32 = P.sb([128, NH, 128], F32, 'X32'); X32b = Buf()
    Xbf = P.sb([128, NH, 128], BF16, 'Xbf'); Xbfb = Buf()
    WT = P.sb([64, NH, 128], BF16, 'WT'); WTb = Buf()
    Ubf = P.sb([128, NH, 64], BF16, 'Ubf'); Ubb = Buf()
    Z32 = P.sb([64, NH, 64], F32, 'Z32'); Zbf = P.sb([64, NH, 64], BF16, 'Zbf'); Zb = Buf()
    ztmp = P.sb([64, NH, 64], F32, 'ztmp'); ztb = Buf()
    cC = P.sb([64, NH], F32, 'cC'); cCb = Buf()
    Y = f32(NW, 'Y'); Yb = Buf()
    if d == 1:
        YFt = f32(NW, 'YFt'); YFb = Buf()
        gat = f32(NW, 'gat'); gatb = Buf()
        bon = P.sb([128, NH], F32, 'bon'); bonb = Buf()
        s1 = P.sb([128, NH], F32, 's1'); s2 = P.sb([128, NH], F32, 's2'); stb = Buf()
        sq = f32(NW, 'sq'); sqb = Buf()
    evi = [0]

    def evac(out_ap, in_ap, reads, writes):
        if evi[0] % 3:
            P.add('act', lambda e_: e_.activation(out=out_ap, in_=in_ap, func=AF.Identity), reads, writes)
        else:
            P.add('dve', lambda e_: e_.tensor_copy(out=out_ap, in_=in_ap), reads, writes)
        evi[0] += 1

    h3 = lambda t: t[:].rearrange("p (h d) -> p h d", d=64)
    P.add('pool', lambda e_: e_.memset(Z32[:], 0.0), [], [Zb])
    P.add('pool', lambda e_: e_.memset(Zbf[:], 0.0), [], [Zb])
    order = range(NCK) if d == 0 else range(NCK - 1, -1, -1)

    def chunk(ck):
        t0 = q * S + ck * 128
        r = slice(t0, t0 + 128)
        for k in inp:
            P.dma('sp', inp[k][0][:], src[k][r, c0:c0 + NW], writes=[inp[k][1]])
        LW, LWb = inp['LW']
        ei = [0]

        def expo(Akey, scale):
            t, tb = E[ei[0] % 3]; ei[0] += 1
            ps, psb = next_ps(C)
            P.add('pe', lambda e_: e_.matmul(ps[:, 0:NW], K[Akey][:], LW[:], start=True, stop=True), [cb, LWb], [psb])
            P.add('act', lambda e_: e_.activation(out=t[:], in_=ps[:, 0:NW], func=AF.Exp, scale=scale), [psb], [tb])
            return t, tb

        def mul(out, outb, a, ab, b, bb, eng):
            P.add(eng, lambda e_: e_.tensor_tensor(out=out[:], in0=a[:], in1=b[:], op=ALU.mult), [ab, bb], [outb])
        E1, E1b = expo('A1', 1.0)
        mul(prod['Kr'][0], prod['Kr'][1], inp['KAP'][0], inp['KAP'][1], E1, E1b, 'dve')
        if d == 1:
            mul(prod['Rr'][0], prod['Rr'][1], inp['R'][0], inp['R'][1], E1, E1b, 'pool')
        E2, E2b = expo('A2', -1.0)
        mul(prod['Br'][0], prod['Br'][1], inp['B'][0], inp['B'][1], E2, E2b, 'pool')
        mul(prod['Kdr'][0], prod['Kdr'][1], inp['KD'][0], inp['KD'][1], E2, E2b, 'pool')
        if d == 0:
            E2p, E2pb = expo('A2', 1.0)
            mul(prod['Rr'][0], prod['Rr'][1], inp['R'][0], inp['R'][1], E2p, E2pb, 'pool')
        E3, E3b = expo('A3', 1.0)
        P.add('dve', lambda e_: e_.tensor_tensor(out=X32[:, :, 64:128], in0=h3(inp['KAP'][0]), in1=h3(E3), op=ALU.mult), [inp['KAP'][1], E3b], [X32b])
        P.add('pool', lambda e_: e_.tensor_copy(out=Xbf[:, :, 64:128], in_=X32[:, :, 64:128]), [X32b], [Xbfb])
        if d == 1:
            mul(prod['Ra'][0], prod['Ra'][1], inp['R'][0], inp['R'][1], E3, E3b, 'pool')
        else:
            E4, E4b = expo('A4', 1.0)
            mul(prod['Ra'][0], prod['Ra'][1], inp['R'][0], inp['R'][1], E4, E4b, 'pool')
        E5, E5b = expo('A5', 1.0)
        mul(tm['Bend'][0], tm['Bend'][1], inp['B'][0], inp['B'][1], E5, E5b, 'dve')
        mul(tm['Kend'][0], tm['Kend'][1], inp['KD'][0], inp['KD'][1], E5, E5b, 'pool')
        P.add('pool', lambda e_: e_.tensor_copy(out=tm['V'][0][:], in_=inp['V'][0][:]), [inp['V'][1]], [tm['V'][1]])
        ps, psb = next_ps(C)
        for h in range(NH):
            P.add('pe', lambda e_, ps=ps, h=h: e_.matmul(ps[0:64, h:h + 1], LW[:, h * 64:(h + 1) * 64], K['ones'][:], start=True, stop=True), [LWb, cb], [psb])
        P.add('act', lambda e_, ps=ps: e_.activation(out=cC[:], in_=ps[0:64, 0:NH], func=AF.Exp), [psb], [cCb])
        yield
        for k in ('Kr', 'Br', 'Kdr', 'Rr', 'Ra'):
            pa, pab = prod[k]
            ta, tab = tr[k]
            for g in range(NG):
                ps, psb = next_ps(C)
                for hi in range(4):
                    h = 4 * g + hi
                    P.add('pe', lambda e_, ps=ps, hi=hi, h=h, pa=pa: e_.transpose(ps[0:64, hi * 128:(hi + 1) * 128], pa[:, h * 64:(h + 1) * 64], C.ident[:]),
                          [pab, C.b_ident], [psb])
                evac(ta[:, 4 * g:4 * g + 4, :], ps[0:64, :].rearrange("p (a t) -> p a t", t=128), [psb], [tab])
        yield

        def score(dst, lk, rk_, mask):
            da, dab = mat[dst]
            for g in range(NG):
                ps, psb = next_ps(C)
                for hi in range(4):
                    h = 4 * g + hi
                    P.add('pe', lambda e_, ps=ps, hi=hi, h=h: e_.matmul(ps[:, hi * 128:(hi + 1) * 128], tr[lk][0][:, h, :], tr[rk_][0][:, h, :], start=True, stop=True),
                          [tr[lk][1], tr[rk_][1]], [psb])
                P.add('dve', lambda e_, ps=ps, g=g, da=da: e_.tensor_tensor(
                    out=da[:, 4 * g:4 * g + 4, :], in0=ps[:, :].rearrange("p (a t) -> p a t", t=128), in1=K[mask][:], op=ALU.mult), [psb, cb], [dab])
        score('P0', 'Kr', 'Br', 'mN_negstrict')
        score('P0T', 'Br', 'Kr', 'mT_negstrict')
        score('LkT', 'Kdr', 'Kr', 'mT_strict')
        yield
        score('MrbT', 'Br', 'Rr', 'mT_M')
        score('MrkT', 'Kdr', 'Rr', 'mT_M')
        for g in range(NH // 8):
            ps, psb = next_ps(C)
            for hi in range(8):
                h = 8 * g + hi
                P.add('pe', lambda e_, ps=ps, hi=hi, h=h: e_.matmul(ps[:, hi * 64:(hi + 1) * 64], mat['LkT'][0][:, h, :], tm['V'][0][:, h * 64:(h + 1) * 64], start=True, stop=True),
                      [mat['LkT'][1], tm['V'][1]], [psb])
            P.add('dve', lambda e_, ps=ps, g=g: e_.tensor_copy(out=X32[:, 8 * g:8 * g + 8, 0:64], in_=ps[:, :].rearrange("p (a i) -> p a i", i=64)), [psb], [X32b])
            P.add('act', lambda e_, ps=ps, g=g: e_.activation(out=Xbf[:, 8 * g:8 * g + 8, 0:64], in_=ps[:, :].rearrange("p (a i) -> p a i", i=64), func=AF.Identity), [psb], [Xbfb])
        yield
        cur, curT = 'P0', 'P0T'
        nxt = [('Pa', 'PaT'), ('Pb', 'PbT')]
        for lev in range(7):
            for g in range(NG):
                ps, psb = next_ps(C)
                for hi in range(4):
                    h = 4 * g + hi
                    P.add('pe', lambda e_, ps=ps, hi=hi, h=h, curT=curT: e_.matmul(ps[:, hi * 128:(hi + 1) * 128], mat[curT][0][:, h, :], Xbf[:, h, :], start=True, stop=True),
                          [mat[curT][1], Xbfb], [psb])
                P.add('dve', lambda e_, ps=ps, g=g: e_.tensor_tensor(out=X32[:, 4 * g:4 * g + 4, :], in0=X32[:, 4 * g:4 * g + 4, :],
                                                                    in1=ps[:, :].rearrange("p (a t) -> p a t", t=128), op=ALU.add), [psb, X32b], [X32b])
            for g in range(NG):
                P.add('act', lambda e_, g=g: e_.activation(out=Xbf[:, 4 * g:4 * g + 4, :], in_=X32[:, 4 * g:4 * g + 4, :], func=AF.Identity), [X32b], [Xbfb])
            if lev < 6:
                n, nT = nxt[lev % 2]
                for g in range(NG):
                    psA, psAb = next_ps(C)
                    psB, psBb = next_ps(C)
                    for hi in range(4):
                        h = 4 * g + hi
                        P.add('pe', lambda e_, psA=psA, hi=hi, h=h, cur=cur, curT=curT: e_.matmul(psA[:, hi * 128:(hi + 1) * 128], mat[curT][0][:, h, :], mat[cur][0][:, h, :], start=True, stop=True),
                              [mat[cur][1], mat[curT][1]], [psAb])
                        P.add('pe', lambda e_, psB=psB, hi=hi, h=h, cur=cur, curT=curT: e_.matmul(psB[:, hi * 128:(hi + 1) * 128], mat[cur][0][:, h, :], mat[curT][0][:, h, :], start=True, stop=True),
                              [mat[cur][1], mat[curT][1]], [psBb])
                    evac(mat[n][0][:, 4 * g:4 * g + 4, :], psA[:, :].rearrange("p (a t) -> p a t", t=128), [psAb], [mat[n][1]])
                    evac(mat[nT][0][:, 4 * g:4 * g + 4, :], psB[:, :].rearrange("p (a t) -> p a t", t=128), [psBb], [mat[nT][1]])
                cur, curT = n, nT
            yield
        for g in range(NG):
            ps, psb = next_ps(C)
            for hi in range(4):
                h = 4 * g + hi
                P.add('pe', lambda e_, ps=ps, hi=hi, h=h: e_.transpose(ps[0:64, hi * 128:(hi + 1) * 128], X32[:, h, 64:128], C.ident[:]), [X32b, C.b_ident], [psb])
            evac(WT[:, 4 * g:4 * g + 4, :], ps[0:64, :].rearrange("p (a t) -> p a t", t=128), [psb], [WTb])
        yield
        for g in range(NH // 8):
            ps, psb = next_ps(C)
            for hi in range(8):
                h = 8 * g + hi
                P.add('pe', lambda e_, ps=ps, hi=hi, h=h: e_.matmul(ps[:, hi * 64:(hi + 1) * 64], WT[:, h, :], Zbf[:, h, :], start=True, stop=True), [WTb, Zb], [psb])
            P.add('dve', lambda e_, ps=ps, g=g: e_.scalar_tensor_tensor(out=Ubf[:, 8 * g:8 * g + 8, :], in0=ps[:, :].rearrange("p (a i) -> p a i", i=64), scalar=-1.0,
                                                                      in1=X32[:, 8 * g:8 * g + 8, 0:64], op0=ALU.mult, op1=ALU.subtract), [psb, X32b], [Ubb])
        yield
        for g in range(NH // 8):
            ps, psb = next_ps(C)
            for hi in range(8):
                h = 8 * g + hi
                o = ps[:, hi * 64:(hi + 1) * 64]
                P.add('pe', lambda e_, o=o, h=h: e_.matmul(o, tr['Ra'][0][:, h, :], Zbf[:, h, :], start=True, stop=False), [tr['Ra'][1], Zb], [psb])
                P.add('pe', lambda e_, o=o, h=h: e_.matmul(o, mat['MrbT'][0][:, h, :], Ubf[:, h, :], start=False, stop=False), [mat['MrbT'][1], Ubb], [psb])
                P.add('pe', lambda e_, o=o, h=h: e_.matmul(o, mat['MrkT'][0][:, h, :], tm['V'][0][:, h * 64:(h + 1) * 64], start=False, stop=True), [mat['MrkT'][1], tm['V'][1]], [psb])
            evac(Y[:, g * 512:(g + 1) * 512], ps[:, :], [psb], [Yb])
        for g in range(NH // 8):
            ps, psb = next_ps(C)
            for hi in range(8):
                h = 8 * g + hi
                o = ps[0:64, hi * 64:(hi + 1) * 64]
                P.add('pe', lambda e_, o=o, h=h: e_.matmul(o, tm['Bend'][0][:, h * 64:(h + 1) * 64], Ubf[:, h, :], start=True, stop=False), [tm['Bend'][1], Ubb], [psb])
                P.add('pe', lambda e_, o=o, h=h: e_.matmul(o, tm['Kend'][0][:, h * 64:(h + 1) * 64], tm['V'][0][:, h * 64:(h + 1) * 64], start=False, stop=True), [tm['Kend'][1], tm['V'][1]], [psb])
            P.add('dve', lambda e_, g=g: e_.tensor_tensor(out=ztmp[:, 8 * g:8 * g + 8, :], in0=Z32[:, 8 * g:8 * g + 8, :],
                                                         in1=bass.AP(cC, 8 * g, [[NH, 64], [1, 8], [0, 64]]), op=ALU.mult), [Zb, cCb], [ztb])
            P.add('dve', lambda e_, ps=ps, g=g: e_.tensor_tensor(out=Z32[:, 8 * g:8 * g + 8, :], in0=ztmp[:, 8 * g:8 * g + 8, :],
                                                                in1=ps[0:64, :].rearrange("p (a i) -> p a i", i=64), op=ALU.add), [ztb, psb], [Zb])
        P.add('act', lambda e_: e_.activation(out=Zbf[:], in_=Z32[:], func=AF.Identity), [Zb], [Zb])
        yield
        if d == 0:
            P.dma('pool', YF[r, c0:c0 + NW], Y[:], reads=[Yb])
        else:
            P.dma('sp', YFt[:], YF[r, c0:c0 + NW], writes=[YFb])
            P.dma('sp', gat[:], PR['GATE'][r, c0:c0 + NW], writes=[gatb])
            P.dma('sp', bon[:], PR['BON'][r, h0:h0 + NH], writes=[bonb])
            P.add('dve', lambda e_: e_.tensor_tensor(out=Y[:], in0=Y[:], in1=YFt[:], op=ALU.add), [Yb, YFb], [Yb])
            P.add('dve', lambda e_: e_.tensor_reduce(out=s1[:], in_=h3(Y), axis=AX.X, op=ALU.add), [Yb], [stb])
            P.add('pool', lambda e_: e_.tensor_tensor(out=sq[:], in0=Y[:], in1=Y[:], op=ALU.mult), [Yb], [sqb])
            P.add('dve', lambda e_: e_.tensor_reduce(out=s2[:], in_=h3(sq), axis=AX.X, op=ALU.add), [sqb], [stb])
            P.add('dve', lambda e_: e_.tensor_single_scalar(out=s1[:], in_=s1[:], scalar=1.0 / 64, op=ALU.mult), [stb], [stb])
            P.add('dve', lambda e_: e_.tensor_single_scalar(out=s2[:], in_=s2[:], scalar=1.0 / 64, op=ALU.mult), [stb], [stb])
            P.add('dve', lambda e_: e_.tensor_tensor(out=sq[:, 0:NH], in0=s1[:], in1=s1[:], op=ALU.mult), [stb, sqb], [sqb])
            P.add('dve', lambda e_: e_.tensor_tensor(out=s2[:], in0=s2[:], in1=sq[:, 0:NH], op=ALU.subtract), [stb, sqb], [stb])
            P.add('dve', lambda e_: e_.tensor_single_scalar(out=s2[:], in_=s2[:], scalar=A_GN_EPS, op=ALU.add), [stb], [stb])
            P.add('act', lambda e_: e_.activation(out=s2[:], in_=s2[:], func=AF.Sqrt), [stb], [stb])
            P.add('dve', lambda e_: e_.reciprocal(out=s2[:], in_=s2[:]), [stb], [stb])
            P.add('dve', lambda e_: e_.tensor_tensor(out=h3(Y), in0=h3(Y), in1=bc_ap(s1, NH, NH, 64), op=ALU.subtract), [Yb, stb], [Yb])
            P.add('dve', lambda e_: e_.tensor_tensor(out=h3(Y), in0=h3(Y), in1=bc_ap(s2, NH, NH, 64), op=ALU.mult), [Yb, stb], [Yb])
            P.add('pool', lambda e_: e_.tensor_tensor(out=Y[:], in0=Y[:], in1=gng[0][:, c0:c0 + NW], op=ALU.mult), [Yb, gng[1]], [Yb])
            P.add('pool', lambda e_: e_.tensor_tensor(out=Y[:], in0=Y[:], in1=gnb[0][:, c0:c0 + NW], op=ALU.add), [Yb, gnb[1]], [Yb])
            P.add('dve', lambda e_: e_.tensor_tensor(out=h3(sq), in0=h3(inp['V'][0]), in1=bc_ap(bon, NH, NH, 64), op=ALU.mult), [inp['V'][1], bonb, sqb], [sqb])
            P.add('dve', lambda e_: e_.tensor_tensor(out=Y[:], in0=Y[:], in1=sq[:], op=ALU.add), [Yb, sqb], [Yb])
            P.add('dve', lambda e_: e_.tensor_tensor(out=Y[:], in0=Y[:], in1=gat[:], op=ALU.mult), [Yb, gatb], [Yb])
            P.dma('pool', yab[r, c0:c0 + NW], Y[:], reads=[Yb])
        yield

    for ck in order:
        yield from chunk(ck)


RWKV_STAGGER = 0


def rwkv_scan(P, C, nc, W, e, PR, d, YF, yab, T, S, NH=8):
    P.mark('rwkv_scan')
    mark = P.sb_mark()
    K = rwkv_consts(P, C, d)
    gng = gnb = None
    if d == 1:
        gng = load_bc(P, W['ab_gn_g'][e], 1024, 'gng'); gnb = load_bc(P, W['ab_gn_b'][e], 1024, 'gnb')
    nseq = T // S
    m2 = P.sb_mark()
    for h0 in range(0, 16, NH):
        for q0 in range(0, nseq, 2):
            gens = [rwkv_stream(P, C, K, W, e, PR, d, YF, yab, S, q, h0, NH, gng, gnb) for q in range(q0, min(nseq, q0 + 2))]
            live = list(gens)
            if len(gens) > 1:
                for _ in range(RWKV_STAGGER):
                    next(gens[0])
            while live:
                for g in list(live):
                    try:
                        next(g)
                    except StopIteration:
                        live.remove(g)
            P.sb_reset(m2)
    P.sb_reset(mark)


WSPEC = [
    ('ab_w_in', (2, 2048, 6400)), ('ab_shift', (2, 3, 3328)), ('ab_w0', (2, 2, 1024)), ('ab_w_up', (2, 2, 64, 1024)),
    ('ab_a0', (2, 2, 1024)), ('ab_a_up', (2, 2, 64, 1024)), ('ab_g_up', (2, 128, 1024)), ('ab_k_k', (2, 1024)),
    ('ab_k_a', (2, 1024)), ('ab_r_k', (2, 16, 64)), ('ab_gn_g', (2, 1024)), ('ab_gn_b', (2, 1024)),
    ('ab_rpb', (2, 16, 15, 31)), ('ab_w_out', (2, 2048, 2048)), ('c_w_in', (2, 2048, 12288)), ('c_gn_g', (2, 4096)),
    ('c_gn_b', (2, 4096)), ('c_w_out', (2, 4096, 2048)), ('ln1_g', (4, 2048)), ('ln1_b', (4, 2048)),
    ('ffn_w_up', (4, 2048, 11008)), ('ffn_conv', (4, 3, 5504)), ('ffn_conv_b', (4, 5504)), ('ffn_w_down', (4, 5504, 2048)),
    ('ln2_g', (4, 2048)), ('ln2_b', (4, 2048)),
]
BIGW = ['ab_w_in', 'ab_w_out', 'c_w_in', 'c_w_out', 'ffn_w_up', 'ffn_w_down']


def build(NSEQ, S, layers, debug_outs=()):
    T = NSEQ * S
    nc = bass.Bass("TRN2", target_bir_lowering=False)
    W = {}
    for name, shp in WSPEC:
        W[name] = nc.dram_tensor(name, list(shp), F32, kind="ExternalInput").ap()
    x_in = nc.dram_tensor("x", [T, D], F32, kind="ExternalInput").ap()
    y_out = nc.dram_tensor("y", [T, D], F32, kind="ExternalOutput").ap()

    def scr(name, shape, dt):
        kind = "ExternalOutput" if name in debug_outs else "Internal"
        return nc.dram_tensor(name, shape, dt, kind=kind).ap()
    WB = {}
    for name in BIGW:
        shp = dict(WSPEC)[name]
        WB[name] = scr(name + "_b", list(shp), BF16)
    xs = [scr("xs0", [T, D], F32), scr("xs1", [T, D], F32)]
    xTs = [scr("xT0", [D, T], BF16), scr("xT1", [D, T], BF16)]
    hTs = [scr(f"hT{q}", [11008, S], F32) for q in range(NSEQ)]
    vgs = [scr(f"vg{q}", [S, 8192], F32) for q in range(NSEQ)]
    gT = scr("gT", [5504, T], BF16)
    qkr = scr("qkr", [4096, T], BF16)
    ytok = scr("ytok", [T, 4096], F32)
    yT = scr("yT", [4096, T], BF16)
    mix = scr("mix", [T, D], F32)

    P = Prog(nc); C = Ctx()
    setup_consts(P, C)
    TT = min(512, S)
    for l in layers:
        e = l // 2
        if l % 2 == 0:
            cast_weight(P, C, W['ab_w_in'][e], WB['ab_w_in'][e], 2048, 6400)
            cast_weight(P, C, W['ab_w_out'][e], WB['ab_w_out'][e], 2048, 2048)
        else:
            cast_weight(P, C, W['c_w_in'][e], WB['c_w_in'][e], 2048, 12288)
            cast_weight(P, C, W['c_w_out'][e], WB['c_w_out'][e], 4096, 2048)
        cast_weight(P, C, W['ffn_w_up'][l], WB['ffn_w_up'][l], 2048, 11008)
        cast_weight(P, C, W['ffn_w_down'][l], WB['ffn_w_down'][l], 5504, 2048)
    xprep(P, C, x_in, xTs[0], T)
    xcur, xTcur, pp = x_in, xTs[0], 0

    def G(xT_, w_, K, N, mode, out, n_off=0, TG=T):
        m = P.sb_mark()
        gemm(P, C, xT_, w_, K, TG, N, mode, StoreEpi(P, C, out, mode), TT=TT, n_off=n_off)
        P.sb_reset(m)

    def GLN(xT_, w_, K, xin, g_row, b_row, xout, xTout, KS=1):
        m = P.sb_mark()
        C.ps_pool = (6, 0)
        gemm(P, C, xT_, w_, K, T, 2048, 'tok', LNEpi(P, C, xin, g_row, b_row, xout, xTout), TT=TT, KS=KS)
        C.ps_pool = (8, 0)
        P.sb_reset(m)

    for li, l in enumerate(layers):
        e = l // 2
        if l % 2 == 0:
            yabT_ = mixer_ab(P, C, nc, W, WB, e, xTcur, mix, T, S, NSEQ, scr_fn=scr, G=G)
        else:
            for q in range(NSEQ):
                sq = slice(q * S, (q + 1) * S)
                G(xTcur[:, sq], WB['c_w_in'][e], 2048, 4096, 'feat', hTs[q][0:4096, :], TG=S)
                G(xTcur[:, sq], WB['c_w_in'][e], 2048, 8192, 'tok', vgs[q], n_off=4096, TG=S)
                ret_rotary(P, C, hTs[q][0:4096, :], qkr[:, sq], S, S)
                ret_attn(P, C, qkr[:, sq], vgs[q], W['c_gn_g'][e], W['c_gn_b'][e], ytok[sq, :], S, S)
            xprep_n(P, C, ytok, yT, T, 4096)
        x1, x1T = xs[pp], xTs[1 - pp]
        if l % 2 == 0:
            G(yabT_, WB['ab_w_out'][e], 2048, 2048, 'tok', mix)
        else:
            G(yT, WB['c_w_out'][e], 4096, 2048, 'tok', mix)
        ln_phase(P, C, xcur, mix, W['ln1_g'][l], W['ln1_b'][l], x1, x1T, T)
        for q in range(NSEQ):
            sq = slice(q * S, (q + 1) * S)
            G(x1T[:, sq], WB['ffn_w_up'][l], 2048, 11008, 'feat', hTs[q], TG=S)
            ffn_mid(P, C, hTs[q], W['ffn_conv'][l], W['ffn_conv_b'][l], gT[:, sq], S, S)
        last = (li == len(layers) - 1)
        x2 = y_out if last else xs[1 - pp]
        x2T = xTs[pp]
        G(gT, WB['ffn_w_down'][l], 5504, 2048, 'tok', mix)
        ln_phase(P, C, x1, mix, W['ln2_g'][l], W['ln2_b'][l], x2, None if last else x2T, T)
        xcur, xTcur = x2, x2T
        pp = 1 - pp
    P.emit()
    return nc, P


def mixer_ab(P, C, nc, W, WB, e, xTcur, mix, T, S, NSEQ, scr_fn, G):
    if not hasattr(C, 'ab_scr'):
        d = {}
        d['hA'] = scr_fn("hA", [T, 3328], F32)
        d['hqk'] = scr_fn("hqk", [2048, T], F32)
        d['hv'] = scr_fn("hv", [T, 1024], F32)
        d['yab'] = scr_fn("yab", [T, 2048], F32)
        d['yabT'] = scr_fn("yabT", [2048, T], BF16)
        d['YF'] = scr_fn("YF", [T, 1024], F32)
        d['PR'] = {k: scr_fn("pr_" + k, [T, 1024], F32) for k in ('R', 'V', 'KAP', 'GATE', 'LW0', 'LW1', 'B0', 'B1', 'KD0', 'KD1')}
        d['PR']['BON'] = scr_fn("pr_BON", [T, 16], F32)
        C.ab_scr = d
    d = C.ab_scr
    G(xTcur, WB['ab_w_in'][e], 2048, 3328, 'tok', d['hA'])
    G(xTcur, WB['ab_w_in'][e], 2048, 2048, 'feat', d['hqk'], n_off=3328)
    G(xTcur, WB['ab_w_in'][e], 2048, 1024, 'tok', d['hv'], n_off=3328 + 2048)
    rwkv_prep(P, C, nc, W, e, d['hA'], d['PR'], T, S)
    rwkv_scan(P, C, nc, W, e, d['PR'], 0, d['YF'], d['yab'], T, S)
    rwkv_scan(P, C, nc, W, e, d['PR'], 1, d['YF'], d['yab'], T, S)
    bS = na_bias_build(P, C, nc, W['ab_rpb'][e], scr_fn, e)
    na_attn(P, C, nc, bS, d['hqk'], d['hv'], d['yab'], T, S)
    xprep_n(P, C, d['yab'], d['yabT'], T, 2048)
    return d['yabT']


_CACHE = {}
NSEQ_CORE = 2
SEQ = 4096


def _assign():
    return [(('p', c), ('s', c) if c < 4 else ('p', c)) for c in range(8)]


def kernel(**inputs):
    if 'nc' not in _CACHE:
        _CACHE['nc'] = build(NSEQ_CORE, SEQ, [0, 1, 2, 3])[0]
    nc = _CACHE['nc']
    xp = np.asarray(inputs['x_prompt'], dtype=np.float32)
    xs_ = np.asarray(inputs['x_sample'], dtype=np.float32)
    wd = {n: np.ascontiguousarray(np.asarray(inputs[n], dtype=np.float32)) for n, _ in WSPEC}
    in_maps = []
    asg = _assign()
    for c in range(8):
        rows = []
        for kind, i in asg[c]:
            rows.append(xp[i] if kind == 'p' else xs_[i])
        m = dict(wd)
        m['x'] = np.ascontiguousarray(np.concatenate(rows, axis=0))
        in_maps.append(m)
    res = run_bass_kernel_spmd(nc, in_maps, core_ids=list(range(8)))
    yp = np.empty_like(xp)
    ys = np.empty_like(xs_)
    for c in range(8):
        y = np.asarray(res.results[c]['y'])
        for slot, (kind, i) in enumerate(asg[c]):
            blk = y[slot * SEQ:(slot + 1) * SEQ]
            if slot == 1 and c >= 4:
                continue
            if kind == 'p':
                yp[i] = blk
            else:
                ys[i] = blk
    return (yp, ys)
```

```python
import numpy as np
import concourse.bass as bass
import concourse.mybir as mybir
from concourse.bass_utils import run_bass_kernel_spmd

F32 = mybir.dt.float32
BF16 = mybir.dt.bfloat16
I32 = mybir.dt.int32
AF = mybir.ActivationFunctionType
ALU = mybir.AluOpType
AX = mybir.AxisListType

ENGS = ('pe', 'act', 'dve', 'pool', 'sp')
DMAQ = {'sp': 16, 'act': 8, 'pool': 8}


class Buf:
    __slots__ = ('name', 'lw', 'rd')

    def __init__(self, name=''):
        self.name = name
        self.lw = None
        self.rd = []


class Op:
    __slots__ = ('eng', 'fn', 'dma', 'deps', 'sig', 'slot', 'slotval', 'idx', 'bar')

    def __init__(self, eng, fn, dma):
        self.eng = eng
        self.fn = fn
        self.dma = dma
        self.deps = ()
        self.sig = False
        self.slot = None
        self.slotval = 0
        self.idx = 0
        self.bar = False


class Prog:
    def __init__(self, nc):
        self.nc = nc
        self.ops = []
        self.last = {e: None for e in ENGS}
        self.sb_off = 16640
        self.sb_base = 16640
        self.nalloc = 0

    def sb(self, shape, dtype, name='t'):
        nbytes = int(np.prod(shape[1:])) * (4 if dtype in (F32, I32) else 2)
        off = (self.sb_off + 63) // 64 * 64
        assert off + nbytes <= 229000, (off, nbytes, name)
        self.sb_off = off + nbytes
        self.nalloc += 1
        return self.nc.alloc_sbuf_tensor_at(f"{name}{self.nalloc}", list(shape), dtype, offset=off)

    def sb_mark(self):
        return self.sb_off

    def sb_reset(self, mark):
        self.sb_off = mark
        self.barrier()

    def add(self, eng, fn, reads=(), writes=(), dma=False):
        op = Op(eng, fn, dma)
        deps = set()
        for b in reads:
            if b.lw is not None:
                deps.add(b.lw)
        for b in writes:
            if b.lw is not None:
                deps.add(b.lw)
            deps.update(b.rd)
        dl = []
        for d in deps:
            if d is op:
                continue
            if (not dma) and (not d.dma) and d.eng == 'pe' and eng == 'pe':
                continue
            d.sig = True
            dl.append(d)
        op.deps = dl
        for b in reads:
            b.rd.append(op)
        for b in writes:
            b.lw = op
            b.rd = []
        self.ops.append(op)
        if not dma:
            self.last[eng] = op
        return op

    def dma(self, q, out, in_, reads=(), writes=(), **kw):
        return self.add(q, lambda e: e.dma_start(out=out, in_=in_, **kw), reads, writes, dma=True)

    def mark(self, name):
        op = Op('mark', None, False)
        op.bar = name
        self.ops.append(op)

    def barrier(self):
        op = Op('all', None, False)
        op.bar = True
        for e in ENGS:
            if self.last[e] is not None:
                self.last[e].sig = True
        self.ops.append(op)

    def emit(self):
        nc = self.nc
        self.barrier()
        engsem = {e: nc.alloc_semaphore(f"s_{e}") for e in ENGS}
        slotsem = {q: [nc.alloc_semaphore(f"d_{q}{i}") for i in range(n)] for q, n in DMAQ.items()}
        slotuse = {q: [0] * n for q, n in DMAQ.items()}
        rr = {q: 0 for q in DMAQ}
        cnt = {e: 0 for e in ENGS}
        waited = {e: {} for e in ENGS}
        streams = {e: [] for e in ENGS}

        def want(e, sem, val):
            if val <= 0:
                return
            w = waited[e]
            k = id(sem)
            if w.get(k, 0) >= val:
                return
            w[k] = val
            streams[e].append(('w', sem, val))

        self.marks = []
        for op in self.ops:
            if op.eng == 'mark':
                self.marks.append((op.bar, dict(cnt)))
                continue
            if op.bar:
                for e in ENGS:
                    for x in ENGS:
                        if x != e:
                            want(e, engsem[x], cnt[x])
                    for q in DMAQ:
                        for i, s in enumerate(slotsem[q]):
                            want(e, s, 16 * slotuse[q][i])
                continue
            e = op.eng
            for d in op.deps:
                if d.dma:
                    want(e, slotsem[d.eng][d.slot], d.slotval)
                else:
                    want(e, engsem[d.eng], d.idx)
            if op.dma:
                k = rr[e]
                rr[e] = (k + 1) % DMAQ[e]
                want(e, slotsem[e][k], 16 * slotuse[e][k])
                slotuse[e][k] += 1
                op.slot = k
                op.slotval = 16 * slotuse[e][k]
                streams[e].append(('d', op, slotsem[e][k]))
            else:
                if op.sig:
                    cnt[e] += 1
                    op.idx = cnt[e]
                streams[e].append(('o', op, engsem[e]))

        def run_stream(eng_handle, items):
            for it in items:
                if it[0] == 'w':
                    eng_handle.wait_ge(it[1], it[2])
                elif it[0] == 'd':
                    it[1].fn(eng_handle).then_inc(it[2], 16)
                else:
                    ins = it[1].fn(eng_handle)
                    if it[1].sig:
                        ins.then_inc(it[2], 1)

        with nc.Block() as block:
            @block.sync
            def _(e):
                run_stream(e, streams['sp'])

            @block.tensor
            def _(e):
                run_stream(e, streams['pe'])

            @block.scalar
            def _(e):
                run_stream(e, streams['act'])

            @block.vector
            def _(e):
                run_stream(e, streams['dve'])

            @block.gpsimd
            def _(e):
                run_stream(e, streams['pool'])
        self.nitems = {e: len(streams[e]) for e in ENGS}


D = 2048
DFF = 5504
ALPHA = 8.0 ** 0.25
LN_EPS = 1e-5


class Ctx:
    pass


def setup_consts(P, C):
    nc = P.nc
    idn = nc.inline_tensor(np.eye(128, dtype=np.float32), "ident_d").ap()
    C.ident = P.sb([128, 128], F32, 'ident')
    C.identb = P.sb([128, 128], BF16, 'identb')
    C.b_ident = Buf('ident')
    P.dma('sp', C.ident[:], idn[:, :], writes=[C.b_ident])
    P.add('dve', lambda e: e.tensor_copy(out=C.identb[:], in_=C.ident[:]), [C.b_ident], [C.b_ident])
    C.ps = [nc.alloc_psum_tensor(f"psb{i}", [128, 512], F32) for i in range(8)]
    C.psb = [Buf(f'ps{i}') for i in range(8)]
    C.psi = 0


def next_ps(C, n=8, base=0):
    if n == 8 and base == 0:
        n, base = getattr(C, 'ps_pool', (8, 0))
    i = base + (C.psi % n)
    C.psi += 1
    return C.ps[i], C.psb[i]


def next_ps_t(C):
    C.psti = getattr(C, 'psti', 0) + 1
    i = 6 + (C.psti % 2)
    return C.ps[i], C.psb[i]


def cast_weight(P, C, src, dst, K, N):
    P.mark('cast_weight')
    mark = P.sb_mark()
    CB = 2048
    NBUF = 6
    st = [(P.sb([128, CB], F32, 'cw'), P.sb([128, CB], BF16, 'cwb'), Buf(), Buf()) for _ in range(NBUF)]
    engs = ['dve', 'act', 'pool']
    i = 0
    for r0 in range(0, K, 128):
        for c0 in range(0, N, CB):
            cn = min(CB, N - c0)
            a, b, ba, bb = st[i % NBUF]
            eng = engs[i % 3]
            P.dma('sp', a[:, 0:cn], src[r0:r0 + 128, c0:c0 + cn], writes=[ba])
            if eng == 'act':
                P.add('act', lambda e, a=a, b=b, cn=cn: e.copy(out=b[:, 0:cn], in_=a[:, 0:cn]), [ba], [bb])
            else:
                P.add(eng, lambda e, a=a, b=b, cn=cn: e.tensor_copy(out=b[:, 0:cn], in_=a[:, 0:cn]), [ba], [bb])
            P.dma('pool', dst[r0:r0 + 128, c0:c0 + cn], b[:, 0:cn], reads=[bb])
            i += 1
    P.sb_reset(mark)


def xprep(P, C, x, xT, T):
    P.mark('xprep')
    mark = P.sb_mark()
    KC = D // 128
    NB = 4
    st = [(P.sb([128, D], F32, 'xp'), P.sb([128, KC, 128], BF16, 'xpt'), Buf(), Buf()) for _ in range(NB)]
    xTv = xT.rearrange("(kc p) t -> p kc t", p=128)
    for ti in range(T // 128):
        a, b, ba, bb = st[ti % NB]
        P.dma('sp', a[:], x[ti * 128:(ti + 1) * 128, :], writes=[ba])
        transpose_to(P, C, a, ba, b, bb, KC)
        P.dma('pool', xTv[:, :, ti * 128:(ti + 1) * 128], b[:], reads=[bb])
    P.sb_reset(mark)


def transpose_to(P, C, a, ba, b, bb, KC, evi=[0], tbanks=False):
    for g in range(0, KC, 4):
        ps, psb = next_ps_t(C) if tbanks else next_ps(C)
        ng = min(4, KC - g)
        for j in range(ng):
            kc = g + j
            P.add('pe', lambda e, ps=ps, j=j, kc=kc: e.transpose(ps[:, j * 128:(j + 1) * 128], a[:, kc * 128:(kc + 1) * 128], C.ident[:]),
                  [ba, C.b_ident], [psb])
        eng = 'act' if evi[0] % 2 else 'dve'
        evi[0] += 1
        src = lambda ps=ps, ng=ng: ps[:, 0:ng * 128].rearrange("p (j t) -> p j t", t=128)
        if eng == 'act':
            P.add('act', lambda e, g=g, ng=ng, src=src: e.copy(out=b[:, g:g + ng, :], in_=src()), [psb], [bb])
        else:
            P.add('dve', lambda e, g=g, ng=ng, src=src: e.tensor_copy(out=b[:, g:g + ng, :], in_=src()), [psb], [bb])


def gemm(P, C, xT, w, K, T, N, mode, epi, TT=512, n_off=0, KS=1):
    P.mark('gemm')
    mark = P.sb_mark()
    KC = K // 128
    NBW = 512
    KW = (KC + KS - 1) // KS
    xt = [(P.sb([128, KC, TT], BF16, 'gx'), Buf()) for _ in range(2)]
    wt = [(P.sb([128, KW, NBW], BF16, 'gw'), Buf()) for _ in range(2)]
    xTv = xT.rearrange("(kc p) t -> p kc t", p=128)
    wv = w.rearrange("(kc p) n -> p kc n", p=128)
    wi = 0
    for tti, t0 in enumerate(range(0, T, TT)):
        tt = min(TT, T - t0)
        xa, xb = xt[tti % 2]
        h = KC // 2
        P.dma('sp', xa[:, 0:h, 0:tt], xTv[:, 0:h, t0:t0 + tt], writes=[xb])
        P.dma('sp', xa[:, h:KC, 0:tt], xTv[:, h:KC, t0:t0 + tt], writes=[xb])
        if hasattr(epi, 'pre_tile'):
            epi.pre_tile(t0, tt)
        for n0 in range(0, N, NBW):
            nn = min(NBW, N - n0)
            if mode == 'tok':
                banks = [next_ps(C) for _ in range(0, tt, 128)]
                for ks in range(KS):
                    k0, k1 = ks * KW, min(KC, (ks + 1) * KW)
                    wa, wb = wt[wi % 2]
                    wi += 1
                    hh = (k0 + k1) // 2
                    P.dma('sp', wa[:, 0:hh - k0, 0:nn], wv[:, k0:hh, n_off + n0:n_off + n0 + nn], writes=[wb])
                    P.dma('sp', wa[:, hh - k0:k1 - k0, 0:nn], wv[:, hh:k1, n_off + n0:n_off + n0 + nn], writes=[wb])
                    for si, s0 in enumerate(range(0, tt, 128)):
                        ps, psb = banks[si]
                        for kc in range(k0, k1):
                            P.add('pe', lambda e, ps=ps, kc=kc, k0=k0, s0=s0, xa=xa, wa=wa, nn=nn: e.matmul(
                                ps[:, 0:nn], xa[:, kc, s0:s0 + 128], wa[:, kc - k0, 0:nn], start=(kc == 0), stop=(kc == KC - 1)),
                                [xb, wb], [psb])
                        if ks == KS - 1:
                            epi(ps, psb, t0 + s0, 128, n0, nn)
            else:
                wa, wb = wt[wi % 2]
                wi += 1
                P.dma('sp', wa[:, 0:h, 0:nn], wv[:, 0:h, n_off + n0:n_off + n0 + nn], writes=[wb])
                P.dma('sp', wa[:, h:KC, 0:nn], wv[:, h:KC, n_off + n0:n_off + n0 + nn], writes=[wb])
                for m0 in range(0, nn, 128):
                    ps, psb = next_ps(C)
                    for kc in range(KC):
                        P.add('pe', lambda e, ps=ps, kc=kc, m0=m0, xa=xa, wa=wa, tt=tt: e.matmul(
                            ps[:, 0:tt], wa[:, kc, m0:m0 + 128], xa[:, kc, 0:tt], start=(kc == 0), stop=(kc == KC - 1)),
                            [xb, wb], [psb])
                    epi(ps, psb, t0, tt, n0 + m0, 128)
    P.sb_reset(mark)


class LNEpi:
    def __init__(self, P, C, xin, g_row, b_row, xout, xTout):
        self.P, self.C, self.xin, self.xout = P, C, xin, xout
        self.KC = D // 128
        self.gt = P.sb([128, D], F32, 'lng'); self.gb = Buf()
        self.bt = P.sb([128, D], F32, 'lnb'); self.bb = Buf()
        P.dma('sp', self.gt[:], bass.AP(g_row.tensor, g_row.offset, [[0, 128], [1, D]]), writes=[self.gb])
        P.dma('sp', self.bt[:], bass.AP(b_row.tensor, b_row.offset, [[0, 128], [1, D]]), writes=[self.bb])
        self.z = [dict(z=P.sb([128, D], F32, 'lz'), zb=Buf(), stats=P.sb([128, 4, 6], F32, 'ls'), mv=P.sb([128, 2], F32, 'lmv'),
                       rstd=P.sb([128, 1], F32, 'lr'), nmr=P.sb([128, 1], F32, 'ln'), sb=Buf()) for _ in range(4)]
        self.xt = [(P.sb([128, self.KC, 128], BF16, 'lxt'), Buf()) for _ in range(2)]
        self.xti = 0
        self.xTv = xTout.rearrange("(kc p) t -> p kc t", p=128) if xTout is not None else None

    def pre_tile(self, t0, tt):
        for si, s0 in enumerate(range(0, tt, 128)):
            s = self.z[si]
            self.P.dma('sp', s['z'][:], self.xin[t0 + s0:t0 + s0 + 128, :], writes=[s['zb']])

    def __call__(self, ps, psb, t0, nt, n0, nn):
        P = self.P
        s = self.z[(t0 // 128) % 4]
        z = s['z']
        P.add('dve', lambda e: e.scalar_tensor_tensor(out=z[:, n0:n0 + nn], in0=z[:, n0:n0 + nn], scalar=ALPHA, in1=ps[:, 0:nn], op0=ALU.mult, op1=ALU.add),
              [psb, s['zb']], [s['zb']])
        if n0 + nn < D:
            return
        r = slice(t0, t0 + 128)
        for c in range(4):
            P.add('dve', lambda e, c=c: e.bn_stats(out=s['stats'][:, c, :], in_=z[:, c * 512:(c + 1) * 512]), [s['zb']], [s['sb']])
        P.add('dve', lambda e: e.bn_aggr(out=s['mv'][:], in_=s['stats'][:].rearrange("p a b -> p (a b)")), [s['sb']], [s['sb']])
        P.add('dve', lambda e: e.tensor_scalar(out=s['rstd'][:], in0=s['mv'][:, 1:2], scalar1=LN_EPS, scalar2=None, op0=ALU.add), [s['sb']], [s['sb']])
        P.add('act', lambda e: e.activation(out=s['rstd'][:], in_=s['rstd'][:], func=AF.Sqrt), [s['sb']], [s['sb']])
        P.add('dve', lambda e: e.reciprocal(out=s['rstd'][:], in_=s['rstd'][:]), [s['sb']], [s['sb']])
        P.add('dve', lambda e: e.scalar_tensor_tensor(out=s['nmr'][:], in0=s['mv'][:, 0:1], scalar=-1.0, in1=s['rstd'][:], op0=ALU.mult, op1=ALU.mult), [s['sb']], [s['sb']])
        P.add('act', lambda e: e.activation(out=z[:], in_=z[:], func=AF.Identity, bias=s['nmr'][:], scale=s['rstd'][:]), [s['zb'], s['sb']], [s['zb']])
        P.add('pool', lambda e: e.tensor_tensor(out=z[:], in0=z[:], in1=self.gt[:], op=ALU.mult), [s['zb'], self.gb], [s['zb']])
        P.add('pool', lambda e: e.tensor_tensor(out=z[:], in0=z[:], in1=self.bt[:], op=ALU.add), [s['zb'], self.bb], [s['zb']])
        P.dma('pool', self.xout[r, :], z[:], reads=[s['zb']])
        if self.xTv is not None:
            xt, xtb = self.xt[self.xti % 2]
            self.xti += 1
            transpose_to(P, self.C, z, s['zb'], xt, xtb, self.KC, tbanks=True)
            P.dma('pool', self.xTv[:, :, r], xt[:], reads=[xtb])


class StoreEpi:
    def __init__(self, P, C, out, mode, dtype=F32):
        self.P, self.C, self.out, self.mode = P, C, out, mode
        self.st = [(P.sb([128, 512], dtype, 'se'), Buf()) for _ in range(4)]
        self.i = 0

    def __call__(self, ps, psb, t0, nt, n0, nn):
        P = self.P
        a, ab = self.st[self.i % 4]
        eng = 'act' if self.i % 2 else 'dve'
        self.i += 1
        if self.mode == 'tok':
            w = nn
            dst = self.out[t0:t0 + nt, n0:n0 + nn]
        else:
            w = nt
            dst = self.out[n0:n0 + nn, t0:t0 + nt]
        if eng == 'act':
            P.add('act', lambda e: e.copy(out=a[:, 0:w], in_=ps[:, 0:w]), [psb], [ab])
        else:
            P.add('dve', lambda e: e.tensor_copy(out=a[:, 0:w], in_=ps[:, 0:w]), [psb], [ab])
        P.dma('pool', dst, a[:, 0:w], reads=[ab])


def bcast_rows(P, C, src_row, n, name):
    t = P.sb([128, n], F32, name)
    b = Buf(name)
    P.dma('sp', t[:], src_row.partition_broadcast(128) if hasattr(src_row, 'partition_broadcast') else src_row, writes=[b])
    return t, b


def ln_phase(P, C, xin, mix, g_row, b_row, xout, xTout, T, final_out=None):
    P.mark('ln_phase')
    mark = P.sb_mark()
    nc = P.nc
    KC = D // 128
    gt = P.sb([128, D], F32, 'lng'); gb = Buf()
    bt = P.sb([128, D], F32, 'lnb'); bb_ = Buf()
    P.dma('sp', gt[:], bass.AP(g_row.tensor, g_row.offset, [[0, 128], [1, D]]), writes=[gb])
    P.dma('sp', bt[:], bass.AP(b_row.tensor, b_row.offset, [[0, 128], [1, D]]), writes=[bb_])
    NB = 4
    st = []
    for _ in range(NB):
        st.append(dict(x=P.sb([128, D], F32, 'lx'), m=P.sb([128, D], F32, 'lm'), xb=Buf(), mb=Buf(),
                       stats=P.sb([128, 4, 6], F32, 'ls'), mv=P.sb([128, 2], F32, 'lmv'), sb=Buf(),
                       rstd=P.sb([128, 1], F32, 'lr'), nmr=P.sb([128, 1], F32, 'ln'),
                       xt=P.sb([128, KC, 128], BF16, 'lxt'), xtb=Buf()))
    xTv = xTout.rearrange("(kc p) t -> p kc t", p=128) if xTout is not None else None
    for ti in range(T // 128):
        s = st[ti % NB]
        x, m = s['x'], s['m']
        r = slice(ti * 128, (ti + 1) * 128)
        P.dma('sp', x[:], xin[r, :], writes=[s['xb']])
        P.dma('sp', m[:], mix[r, :], writes=[s['mb']])
        P.add('dve', lambda e, x=x, m=m: e.scalar_tensor_tensor(out=m[:], in0=x[:], scalar=ALPHA, in1=m[:], op0=ALU.mult, op1=ALU.add),
              [s['xb'], s['mb']], [s['mb']])
        for c in range(4):
            P.add('dve', lambda e, s=s, m=m, c=c: e.bn_stats(out=s['stats'][:, c, :], in_=m[:, c * 512:(c + 1) * 512]), [s['mb']], [s['sb']])
        P.add('dve', lambda e, s=s: e.bn_aggr(out=s['mv'][:], in_=s['stats'][:].rearrange("p a b -> p (a b)")), [s['sb']], [s['sb']])
        P.add('dve', lambda e, s=s: e.tensor_scalar(out=s['rstd'][:], in0=s['mv'][:, 1:2], scalar1=LN_EPS, scalar2=None, op0=ALU.add),
              [s['sb']], [s['sb']])
        P.add('act', lambda e, s=s: e.activation(out=s['rstd'][:], in_=s['rstd'][:], func=AF.Sqrt), [s['sb']], [s['sb']])
        P.add('dve', lambda e, s=s: e.reciprocal(out=s['rstd'][:], in_=s['rstd'][:]), [s['sb']], [s['sb']])
        P.add('dve', lambda e, s=s: e.scalar_tensor_tensor(out=s['nmr'][:], in0=s['mv'][:, 0:1], scalar=-1.0, in1=s['rstd'][:], op0=ALU.mult, op1=ALU.mult),
              [s['sb']], [s['sb']])
        P.add('act', lambda e, s=s, x=x, m=m: e.activation(out=x[:], in_=m[:], func=AF.Identity, bias=s['nmr'][:], scale=s['rstd'][:]),
              [s['mb'], s['sb']], [s['xb']])
        P.add('pool', lambda e, x=x: e.tensor_tensor(out=x[:], in0=x[:], in1=gt[:], op=ALU.mult), [s['xb'], gb], [s['xb']])
        P.add('dve', lambda e, x=x: e.tensor_tensor(out=x[:], in0=x[:], in1=bt[:], op=ALU.add), [s['xb'], bb_], [s['xb']])
        P.dma('pool', xout[r, :], x[:], reads=[s['xb']])
        if xTv is not None:
            transpose_to(P, C, x, s['xb'], s['xt'], s['xtb'], KC)
            P.dma('pool', xTv[:, :, r], s['xt'][:], reads=[s['xtb']])
    P.sb_reset(mark)


def ffn_mid(P, C, hT, conv_w, conv_b, gT, T, S):
    P.mark('ffn_mid')
    mark = P.sb_mark()
    NCH = DFF // 128
    cw = P.sb([128, NCH, 3], F32, 'fcw'); cwb = Buf()
    cb = P.sb([128, NCH], F32, 'fcb'); cbb = Buf()
    for k in range(3):
        P.dma('sp', cw[:, :, k], conv_w[k, :].rearrange("(c p) -> p c", p=128), writes=[cwb], allow_slow_non_contiguous=True)
    P.dma('sp', cb[:], conv_b.rearrange("(c p) -> p c", p=128), writes=[cbb], allow_slow_non_contiguous=True)
    NB = 3
    st = [dict(g=P.sb([128, S + 2], F32, 'fg'), u=P.sb([128, S], F32, 'fu'), a=P.sb([128, S], F32, 'fa'),
               o=P.sb([128, S], BF16, 'fo'), gb=Buf(), ub=Buf(), ab=Buf(), ob=Buf()) for _ in range(NB)]
    for s in st:
        P.add('pool', lambda e, s=s: e.memset(s['g'][:, 0:1], 0.0), [], [s['gb']])
        P.add('pool', lambda e, s=s: e.memset(s['g'][:, S + 1:S + 2], 0.0), [], [s['gb']])
    i = 0
    for q in range(T // S):
        for ch in range(NCH):
            s = st[i % NB]
            i += 1
            g, u, a, o = s['g'], s['u'], s['a'], s['o']
            tr = slice(q * S, (q + 1) * S)
            P.dma('sp', g[:, 1:S + 1], hT[ch * 128:(ch + 1) * 128, tr], writes=[s['gb']])
            P.dma('sp', u[:], hT[DFF + ch * 128:DFF + (ch + 1) * 128, tr], writes=[s['ub']])
            P.add('act', lambda e, g=g, a=a, ch=ch: e.activation(out=a[:], in_=g[:, 1:S + 1], func=AF.Identity, bias=cb[:, ch:ch + 1], scale=cw[:, ch, 1:2]),
                  [s['gb'], cwb, cbb], [s['ab']])
            P.add('dve', lambda e, g=g, a=a, ch=ch: e.scalar_tensor_tensor(out=a[:], in0=g[:, 0:S], scalar=cw[:, ch, 0:1], in1=a[:], op0=ALU.mult, op1=ALU.add),
                  [s['gb'], s['ab'], cwb], [s['ab']])
            P.add('dve', lambda e, g=g, a=a, ch=ch: e.scalar_tensor_tensor(out=a[:], in0=g[:, 2:S + 2], scalar=cw[:, ch, 2:3], in1=a[:], op0=ALU.mult, op1=ALU.add),
                  [s['gb'], s['ab'], cwb], [s['ab']])
            P.add('act', lambda e, a=a: e.activation(out=a[:], in_=a[:], func=AF.Gelu), [s['ab']], [s['ab']])
            P.add('pool', lambda e, a=a, u=u, o=o: e.tensor_tensor(out=o[:], in0=a[:], in1=u[:], op=ALU.mult), [s['ab'], s['ub']], [s['ob']])
            P.dma('pool', gT[ch * 128:(ch + 1) * 128, tr], o[:], reads=[s['ob']])
    P.sb_reset(mark)


def xprep_n(P, C, x, xT, T, DD):
    P.mark('xprep_n')
    mark = P.sb_mark()
    KC = DD // 128
    NB = 4
    st = [(P.sb([128, DD], F32, 'xp'), P.sb([128, KC, 128], BF16, 'xpt'), Buf(), Buf()) for _ in range(NB)]
    xTv = xT.rearrange("(kc p) t -> p kc t", p=128)
    for ti in range(T // 128):
        a, b, ba, bb = st[ti % NB]
        P.dma('sp', a[:], x[ti * 128:(ti + 1) * 128, :], writes=[ba])
        transpose_to(P, C, a, ba, b, bb, KC)
        P.dma('pool', xTv[:, :, ti * 128:(ti + 1) * 128], b[:], reads=[bb])
    P.sb_reset(mark)


def rot_tables(S):
    half = 128
    inv = (10000.0 ** (-np.arange(half, dtype=np.float32) / half)).astype(np.float32)
    pos = np.arange(S, dtype=np.float32)
    ang = (pos[:, None] * inv[None, :]).astype(np.float32)
    return np.cos(ang).T.astype(np.float32).copy(), np.sin(ang).T.astype(np.float32).copy()


def ret_rotary(P, C, qkT, qkr, T, S):
    P.mark('ret_rotary')
    nc = P.nc
    mark = P.sb_mark()
    if not hasattr(C, 'rot'):
        ct, sn = rot_tables(S)
        C.rot = (nc.inline_tensor(ct, "rot_cos").ap(), nc.inline_tensor(sn, "rot_sin").ap())
    cd, sd = C.rot
    cos = P.sb([128, S], F32, 'cos'); sin = P.sb([128, S], F32, 'sin')
    cosk = P.sb([128, S], F32, 'cosk'); sink = P.sb([128, S], F32, 'sink')
    tb = Buf()
    P.dma('sp', cos[:], cd[:, :], writes=[tb])
    P.dma('sp', sin[:], sd[:, :], writes=[tb])
    P.add('act', lambda e: e.mul(out=cosk[:], in_=cos[:], mul=1.0 / 16), [tb], [tb])
    P.add('act', lambda e: e.mul(out=sink[:], in_=sin[:], mul=1.0 / 16), [tb], [tb])
    NB = 2 if S <= 2048 else 1
    st = [dict(x1=P.sb([128, S], F32, 'r1'), x2=P.sb([128, S], F32, 'r2'), a=P.sb([128, S], F32, 'ra'), b=P.sb([128, S], F32, 'rb'),
               o1=P.sb([128, S], BF16, 'ro1'), o2=P.sb([128, S], BF16, 'ro2'), xb=Buf(), ab=Buf(), ob=Buf()) for _ in range(NB)]
    i = 0
    for q in range(T // S):
        tr = slice(q * S, (q + 1) * S)
        for hh in range(16):
            s = st[i % NB]; i += 1
            c_, s_ = (cos, sin) if hh < 8 else (cosk, sink)
            r0 = hh * 256
            x1, x2, a, b, o1, o2 = s['x1'], s['x2'], s['a'], s['b'], s['o1'], s['o2']
            P.dma('sp', x1[:], qkT[r0:r0 + 128, tr], writes=[s['xb']])
            P.dma('sp', x2[:], qkT[r0 + 128:r0 + 256, tr], writes=[s['xb']])
            P.add('dve', lambda e, x1=x1, a=a, c_=c_: e.tensor_tensor(out=a[:], in0=x1[:], in1=c_[:], op=ALU.mult), [s['xb'], tb], [s['ab']])
            P.add('pool', lambda e, x2=x2, b=b, s_=s_: e.tensor_tensor(out=b[:], in0=x2[:], in1=s_[:], op=ALU.mult), [s['xb'], tb], [s['ab']])
            P.add('dve', lambda e, a=a, b=b, o1=o1: e.tensor_tensor(out=o1[:], in0=a[:], in1=b[:], op=ALU.subtract), [s['ab']], [s['ob']])
            P.add('pool', lambda e, x1=x1, a=a, s_=s_: e.tensor_tensor(out=a[:], in0=x1[:], in1=s_[:], op=ALU.mult), [s['xb'], tb, s['ab']], [s['ab']])
            P.add('dve', lambda e, x2=x2, b=b, c_=c_: e.tensor_tensor(out=b[:], in0=x2[:], in1=c_[:], op=ALU.mult), [s['xb'], tb, s['ab']], [s['ab']])
            P.add('pool', lambda e, a=a, b=b, o2=o2: e.tensor_tensor(out=o2[:], in0=a[:], in1=b[:], op=ALU.add), [s['ab']], [s['ob']])
            P.dma('pool', qkr[r0:r0 + 128, tr], o1[:], reads=[s['ob']])
            P.dma('pool', qkr[r0 + 128:r0 + 256, tr], o2[:], reads=[s['ob']])
    P.sb_reset(mark)


def ret_attn(P, C, qkr, vg, gn_g, gn_b, ytok, T, S):
    P.mark('ret_attn')
    nc = P.nc
    mark = P.sb_mark()
    NQB = S // 512
    NKC = S // 128
    lg = [float(np.log(np.float32(1.0) - np.float32(2.0) ** np.float32(-5.0 - h))) for h in range(8)]
    d0 = P.sb([128, 512], F32, 'd0'); d0b = Buf()
    P.add('pool', lambda e: e.iota(d0[:], [[1, 512]], base=0, channel_multiplier=-1, allow_small_or_imprecise_dtypes=True), [], [d0b])
    deltas = sorted({qb * 512 - kc * 128 for qb in range(NQB) for kc in range(NKC)})
    gam = {dl: P.sb([128, 512], BF16, 'gam') for dl in deltas}
    gamb = Buf()
    gtmp = [(P.sb([128, 512], F32, 'gt'), Buf()) for _ in range(2)]
    gg = P.sb([128, 4096], F32, 'gng'); gbt = P.sb([128, 4096], F32, 'gnb'); ggb = Buf()
    P.dma('sp', gg[:], bass.AP(gn_g.tensor, gn_g.offset, [[0, 128], [1, 4096]]), writes=[ggb])
    P.dma('sp', gbt[:], bass.AP(gn_b.tensor, gn_b.offset, [[0, 128], [1, 4096]]), writes=[ggb])
    qt = P.sb([128, 2, S], BF16, 'qt'); kt = P.sb([128, 2, S], BF16, 'kt'); qkb = Buf()
    vt = P.sb([128, NKC, 512], BF16, 'vt'); vb = Buf()
    vst = [(P.sb([128, 512], F32, 'vs'), Buf()) for _ in range(2)]
    pt = [(P.sb([128, 512], BF16, 'pt'), Buf()) for _ in range(3)]
    ep = [dict(o=P.sb([128, 512], F32, 'eo'), g=P.sb([128, 512], F32, 'eg'), st=P.sb([128, 6], F32, 'es'), mv=P.sb([128, 2], F32, 'em'),
               r=P.sb([128, 1], F32, 'er'), n=P.sb([128, 1], F32, 'en'), ob=Buf(), gb=Buf(), sb=Buf()) for _ in range(4)]
    pending = []

    def step_pending(n):
        for _ in range(n):
            while pending:
                try:
                    next(pending[0])
                    break
                except StopIteration:
                    pending.pop(0)

    obank = [(C.ps[i], C.psb[i]) for i in range(4)]
    ostage = [(P.sb([128, 512], F32, 'ost'), Buf()) for _ in range(8)]
    sbank = [(C.ps[4 + i], C.psb[4 + i]) for i in range(3)]
    cnt = 0
    ec = 0
    for h in range(8):
        for gi, dl in enumerate(deltas):
            t, tbuf = gtmp[gi % 2]
            P.add('act', lambda e, t=t, dl=dl: e.activation(out=t[:], in_=d0[:], func=AF.Abs, bias=float(dl), scale=1.0), [d0b], [tbuf])
            P.add('act', lambda e, t=t, dl=dl, h=h: e.activation(out=gam[dl][:], in_=t[:], func=AF.Exp, scale=lg[h]), [tbuf], [gamb])
        for q in range(T // S):
            tb0 = q * S
            P.dma('sp', qt[:], qkr[h * 256:(h + 1) * 256, tb0:tb0 + S].rearrange("(c p) t -> p c t", p=128), writes=[qkb])
            P.dma('sp', kt[:], qkr[2048 + h * 256:2048 + (h + 1) * 256, tb0:tb0 + S].rearrange("(c p) t -> p c t", p=128), writes=[qkb])
            for kc in range(NKC):
                a, ab = vst[kc % 2]
                P.dma('sp', a[:], vg[tb0 + kc * 128:tb0 + (kc + 1) * 128, h * 512:(h + 1) * 512], writes=[ab])
                P.add('pool', lambda e, a=a, kc=kc: e.tensor_copy(out=vt[:, kc, :], in_=a[:]), [ab], [vb])
            for qb in range(NQB):
                def emit_s(kc, cnt):
                    sp_, spb = sbank[cnt % 3]
                    p_, pb = pt[cnt % 3]
                    for c in range(2):
                        P.add('pe', lambda e, sp_=sp_, c=c, kc=kc, qb=qb: e.matmul(sp_[:, :], kt[:, c, kc * 128:(kc + 1) * 128], qt[:, c, qb * 512:(qb + 1) * 512],
                                                                                  start=(c == 0), stop=(c == 1)), [qkb], [spb])
                    dl = qb * 512 - kc * 128
                    P.add('dve', lambda e, sp_=sp_, p_=p_, dl=dl: e.tensor_tensor(out=p_[:], in0=sp_[:, :], in1=gam[dl][:], op=ALU.mult), [spb, gamb], [pb])

                def emit_pv(kc, cnt):
                    p_, pb = pt[cnt % 3]
                    for sub in range(4):
                        ob_, obb = obank[sub]
                        P.add('pe', lambda e, ob_=ob_, p_=p_, sub=sub, kc=kc: e.matmul(ob_[:, :], p_[:, sub * 128:(sub + 1) * 128], vt[:, kc, :],
                                                                                     start=(kc == 0), stop=(kc == NKC - 1)), [pb, vb], [obb])
                emit_s(0, cnt)
                emit_s(1, cnt + 1)
                for kc in range(NKC):
                    if kc + 2 < NKC:
                        emit_s(kc + 2, cnt + 2)
                    emit_pv(kc, cnt)
                    cnt += 1
                    if kc >= 2:
                        step_pending(1)
                while len(pending) > 1:
                    for _ in pending[0]:
                        pass
                    pending.pop(0)
                for sub in range(4):
                    ob_p, obb_p = obank[sub]
                    ob_, obb = ostage[(ec + sub) % 8]
                    if sub % 2:
                        P.add('act', lambda e, ob_=ob_, ob_p=ob_p: e.activation(out=ob_[:], in_=ob_p[:, :], func=AF.Identity), [obb_p], [obb])
                    else:
                        P.add('dve', lambda e, ob_=ob_, ob_p=ob_p: e.tensor_copy(out=ob_[:], in_=ob_p[:, :]), [obb_p], [obb])
                def epi_gen(ec0, qb=qb, h=h, tb0=tb0):
                    for sub in range(4):
                        ob_, obb = ostage[(ec0 + sub) % 8]
                        s = ep[(ec0 + sub) % 4]
                        r = slice(tb0 + qb * 512 + sub * 128, tb0 + qb * 512 + (sub + 1) * 128)
                        o, g = s['o'], s['g']
                        P.dma('sp', g[:], vg[r, 4096 + h * 512:4096 + (h + 1) * 512], writes=[s['gb']])
                        P.add('act', lambda e, g=g: e.activation(out=g[:], in_=g[:], func=AF.Silu), [s['gb']], [s['gb']])
                        P.add('dve', lambda e, s=s, ob_=ob_: e.bn_stats(out=s['st'][:], in_=ob_[:, :]), [obb], [s['sb']])
                        yield
                        P.add('dve', lambda e, s=s: e.bn_aggr(out=s['mv'][:], in_=s['st'][:]), [s['sb']], [s['sb']])
                        P.add('dve', lambda e, s=s: e.tensor_scalar(out=s['r'][:], in0=s['mv'][:, 1:2], scalar1=LN_EPS, scalar2=None, op0=ALU.add), [s['sb']], [s['sb']])
                        yield
                        P.add('act', lambda e, s=s: e.activation(out=s['r'][:], in_=s['r'][:], func=AF.Sqrt), [s['sb']], [s['sb']])
                        yield
                        P.add('dve', lambda e, s=s: e.reciprocal(out=s['r'][:], in_=s['r'][:]), [s['sb']], [s['sb']])
                        P.add('dve', lambda e, s=s: e.scalar_tensor_tensor(out=s['n'][:], in0=s['mv'][:, 0:1], scalar=-1.0, in1=s['r'][:], op0=ALU.mult, op1=ALU.mult), [s['sb']], [s['sb']])
                        yield
                        P.add('act', lambda e, s=s, o=o, ob_=ob_: e.activation(out=o[:], in_=ob_[:, :], func=AF.Identity, bias=s['n'][:], scale=s['r'][:]), [obb, s['sb']], [s['ob']])
                        yield
                        P.add('pool', lambda e, o=o, h=h: e.tensor_tensor(out=o[:], in0=o[:], in1=gg[:, h * 512:(h + 1) * 512], op=ALU.mult), [s['ob'], ggb], [s['ob']])
                        P.add('pool', lambda e, o=o, h=h: e.tensor_tensor(out=o[:], in0=o[:], in1=gbt[:, h * 512:(h + 1) * 512], op=ALU.add), [s['ob'], ggb], [s['ob']])
                        yield
                        P.add('dve', lambda e, o=o, g=g: e.tensor_tensor(out=o[:], in0=o[:], in1=g[:], op=ALU.mult), [s['ob'], s['gb']], [s['ob']])
                        P.dma('pool', ytok[r, h * 512:(h + 1) * 512], o[:], reads=[s['ob']])
                        yield
                pending.append(epi_gen(ec))
                ec += 4
    for g_ in pending:
        for _ in g_:
            pass
    P.sb_reset(mark)


NEG = -30000.0


def na_bias_build(P, C, nc, rpb, scr_fn, e):
    P.mark('na_bias_build')
    mark = P.sb_mark()
    biasS = scr_fn(f"na_bias{e}", [240, 4096], F32)
    G = P.sb([31, 240], F32, 'nG'); Gb = Buf()
    for h_ in range(16):
        P.dma('sp', G[:, h_ * 15:(h_ + 1) * 15], rpb[h_].rearrange("r d -> d r"), writes=[Gb], allow_slow_non_contiguous=True)
    M = P.sb([31, 4096], F32, 'nM'); Mb = Buf()
    P.add('pool', lambda e_: e_.iota(M[:].rearrange("p (a b) -> p a b", b=64), [[1, 64], [-1, 64]], base=15, channel_multiplier=-1,
                                     allow_small_or_imprecise_dtypes=True), [], [Mb])
    P.add('dve', lambda e_: e_.tensor_single_scalar(out=M[:], in_=M[:], scalar=0.0, op=ALU.is_equal), [Mb], [Mb])
    A = P.sb([120, 4096], F32, 'nA'); Q = P.sb([120, 4096], F32, 'nQ'); nb = Buf()
    P.add('pool', lambda e_: e_.iota(A[:].rearrange("p (a b) -> p a b", b=64), [[1, 64], [0, 64]], base=0, channel_multiplier=0,
                                     allow_small_or_imprecise_dtypes=True), [], [nb])
    P.add('pool', lambda e_: e_.iota(Q[:].rearrange("p (a b) -> p a b", b=64), [[0, 64], [1, 64]], base=0, channel_multiplier=0,
                                     allow_small_or_imprecise_dtypes=True), [nb], [nb])
    for (s1, op) in ((-8.0, ALU.add), (0.0, ALU.max), (48.0, ALU.min)):
        P.add('dve', lambda e_, s1=s1, op=op: e_.tensor_single_scalar(out=Q[:], in_=Q[:], scalar=s1, op=op), [nb], [nb])
    P.add('dve', lambda e_: e_.tensor_tensor(out=A[:], in0=A[:], in1=Q[:], op=ALU.subtract), [nb], [nb])
    P.add('dve', lambda e_: e_.tensor_single_scalar(out=Q[:], in_=A[:], scalar=0.0, op=ALU.is_ge), [nb], [nb])
    P.add('dve', lambda e_: e_.tensor_single_scalar(out=A[:], in_=A[:], scalar=15.0, op=ALU.is_le), [nb], [nb])
    P.add('dve', lambda e_: e_.tensor_tensor(out=A[:], in0=A[:], in1=Q[:], op=ALU.mult), [nb], [nb])
    P.add('dve', lambda e_: e_.tensor_single_scalar(out=A[:], in_=A[:], scalar=-1.0, op=ALU.add), [nb], [nb])
    P.add('dve', lambda e_: e_.tensor_single_scalar(out=A[:], in_=A[:], scalar=-NEG, op=ALU.mult), [nb], [nb])
    st = [(P.sb([120, 512], F32, 'nbs'), Buf()) for _ in range(2)]
    i = 0
    for half in range(2):
        for cb in range(8):
            ps, psb = next_ps(C)
            P.add('pe', lambda e_, ps=ps, half=half, cb=cb: e_.matmul(ps[0:120, :], G[:, half * 120:(half + 1) * 120], M[:, cb * 512:(cb + 1) * 512],
                                                                     start=True, stop=True), [Gb, Mb], [psb])
            a, ab = st[i % 2]; i += 1
            P.add('dve', lambda e_, ps=ps, a=a, cb=cb: e_.tensor_tensor(out=a[:], in0=ps[0:120, :], in1=A[:, cb * 512:(cb + 1) * 512], op=ALU.add), [psb, nb], [ab])
            P.dma('pool', biasS[half * 120:(half + 1) * 120, cb * 512:(cb + 1) * 512], a[:], reads=[ab])
    P.sb_reset(mark)
    return biasS


def na_attn(P, C, nc, biasS, hqk, hv, yab, T, S, stage=3):
    P.mark('na_attn')
    mark = P.sb_mark()
    rows = S // 64
    NCH = S // 128
    assert rows >= 8
    B2 = P.sb([128, 16, 14, 64], F32, 'nB2'); B2b = Buf()
    bv = biasS.rearrange("(h r) (k q) -> k h r q", r=15, q=64)
    for i2 in range(2):
        for h in range(16):
            P.dma('sp', B2[i2 * 64:(i2 + 1) * 64, h, :, :], bv[:, h, i2:i2 + 14, :], writes=[B2b])
    stg = P.sb([64, 2, S], F32, 'nstg'); stgb = Buf()
    qTb = P.sb([64, 2, S], BF16, 'nq'); kTb = P.sb([64, 2, S], BF16, 'nk'); qkb = Buf()
    vstg = P.sb([128, NCH, 128], F32, 'nvs'); vsb = Buf()
    Vt = P.sb([128, NCH, 2, 80], BF16, 'nV'); Vt2 = P.sb([128, NCH, 2, 80], BF16, 'nV2'); Vb = Buf()
    P.add('pool', lambda e_: e_.memset(Vt[:], 1.0), [], [Vb])
    P.add('pool', lambda e_: e_.memset(Vt2[:], 1.0), [], [Vb])
    sc = [(P.sb([128, 512], F32, 'nsc'), Buf()) for _ in range(3)]
    pT = [(P.sb([128, 512], BF16, 'npT'), Buf()) for _ in range(3)]
    yr = [(P.sb([64, 128], F32, 'nyr'), P.sb([64, 2], F32, 'nrc'), Buf()) for _ in range(2)]
    sbank = [(C.ps[0], C.psb[0]), (C.ps[1], C.psb[1]), (C.ps[4], C.psb[4])]
    obank = [(C.ps[2], C.psb[2]), (C.ps[3], C.psb[3])]
    it = 0
    for q in range(T // S):
        t0 = q * S
        for hp in range(8):
            P.dma('sp', stg[:], hqk[hp * 128:(hp + 1) * 128, t0:t0 + S].rearrange("(a d) t -> d a t", d=64), writes=[stgb])
            P.add('act', lambda e_: e_.mul(out=qTb[:], in_=stg[:], mul=0.125), [stgb], [qkb])
            P.dma('sp', stg[:], hqk[1024 + hp * 128:1024 + (hp + 1) * 128, t0:t0 + S].rearrange("(a d) t -> d a t", d=64), writes=[stgb])
            P.add('dve', lambda e_: e_.tensor_copy(out=kTb[:], in_=stg[:]), [stgb], [qkb])
            P.dma('sp', vstg[:], hv[t0:t0 + S, hp * 128:(hp + 1) * 128].rearrange("(c p) f -> p c f", p=128), writes=[vsb])
            P.add('pool', lambda e_: e_.tensor_copy(out=Vt[:, :, :, 0:64], in_=vstg[:].rearrange("p c (a d) -> p c a d", d=64)), [vsb], [Vb])
            P.dma('sp', vstg[:, 0:NCH - 1, :], hv[t0 + 64:t0 + S - 64, hp * 128:(hp + 1) * 128].rearrange("(c p) f -> p c f", p=128), writes=[vsb])
            P.add('pool', lambda e_: e_.tensor_copy(out=Vt2[:, 0:NCH - 1, :, 0:64], in_=vstg[:, 0:NCH - 1, :].rearrange("p c (a d) -> p c a d", d=64)), [vsb], [Vb])
            rlist = list(range(rows))

            def part_a(r, it):
                rs = min(max(r - 4, 0), rows - 8)
                ro0 = rs - r + 7
                sp_, spb = sbank[it % 3]
                s_, sb_ = sc[it % 3]
                p_, pb = pT[it % 3]
                for hh in range(2):
                    for c in range(4):
                        k0 = rs * 64 + c * 128
                        P.add('pe', lambda e_, sp_=sp_, hh=hh, c=c, k0=k0, r=r: e_.matmul(
                            sp_[:, (hh * 4 + c) * 64:(hh * 4 + c + 1) * 64], kTb[:, hh, k0:k0 + 128], qTb[:, hh, r * 64:(r + 1) * 64],
                            start=True, stop=True), [qkb], [spb])
                for hh in range(2):
                    h = 2 * hp + hh
                    P.add('dve', lambda e_, sp_=sp_, s_=s_, hh=hh, h=h, ro0=ro0: e_.tensor_tensor(
                        out=s_[:, hh * 256:(hh + 1) * 256].rearrange("p (c q) -> p c q", q=64),
                        in0=sp_[:, hh * 256:(hh + 1) * 256].rearrange("p (c q) -> p c q", q=64),
                        in1=B2[:, h, ro0:ro0 + 7:2, :], op=ALU.add), [spb, B2b], [sb_])
                P.add('act', lambda e_, s_=s_, p_=p_: e_.activation(out=p_[:], in_=s_[:], func=AF.Exp), [sb_], [pb])

            def part_b(r, it):
                rs = min(max(r - 4, 0), rows - 8)
                p_, pb = pT[it % 3]
                ob_, obb = obank[it % 2]
                y_, rc_, yb_ = yr[it % 2]
                for hh in range(2):
                    for c in range(4):
                        if rs % 2 == 0:
                            vap = Vt[:, rs // 2 + c, hh, 0:65]
                        else:
                            vap = Vt2[:, (rs - 1) // 2 + c, hh, 0:65]
                        P.add('pe', lambda e_, ob_=ob_, p_=p_, hh=hh, c=c, vap=vap: e_.matmul(
                            ob_[0:64, hh * 128:hh * 128 + 65], p_[:, (hh * 4 + c) * 64:(hh * 4 + c + 1) * 64], vap, start=(c == 0), stop=(c == 3)), [pb, Vb], [obb])
                for hh in range(2):
                    P.add('dve', lambda e_, ob_=ob_, rc_=rc_, hh=hh: e_.reciprocal(out=rc_[:, hh:hh + 1], in_=ob_[0:64, hh * 128 + 64:hh * 128 + 65]), [obb], [yb_])
                    P.add('dve', lambda e_, ob_=ob_, rc_=rc_, y_=y_, hh=hh: e_.tensor_scalar(
                        out=y_[:, hh * 64:(hh + 1) * 64], in0=ob_[0:64, hh * 128:hh * 128 + 64], scalar1=rc_[:, hh:hh + 1], scalar2=None, op0=ALU.mult), [obb, yb_], [yb_])
                P.dma('pool', yab[t0 + r * 64:t0 + (r + 1) * 64, 1024 + hp * 128:1024 + (hp + 1) * 128], y_[:], reads=[yb_])
            part_a(0, it)
            part_a(1, it + 1)
            for r in rlist:
                if r + 2 < rows:
                    part_a(r + 2, it + 2)
                part_b(r, it)
                it += 1
    P.sb_reset(mark)


A_GN_EPS = 64e-5
EHALF = float(np.exp(-0.5))


def bc_ap(t, ncols_src, n1, n2):
    return bass.AP(t, 0, [[ncols_src, t.shape[0]], [1, n1], [0, n2]])


def load_bc(P, row_ap, n, name):
    t = P.sb([128, n], F32, name); b = Buf(name)
    P.dma('sp', t[:], bass.AP(row_ap.tensor, row_ap.offset, [[0, 128], [1, n]]), writes=[b])
    return t, b


def rwkv_prep(P, C, nc, W, e, hA, PR, T, S):
    P.mark('rwkv_prep')
    mark = P.sb_mark()
    sh = [load_bc(P, W['ab_shift'][e, k], 3328, f'sh{k}') for k in range(3)]
    w0 = [load_bc(P, W['ab_w0'][e, d], 1024, f'w0{d}') for d in range(2)]
    a0 = [load_bc(P, W['ab_a0'][e, d], 1024, f'a0{d}') for d in range(2)]
    kk_ = load_bc(P, W['ab_k_k'][e], 1024, 'kk')
    ka_ = load_bc(P, W['ab_k_a'][e], 1024, 'ka')
    rk_ = load_bc(P, W['ab_r_k'][e].rearrange("h d -> (h d)"), 1024, 'rk')
    wst = P.sb([128, 2048], F32, 'wst'); wsb = Buf()
    wup = P.sb([64, 2, 1024], BF16, 'wup'); aup = P.sb([64, 2, 1024], BF16, 'aup'); gup = P.sb([128, 1024], BF16, 'gup'); lwb = Buf()
    P.dma('sp', wst[0:64, :].rearrange("p (d n) -> p d n", d=2), W['ab_w_up'][e].rearrange("d k n -> k d n"), writes=[wsb])
    P.add('dve', lambda e_: e_.tensor_copy(out=wup[:].rearrange("p d n -> p (d n)"), in_=wst[0:64, :]), [wsb], [lwb])
    P.dma('sp', wst[0:64, :].rearrange("p (d n) -> p d n", d=2), W['ab_a_up'][e].rearrange("d k n -> k d n"), writes=[wsb])
    P.add('dve', lambda e_: e_.tensor_copy(out=aup[:].rearrange("p d n -> p (d n)"), in_=wst[0:64, :]), [wsb], [lwb])
    P.dma('sp', wst[:, 0:1024], W['ab_g_up'][e], writes=[wsb])
    P.add('dve', lambda e_: e_.tensor_copy(out=gup[:], in_=wst[:, 0:1024]), [wsb], [lwb])
    hbufs = [(P.sb([128, 3328], F32, 'hp'), P.sb([128, 3328], F32, 'hc'), P.sb([128, 3328], F32, 'hn'), Buf(), Buf(), Buf()) for _ in range(2)]
    L = P.sb([128, 256], F32, 'L'); Lb = Buf()
    LT = P.sb([128, 3, 128], BF16, 'LT'); LTb = Buf()
    t1 = P.sb([128, 1024], F32, 't1'); t1b = Buf()
    t2 = P.sb([128, 1024], F32, 't2'); t2b = Buf()
    kap = P.sb([128, 1024], F32, 'kap'); kapb = Buf()
    ad = P.sb([128, 1024], F32, 'ad'); adb = Buf()
    o1 = [(P.sb([128, 1024], F32, 'o1'), Buf()) for _ in range(3)]
    sm = P.sb([128, 16], F32, 'sm'); smb = Buf()
    bon = P.sb([128, 16], F32, 'bon'); bonb = Buf()
    oi = [0]

    def outbuf():
        o = o1[oi[0] % 3]; oi[0] += 1
        return o
    def _tile(ti):
        hp, hc, hn, hpb, hcb, hnb = hbufs[ti % 2]
        t0 = ti * 128
        r = slice(t0, t0 + 128)
        first = (t0 % S == 0)
        lastt = ((t0 + 128) % S == 0)
        P.dma('sp', hc[:], hA[r, :], writes=[hcb])
        if first:
            P.add('pool', lambda e_: e_.memset(hp[0:1, :], 0.0), [], [hpb])
            P.dma('sp', hp[1:128, :], hA[t0:t0 + 127, :], writes=[hpb])
        else:
            P.dma('sp', hp[:], hA[t0 - 1:t0 + 127, :], writes=[hpb])
        if lastt:
            P.add('pool', lambda e_: e_.memset(hn[:], 0.0), [], [hnb])
            P.dma('sp', hn[0:127, :], hA[t0 + 1:t0 + 128, :], writes=[hnb])
        else:
            P.dma('sp', hn[:], hA[t0 + 1:t0 + 129, :], writes=[hnb])
        P.add('dve', lambda e_: e_.tensor_tensor(out=hc[:], in0=hc[:], in1=sh[1][0][:], op=ALU.mult), [hcb, sh[1][1]], [hcb])
        P.add('pool', lambda e_: e_.tensor_tensor(out=hp[:], in0=hp[:], in1=sh[0][0][:], op=ALU.mult), [hpb, sh[0][1]], [hpb])
        P.add('pool', lambda e_: e_.tensor_tensor(out=hn[:], in0=hn[:], in1=sh[2][0][:], op=ALU.mult), [hnb, sh[2][1]], [hnb])
        P.add('dve', lambda e_: e_.tensor_tensor(out=hc[:], in0=hc[:], in1=hp[:], op=ALU.add), [hcb, hpb], [hcb])
        P.add('dve', lambda e_: e_.tensor_tensor(out=hc[:], in0=hc[:], in1=hn[:], op=ALU.add), [hcb, hnb], [hcb])
        R_ = hc[:, 0:1024]; K_ = hc[:, 1024:2048]; V_ = hc[:, 2048:3072]
        P.dma('pool', PR['R'][r, :], R_, reads=[hcb])
        P.dma('pool', PR['V'][r, :], V_, reads=[hcb])
        P.add('act', lambda e_: e_.activation(out=L[:, 0:64], in_=hc[:, 3072:3136], func=AF.Tanh), [hcb], [Lb])
        P.add('act', lambda e_: e_.activation(out=L[:, 64:128], in_=hc[:, 3136:3200], func=AF.Identity), [hcb], [Lb])
        P.add('act', lambda e_: e_.activation(out=L[:, 128:256], in_=hc[:, 3200:3328], func=AF.Sigmoid), [hcb], [Lb])
        ps, psb = next_ps(C)
        P.add('pe', lambda e_, ps=ps: e_.transpose(ps[0:64, 0:128], L[:, 0:64], C.ident[:]), [Lb, C.b_ident], [psb])
        P.add('pe', lambda e_, ps=ps: e_.transpose(ps[0:64, 128:256], L[:, 64:128], C.ident[:]), [Lb, C.b_ident], [psb])
        P.add('pe', lambda e_, ps=ps: e_.transpose(ps[:, 256:384], L[:, 128:256], C.ident[:]), [Lb, C.b_ident], [psb])
        P.add('dve', lambda e_, ps=ps: e_.tensor_copy(out=LT[0:64, 0:2, :], in_=ps[0:64, 0:256].rearrange("p (a t) -> p a t", t=128)), [psb], [LTb])
        P.add('dve', lambda e_, ps=ps: e_.tensor_copy(out=LT[:, 2, :], in_=ps[:, 256:384]), [psb], [LTb])
        P.add('pool', lambda e_: e_.tensor_tensor(out=kap[:], in0=K_, in1=kk_[0][:], op=ALU.mult), [hcb, kk_[1]], [kapb])
        P.add('dve', lambda e_: e_.tensor_tensor(out=t1[:], in0=kap[:], in1=kap[:], op=ALU.mult), [kapb], [t1b])
        P.add('dve', lambda e_: e_.tensor_reduce(out=sm[:], in_=t1[:].rearrange("p (h d) -> p h d", d=64), axis=AX.X, op=ALU.add), [t1b], [smb])
        P.add('act', lambda e_: e_.activation(out=sm[:], in_=sm[:], func=AF.Sqrt), [smb], [smb])
        P.add('dve', lambda e_: e_.tensor_single_scalar(out=sm[:], in_=sm[:], scalar=1e-12, op=ALU.max), [smb], [smb])
        P.add('dve', lambda e_: e_.reciprocal(out=sm[:], in_=sm[:]), [smb], [smb])
        P.add('dve', lambda e_: e_.tensor_tensor(out=kap[:].rearrange("p (h d) -> p h d", d=64), in0=kap[:].rearrange("p (h d) -> p h d", d=64),
                                                 in1=bc_ap(sm, 16, 16, 64), op=ALU.mult), [kapb, smb], [kapb])
        P.dma('pool', PR['KAP'][r, :], kap[:], reads=[kapb])
        o, ob = outbuf()
        for hf in range(2):
            ps, psb = next_ps(C)
            P.add('pe', lambda e_, ps=ps, hf=hf: e_.matmul(ps[:, :], LT[:, 2, :], gup[:, hf * 512:(hf + 1) * 512], start=True, stop=True), [LTb, lwb], [psb])
            P.add('act', lambda e_, ps=ps, hf=hf, o=o: e_.activation(out=o[:, hf * 512:(hf + 1) * 512], in_=ps[:, :], func=AF.Identity), [psb], [ob])
        P.dma('pool', PR['GATE'][r, :], o[:], reads=[ob])
        for d in range(2):
            o, ob = outbuf()
            for hf in range(2):
                ps, psb = next_ps(C)
                P.add('pe', lambda e_, ps=ps, hf=hf, d=d: e_.matmul(ps[:, :], LT[0:64, 0, :], wup[:, d, hf * 512:(hf + 1) * 512], start=True, stop=True), [LTb, lwb], [psb])
                P.add('dve', lambda e_, ps=ps, hf=hf, d=d: e_.tensor_tensor(out=t1[:, hf * 512:(hf + 1) * 512], in0=ps[:, :], in1=w0[d][0][:, hf * 512:(hf + 1) * 512], op=ALU.add), [psb, w0[d][1]], [t1b])
            P.add('act', lambda e_: e_.activation(out=t1[:], in_=t1[:], func=AF.Sigmoid), [t1b], [t1b])
            P.add('act', lambda e_, o=o: e_.mul(out=o[:], in_=t1[:], mul=-EHALF), [t1b], [ob])
            P.dma('pool', PR[f'LW{d}'][r, :], o[:], reads=[ob])
            for hf in range(2):
                ps, psb = next_ps(C)
                P.add('pe', lambda e_, ps=ps, hf=hf, d=d: e_.matmul(ps[:, :], LT[0:64, 1, :], aup[:, d, hf * 512:(hf + 1) * 512], start=True, stop=True), [LTb, lwb], [psb])
                P.add('dve', lambda e_, ps=ps, hf=hf, d=d: e_.tensor_tensor(out=ad[:, hf * 512:(hf + 1) * 512], in0=ps[:, :], in1=a0[d][0][:, hf * 512:(hf + 1) * 512], op=ALU.add), [psb, a0[d][1]], [adb])
            P.add('act', lambda e_: e_.activation(out=ad[:], in_=ad[:], func=AF.Sigmoid), [adb], [adb])
            o, ob = outbuf()
            P.add('pool', lambda e_, o=o: e_.tensor_tensor(out=o[:], in0=kap[:], in1=ad[:], op=ALU.mult), [kapb, adb], [ob])
            P.dma('pool', PR[f'B{d}'][r, :], o[:], reads=[ob])
            P.add('dve', lambda e_: e_.scalar_tensor_tensor(out=t2[:], in0=ad[:], scalar=-1.0, in1=ka_[0][:], op0=ALU.add, op1=ALU.mult), [adb, ka_[1]], [t2b])
            o, ob = outbuf()
            P.add('dve', lambda e_, o=o: e_.scalar_tensor_tensor(out=o[:], in0=t2[:], scalar=1.0, in1=K_, op0=ALU.add, op1=ALU.mult), [t2b, hcb], [ob])
            P.dma('pool', PR[f'KD{d}'][r, :], o[:], reads=[ob])
            if d == 0:
                P.add('pool', lambda e_, o=o: e_.tensor_tensor(out=t2[:], in0=o[:], in1=R_, op=ALU.mult), [ob, hcb, t2b], [t2b])
                P.add('pool', lambda e_: e_.tensor_tensor(out=t2[:], in0=t2[:], in1=rk_[0][:], op=ALU.mult), [t2b, rk_[1]], [t2b])
                P.add('dve', lambda e_: e_.tensor_reduce(out=bon[:], in_=t2[:].rearrange("p (h d) -> p h d", d=64), axis=AX.X, op=ALU.add), [t2b], [bonb])
                P.dma('pool', PR['BON'][r, :], bon[:], reads=[bonb])
    for ti in range(T // 128):
        _tile(ti)
    P.sb_reset(mark)


def rwkv_consts(P, C, d):
    K = {}
    D = P.sb([128, 128], F32, 'cD'); Db = Buf()
    P.add('pool', lambda e_: e_.iota(D[:], [[1, 128]], base=0, channel_multiplier=-1, allow_small_or_imprecise_dtypes=True), [], [Db])
    PI = P.sb([128, 128], F32, 'cPI'); PIb = Buf()
    P.add('pool', lambda e_: e_.iota(PI[:], [[0, 128]], base=0, channel_multiplier=1, allow_small_or_imprecise_dtypes=True), [], [PIb])
    cb = Buf()

    def mk(name, src, srcb, scalar, op):
        t = P.sb([128, 128], F32, name)
        P.add('dve', lambda e_: e_.tensor_single_scalar(out=t[:], in_=src[:], scalar=scalar, op=op), [srcb], [cb])
        return t
    TRI = mk('cTRI', D, Db, 0.0, ALU.is_ge if d == 0 else ALU.is_le)
    TRIS = mk('cTRIS', D, Db, 0.0, ALU.is_gt if d == 0 else ALU.is_lt)
    HALF = mk('cHALF', PI, PIb, 63.5, ALU.is_lt if d == 0 else ALU.is_gt)
    A1 = P.sb([128, 128], F32, 'cA1'); A2 = P.sb([128, 128], F32, 'cA2'); A5 = P.sb([128, 128], F32, 'cA5')
    P.add('dve', lambda e_: e_.tensor_tensor(out=A1[:], in0=TRIS[:], in1=HALF[:], op=ALU.subtract), [cb], [cb])
    P.add('dve', lambda e_: e_.tensor_tensor(out=A2[:], in0=TRI[:], in1=HALF[:], op=ALU.subtract), [cb], [cb])
    P.add('dve', lambda e_: e_.tensor_scalar(out=A5[:], in0=TRI[:], scalar1=-1.0, scalar2=None, op0=ALU.mult), [cb], [cb])
    P.add('dve', lambda e_: e_.tensor_single_scalar(out=A5[:], in_=A5[:], scalar=1.0, op=ALU.add), [cb], [cb])
    K['A1'], K['A2'], K['A3'], K['A4'], K['A5'] = A1, A2, TRIS, TRI, A5
    ones = P.sb([128, 1], F32, 'cone')
    P.add('dve', lambda e_: e_.memset(ones[:], 1.0), [], [cb])
    K['ones'] = ones
    D4 = P.sb([128, 4, 128], F32, 'cD4'); D4b = Buf()
    P.add('pool', lambda e_: e_.iota(D4[:], [[0, 4], [1, 128]], base=0, channel_multiplier=-1, allow_small_or_imprecise_dtypes=True), [], [D4b])

    def mk4(name, op, neg=False):
        t = P.sb([128, 4, 128], F32, name)
        P.add('dve', lambda e_: e_.tensor_single_scalar(out=t[:], in_=D4[:], scalar=0.0, op=op), [D4b], [cb])
        if neg:
            P.add('dve', lambda e_: e_.tensor_single_scalar(out=t[:], in_=t[:], scalar=-1.0, op=ALU.mult), [cb], [cb])
        return t
    K['mT_strict'] = mk4('mTs', ALU.is_gt if d == 0 else ALU.is_lt)
    K['mT_M'] = mk4('mTm', ALU.is_ge, False) if d == 0 else K['mT_strict']
    K['mT_negstrict'] = mk4('mTn', ALU.is_gt if d == 0 else ALU.is_lt, True)
    K['mN_negstrict'] = mk4('mNn', ALU.is_lt if d == 0 else ALU.is_gt, True)
    K['b'] = cb
    return K


def rwkv_stream(P, C, K, W, e, PR, d, YF, yab, S, q, h0, NH, gng, gnb):
    cb = K['b']
    NCK = S // 128
    NW = NH * 64
    NG = NH // 4
    c0 = h0 * 64
    f32 = lambda n, nm: P.sb([128, n], F32, nm)
    inp = {k: (f32(NW, 'i' + k), Buf()) for k in ('R', 'V', 'KAP', 'LW', 'B', 'KD')}
    src = {'R': PR['R'], 'V': PR['V'], 'KAP': PR['KAP'], 'LW': PR[f'LW{d}'], 'B': PR[f'B{d}'], 'KD': PR[f'KD{d}']}
    E = [(f32(NW, 'E'), Buf()) for _ in range(3)]
    prod = {k: (f32(NW, 'p' + k), Buf()) for k in ('Kr', 'Br', 'Kdr', 'Rr', 'Ra')}
    tm = {k: (P.sb([128, NW], BF16, 'b' + k), Buf()) for k in ('Bend', 'Kend', 'V')}
    tr = {k: (P.sb([64, NH, 128], BF16, 't' + k), Buf()) for k in ('Kr', 'Br', 'Kdr', 'Rr', 'Ra')}
    mat = {k: (P.sb([128, NH, 128], BF16, 'm' + k), Buf()) for k in ('P0', 'P0T', 'LkT', 'MrbT', 'MrkT', 'Pa', 'PaT', 'Pb', 'PbT')}
    X32 = P.sb([128, NH, 128], F32, 'X32'); X32b = Buf()
    Xbf = P.sb([128, NH, 128], BF16, 'Xbf'); Xbfb = Buf()
    WT = P.sb([64, NH, 128], BF16, 'WT'); WTb = Buf()
    Ubf = P.sb([128, NH, 64], BF16, 'Ubf'); Ubb = Buf()
    Z32 = P.sb([64, NH, 64], F32, 'Z32'); Zbf = P.sb([64, NH, 64], BF16, 'Zbf'); Zb = Buf()
    ztmp = P.sb([64, NH, 64], F32, 'ztmp'); ztb = Buf()
    cC = P.sb([64, NH], F32, 'cC'); cCb = Buf()
    Y = f32(NW, 'Y'); Yb = Buf()
    if d == 1:
        YFt = f32(NW, 'YFt'); YFb = Buf()
        gat = f32(NW, 'gat'); gatb = Buf()
        bon = P.sb([128, NH], F32, 'bon'); bonb = Buf()
        s1 = P.sb([128, NH], F32, 's1'); s2 = P.sb([128, NH], F32, 's2'); stb = Buf()
        sq = f32(NW, 'sq'); sqb = Buf()
    evi = [0]

    def evac(out_ap, in_ap, reads, writes):
        if evi[0] % 3:
            P.add('act', lambda e_: e_.activation(out=out_ap, in_=in_ap, func=AF.Identity), reads, writes)
        else:
            P.add('dve', lambda e_: e_.tensor_copy(out=out_ap, in_=in_ap), reads, writes)
        evi[0] += 1

    h3 = lambda t: t[:].rearrange("p (h d) -> p h d", d=64)
    P.add('pool', lambda e_: e_.memset(Z32[:], 0.0), [], [Zb])
    P.add('pool', lambda e_: e_.memset(Zbf[:], 0.0), [], [Zb])
    order = range(NCK) if d == 0 else range(NCK - 1, -1, -1)

    def chunk(ck):
        t0 = q * S + ck * 128
        r = slice(t0, t0 + 128)
        for k in inp:
            P.dma('sp', inp[k][0][:], src[k][r, c0:c0 + NW], writes=[inp[k][1]])
        LW, LWb = inp['LW']
        ei = [0]

        def expo(Akey, scale):
            t, tb = E[ei[0] % 3]; ei[0] += 1
            ps, psb = next_ps(C)
            P.add('pe', lambda e_: e_.matmul(ps[:, 0:NW], K[Akey][:], LW[:], start=True, stop=True), [cb, LWb], [psb])
            P.add('act', lambda e_: e_.activation(out=t[:], in_=ps[:, 0:NW], func=AF.Exp, scale=scale), [psb], [tb])
            return t, tb

        def mul(out, outb, a, ab, b, bb, eng):
            P.add(eng, lambda e_: e_.tensor_tensor(out=out[:], in0=a[:], in1=b[:], op=ALU.mult), [ab, bb], [outb])
        E1, E1b = expo('A1', 1.0)
        mul(prod['Kr'][0], prod['Kr'][1], inp['KAP'][0], inp['KAP'][1], E1, E1b, 'dve')
        if d == 1:
            mul(prod['Rr'][0], prod['Rr'][1], inp['R'][0], inp['R'][1], E1, E1b, 'pool')
        E2, E2b = expo('A2', -1.0)
        mul(prod['Br'][0], prod['Br'][1], inp['B'][0], inp['B'][1], E2, E2b, 'pool')
        mul(prod['Kdr'][0], prod['Kdr'][1], inp['KD'][0], inp['KD'][1], E2, E2b, 'pool')
        if d == 0:
            E2p, E2pb = expo('A2', 1.0)
            mul(prod['Rr'][0], prod['Rr'][1], inp['R'][0], inp['R'][1], E2p, E2pb, 'pool')
        E3, E3b = expo('A3', 1.0)
        P.add('dve', lambda e_: e_.tensor_tensor(out=X32[:, :, 64:128], in0=h3(inp['KAP'][0]), in1=h3(E3), op=ALU.mult), [inp['KAP'][1], E3b], [X32b])
        P.add('pool', lambda e_: e_.tensor_copy(out=Xbf[:, :, 64:128], in_=X32[:, :, 64:128]), [X32b], [Xbfb])
        if d == 1:
            mul(prod['Ra'][0], prod['Ra'][1], inp['R'][0], inp['R'][1], E3, E3b, 'pool')
        else:
            E4, E4b = expo('A4', 1.0)
            mul(prod['Ra'][0], prod['Ra'][1], inp['R'][0], inp['R'][1], E4, E4b, 'pool')
        E5, E5b = expo('A5', 1.0)
        mul(tm['Bend'][0], tm['Bend'][1], inp['B'][0], inp['B'][1], E5, E5b, 'dve')
        mul(tm['Kend'][0], tm['Kend'][1], inp['KD'][0], inp['KD'][1], E5, E5b, 'pool')
        P.add('pool', lambda e_: e_.tensor_copy(out=tm['V'][0][:], in_=inp['V'][0][:]), [inp['V'][1]], [tm['V'][1]])
        ps, psb = next_ps(C)
        for h in range(NH):
            P.add('pe', lambda e_, ps=ps, h=h: e_.matmul(ps[0:64, h:h + 1], LW[:, h * 64:(h + 1) * 64], K['ones'][:], start=True, stop=True), [LWb, cb], [psb])
        P.add('act', lambda e_, ps=ps: e_.activation(out=cC[:], in_=ps[0:64, 0:NH], func=AF.Exp), [psb], [cCb])
        yield
        for k in ('Kr', 'Br', 'Kdr', 'Rr', 'Ra'):
            pa, pab = prod[k]
            ta, tab = tr[k]
            for g in range(NG):
                ps, psb = next_ps(C)
                for hi in range(4):
                    h = 4 * g + hi
                    P.add('pe', lambda e_, ps=ps, hi=hi, h=h, pa=pa: e_.transpose(ps[0:64, hi * 128:(hi + 1) * 128], pa[:, h * 64:(h + 1) * 64], C.ident[:]),
                          [pab, C.b_ident], [psb])
                evac(ta[:, 4 * g:4 * g + 4, :], ps[0:64, :].rearrange("p (a t) -> p a t", t=128), [psb], [tab])
        yield

        def score(dst, lk, rk_, mask):
            da, dab = mat[dst]
            for g in range(NG):
                ps, psb = next_ps(C)
                for hi in range(4):
                    h = 4 * g + hi
                    P.add('pe', lambda e_, ps=ps, hi=hi, h=h: e_.matmul(ps[:, hi * 128:(hi + 1) * 128], tr[lk][0][:, h, :], tr[rk_][0][:, h, :], start=True, stop=True),
                          [tr[lk][1], tr[rk_][1]], [psb])
                P.add('dve', lambda e_, ps=ps, g=g, da=da: e_.tensor_tensor(
                    out=da[:, 4 * g:4 * g + 4, :], in0=ps[:, :].rearrange("p (a t) -> p a t", t=128), in1=K[mask][:], op=ALU.mult), [psb, cb], [dab])
        score('P0', 'Kr', 'Br', 'mN_negstrict')
        score('P0T', 'Br', 'Kr', 'mT_negstrict')
        score('LkT', 'Kdr', 'Kr', 'mT_strict')
        yield
        score('MrbT', 'Br', 'Rr', 'mT_M')
        score('MrkT', 'Kdr', 'Rr', 'mT_M')
        for g in range(NH // 8):
            ps, psb = next_ps(C)
            for hi in range(8):
                h = 8 * g + hi
                P.add('pe', lambda e_, ps=ps, hi=hi, h=h: e_.matmul(ps[:, hi * 64:(hi + 1) * 64], mat['LkT'][0][:, h, :], tm['V'][0][:, h * 64:(h + 1) * 64], start=True, stop=True),
                      [mat['LkT'][1], tm['V'][1]], [psb])
            P.add('dve', lambda e_, ps=ps, g=g: e_.tensor_copy(out=X32[:, 8 * g:8 * g + 8, 0:64], in_=ps[:, :].rearrange("p (a i) -> p a i", i=64)), [psb], [X32b])
            P.add('act', lambda e_, ps=ps, g=g: e_.activation(out=Xbf[:, 8 * g:8 * g + 8, 0:64], in_=ps[:, :].rearrange("p (a i) -> p a i", i=64), func=AF.Identity), [psb], [Xbfb])
        yield
        cur, curT = 'P0', 'P0T'
        nxt = [('Pa', 'PaT'), ('Pb', 'PbT')]
        for lev in range(7):
            for g in range(NG):
                ps, psb = next_ps(C)
                for hi in range(4):
                    h = 4 * g + hi
                    P.add('pe', lambda e_, ps=ps, hi=hi, h=h, curT=curT: e_.matmul(ps[:, hi * 128:(hi + 1) * 128], mat[curT][0][:, h, :], Xbf[:, h, :], start=True, stop=True),
                          [mat[curT][1], Xbfb], [psb])
                P.add('dve', lambda e_, ps=ps, g=g: e_.tensor_tensor(out=X32[:, 4 * g:4 * g + 4, :], in0=X32[:, 4 * g:4 * g + 4, :],
                                                                    in1=ps[:, :].rearrange("p (a t) -> p a t", t=128), op=ALU.add), [psb, X32b], [X32b])
            for g in range(NG):
                P.add('act', lambda e_, g=g: e_.activation(out=Xbf[:, 4 * g:4 * g + 4, :], in_=X32[:, 4 * g:4 * g + 4, :], func=AF.Identity), [X32b], [Xbfb])
            if lev < 6:
                n, nT = nxt[lev % 2]
                for g in range(NG):
                    psA, psAb = next_ps(C)
                    psB, psBb = next_ps(C)
                    for hi in range(4):
                        h = 4 * g + hi
                        P.add('pe', lambda e_, psA=psA, hi=hi, h=h, cur=cur, curT=curT: e_.matmul(psA[:, hi * 128:(hi + 1) * 128], mat[curT][0][:, h, :], mat[cur][0][:, h, :], start=True, stop=True),
                              [mat[cur][1], mat[curT][1]], [psAb])
                        P.add('pe', lambda e_, psB=psB, hi=hi, h=h, cur=cur, curT=curT: e_.matmul(psB[:, hi * 128:(hi + 1) * 128], mat[cur][0][:, h, :], mat[curT][0][:, h, :], start=True, stop=True),
                              [mat[cur][1], mat[curT][1]], [psBb])
                    evac(mat[n][0][:, 4 * g:4 * g + 4, :], psA[:, :].rearrange("p (a t) -> p a t", t=128), [psAb], [mat[n][1]])
                    evac(mat[nT][0][:, 4 * g:4 * g + 4, :], psB[:, :].rearrange("p (a t) -> p a t", t=128), [psBb], [mat[nT][1]])
                cur, curT = n, nT
            yield
        for g in range(NG):
            ps, psb = next_ps(C)
            for hi in range(4):
                h = 4 * g + hi
                P.add('pe', lambda e_, ps=ps, hi=hi, h=h: e_.transpose(ps[0:64, hi * 128:(hi + 1) * 128], X32[:, h, 64:128], C.ident[:]), [X32b, C.b_ident], [psb])
            evac(WT[:, 4 * g:4 * g + 4, :], ps[0:64, :].rearrange("p (a t) -> p a t", t=128), [psb], [WTb])
        yield
        for g in range(NH // 8):
            ps, psb = next_ps(C)
            for hi in range(8):
                h = 8 * g + hi
                P.add('pe', lambda e_, ps=ps, hi=hi, h=h: e_.matmul(ps[:, hi * 64:(hi + 1) * 64], WT[:, h, :], Zbf[:, h, :], start=True, stop=True), [WTb, Zb], [psb])
            P.add('dve', lambda e_, ps=ps, g=g: e_.scalar_tensor_tensor(out=Ubf[:, 8 * g:8 * g + 8, :], in0=ps[:, :].rearrange("p (a i) -> p a i", i=64), scalar=-1.0,
                                                                      in1=X32[:, 8 * g:8 * g + 8, 0:64], op0=ALU.mult, op1=ALU.subtract), [psb, X32b], [Ubb])
        yield
        for g in range(NH // 8):
            ps, psb = next_ps(C)
            for hi in range(8):
                h = 8 * g + hi
                o = ps[:, hi * 64:(hi + 1) * 64]
                P.add('pe', lambda e_, o=o, h=h: e_.matmul(o, tr['Ra'][0][:, h, :], Zbf[:, h, :], start=True, stop=False), [tr['Ra'][1], Zb], [psb])
                P.add('pe', lambda e_, o=o, h=h: e_.matmul(o, mat['MrbT'][0][:, h, :], Ubf[:, h, :], start=False, stop=False), [mat['MrbT'][1], Ubb], [psb])
                P.add('pe', lambda e_, o=o, h=h: e_.matmul(o, mat['MrkT'][0][:, h, :], tm['V'][0][:, h * 64:(h + 1) * 64], start=False, stop=True), [mat['MrkT'][1], tm['V'][1]], [psb])
            evac(Y[:, g * 512:(g + 1) * 512], ps[:, :], [psb], [Yb])
        for g in range(NH // 8):
            ps, psb = next_ps(C)
            for hi in range(8):
                h = 8 * g + hi
                o = ps[0:64, hi * 64:(hi + 1) * 64]
                P.add('pe', lambda e_, o=o, h=h: e_.matmul(o, tm['Bend'][0][:, h * 64:(h + 1) * 64], Ubf[:, h, :], start=True, stop=False), [tm['Bend'][1], Ubb], [psb])
                P.add('pe', lambda e_, o=o, h=h: e_.matmul(o, tm['Kend'][0][:, h * 64:(h + 1) * 64], tm['V'][0][:, h * 64:(h + 1) * 64], start=False, stop=True), [tm['Kend'][1], tm['V'][1]], [psb])
            P.add('dve', lambda e_, g=g: e_.tensor_tensor(out=ztmp[:, 8 * g:8 * g + 8, :], in0=Z32[:, 8 * g:8 * g + 8, :],
                                                         in1=bass.AP(cC, 8 * g, [[NH, 64], [1, 8], [0, 64]]), op=ALU.mult), [Zb, cCb], [ztb])
            P.add('dve', lambda e_, ps=ps, g=g: e_.tensor_tensor(out=Z32[:, 8 * g:8 * g + 8, :], in0=ztmp[:, 8 * g:8 * g + 8, :],
                                                                in1=ps[0:64, :].rearrange("p (a i) -> p a i", i=64), op=ALU.add), [ztb, psb], [Zb])
        P.add('act', lambda e_: e_.activation(out=Zbf[:], in_=Z32[:], func=AF.Identity), [Zb], [Zb])
        yield
        if d == 0:
            P.dma('pool', YF[r, c0:c0 + NW], Y[:], reads=[Yb])
        else:
            P.dma('sp', YFt[:], YF[r, c0:c0 + NW], writes=[YFb])
            P.dma('sp', gat[:], PR['GATE'][r, c0:c0 + NW], writes=[gatb])
            P.dma('sp', bon[:], PR['BON'][r, h0:h0 + NH], writes=[bonb])
            P.add('dve', lambda e_: e_.tensor_tensor(out=Y[:], in0=Y[:], in1=YFt[:], op=ALU.add), [Yb, YFb], [Yb])
            P.add('dve', lambda e_: e_.tensor_reduce(out=s1[:], in_=h3(Y), axis=AX.X, op=ALU.add), [Yb], [stb])
            P.add('pool', lambda e_: e_.tensor_tensor(out=sq[:], in0=Y[:], in1=Y[:], op=ALU.mult), [Yb], [sqb])
            P.add('dve', lambda e_: e_.tensor_reduce(out=s2[:], in_=h3(sq), axis=AX.X, op=ALU.add), [sqb], [stb])
            P.add('dve', lambda e_: e_.tensor_single_scalar(out=s1[:], in_=s1[:], scalar=1.0 / 64, op=ALU.mult), [stb], [stb])
            P.add('dve', lambda e_: e_.tensor_single_scalar(out=s2[:], in_=s2[:], scalar=1.0 / 64, op=ALU.mult), [stb], [stb])
            P.add('dve', lambda e_: e_.tensor_tensor(out=sq[:, 0:NH], in0=s1[:], in1=s1[:], op=ALU.mult), [stb, sqb], [sqb])
            P.add('dve', lambda e_: e_.tensor_tensor(out=s2[:], in0=s2[:], in1=sq[:, 0:NH], op=ALU.subtract), [stb, sqb], [stb])
            P.add('dve', lambda e_: e_.tensor_single_scalar(out=s2[:], in_=s2[:], scalar=A_GN_EPS, op=ALU.add), [stb], [stb])
            P.add('act', lambda e_: e_.activation(out=s2[:], in_=s2[:], func=AF.Sqrt), [stb], [stb])
            P.add('dve', lambda e_: e_.reciprocal(out=s2[:], in_=s2[:]), [stb], [stb])
            P.add('dve', lambda e_: e_.tensor_tensor(out=h3(Y), in0=h3(Y), in1=bc_ap(s1, NH, NH, 64), op=ALU.subtract), [Yb, stb], [Yb])
            P.add('dve', lambda e_: e_.tensor_tensor(out=h3(Y), in0=h3(Y), in1=bc_ap(s2, NH, NH, 64), op=ALU.mult), [Yb, stb], [Yb])
            P.add('pool', lambda e_: e_.tensor_tensor(out=Y[:], in0=Y[:], in1=gng[0][:, c0:c0 + NW], op=ALU.mult), [Yb, gng[1]], [Yb])
            P.add('pool', lambda e_: e_.tensor_tensor(out=Y[:], in0=Y[:], in1=gnb[0][:, c0:c0 + NW], op=ALU.add), [Yb, gnb[1]], [Yb])
            P.add('dve', lambda e_: e_.tensor_tensor(out=h3(sq), in0=h3(inp['V'][0]), in1=bc_ap(bon, NH, NH, 64), op=ALU.mult), [inp['V'][1], bonb, sqb], [sqb])
            P.add('dve', lambda e_: e_.tensor_tensor(out=Y[:], in0=Y[:], in1=sq[:], op=ALU.add), [Yb, sqb], [Yb])
            P.add('dve', lambda e_: e_.tensor_tensor(out=Y[:], in0=Y[:], in1=gat[:], op=ALU.mult), [Yb, gatb], [Yb])
            P.dma('pool', yab[r, c0:c0 + NW], Y[:], reads=[Yb])
        yield

    for ck in order:
        yield from chunk(ck)


RWKV_STAGGER = 0


def rwkv_scan(P, C, nc, W, e, PR, d, YF, yab, T, S, NH=8):
    P.mark('rwkv_scan')
    mark = P.sb_mark()
    K = rwkv_consts(P, C, d)
    gng = gnb = None
    if d == 1:
        gng = load_bc(P, W['ab_gn_g'][e], 1024, 'gng'); gnb = load_bc(P, W['ab_gn_b'][e], 1024, 'gnb')
    nseq = T // S
    m2 = P.sb_mark()
    for h0 in range(0, 16, NH):
        for q0 in range(0, nseq, 2):
            gens = [rwkv_stream(P, C, K, W, e, PR, d, YF, yab, S, q, h0, NH, gng, gnb) for q in range(q0, min(nseq, q0 + 2))]
            live = list(gens)
            if len(gens) > 1:
                for _ in range(RWKV_STAGGER):
                    next(gens[0])
            while live:
                for g in list(live):
                    try:
                        next(g)
                    except StopIteration:
                        live.remove(g)
            P.sb_reset(m2)
    P.sb_reset(mark)


WSPEC = [
    ('ab_w_in', (2, 2048, 6400)), ('ab_shift', (2, 3, 3328)), ('ab_w0', (2, 2, 1024)), ('ab_w_up', (2, 2, 64, 1024)),
    ('ab_a0', (2, 2, 1024)), ('ab_a_up', (2, 2, 64, 1024)), ('ab_g_up', (2, 128, 1024)), ('ab_k_k', (2, 1024)),
    ('ab_k_a', (2, 1024)), ('ab_r_k', (2, 16, 64)), ('ab_gn_g', (2, 1024)), ('ab_gn_b', (2, 1024)),
    ('ab_rpb', (2, 16, 15, 31)), ('ab_w_out', (2, 2048, 2048)), ('c_w_in', (2, 2048, 12288)), ('c_gn_g', (2, 4096)),
    ('c_gn_b', (2, 4096)), ('c_w_out', (2, 4096, 2048)), ('ln1_g', (4, 2048)), ('ln1_b', (4, 2048)),
    ('ffn_w_up', (4, 2048, 11008)), ('ffn_conv', (4, 3, 5504)), ('ffn_conv_b', (4, 5504)), ('ffn_w_down', (4, 5504, 2048)),
    ('ln2_g', (4, 2048)), ('ln2_b', (4, 2048)),
]
BIGW = ['ab_w_in', 'ab_w_out', 'c_w_in', 'c_w_out', 'ffn_w_up', 'ffn_w_down']


def build(NSEQ, S, layers, debug_outs=()):
    T = NSEQ * S
    nc = bass.Bass("TRN2", target_bir_lowering=False)
    W = {}
    for name, shp in WSPEC:
        W[name] = nc.dram_tensor(name, list(shp), F32, kind="ExternalInput").ap()
    x_in = nc.dram_tensor("x", [T, D], F32, kind="ExternalInput").ap()
    y_out = nc.dram_tensor("y", [T, D], F32, kind="ExternalOutput").ap()

    def scr(name, shape, dt):
        kind = "ExternalOutput" if name in debug_outs else "Internal"
        return nc.dram_tensor(name, shape, dt, kind=kind).ap()
    WB = {}
    for name in BIGW:
        shp = dict(WSPEC)[name]
        WB[name] = scr(name + "_b", list(shp), BF16)
    xs = [scr("xs0", [T, D], F32), scr("xs1", [T, D], F32)]
    xTs = [scr("xT0", [D, T], BF16), scr("xT1", [D, T], BF16)]
    hTs = [scr(f"hT{q}", [11008, S], F32) for q in range(NSEQ)]
    vgs = [scr(f"vg{q}", [S, 8192], F32) for q in range(NSEQ)]
    gT = scr("gT", [5504, T], BF16)
    qkr = scr("qkr", [4096, T], BF16)
    ytok = scr("ytok", [T, 4096], F32)
    yT = scr("yT", [4096, T], BF16)
    mix = scr("mix", [T, D], F32)

    P = Prog(nc); C = Ctx()
    setup_consts(P, C)
    TT = min(512, S)
    for l in layers:
        e = l // 2
        if l % 2 == 0:
            cast_weight(P, C, W['ab_w_in'][e], WB['ab_w_in'][e], 2048, 6400)
            cast_weight(P, C, W['ab_w_out'][e], WB['ab_w_out'][e], 2048, 2048)
        else:
            cast_weight(P, C, W['c_w_in'][e], WB['c_w_in'][e], 2048, 12288)
            cast_weight(P, C, W['c_w_out'][e], WB['c_w_out'][e], 4096, 2048)
        cast_weight(P, C, W['ffn_w_up'][l], WB['ffn_w_up'][l], 2048, 11008)
        cast_weight(P, C, W['ffn_w_down'][l], WB['ffn_w_down'][l], 5504, 2048)
    xprep(P, C, x_in, xTs[0], T)
    xcur, xTcur, pp = x_in, xTs[0], 0

    def G(xT_, w_, K, N, mode, out, n_off=0, TG=T):
        m = P.sb_mark()
        gemm(P, C, xT_, w_, K, TG, N, mode, StoreEpi(P, C, out, mode), TT=TT, n_off=n_off)
        P.sb_reset(m)

    def GLN(xT_, w_, K, xin, g_row, b_row, xout, xTout, KS=1):
        m = P.sb_mark()
        C.ps_pool = (6, 0)
        gemm(P, C, xT_, w_, K, T, 2048, 'tok', LNEpi(P, C, xin, g_row, b_row, xout, xTout), TT=TT, KS=KS)
        C.ps_pool = (8, 0)
        P.sb_reset(m)

    for li, l in enumerate(layers):
        e = l // 2
        if l % 2 == 0:
            yabT_ = mixer_ab(P, C, nc, W, WB, e, xTcur, mix, T, S, NSEQ, scr_fn=scr, G=G)
        else:
            for q in range(NSEQ):
                sq = slice(q * S, (q + 1) * S)
                G(xTcur[:, sq], WB['c_w_in'][e], 2048, 4096, 'feat', hTs[q][0:4096, :], TG=S)
                G(xTcur[:, sq], WB['c_w_in'][e], 2048, 8192, 'tok', vgs[q], n_off=4096, TG=S)
                ret_rotary(P, C, hTs[q][0:4096, :], qkr[:, sq], S, S)
                ret_attn(P, C, qkr[:, sq], vgs[q], W['c_gn_g'][e], W['c_gn_b'][e], ytok[sq, :], S, S)
            xprep_n(P, C, ytok, yT, T, 4096)
        x1, x1T = xs[pp], xTs[1 - pp]
        if l % 2 == 0:
            G(yabT_, WB['ab_w_out'][e], 2048, 2048, 'tok', mix)
        else:
            G(yT, WB['c_w_out'][e], 4096, 2048, 'tok', mix)
        ln_phase(P, C, xcur, mix, W['ln1_g'][l], W['ln1_b'][l], x1, x1T, T)
        for q in range(NSEQ):
            sq = slice(q * S, (q + 1) * S)
            G(x1T[:, sq], WB['ffn_w_up'][l], 2048, 11008, 'feat', hTs[q], TG=S)
            ffn_mid(P, C, hTs[q], W['ffn_conv'][l], W['ffn_conv_b'][l], gT[:, sq], S, S)
        last = (li == len(layers) - 1)
        x2 = y_out if last else xs[1 - pp]
        x2T = xTs[pp]
        G(gT, WB['ffn_w_down'][l], 5504, 2048, 'tok', mix)
        ln_phase(P, C, x1, mix, W['ln2_g'][l], W['ln2_b'][l], x2, None if last else x2T, T)
        xcur, xTcur = x2, x2T
        pp = 1 - pp
    P.emit()
    return nc, P


def mixer_ab(P, C, nc, W, WB, e, xTcur, mix, T, S, NSEQ, scr_fn, G):
    if not hasattr(C, 'ab_scr'):
        d = {}
        d['hA'] = scr_fn("hA", [T, 3328], F32)
        d['hqk'] = scr_fn("hqk", [2048, T], F32)
        d['hv'] = scr_fn("hv", [T, 1024], F32)
        d['yab'] = scr_fn("yab", [T, 2048], F32)
        d['yabT'] = scr_fn("yabT", [2048, T], BF16)
        d['YF'] = scr_fn("YF", [T, 1024], F32)
        d['PR'] = {k: scr_fn("pr_" + k, [T, 1024], F32) for k in ('R', 'V', 'KAP', 'GATE', 'LW0', 'LW1', 'B0', 'B1', 'KD0', 'KD1')}
        d['PR']['BON'] = scr_fn("pr_BON", [T, 16], F32)
        C.ab_scr = d
    d = C.ab_scr
    G(xTcur, WB['ab_w_in'][e], 2048, 3328, 'tok', d['hA'])
    G(xTcur, WB['ab_w_in'][e], 2048, 2048, 'feat', d['hqk'], n_off=3328)
    G(xTcur, WB['ab_w_in'][e], 2048, 1024, 'tok', d['hv'], n_off=3328 + 2048)
    rwkv_prep(P, C, nc, W, e, d['hA'], d['PR'], T, S)
    rwkv_scan(P, C, nc, W, e, d['PR'], 0, d['YF'], d['yab'], T, S)
    rwkv_scan(P, C, nc, W, e, d['PR'], 1, d['YF'], d['yab'], T, S)
    bS = na_bias_build(P, C, nc, W['ab_rpb'][e], scr_fn, e)
    na_attn(P, C, nc, bS, d['hqk'], d['hv'], d['yab'], T, S)
    xprep_n(P, C, d['yab'], d['yabT'], T, 2048)
    return d['yabT']


_CACHE = {}
NSEQ_CORE = 2
SEQ = 4096


def _assign():
    return [(('p', c), ('s', c) if c < 4 else ('p', c)) for c in range(8)]


def kernel(**inputs):
    if 'nc' not in _CACHE:
        _CACHE['nc'] = build(NSEQ_CORE, SEQ, [0, 1, 2, 3])[0]
    nc = _CACHE['nc']
    xp = np.asarray(inputs['x_prompt'], dtype=np.float32)
    xs_ = np.asarray(inputs['x_sample'], dtype=np.float32)
    wd = {n: np.ascontiguousarray(np.asarray(inputs[n], dtype=np.float32)) for n, _ in WSPEC}
    in_maps = []
    asg = _assign()
    for c in range(8):
        rows = []
        for kind, i in asg[c]:
            rows.append(xp[i] if kind == 'p' else xs_[i])
        m = dict(wd)
        m['x'] = np.ascontiguousarray(np.concatenate(rows, axis=0))
        in_maps.append(m)
    res = run_bass_kernel_spmd(nc, in_maps, core_ids=list(range(8)))
    yp = np.empty_like(xp)
    ys = np.empty_like(xs_)
    for c in range(8):
        y = np.asarray(res.results[c]['y'])
        for slot, (kind, i) in enumerate(asg[c]):
            blk = y[slot * SEQ:(slot + 1) * SEQ]
            if slot == 1 and c >= 4:
                continue
            if kind == 'p':
                yp[i] = blk
            else:
                ys[i] = blk
    return (yp, ys)
```
